# Optimizing a Trainium2 kernel written in Bass

```python
import jax, jax.numpy as jnp
from jax import lax
import numpy as np

D_MODEL = 1024
BATCH = 2
SEQ = 8192
DEPTH = 2

GRID_W = 64
N_MEM = 256
EPS = 1e-6
POOL_WIDTH = D_MODEL // 2
POOL_GROUPS = 4
POOL_GROUP_W = POOL_WIDTH // POOL_GROUPS
POOL_WINDOWS = (2, 4, 8, 16)
MLA_HEADS = 8
MLA_NOPE = 64
MLA_ROPE = 32
MLA_V = 64
MLA_QK = MLA_NOPE + MLA_ROPE
Q_LORA = D_MODEL // 4
KV_LORA = D_MODEL // 8
ROPE_THETA = 10000.0
Q_BLOCK = 128
W_IN_EVEN = POOL_WIDTH + Q_LORA + KV_LORA + MLA_ROPE
MIX_EVEN = POOL_WIDTH + MLA_HEADS * MLA_V
NA_HEADS = 16
NA_HEAD_DIM = 64
NA_KH = 8
NA_KW = 16
NA_WIDTH = NA_HEADS * NA_HEAD_DIM
MEM_HEADS = 4
MEM_HEAD_DIM = D_MODEL // MEM_HEADS
D_FF = 4 * D_MODEL

kernel_name = "hybrid_pool_mla_natten_encoder"


def rms_norm(x, g):
    xf = x.astype(jnp.float32)
    y = xf * lax.rsqrt(jnp.mean(xf * xf, axis=-1, keepdims=True) + EPS)
    return (y * g.astype(jnp.float32)).astype(x.dtype)


def rotary(x, pos):
    half = x.shape[-1] // 2
    freqs = ROPE_THETA ** (-jnp.arange(half, dtype=jnp.float32) / half)
    ang = pos[:, None] * freqs[None, :]
    cos = jnp.cos(ang)[None, :, None, :].astype(x.dtype)
    sin = jnp.sin(ang)[None, :, None, :].astype(x.dtype)
    x1, x2 = x[..., :half], x[..., half:]
    return jnp.concatenate([x1 * cos - x2 * sin, x1 * sin + x2 * cos], axis=-1)


def pool_mixer(u, pool_w, pool_scale):
    B, S, _ = u.shape
    ug = u.reshape(B, S, POOL_GROUPS, POOL_GROUP_W).astype(jnp.float32)
    cs = jnp.concatenate([jnp.zeros((B, 1, POOL_GROUPS, POOL_GROUP_W), jnp.float32),
                          jnp.cumsum(ug, axis=1)], axis=1)
    t = jnp.arange(S)
    outs = []
    for g, w in enumerate(POOL_WINDOWS):
        lo = jnp.clip(t - w // 2, 0, S - 1)
        hi = jnp.clip(t + w - 1 - w // 2, 0, S - 1)
        sums = cs[:, hi + 1, g] - cs[:, lo, g]
        cnt = (hi - lo + 1).astype(jnp.float32)[None, :, None]
        outs.append(sums / cnt - ug[:, :, g])
    d = jnp.stack(outs, axis=2).astype(u.dtype)
    y = jnp.einsum('bsgc,gcd->bsgd', d, pool_w).reshape(B, S, POOL_WIDTH)
    return y * pool_scale


def blocked_attention(q, k, v):
    B, S, H, Dk = q.shape
    nb = S // Q_BLOCK
    scale = Dk ** -0.5
    qb = q.reshape(B, nb, Q_BLOCK, H, Dk).transpose(1, 0, 2, 3, 4)

    def one_block(qi):
        s = jnp.einsum('bqhd,bkhd->bhqk', qi, k).astype(jnp.float32) * scale
        p = jax.nn.softmax(s, axis=-1).astype(v.dtype)
        return jnp.einsum('bhqk,bkhd->bqhd', p, v)

    o = lax.map(one_block, qb)
    return o.transpose(1, 0, 2, 3, 4).reshape(B, S, H, v.shape[-1])


def mla_mixer(c_q, c_kv, k_r, q_lora_g, w_uq, kv_lora_g, w_ukv, q_g, k_g):
    B, S, _ = c_q.shape
    q = (rms_norm(c_q, q_lora_g) @ w_uq).reshape(B, S, MLA_HEADS, MLA_QK)
    kv = (rms_norm(c_kv, kv_lora_g) @ w_ukv).reshape(B, S, MLA_HEADS, MLA_NOPE + MLA_V)
    k_nope, v = kv[..., :MLA_NOPE], kv[..., MLA_NOPE:]
    k = jnp.concatenate([k_nope, jnp.broadcast_to(k_r[:, :, None, :], (B, S, MLA_HEADS, MLA_ROPE))], axis=-1)
    q = rms_norm(q, q_g)
    k = rms_norm(k, k_g)
    pos = jnp.arange(S, dtype=jnp.float32)
    q = jnp.concatenate([q[..., :MLA_NOPE], rotary(q[..., MLA_NOPE:], pos)], axis=-1)
    k = jnp.concatenate([k[..., :MLA_NOPE], rotary(k[..., MLA_NOPE:], pos)], axis=-1)
    o = blocked_attention(q, k, v)
    return o.reshape(B, S, MLA_HEADS * MLA_V)


def neighbourhood_attention(q, k, v, rpb):
    B, S, H, Dh = q.shape
    rows = S // GRID_W
    kh = min(NA_KH, rows)
    kw = NA_KW
    scale = Dh ** -0.5
    qg = q.reshape(B, rows, GRID_W, H, Dh).transpose(1, 0, 2, 3, 4)
    kg = k.reshape(B, rows, GRID_W, H, Dh)
    vg = v.reshape(B, rows, GRID_W, H, Dh)
    cols = jnp.arange(GRID_W)
    c0 = jnp.clip(cols - kw // 2, 0, GRID_W - kw)
    col_idx = c0[:, None] + jnp.arange(kw)[None, :]
    dc_idx = col_idx - cols[:, None] + (NA_KW - 1)

    def one_row(args):
        r, q_row = args
        r0 = jnp.clip(r - kh // 2, 0, rows - kh)
        k_rows = lax.dynamic_slice_in_dim(kg, r0, kh, axis=1)
        v_rows = lax.dynamic_slice_in_dim(vg, r0, kh, axis=1)
        k_win = k_rows[:, :, col_idx]
        v_win = v_rows[:, :, col_idx]
        s = jnp.einsum('bchd,brckhd->bhcrk', q_row, k_win).astype(jnp.float32) * scale
        dr_idx = r0 + jnp.arange(kh) - r + (NA_KH - 1)
        bias = rpb[:, dr_idx[:, None, None], dc_idx[None, :, :]]
        s = s + bias.transpose(0, 2, 1, 3)[None].astype(jnp.float32)
        p = jax.nn.softmax(s.reshape(B, H, GRID_W, kh * kw), axis=-1)
        p = p.reshape(B, H, GRID_W, kh, kw).astype(v.dtype)
        return jnp.einsum('bhcrk,brckhd->bchd', p, v_win)

    o = lax.map(one_row, (jnp.arange(rows), qg))
    return o.transpose(1, 0, 2, 3, 4).reshape(B, S, H * Dh)


def memory_cross_attention(h, mem_k, mem_v, w_q, q_g, w_o):
    B, S, _ = h.shape
    q = rms_norm((h @ w_q).reshape(B, S, MEM_HEADS, MEM_HEAD_DIM), q_g)
    s = jnp.einsum('bshd,bmhd->bhsm', q, mem_k).astype(jnp.float32) * (MEM_HEAD_DIM ** -0.5)
    p = jax.nn.softmax(s, axis=-1).astype(mem_v.dtype)
    o = jnp.einsum('bhsm,bmhd->bshd', p, mem_v).reshape(B, S, MEM_HEADS * MEM_HEAD_DIM)
    return o @ w_o


def squared_relu_mlp(h, w1, w2):
    a = jax.nn.relu(h @ w1)
    return (a * a) @ w2


def setup_inputs(seed: int = 0) -> dict:
    key = jax.random.key(seed)
    ks = iter(jax.random.split(key, 40))
    n_even = (DEPTH + 1) // 2
    n_odd = DEPTH // 2

    def w(shape, fan_in):
        return jax.random.normal(next(ks), shape, jnp.float32) * (fan_in ** -0.5)

    def gain(shape):
        return 1.0 + 0.02 * jax.random.normal(next(ks), shape, jnp.float32)

    return {
        "x": jax.random.normal(next(ks), (BATCH, SEQ, D_MODEL), jnp.float32),
        "mem": jax.random.normal(next(ks), (BATCH, N_MEM, D_MODEL), jnp.float32),
        "mix_norm_g": gain((DEPTH, D_MODEL)),
        "xattn_norm_g": gain((DEPTH, D_MODEL)),
        "ff_norm_g": gain((DEPTH, D_MODEL)),
        "w_mem_q": w((DEPTH, D_MODEL, MEM_HEADS * MEM_HEAD_DIM), D_MODEL),
        "mem_q_g": gain((DEPTH, MEM_HEAD_DIM)),
        "w_mem_o": w((DEPTH, MEM_HEADS * MEM_HEAD_DIM, D_MODEL), MEM_HEADS * MEM_HEAD_DIM),
        "w_ff1": w((DEPTH, D_MODEL, D_FF), D_MODEL),
        "w_ff2": w((DEPTH, D_FF, D_MODEL), D_FF),
        "mem_tok_norm_g": gain((D_MODEL,)),
        "w_mem_kv": w((D_MODEL, 2 * MEM_HEADS * MEM_HEAD_DIM), D_MODEL),
        "mem_k_g": gain((MEM_HEAD_DIM,)),
        "w_in_e": w((n_even, D_MODEL, W_IN_EVEN), D_MODEL),
        "pool_w": w((n_even, POOL_GROUPS, POOL_GROUP_W, POOL_GROUP_W), POOL_GROUP_W),
        "pool_scale": gain((n_even, POOL_WIDTH)),
        "q_lora_g": gain((n_even, Q_LORA)),
        "w_uq": w((n_even, Q_LORA, MLA_HEADS * MLA_QK), Q_LORA),
        "kv_lora_g": gain((n_even, KV_LORA)),
        "w_ukv": w((n_even, KV_LORA, MLA_HEADS * (MLA_NOPE + MLA_V)), KV_LORA),
        "mla_q_g": gain((n_even, MLA_QK)),
        "mla_k_g": gain((n_even, MLA_QK)),
        "w_out_e": w((n_even, MIX_EVEN, D_MODEL), MIX_EVEN),
        "w_qkv_o": w((n_odd, D_MODEL, 3 * NA_WIDTH), D_MODEL),
        "na_q_g": gain((n_odd, NA_HEAD_DIM)),
        "na_k_g": gain((n_odd, NA_HEAD_DIM)),
        "na_rpb": 0.1 * jax.random.normal(next(ks), (n_odd, NA_HEADS, 2 * NA_KH - 1, 2 * NA_KW - 1), jnp.float32),
        "w_out_o": w((n_odd, NA_WIDTH, D_MODEL), NA_WIDTH),
    }


def reference(x, mem, mix_norm_g, xattn_norm_g, ff_norm_g, w_mem_q, mem_q_g, w_mem_o,
              w_ff1, w_ff2, mem_tok_norm_g, w_mem_kv, mem_k_g, w_in_e, pool_w, pool_scale,
              q_lora_g, w_uq, kv_lora_g, w_ukv, mla_q_g, mla_k_g, w_out_e, w_qkv_o,
              na_q_g, na_k_g, na_rpb, w_out_o):
    B, S, _ = x.shape
    mkv = (rms_norm(mem, mem_tok_norm_g) @ w_mem_kv).reshape(B, mem.shape[1], 2, MEM_HEADS, MEM_HEAD_DIM)
    mem_k = rms_norm(mkv[:, :, 0], mem_k_g)
    mem_v = mkv[:, :, 1]

    for i in range(DEPTH):
        h = rms_norm(x, mix_norm_g[i])
        if i % 2 == 0:
            e = i // 2
            u = h @ w_in_e[e]
            o1 = POOL_WIDTH
            o2 = o1 + Q_LORA
            o3 = o2 + KV_LORA
            a = pool_mixer(u[..., :o1], pool_w[e], pool_scale[e])
            b = mla_mixer(u[..., o1:o2], u[..., o2:o3], u[..., o3:],
                          q_lora_g[e], w_uq[e], kv_lora_g[e], w_ukv[e], mla_q_g[e], mla_k_g[e])
            x = x + jnp.concatenate([a, b], axis=-1) @ w_out_e[e]
        else:
            o = i // 2
            qkv = (h @ w_qkv_o[o]).reshape(B, S, 3, NA_HEADS, NA_HEAD_DIM)
            q = rms_norm(qkv[:, :, 0], na_q_g[o])
            k = rms_norm(qkv[:, :, 1], na_k_g[o])
            v = qkv[:, :, 2]
            c = neighbourhood_attention(q, k, v, na_rpb[o])
            x = x + c @ w_out_o[o]
        x = x + memory_cross_attention(rms_norm(x, xattn_norm_g[i]), mem_k, mem_v,
                                       w_mem_q[i], mem_q_g[i], w_mem_o[i])
        x = x + squared_relu_mlp(rms_norm(x, ff_norm_g[i]), w_ff1[i], w_ff2[i])
    return x
```

```python
import concourse.bass as bass
import concourse.mybir as mybir

ENGS = ("pe", "act", "dve", "pool", "sp")
SAME_ENG_SYNC = {"pe": False, "act": True, "dve": True, "pool": True, "sp": False}
NSLOT = 12


class Op:
    __slots__ = ("id", "eng", "fn", "deps", "dma", "flag", "sem", "val", "slot_guard", "nwaits")

    def __init__(self, id, eng, fn, dma):
        self.id = id
        self.eng = eng
        self.fn = fn
        self.dma = dma
        self.deps = []
        self.flag = False
        self.sem = None
        self.val = None
        self.slot_guard = None


class Prog:
    def __init__(self):
        self.ops = []
        self.by_eng = {e: [] for e in ENGS}
        self.last_w = {}
        self.readers = {}
        self.dma_count = {e: 0 for e in ENGS}

    _cap = None

    def capture(self):
        self._cap = []

    def end_capture(self):
        c = self._cap
        self._cap = None
        return c

    def mark(self):
        self._mark = len(self._cap)

    def replay(self, lst):
        for it in lst:
            self.op(*it)

    @staticmethod
    def zipmerge(a, b):
        out = []
        na, nb = len(a), len(b)
        ia = ib = 0
        while ia < na or ib < nb:
            if ib >= nb or (ia < na and ia * max(nb, 1) <= ib * max(na, 1)):
                out.append(a[ia]); ia += 1
            else:
                out.append(b[ib]); ib += 1
        return out

    def pipeline(self, n, tile_fn, split=0.5):
        prevB = []
        for t in range(n):
            self.capture()
            self._mark = None
            tile_fn(t)
            L = self.end_capture()
            h = self._mark if self._mark is not None else int(len(L) * split)
            self.replay(self.zipmerge(L[:h], prevB))
            prevB = L[h:]
        self.replay(prevB)

    def op(self, eng, fn, reads=(), writes=(), dma=False):
        if self._cap is not None:
            self._cap.append((eng, fn, reads, writes, dma))
            return None
        o = Op(len(self.ops), eng, fn, dma)
        reads = list(reads)
        writes = list(writes) + [r for r in reads if isinstance(r, str) and r.startswith("bank")]
        deps = set()
        for r in reads:
            w = self.last_w.get(r)
            if w is not None:
                deps.add(w)
        for w_ in writes:
            w = self.last_w.get(w_)
            if w is not None:
                deps.add(w)
            for rd in self.readers.get(w_, ()):
                deps.add(rd)
        deps.discard(o.id)
        o.deps = sorted(deps)
        for r in reads:
            self.readers.setdefault(r, []).append(o.id)
        for w_ in writes:
            self.last_w[w_] = o.id
            self.readers[w_] = []
        self.ops.append(o)
        self.by_eng[eng].append(o)
        return o

    def pe(self, fn, reads=(), writes=()):
        return self.op("pe", fn, reads, writes)

    def act(self, fn, reads=(), writes=()):
        return self.op("act", fn, reads, writes)

    def dve(self, fn, reads=(), writes=()):
        return self.op("dve", fn, reads, writes)

    def pool(self, fn, reads=(), writes=()):
        return self.op("pool", fn, reads, writes)

    def dma(self, eng, fn, reads=(), writes=()):
        return self.op(eng, fn, reads, writes, dma=True)

    def emit(self, nc, final_ops=()):
        ops = self.ops
        for o in ops:
            for d in o.deps:
                do = ops[d]
                if do.dma or do.eng != o.eng or SAME_ENG_SYNC[o.eng]:
                    do.flag = True
        for o in final_ops:
            o.flag = True
        import contextlib
        with contextlib.ExitStack() as es:
            esem = {e: es.enter_context(nc.semaphore("s_" + e)) for e in ENGS}
            dsem = {}
            for e in ENGS:
                if self.dma_count_total(e) > 0:
                    dsem[e] = [es.enter_context(nc.semaphore("d_%s_%d" % (e, i))) for i in range(NSLOT)]
            cnt = {e: 0 for e in ENGS}
            dcnt = {e: 0 for e in ENGS}
            semkey = {}
            for o in ops:
                if o.dma:
                    j = dcnt[o.eng]
                    dcnt[o.eng] += 1
                    s = dsem[o.eng][j % NSLOT]
                    o.sem = s
                    o.val = 16 * (j // NSLOT + 1)
                    o.slot_guard = (s, 16 * (j // NSLOT)) if j >= NSLOT else None
                    semkey[id(s)] = ("d", o.eng, j % NSLOT)
                elif o.flag:
                    cnt[o.eng] += 1
                    o.sem = esem[o.eng]
                    o.val = cnt[o.eng]
            clock = {e: {} for e in ENGS}
            opvc = [None] * len(ops)
            plan = {e: [] for e in ENGS}
            for o in ops:
                ck = clock[o.eng]
                waits = {}
                for d in o.deps:
                    do = ops[d]
                    if not (do.dma or do.eng != o.eng or SAME_ENG_SYNC[o.eng]):
                        continue
                    k = id(do.sem)
                    if ck.get(k, 0) >= do.val:
                        continue
                    if k not in waits or waits[k][1] < do.val:
                        waits[k] = (do.sem, do.val, d)
                if o.slot_guard is not None:
                    s, v = o.slot_guard
                    k = id(s)
                    if ck.get(k, 0) < v and (k not in waits or waits[k][1] < v):
                        waits[k] = (s, v, None)
                wl = list(waits.items())
                keep = []
                for k, (s, v, d) in wl:
                    implied = False
                    for k2, (s2, v2, d2) in wl:
                        if k2 == k or d2 is None:
                            continue
                        vc2 = opvc[d2]
                        if vc2 is not None and vc2.get(k, 0) >= v:
                            implied = True
                            break
                    if not implied:
                        keep.append((s, v))
                for k, (s, v, d) in wl:
                    if ck.get(k, 0) < v:
                        ck[k] = v
                    if d is not None and opvc[d] is not None:
                        for kk, vv in opvc[d].items():
                            if ck.get(kk, 0) < vv:
                                ck[kk] = vv
                if o.sem is not None:
                    vc = dict(ck)
                    vc[id(o.sem)] = o.val
                    opvc[o.id] = vc
                    if not o.dma:
                        pass
                plan[o.eng].append((o, keep))
            self.stats = {e: (len(plan[e]), sum(len(k) for _, k in plan[e])) for e in ENGS}

            def run(eng_handle, lst, final_waits):
                for o, keep in lst:
                    for (s, v) in keep[1:]:
                        eng_handle.wait_ge(s, v)
                    ins = o.fn(eng_handle)
                    if keep:
                        s, v = keep[0]
                        if isinstance(ins, tuple):
                            ins[0]._wait_ge(s, v)
                        else:
                            ins._wait_ge(s, v)
                    if o.sem is not None:
                        last = ins[1] if isinstance(ins, tuple) else ins
                        last.then_inc(o.sem, 16 if o.dma else 1)
                for (s, v) in final_waits:
                    eng_handle.wait_ge(s, v)

            fw = [(o.sem, o.val) for o in final_ops]
            with nc.Block() as block:
                @block.tensor
                def _(e):
                    run(e, plan["pe"], [])

                @block.scalar
                def _(e):
                    run(e, plan["act"], [])

                @block.vector
                def _(e):
                    run(e, plan["dve"], [])

                @block.gpsimd
                def _(e):
                    run(e, plan["pool"], [])

                @block.sync
                def _(e):
                    run(e, plan["sp"], fw)

    def dma_count_total(self, e):
        return sum(1 for o in self.by_eng[e] if o.dma)


import contextlib
import numpy as np
from concourse.bass_utils import run_bass_kernel_spmd

F32 = mybir.dt.float32
BF16 = mybir.dt.bfloat16
AF = mybir.ActivationFunctionType
ALU = mybir.AluOpType
AX = mybir.AxisListType

D = 1024
SEQ = 8192
WT = 2560
NT = 20
NB = 5
NTB = 64
EPS = 1e-6
NEG = -30000.0


def build_program(stage=99):
    nc = bass.Bass("TRN2", target_bir_lowering=False)
    P = Prog()
    G = contextlib.ExitStack()
    uid = [0]
    bar_from = [0]

    def di(name, shape, dt=F32):
        return nc.dram_tensor(name, list(shape), dt, kind="ExternalInput").ap()

    def sb(st, name, shape, dt):
        uid[0] += 1
        return st.enter_context(nc.sbuf_tensor("%s_%d" % (name, uid[0]), list(shape), dt))

    def barrier():
        lasts = [P.by_eng[e][-1].id for e in ENGS if P.by_eng[e]]
        dmas = [o.id for o in P.ops[bar_from[0]:] if o.dma]
        bar_from[0] = len(P.ops)
        deps = sorted(set(lasts + dmas))
        for e in ENGS:
            o = P.op(e, lambda eng: eng.nop())
            o.deps = list(deps)
        P.last_w = {}
        P.readers = {}

    xw = di("xw", [WT, D]); xh = di("xh", [16, D]); xb = di("xb", [SEQ, D]); mem = di("mem", [256, D])
    mix_norm_g = di("mix_norm_g", [2, D]); xattn_norm_g = di("xattn_norm_g", [2, D]); ff_norm_g = di("ff_norm_g", [2, D])
    w_mem_q = di("w_mem_q", [2, D, D]); mem_q_g = di("mem_q_g", [2, 256]); w_mem_o = di("w_mem_o", [2, D, D])
    w_ff1 = di("w_ff1", [2, D, 4096]); w_ff2 = di("w_ff2", [2, 4096, D])
    mem_tok_norm_g = di("mem_tok_norm_g", [D]); w_mem_kv = di("w_mem_kv", [D, 2048]); mem_k_g = di("mem_k_g", [256])
    w_in_e = di("w_in_e", [1, D, 928]); pool_w = di("pool_w", [1, 4, 128, 128]); pool_scale = di("pool_scale", [1, 512])
    q_lora_g = di("q_lora_g", [1, 256]); w_uq = di("w_uq", [1, 256, 768]); kv_lora_g = di("kv_lora_g", [1, 128])
    w_ukv = di("w_ukv", [1, 128, 1024]); mla_q_g = di("mla_q_g", [1, 96]); mla_k_g = di("mla_k_g", [1, 96])
    w_out_e = di("w_out_e", [1, D, D]); w_qkv_o = di("w_qkv_o", [1, D, 3072])
    na_q_g = di("na_q_g", [1, 64]); na_k_g = di("na_k_g", [1, 64]); w_out_o = di("w_out_o", [1, D, D])
    cs_w = di("cs_w", [WT, 64]); cs_b = di("cs_b", [SEQ, 64]); invc = di("invc", [4, 2568])
    nab = di("nab", [16, 128, 25, 128])
    out = nc.dram_tensor("out", [WT, D], F32, kind="ExternalOutput").ap()
    kT_scr = nc.dram_tensor("kT_scr", [8, 96, SEQ], BF16, kind="Internal").ap()
    v_scr = nc.dram_tensor("v_scr", [8, 128, NTB, 64], BF16, kind="Internal").ap()
    qT_scr = nc.dram_tensor("qT_scr", [8, 96, WT], BF16, kind="Internal").ap()
    nq_scr = nc.dram_tensor("nq_scr", [8, 128, WT], BF16, kind="Internal").ap()
    nk_scr = nc.dram_tensor("nk_scr", [8, 128, WT], BF16, kind="Internal").ap()
    nv_scr = nc.dram_tensor("nv_scr", [8, 128, NT, 128], BF16, kind="Internal").ap()

    x = sb(G, "x", [128, NT, D], F32)
    ident = sb(G, "ident", [128, 128], BF16)
    identf = sb(G, "identf", [128, 128], F32)
    ones_b = sb(G, "ones_b", [128, 128], BF16)
    eps_t = sb(G, "eps", [128, 1], F32)
    memkT = sb(G, "memkT", [128, 4, 2, 256], BF16)
    memv = sb(G, "memv", [128, 2, 4, 256], BF16)
    SQ = [sb(G, "sq", [128, 1024], F32), sb(G, "sq", [128, 1024], F32)]
    SSQ = [sb(G, "ssq", [128, 1], F32) for _ in range(2)]
    RSTD = [sb(G, "rstd", [128, 1], F32) for _ in range(2)]
    HB = [sb(G, "hb", [128, D], BF16) for _ in range(2)]
    SS32 = [sb(G, "ss32", [128, 32], F32) for _ in range(2)]
    RS32 = [sb(G, "rs32", [128, 32], F32) for _ in range(2)]
    sq, ssq, rstd, hb, ss32, rs32 = SQ[0], SSQ[0], RSTD[0], HB[0], SS32[0], RS32[0]
    gn = sb(G, "gn", [128, D], F32)
    banks = [G.enter_context(nc.psum_tensor("bank%d" % i, [128, 512], F32)) for i in range(8)]
    B = ["bank%d" % i for i in range(8)]

    def bbf(i):
        return banks[i][:].bitcast(BF16)

    act, dve, pe, pool = P.act, P.dve, P.pe, P.pool

    def rstd_from(ss_ap, ss_res, dim, out_ap, out_res):
        n = ss_ap.shape[0]
        act(lambda e: e.activation(out=out_ap, in_=ss_ap, func=AF.Ln, scale=1.0 / dim, bias=eps_t[0:n, :]),
            reads=[ss_res, "eps"], writes=[out_res])
        act(lambda e: e.activation(out=out_ap, in_=out_ap, func=AF.Exp, scale=-0.5), reads=[out_res], writes=[out_res])

    def norm_hT(src_ap, src_res, hT_dst, hT_res, n=128, par=0, bank=0):
        sq_, ssq_, rstd_, hb_ = SQ[par], SSQ[par], RSTD[par], HB[par]
        sfx = "" if par == 0 else "_p1"
        act(lambda e: e.activation(out=sq_[0:n, 0:D], in_=src_ap, func=AF.Square, accum_out=ssq_[0:n, :]),
            reads=[src_res], writes=["sq" + sfx, "ssq" + sfx])
        rstd_from(ssq_[0:n, :], "ssq" + sfx, D, rstd_[0:n, :], "rstd" + sfx)
        dve(lambda e: e.scalar_tensor_tensor(out=hb_[0:n, :], in0=src_ap, scalar=rstd_[0:n, 0:1], in1=gn[0:n, :],
                                             op0=ALU.mult, op1=ALU.mult),
            reads=[src_res, "rstd" + sfx, "gn"], writes=["hb" + sfx])
        pT = bbf(bank)
        for k in range(8):
            pe(lambda e, k=k: e.transpose(out=pT[:, k * n:(k + 1) * n], in_=hb_[0:n, k * 128:(k + 1) * 128],
                                          identity=ident[0:n, 0:n]),
               reads=["hb" + sfx, "ident"], writes=[B[bank]])
        act(lambda e: e.copy(out=hT_dst, in_=pT[:, 0:8 * n].rearrange("p (k t) -> p k t", k=8)),
            reads=[B[bank]], writes=[hT_res])

    def load_w(dst, w_ap, res, eng="pool"):
        K = w_ap.shape[0]
        for k in range((K + 127) // 128):
            r = min(128, K - k * 128)
            P.dma(eng, lambda e, k=k, r=r: e.dma_start(out=dst[0:r, k, :], in_=w_ap[k * 128:k * 128 + r, :]),
                  writes=[(res, k)])

    def wres(res, nk):
        return [(res, k) for k in range(nk)]

    def load_bc(dst, vec_ap, res, eng="sp"):
        P.dma(eng, lambda e: e.dma_start(out=dst, in_=vec_ap.partition_broadcast(128)), writes=[res])

    def resid_add(t, pbank_lo, pbank_hi):
        for half, bk in ((0, pbank_lo), (1, pbank_hi)):
            dve(lambda e, half=half, bk=bk: e.tensor_tensor(out=x[:, t, half * 512:(half + 1) * 512],
                                                            in0=banks[bk][:], in1=x[:, t, half * 512:(half + 1) * 512],
                                                            op=ALU.add),
                reads=[B[bk], ("x", t)], writes=[("x", t)])

    def proj_resid(t, lhs_fn, lhs_res, nk, w_sb, w_res, k0=0):
        for half in range(2):
            for k in range(nk):
                pe(lambda e, half=half, k=k: e.matmul(banks[6 + half][:], lhsT=lhs_fn(k), rhs=w_sb[:, k0 + k, half * 512:(half + 1) * 512],
                                                       start=(k == 0), stop=(k == nk - 1)),
                   reads=lhs_res + [(w_res, k0 + k)], writes=[B[6 + half]])
        resid_add(t, 6, 7)

    pool(lambda e: e.memset(identf[:], 0.0), writes=["identf"])
    pool(lambda e: e.affine_select(out=identf[:], in_=identf[:], pattern=[[-1, 128]], compare_op=ALU.not_equal,
                                   fill=1.0, base=0, channel_multiplier=1), reads=["identf"], writes=["identf"])
    dve(lambda e: e.tensor_copy(out=ident[:], in_=identf[:]), reads=["identf"], writes=["ident"])
    dve(lambda e: e.memset(ones_b[:], 1.0), writes=["ones_b"])
    dve(lambda e: e.memset(eps_t[:], EPS), writes=["eps"])
    for t in range(NT):
        P.dma("sp", lambda e, t=t: e.dma_start(out=x[:, t, :], in_=xw[t * 128:(t + 1) * 128, :]), writes=[("x", t)])
    barrier()

    def phase_mem():
        with contextlib.ExitStack() as S:
            wkv = sb(S, "wkv", [128, 8, 2048], BF16)
            gk = sb(S, "gk", [128, 256], F32)
            memt = sb(S, "memt", [128, D], F32)
            hTm = sb(S, "hTm", [128, 8, 128], BF16)
            kn = sb(S, "kn", [128, D], F32)
            kbm = sb(S, "kbm", [128, D], BF16)
            load_w(wkv, w_mem_kv, "wkv")
            load_bc(gn[:], mem_tok_norm_g, "gn")
            load_bc(gk[:], mem_k_g, "gk")
            for m in range(2):
                P.dma("sp", lambda e, m=m: e.dma_start(out=memt[:], in_=mem[m * 128:(m + 1) * 128, :]), writes=["memt"])
                norm_hT(memt[:], "memt", hTm[:], "hTm")
                for n4 in range(4):
                    for k in range(8):
                        pe(lambda e, n4=n4, k=k: e.matmul(banks[1 + n4][:], lhsT=hTm[:, k, :], rhs=wkv[:, k, n4 * 512:(n4 + 1) * 512],
                                                          start=(k == 0), stop=(k == 7)),
                           reads=["hTm", ("wkv", k)], writes=[B[1 + n4]])
                for j in range(2):
                    act(lambda e, j=j: e.activation(out=sq[:, j * 512:(j + 1) * 512], in_=banks[1 + j][:], func=AF.Square),
                        reads=[B[1 + j]], writes=["sq"])
                dve(lambda e: e.tensor_reduce(out=ss32[:, 0:4], in_=sq[:, 0:D].rearrange("p (h d) -> p h d", h=4), axis=AX.X, op=ALU.add),
                    reads=["sq"], writes=["ss32"])
                rstd_from(ss32[:, 0:4], "ss32", 256, rs32[:, 0:4], "rs32")
                for j in range(2):
                    dve(lambda e, j=j: e.tensor_tensor(out=kn[:, j * 512:(j + 1) * 512].rearrange("p (h d) -> p h d", h=2),
                                                       in0=banks[1 + j][:].rearrange("p (h d) -> p h d", h=2),
                                                       in1=rs32[:, 2 * j:2 * j + 2].unsqueeze(2).to_broadcast([128, 2, 256]), op=ALU.mult),
                        reads=[B[1 + j], "rs32"], writes=["kn"])
                dve(lambda e: e.tensor_tensor(out=kbm[:].rearrange("p (h d) -> p h d", h=4), in0=kn[:].rearrange("p (h d) -> p h d", h=4),
                                              in1=gk[:].unsqueeze(1).to_broadcast([128, 4, 256]), op=ALU.mult),
                    reads=["kn", "gk"], writes=["kbm"])
                pT = bbf(0)
                for j in range(8):
                    pe(lambda e, j=j: e.transpose(out=pT[:, j * 128:(j + 1) * 128], in_=kbm[:, j * 128:(j + 1) * 128], identity=ident[:]),
                       reads=["kbm", "ident"], writes=[B[0]])
                act(lambda e, m=m: e.copy(out=memkT[:, :, :, m * 128:(m + 1) * 128],
                                          in_=pT[:, 0:1024].rearrange("p (h j t) -> p h j t", h=4, j=2)),
                    reads=[B[0]], writes=["memkT"])
                for j in range(2):
                    act(lambda e, m=m, j=j: e.copy(out=memv[:, m, 2 * j:2 * j + 2, :], in_=banks[3 + j][:].rearrange("p (h d) -> p h d", h=2)),
                        reads=[B[3 + j]], writes=["memv"])
            barrier()

    def rotary(src, src_res, cst, cs_res, dst, dst_res, t1, t2, shape3, sfx=""):
        H = shape3
        cc = cst[:, 0:32].unsqueeze(1).to_broadcast([128, H, 32])
        ns = cst[:, 32:48].unsqueeze(1).to_broadcast([128, H, 16])
        ps_ = cst[:, 48:64].unsqueeze(1).to_broadcast([128, H, 16])
        dve(lambda e: e.tensor_tensor(out=t1, in0=src, in1=cc, op=ALU.mult), reads=[src_res, cs_res], writes=["rot_t1" + sfx])
        dve(lambda e: e.tensor_tensor(out=t2[:, :, 0:16], in0=src[:, :, 16:32], in1=ns, op=ALU.mult), reads=[src_res, cs_res], writes=["rot_t2a" + sfx])
        dve(lambda e: e.tensor_tensor(out=t2[:, :, 16:32], in0=src[:, :, 0:16], in1=ps_, op=ALU.mult), reads=[src_res, cs_res], writes=["rot_t2b" + sfx])
        dve(lambda e: e.tensor_tensor(out=dst, in0=t1, in1=t2, op=ALU.add), reads=["rot_t1" + sfx, "rot_t2a" + sfx, "rot_t2b" + sfx], writes=[dst_res])

    def phase_A():
        with contextlib.ExitStack() as S:
            w_kv = sb(S, "w_kv", [128, 8, 160], BF16)
            wukv = sb(S, "wukv", [128, 1, 1024], BF16)
            gkv = sb(S, "gkv", [128, 128], F32)
            gk = sb(S, "gk96", [128, 96], F32)
            csb = sb(S, "csb", [128, NTB, 64], F32)
            two = lambda name, shape, dt: [sb(S, name, shape, dt) for _ in range(2)]
            xt = two("xt", [128, D], F32)
            hTt_ = two("hTt", [128, 8, 128], BF16)
            ckn_ = two("ckn", [128, 128], BF16)
            ckT_ = two("ckT", [128, 128], BF16)
            kn_ = two("kn", [128, 8, 64], F32)
            kb2 = two("kb", [128, 8, 96], BF16)
            krg_ = two("krg", [128, 1, 32], F32)
            krr_ = two("krr", [128, 1, 32], F32)
            t1_ = two("t1", [128, 1, 32], F32)
            t2_ = two("t2", [128, 1, 32], F32)
            vb_ = two("vb", [128, 8, 64], BF16)
            kst_ = two("kst", [96, 8, 512], BF16)
            ssr_ = two("ssr", [128, 1], F32)
            load_w(w_kv, w_in_e[0][:, 768:928], "w_kv")
            load_w(wukv, w_ukv[0], "wukv")
            load_bc(gn[:], mix_norm_g[0], "gn")
            load_bc(gkv[:], kv_lora_g[0], "gkv")
            load_bc(gk[:], mla_k_g[0], "gk")
            P.dma("sp", lambda e: e.dma_start(out=csb[:], in_=cs_b.rearrange("(c p) f -> p c f", p=128)), writes=["csb"])
            def tileA(t):
                p = t % 2
                x_ = "_%d" % p
                bT, bC, bK = 4 * p, 4 * p + 1, (4 * p + 2, 4 * p + 3)
                xtt, hTt, ckn, ckT, kn, kb_, krg, krr, t1, t2, vb, ssr = (xt[p], hTt_[p], ckn_[p], ckT_[p], kn_[p], kb2[p], krg_[p],
                                                                          krr_[p], t1_[p], t2_[p], vb_[p], ssr_[p])
                sq_, ssq_, rstd_, ss32_, rs32_ = SQ[p], SSQ[p], RSTD[p], SS32[p], RS32[p]
                sfx = "" if p == 0 else "_p1"
                kst = kst_[(t // 4) % 2]
                kstr = "kst%d" % ((t // 4) % 2)
                P.dma("sp", lambda e, t=t, xtt=xtt: e.dma_start(out=xtt[:], in_=xb[t * 128:(t + 1) * 128, :]), writes=["xt" + x_])
                norm_hT(xtt[:], "xt" + x_, hTt[:], "hTt" + x_, par=p, bank=bT)
                pC = banks[bC]
                for k in range(8):
                    pe(lambda e, k=k, pC=pC, hTt=hTt: e.matmul(pC[:, 0:160], lhsT=hTt[:, k, :], rhs=w_kv[:, k, :], start=(k == 0), stop=(k == 7)),
                       reads=["hTt" + x_, ("w_kv", k)], writes=[B[bC]])
                act(lambda e, pC=pC, sq_=sq_, ssq_=ssq_: e.activation(out=sq_[:, 0:128], in_=pC[:, 0:128], func=AF.Square, accum_out=ssq_[:]),
                    reads=[B[bC]], writes=["sq" + sfx, "ssq" + sfx])
                rstd_from(ssq_[:], "ssq" + sfx, 128, rstd_[:], "rstd" + sfx)
                dve(lambda e, pC=pC, ckn=ckn, rstd_=rstd_: e.scalar_tensor_tensor(out=ckn[:], in0=pC[:, 0:128], scalar=rstd_[:, 0:1], in1=gkv[:], op0=ALU.mult, op1=ALU.mult),
                    reads=[B[bC], "rstd" + sfx, "gkv"], writes=["ckn" + x_])
                pT2 = bbf(bT)
                pe(lambda e, pT2=pT2, ckn=ckn: e.transpose(out=pT2[:, 0:128], in_=ckn[:], identity=ident[:]), reads=["ckn" + x_, "ident"], writes=[B[bT]])
                act(lambda e, pT2=pT2, ckT=ckT: e.copy(out=ckT[:], in_=pT2[:, 0:128]), reads=[B[bT]], writes=["ckT" + x_])
                for j in range(2):
                    pe(lambda e, j=j, ckT=ckT, bK=bK: e.matmul(banks[bK[j]][:], lhsT=ckT[:], rhs=wukv[:, 0, j * 512:(j + 1) * 512], start=True, stop=True),
                       reads=["ckT" + x_, ("wukv", 0)], writes=[B[bK[j]]])
                for j in range(2):
                    act(lambda e, j=j, bK=bK, sq_=sq_: e.activation(out=sq_[:, j * 256:(j + 1) * 256].rearrange("p (h d) -> p h d", h=4),
                                                              in_=banks[bK[j]][:].rearrange("p (h d) -> p h d", h=4)[:, :, 0:64], func=AF.Square),
                        reads=[B[bK[j]]], writes=["sq" + sfx])
                dve(lambda e, sq_=sq_, ss32_=ss32_: e.tensor_reduce(out=ss32_[:, 0:8], in_=sq_[:, 0:512].rearrange("p (h d) -> p h d", h=8), axis=AX.X, op=ALU.add),
                    reads=["sq" + sfx], writes=["ss32" + sfx])
                act(lambda e, pC=pC, sq_=sq_, ssr=ssr: e.activation(out=sq_[:, 512:544], in_=pC[:, 128:160], func=AF.Square, accum_out=ssr[:]),
                    reads=[B[bC]], writes=["sq2" + sfx, "ssr" + x_])
                dve(lambda e, ss32_=ss32_, ssr=ssr: e.tensor_scalar(out=ss32_[:, 0:8], in0=ss32_[:, 0:8], scalar1=ssr[:, 0:1], scalar2=None, op0=ALU.add),
                    reads=["ss32" + sfx, "ssr" + x_], writes=["ss32" + sfx])
                rstd_from(ss32_[:, 0:8], "ss32" + sfx, 96, rs32_[:, 0:8], "rs32" + sfx)
                for j in range(2):
                    dve(lambda e, j=j, bK=bK, kn=kn, rs32_=rs32_: e.tensor_tensor(out=kn[:, 4 * j:4 * j + 4, :], in0=banks[bK[j]][:].rearrange("p (h d) -> p h d", h=4)[:, :, 0:64],
                                                                             in1=rs32_[:, 4 * j:4 * j + 4].unsqueeze(2).to_broadcast([128, 4, 64]), op=ALU.mult),
                        reads=[B[bK[j]], "rs32" + sfx], writes=["kn" + x_])
                pool(lambda e, kb_=kb_, kn=kn: e.tensor_tensor(out=kb_[:, :, 0:64], in0=kn[:], in1=gk[:, 0:64].unsqueeze(1).to_broadcast([128, 8, 64]), op=ALU.mult),
                     reads=["kn" + x_, "gk"], writes=["kb_n" + x_])
                dve(lambda e, krg=krg, pC=pC: e.tensor_tensor(out=krg[:, 0, :], in0=pC[:, 128:160], in1=gk[:, 64:96], op=ALU.mult),
                    reads=[B[bC], "gk"], writes=["krg" + x_])
                rotary(krg[:], "krg" + x_, csb[:, t, :], "csb", krr[:], "krr" + x_, t1[:], t2[:], 1, sfx=x_)
                dve(lambda e, kb_=kb_, krr=krr, rs32_=rs32_: e.tensor_tensor(out=kb_[:, :, 64:96], in0=krr[:, 0, :].unsqueeze(1).to_broadcast([128, 8, 32]),
                                                                        in1=rs32_[:, 0:8].unsqueeze(2).to_broadcast([128, 8, 32]), op=ALU.mult),
                    reads=["krr" + x_, "rs32" + sfx], writes=["kb_r" + x_])
                for j in range(2):
                    act(lambda e, j=j, bK=bK, vb=vb: e.copy(out=vb[:, 4 * j:4 * j + 4, :], in_=banks[bK[j]][:].rearrange("p (h d) -> p h d", h=4)[:, :, 64:128]),
                        reads=[B[bK[j]]], writes=["vb" + x_])
                P.dma("sp", lambda e, t=t, vb=vb: e.dma_start(out=v_scr[:, :, t, :].rearrange("h p d -> p h d"), in_=vb[:]), reads=["vb" + x_], writes=[("v_scr", t)])
                pT3 = bbf(bT)
                for h in range(8):
                    pe(lambda e, h=h, pT3=pT3, kb_=kb_: e.transpose(out=pT3[0:96, h * 128:(h + 1) * 128], in_=kb_[:, h, :], identity=ident[:]),
                       reads=["kb_n" + x_, "kb_r" + x_, "ident"], writes=[B[bT]])
                tt = t % 4
                act(lambda e, tt=tt, pT3=pT3, kst=kst: e.copy(out=kst[:, :, tt * 128:(tt + 1) * 128], in_=pT3[0:96, 0:1024].rearrange("p (h t) -> p h t", h=8)),
                    reads=[B[bT]], writes=[kstr])
                if tt == 3:
                    b4 = t // 4
                    P.dma("sp", lambda e, b4=b4, kst=kst: e.dma_start(out=kT_scr[:, :, b4 * 512:(b4 + 1) * 512].rearrange("h d t -> d h t"), in_=kst[:]),
                          reads=[kstr], writes=[("kT_scr", b4)])
            P.pipeline(NTB, tileA)
            barrier()

    def phase_B1():
        with contextlib.ExitStack() as S:
            w_q = sb(S, "w_q", [128, 8, 256], BF16)
            wuq = sb(S, "wuq", [128, 2, 768], BF16)
            gql = sb(S, "gql", [128, 256], F32)
            gq = sb(S, "gq96", [128, 96], F32)
            csw = sb(S, "csw", [128, NT, 64], F32)
            two = lambda name, shape, dt: [sb(S, name, shape, dt) for _ in range(2)]
            hTt_ = two("hTt", [128, 8, 128], BF16)
            cqn_ = two("cqn", [128, 256], BF16)
            cqT_ = two("cqT", [128, 2, 128], BF16)
            qn_ = two("qn", [128, 8, 96], F32)
            qr_ = two("qr", [128, 8, 32], F32)
            qb2 = two("qb", [128, 8, 96], BF16)
            t1_ = two("t1", [128, 8, 32], F32)
            t2_ = two("t2", [128, 8, 32], F32)
            qst_ = two("qst", [96, 8, 512], BF16)
            load_w(w_q, w_in_e[0][:, 512:768], "w_q")
            load_w(wuq, w_uq[0], "wuq")
            load_bc(gn[:], mix_norm_g[0], "gn")
            load_bc(gql[:], q_lora_g[0], "gql")
            load_bc(gq[:], mla_q_g[0], "gq")
            P.dma("sp", lambda e: e.dma_start(out=csw[:], in_=cs_w.rearrange("(c p) f -> p c f", p=128)), writes=["csw"])

            def tileB(t):
                p = t % 2
                x_ = "_%d" % p
                sfx = "" if p == 0 else "_p1"
                bT, bC, bQ = 4 * p, 4 * p + 1, (4 * p + 2, 4 * p + 3)
                hTt, cqn, cqT, qn, qr, qb_, t1, t2 = hTt_[p], cqn_[p], cqT_[p], qn_[p], qr_[p], qb2[p], t1_[p], t2_[p]
                sq_, ssq_, rstd_, ss32_, rs32_ = SQ[p], SSQ[p], RSTD[p], SS32[p], RS32[p]
                qst = qst_[(t // 4) % 2]
                qstr = "qst%d" % ((t // 4) % 2)
                norm_hT(x[:, t, :], ("x", t), hTt[:], "hTt" + x_, par=p, bank=bT)
                pC = banks[bC]
                for k in range(8):
                    pe(lambda e, k=k: e.matmul(pC[:, 0:256], lhsT=hTt[:, k, :], rhs=w_q[:, k, :], start=(k == 0), stop=(k == 7)),
                       reads=["hTt" + x_, ("w_q", k)], writes=[B[bC]])
                act(lambda e: e.activation(out=sq_[:, 0:256], in_=pC[:, 0:256], func=AF.Square, accum_out=ssq_[:]),
                    reads=[B[bC]], writes=["sq" + sfx, "ssq" + sfx])
                rstd_from(ssq_[:], "ssq" + sfx, 256, rstd_[:], "rstd" + sfx)
                dve(lambda e: e.scalar_tensor_tensor(out=cqn[:], in0=pC[:, 0:256], scalar=rstd_[:, 0:1], in1=gql[:], op0=ALU.mult, op1=ALU.mult),
                    reads=[B[bC], "rstd" + sfx, "gql"], writes=["cqn" + x_])
                pT2 = bbf(bT)
                for j in range(2):
                    pe(lambda e, j=j: e.transpose(out=pT2[:, j * 128:(j + 1) * 128], in_=cqn[:, j * 128:(j + 1) * 128], identity=ident[:]),
                       reads=["cqn" + x_, "ident"], writes=[B[bT]])
                act(lambda e: e.copy(out=cqT[:], in_=pT2[:, 0:256].rearrange("p (j t) -> p j t", j=2)), reads=[B[bT]], writes=["cqT" + x_])
                for (bk, c0, c1) in ((bQ[0], 0, 480), (bQ[1], 480, 768)):
                    for j in range(2):
                        pe(lambda e, bk=bk, c0=c0, c1=c1, j=j: e.matmul(banks[bk][:, 0:c1 - c0], lhsT=cqT[:, j, :], rhs=wuq[:, j, c0:c1],
                                                                        start=(j == 0), stop=(j == 1)),
                           reads=["cqT" + x_, ("wuq", j)], writes=[B[bk]])
                segs = ((bQ[0], 0, 5), (bQ[1], 5, 8))
                for (bk, h0, h1) in segs:
                    nh = h1 - h0
                    act(lambda e, bk=bk, h0=h0, nh=nh: e.activation(out=sq_[:, h0 * 96:(h0 + nh) * 96], in_=banks[bk][:, 0:nh * 96], func=AF.Square),
                        reads=[B[bk]], writes=["sq" + sfx])
                dve(lambda e: e.tensor_reduce(out=ss32_[:, 0:8], in_=sq_[:, 0:768].rearrange("p (h d) -> p h d", h=8), axis=AX.X, op=ALU.add),
                    reads=["sq" + sfx], writes=["ss32" + sfx])
                rstd_from(ss32_[:, 0:8], "ss32" + sfx, 96, rs32_[:, 0:8], "rs32" + sfx)
                for (bk, h0, h1) in segs:
                    nh = h1 - h0
                    dve(lambda e, bk=bk, h0=h0, nh=nh: e.tensor_tensor(out=qn[:, h0:h0 + nh, :], in0=banks[bk][:, 0:nh * 96].rearrange("p (h d) -> p h d", h=nh),
                                                                       in1=rs32_[:, h0:h0 + nh].unsqueeze(2).to_broadcast([128, nh, 96]), op=ALU.mult),
                        reads=[B[bk], "rs32" + sfx], writes=["qn" + x_])
                pool(lambda e: e.tensor_tensor(out=qb_[:, :, 0:64], in0=qn[:, :, 0:64], in1=gq[:, 0:64].unsqueeze(1).to_broadcast([128, 8, 64]), op=ALU.mult),
                     reads=["qn" + x_, "gq"], writes=["qb_n" + x_])
                dve(lambda e: e.tensor_tensor(out=qr[:], in0=qn[:, :, 64:96], in1=gq[:, 64:96].unsqueeze(1).to_broadcast([128, 8, 32]), op=ALU.mult),
                    reads=["qn" + x_, "gq"], writes=["qr" + x_])
                rotary(qr[:], "qr" + x_, csw[:, t, :], "csw", qb_[:, :, 64:96], "qb_r" + x_, t1[:], t2[:], 8, sfx=x_)
                pT3 = bbf(bT)
                for h in range(8):
                    pe(lambda e, h=h: e.transpose(out=pT3[0:96, h * 128:(h + 1) * 128], in_=qb_[:, h, :], identity=ident[:]),
                       reads=["qb_n" + x_, "qb_r" + x_, "ident"], writes=[B[bT]])
                tt = t % 4
                act(lambda e, tt=tt: e.copy(out=qst[:, :, tt * 128:(tt + 1) * 128], in_=pT3[0:96, 0:1024].rearrange("p (h t) -> p h t", h=8)),
                    reads=[B[bT]], writes=[qstr])
                if tt == 3:
                    b4 = t // 4
                    P.dma("sp", lambda e, b4=b4: e.dma_start(out=qT_scr[:, :, b4 * 512:(b4 + 1) * 512].rearrange("h d t -> d h t"), in_=qst[:]),
                          reads=[qstr], writes=[("qT_scr", b4)])

            P.pipeline(NT, tileB)
            barrier()

    def phase_B2():
        with contextlib.ExitStack() as S:
            w_p = sb(S, "w_p", [128, 8, 512], BF16)
            pw = sb(S, "pw", [128, 4, 128], BF16)
            psc = sb(S, "psc", [128, 4], F32)
            wout = sb(S, "wout", [128, 4, D], BF16)
            inv_t = sb(S, "inv_t", [128, 4, 512], F32)
            U = sb(S, "U", [128, 4, 528], F32)
            UH = sb(S, "UH", [128, 4, 16], F32)
            a2 = sb(S, "a2", [128, 3, 528], F32)
            a4 = sb(S, "a4", [128, 2, 528], F32)
            a8 = sb(S, "a8", [128, 1, 528], F32)
            Sm = sb(S, "Sm", [128, 4, 512], F32)
            Dt = sb(S, "Dt", [128, 4, 512], BF16)
            hTb = sb(S, "hTb", [128, 8, 512], BF16)
            hTh = sb(S, "hTh", [128, 8, 16], BF16)
            xht = sb(S, "xht", [16, D], F32)
            yT = sb(S, "yT", [128, 4, WT], BF16)
            load_w(w_p, w_in_e[0][:, 0:512], "w_p")
            for g in range(4):
                P.dma("pool", lambda e, g=g: e.dma_start(out=pw[:, g, :], in_=pool_w[0, g]), writes=[("pw", g)])
            load_w(wout, w_out_e[0][0:512, :], "wout")
            load_bc(gn[:], mix_norm_g[0], "gn")
            P.dma("sp", lambda e: e.dma_start(out=psc[:], in_=pool_scale[0].rearrange("(g d) -> d g", g=4), allow_slow_non_contiguous=True), writes=["psc"])
            P.dma("sp", lambda e: e.dma_start(out=xht[:], in_=xh[:, :]), writes=["xht"])
            dve(lambda e: e.memset(U[:], 0.0), writes=["U"])
            norm_hT(xht[:], "xht", hTh[:], "hTh", n=16)
            for g in range(4):
                for k in range(8):
                    pe(lambda e, g=g, k=k: e.matmul(banks[1][:, g * 16:(g + 1) * 16], lhsT=w_p[:, k, g * 128:(g + 1) * 128], rhs=hTh[:, k, :],
                                                    start=(k == 0), stop=(k == 7)),
                       reads=["hTh", ("w_p", k)], writes=[B[1]])
            act(lambda e: e.copy(out=UH[:], in_=banks[1][:, 0:64].rearrange("p (g t) -> p g t", g=4)), reads=[B[1]], writes=["UH"])
            dve(lambda e: e.tensor_copy(out=U[:, :, 520:528], in_=UH[:, :, 0:8]), reads=["UH", "U"], writes=["U"])

            def pool_step(b, ncols, tok0, c0):
                P.dma("sp", lambda e: e.dma_start(out=inv_t[:, :, 0:(512 if b < NB else 8)],
                                                  in_=invc[:, b * 512:b * 512 + (512 if b < NB else 8)].partition_broadcast(128)),
                      writes=["inv_t"])
                pool(lambda e: e.tensor_tensor(out=a2[:, :, 0:527], in0=U[:, 1:4, 0:527], in1=U[:, 1:4, 1:528], op=ALU.add), reads=["U"], writes=["a2"])
                pool(lambda e: e.tensor_tensor(out=a4[:, :, 0:525], in0=a2[:, 1:3, 0:525], in1=a2[:, 1:3, 2:527], op=ALU.add), reads=["a2"], writes=["a4"])
                pool(lambda e: e.tensor_tensor(out=a8[:, :, 0:521], in0=a4[:, 1:2, 0:521], in1=a4[:, 1:2, 4:525], op=ALU.add), reads=["a4"], writes=["a8"])
                dve(lambda e: e.tensor_tensor(out=Sm[:, 0, :], in0=U[:, 0, 7:519], in1=U[:, 0, 8:520], op=ALU.add), reads=["U"], writes=["Sm0"])
                dve(lambda e: e.tensor_tensor(out=Sm[:, 1, :], in0=a2[:, 0, 6:518], in1=a2[:, 0, 8:520], op=ALU.add), reads=["a2"], writes=["Sm1"])
                dve(lambda e: e.tensor_tensor(out=Sm[:, 2, :], in0=a4[:, 0, 4:516], in1=a4[:, 0, 8:520], op=ALU.add), reads=["a4"], writes=["Sm2"])
                dve(lambda e: e.tensor_tensor(out=Sm[:, 3, :], in0=a8[:, 0, 0:512], in1=a8[:, 0, 8:520], op=ALU.add), reads=["a8"], writes=["Sm3"])
                dve(lambda e: e.tensor_tensor(out=Sm[:], in0=Sm[:], in1=inv_t[:], op=ALU.mult), reads=["Sm0", "Sm1", "Sm2", "Sm3", "inv_t"], writes=["Sm"])
                dve(lambda e: e.tensor_tensor(out=Dt[:], in0=Sm[:], in1=U[:, :, 8:520], op=ALU.subtract), reads=["Sm", "U"], writes=["Dt"])
                for g in range(4):
                    pe(lambda e, g=g: e.matmul(banks[2 + (g % 2)][:], lhsT=pw[:, g, :], rhs=Dt[:, g, :], start=True, stop=True),
                       reads=["Dt", ("pw", g)], writes=[B[2 + (g % 2)]])
                    act(lambda e, g=g: e.activation(out=yT[:, g, tok0:tok0 + ncols], in_=banks[2 + (g % 2)][:, c0:c0 + ncols], func=AF.Copy, scale=psc[:, g:g + 1]),
                        reads=[B[2 + (g % 2)], "psc"], writes=[("yT", g)])

            for b in range(NB):
                for tt in range(4):
                    t = 4 * b + tt
                    norm_hT(x[:, t, :], ("x", t), hTb[:, :, tt * 128:(tt + 1) * 128], "hTb")
                dve(lambda e: e.tensor_copy(out=U[:, :, 0:16], in_=U[:, :, 512:528]), reads=["U"], writes=["U"])
                for g in range(4):
                    for k in range(8):
                        pe(lambda e, g=g, k=k: e.matmul(banks[4 + (g % 2)][:], lhsT=w_p[:, k, g * 128:(g + 1) * 128], rhs=hTb[:, k, :],
                                                        start=(k == 0), stop=(k == 7)),
                           reads=["hTb", ("w_p", k)], writes=[B[4 + (g % 2)]])
                    act(lambda e, g=g: e.copy(out=U[:, g, 16:528], in_=banks[4 + (g % 2)][:]), reads=[B[4 + (g % 2)], "U"], writes=["U"])
                if b == 0:
                    pool_step(0, 504, 0, 8)
                else:
                    pool_step(b, 512, 512 * b - 8, 0)
            dve(lambda e: e.tensor_copy(out=U[:, :, 0:16], in_=U[:, :, 512:528]), reads=["U"], writes=["U"])
            dve(lambda e: e.tensor_copy(out=U[:, :, 16:24], in_=UH[:, :, 8:16]), reads=["UH", "U"], writes=["U"])
            pool_step(NB, 8, WT - 8, 0)
            for t in range(NT):
                proj_resid(t, lambda k, t=t: yT[:, k, t * 128:(t + 1) * 128], [("yT", g) for g in range(4)], 4, wout, "wout")
            barrier()

    def phase_C():
        with contextlib.ExitStack() as S:
            oT = sb(S, "oT", [128, 4, WT], BF16)
            with contextlib.ExitStack() as S2:
                kh = [sb(S2, "kh", [96, SEQ], BF16) for _ in range(2)]
                va = [sb(S2, "va", [128, NTB, 128], BF16) for _ in range(2)]
                qh = [sb(S2, "qh", [96, WT], BF16) for _ in range(2)]
                E = [sb(S2, "E", [128, 512], BF16) for _ in range(3)]
                rec = sb(S2, "rec", [128, 512], F32)
                for p in range(2):
                    dve(lambda e, p=p: e.memset(va[p][:, :, (1 - p) * 64:(1 - p) * 64 + 64], 1.0), writes=[("va1", p)])
                scale = 96 ** -0.5
                for h in range(8):
                    p = h % 2
                    lo, hi = p * 64, p * 64 + 64
                    dlo, dhi = (1 - p) * 64, (1 - p) * 64 + 64
                    P.dma("sp", lambda e, h=h, p=p: e.dma_start(out=kh[p][:], in_=kT_scr[h]), writes=[("kh", p)])
                    P.dma("sp", lambda e, h=h, p=p: e.dma_start(out=va[p][:, :, p * 64:p * 64 + 64], in_=v_scr[h]), writes=[("va", p)])
                    P.dma("sp", lambda e, h=h, p=p: e.dma_start(out=qh[p][:], in_=qT_scr[h]), writes=[("qh", p)])
                    for qb in range(NB):
                        ob = 3 + (qb % 2)

                        def S_mm(c, qb=qb, p=p):
                            pe(lambda e: e.matmul(banks[c % 3][:], lhsT=kh[p][:, c * 128:(c + 1) * 128], rhs=qh[p][:, qb * 512:(qb + 1) * 512],
                                                  start=True, stop=True),
                               reads=[("kh", p), ("qh", p)], writes=[B[c % 3]])
                        S_mm(0)
                        S_mm(1)
                        for c in range(NTB):
                            act(lambda e, c=c: e.activation(out=E[c % 3][:], in_=banks[c % 3][:], func=AF.Exp, scale=scale),
                                reads=[B[c % 3]], writes=[("E", c % 3)])
                            pe(lambda e, c=c, p=p, ob=ob: e.matmul(banks[ob][:], lhsT=va[p][:, c, :], rhs=E[c % 3][:], start=(c == 0), stop=(c == NTB - 1)),
                               reads=[("va", p), ("va1", p), ("E", c % 3)], writes=[B[ob]])
                            if c + 2 < NTB:
                                S_mm(c + 2)
                        dve(lambda e, ob=ob, lo=lo, hi=hi, dlo=dlo, dhi=dhi: e.tensor_copy(out=rec[lo:hi, :], in_=banks[ob][dlo:dhi, :]), reads=[B[ob]], writes=["rec"])
                        dve(lambda e, lo=lo, hi=hi: e.reciprocal(out=rec[lo:hi, :], in_=rec[lo:hi, :]), reads=["rec"], writes=["rec"])
                        dve(lambda e, ob=ob, h=h, qb=qb, lo=lo, hi=hi: e.tensor_tensor(out=oT[lo:hi, h // 2, qb * 512:(qb + 1) * 512], in0=banks[ob][lo:hi, :],
                                                                         in1=rec[lo:hi, :], op=ALU.mult),
                            reads=[B[ob], "rec"], writes=[("oT", h // 2)])
                barrier()
            with contextlib.ExitStack() as S2:
                wout = sb(S2, "wout2", [128, 4, D], BF16)
                load_w(wout, w_out_e[0][512:1024, :], "wout2")
                for t in range(NT):
                    proj_resid(t, lambda k, t=t: oT[:, k, t * 128:(t + 1) * 128], [("oT", g) for g in range(4)], 4, wout, "wout2")
                barrier()

    def phase_xattn(L):
        with contextlib.ExitStack() as S:
            wq = sb(S, "wq", [128, 8, D], BF16)
            wo = sb(S, "wo", [128, 8, D], BF16)
            gq = sb(S, "gq256", [128, 256], F32)
            hTt = sb(S, "hTt", [128, 8, 128], BF16)
            qn = sb(S, "qn", [128, D], F32)
            qb_ = sb(S, "qb", [128, D], BF16)
            qT2 = [sb(S, "qT", [128, 8, 512], BF16) for _ in range(2)]
            E = [sb(S, "E", [128, 512], BF16) for _ in range(2)]
            rec = sb(S, "rec", [128, 512], F32)
            oTb = sb(S, "oTb", [128, 8, 512], BF16)
            load_w(wq, w_mem_q[L], "wq")
            load_w(wo, w_mem_o[L], "wo")
            load_bc(gn[:], xattn_norm_g[L], "gn")
            load_bc(gq[:], mem_q_g[L], "gq")
            scale = 256 ** -0.5

            def block(b):
                qT = qT2[b % 2]
                qTr = "qT%d" % (b % 2)
                for tt in range(4):
                    t = 4 * b + tt
                    norm_hT(x[:, t, :], ("x", t), hTt[:], "hTt")
                    for half in range(2):
                        for k in range(8):
                            pe(lambda e, half=half, k=k: e.matmul(banks[1 + half][:], lhsT=hTt[:, k, :], rhs=wq[:, k, half * 512:(half + 1) * 512],
                                                                   start=(k == 0), stop=(k == 7)),
                               reads=["hTt", ("wq", k)], writes=[B[1 + half]])
                    for hh in range(4):
                        act(lambda e, hh=hh: e.activation(out=sq[:, hh * 256:(hh + 1) * 256], in_=banks[1 + hh // 2][:, (hh % 2) * 256:(hh % 2 + 1) * 256],
                                                          func=AF.Square, accum_out=ss32[:, hh:hh + 1]),
                            reads=[B[1 + hh // 2]], writes=["sq", "ss32"])
                    rstd_from(ss32[:, 0:4], "ss32", 256, rs32[:, 0:4], "rs32")
                    for hh in range(4):
                        dve(lambda e, hh=hh: e.scalar_tensor_tensor(out=qb_[:, hh * 256:(hh + 1) * 256], in0=banks[1 + hh // 2][:, (hh % 2) * 256:(hh % 2 + 1) * 256],
                                                                    scalar=rs32[:, hh:hh + 1], in1=gq[:], op0=ALU.mult, op1=ALU.mult),
                            reads=[B[1 + hh // 2], "rs32", "gq"], writes=["qb"])
                    pT = bbf(0)
                    for j in range(8):
                        pe(lambda e, j=j: e.transpose(out=pT[:, j * 128:(j + 1) * 128], in_=qb_[:, j * 128:(j + 1) * 128], identity=ident[:]),
                           reads=["qb", "ident"], writes=[B[0]])
                    act(lambda e, tt=tt, qT=qT: e.copy(out=qT[:, :, tt * 128:(tt + 1) * 128], in_=pT[:, 0:1024].rearrange("p (j t) -> p j t", j=8)),
                        reads=[B[0]], writes=[qTr])
                P.mark()
                for h in range(4):
                    for m in range(2):
                        for j in range(2):
                            pe(lambda e, h=h, m=m, j=j, qT=qT: e.matmul(banks[3 + m][:], lhsT=memkT[:, h, j, m * 128:(m + 1) * 128], rhs=qT[:, 2 * h + j, :],
                                                                         start=(j == 0), stop=(j == 1)),
                               reads=[qTr, "memkT"], writes=[B[3 + m]])
                        act(lambda e, m=m: e.activation(out=E[m][:], in_=banks[3 + m][:], func=AF.Exp, scale=scale), reads=[B[3 + m]], writes=[("E", m)])
                    for m in range(2):
                        pe(lambda e, m=m: e.matmul(banks[5][:], lhsT=ones_b[:], rhs=E[m][:], start=(m == 0), stop=(m == 1)),
                           reads=["ones_b", ("E", m)], writes=[B[5]])
                    act(lambda e: e.activation(out=rec[:], in_=banks[5][:], func=AF.Ln), reads=[B[5]], writes=["rec"])
                    act(lambda e: e.activation(out=rec[:], in_=rec[:], func=AF.Exp, scale=-1.0), reads=["rec"], writes=["rec"])
                    for dv in range(2):
                        for m in range(2):
                            pe(lambda e, h=h, m=m, dv=dv: e.matmul(banks[6 + dv][:], lhsT=memv[:, m, h, dv * 128:(dv + 1) * 128], rhs=E[m][:],
                                                                    start=(m == 0), stop=(m == 1)),
                               reads=["memv", ("E", m)], writes=[B[6 + dv]])
                        dve(lambda e, h=h, dv=dv: e.tensor_tensor(out=oTb[:, 2 * h + dv, :], in0=banks[6 + dv][:], in1=rec[:], op=ALU.mult),
                            reads=[B[6 + dv], "rec"], writes=["oTb"])
                for tt in range(4):
                    t = 4 * b + tt
                    proj_resid(t, lambda k, tt=tt: oTb[:, k, tt * 128:(tt + 1) * 128], ["oTb"], 8, wo, "wo")

            P.pipeline(NB, block)
            barrier()

    def phase_mlp(L):
        NPASS = 8
        with contextlib.ExitStack() as S:
            hT = sb(S, "hTall", [128, 8, WT], BF16)
            w1 = [sb(S, "w1", [128, 8, 512], BF16) for _ in range(2)]
            w2 = [sb(S, "w2", [128, 4, D], BF16) for _ in range(2)]
            aT = [sb(S, "aT", [128, 512], BF16) for _ in range(4)]
            s2 = [sb(S, "s2", [128, 512], F32) for _ in range(2)]
            load_bc(gn[:], ff_norm_g[L], "gn")

            def load_pass(ps_):
                b = ps_ % 2
                load_w(w1[b], w_ff1[L][:, ps_ * 512:(ps_ + 1) * 512], ("w1", b))
                load_w(w2[b], w_ff2[L][ps_ * 512:(ps_ + 1) * 512, :], ("w2", b))
            load_pass(0)

            def norm_block(b):
                P.capture()
                for tt in range(4):
                    t = 4 * b + tt
                    norm_hT(x[:, t, :], ("x", t), hT[:, :, t * 128:(t + 1) * 128], ("hT", t // 4), par=t % 2, bank=(0, 3)[t % 2])
                return P.end_capture()

            P.replay(norm_block(0))
            for ps_ in range(NPASS):
                pb = ps_ % 2
                if ps_ + 1 < NPASS:
                    load_pass(ps_ + 1)
                for b in range(NB):
                  if ps_ == 0:
                    P.capture()
                  if True:
                    for f in range(4):
                        zb = 1 + (f % 2)
                        for k in range(8):
                            pe(lambda e, f=f, k=k, zb=zb, pb=pb, b=b: e.matmul(banks[zb][:], lhsT=w1[pb][:, k, f * 128:(f + 1) * 128],
                                                                                 rhs=hT[:, k, b * 512:(b + 1) * 512], start=(k == 0), stop=(k == 7)),
                               reads=[("hT", b), (("w1", pb), k)], writes=[B[zb]])
                        act(lambda e, f=f, zb=zb: e.activation(out=s2[f % 2][:], in_=banks[zb][:], func=AF.Square), reads=[B[zb]], writes=[("s2", f % 2)])
                        dve(lambda e, f=f, zb=zb: e.scalar_tensor_tensor(out=aT[f][:], in0=banks[zb][:], scalar=0.0, in1=s2[f % 2][:],
                                                                          op0=ALU.is_gt, op1=ALU.mult),
                            reads=[B[zb], ("s2", f % 2)], writes=[("aT", f)])
                    for tt in range(4):
                        t = 4 * b + tt
                        proj_resid(t, lambda k, tt=tt: aT[k][:, tt * 128:(tt + 1) * 128], [("aT", f) for f in range(4)], 4, w2[pb], ("w2", pb))
                  if ps_ == 0:
                    Cb = P.end_capture()
                    Nb = norm_block(b + 1) if b + 1 < NB else []
                    P.replay(P.zipmerge(Cb, Nb))
            barrier()

    def phase_G1():
        with contextlib.ExitStack() as S:
            wqkv = sb(S, "wqkv", [128, 8, 3072], BF16)
            gqk = sb(S, "gqk", [128, 2, 64], F32)
            two = lambda name, shape, dt: [sb(S, name, shape, dt) for _ in range(2)]
            hTt_ = two("hTt", [128, 8, 128], BF16)
            qkn_ = two("qkn", [128, 1024], F32)
            qkb_ = two("qkb", [128, 1024], BF16)
            qkst_ = two("qkst", [128, 8, 128], BF16)
            vst_ = two("vst", [128, 4, 128], BF16)
            load_w(wqkv, w_qkv_o[0], "wqkv")
            load_bc(gn[:], mix_norm_g[1], "gn")
            load_bc(gqk[:, 0, :], na_q_g[0], "gqk0")
            load_bc(gqk[:, 1, :], na_k_g[0], "gqk1")

            def unit(n):
                t, u = n // 2, n % 2
                x_ = "_%d" % u
                sfx = "" if u == 0 else "_p1"
                bT, bQ, bV = 4 * u, (4 * u + 1, 4 * u + 2), 4 * u + 3
                hTt = hTt_[t % 2]
                hr = "hTt_%d" % (t % 2)
                qkn, qkb, qkst, vst = qkn_[u], qkb_[u], qkst_[u], vst_[u]
                sq_, ss32_, rs32_ = SQ[u], SS32[u], RS32[u]
                if u == 0:
                    norm_hT(x[:, t, :], ("x", t), hTt[:], hr, par=0, bank=bT)
                for n2 in range(2):
                    for k in range(8):
                        pe(lambda e, n2=n2, k=k: e.matmul(banks[bQ[n2]][:], lhsT=hTt[:, k, :], rhs=wqkv[:, k, u * 1024 + n2 * 512:u * 1024 + (n2 + 1) * 512],
                                                          start=(k == 0), stop=(k == 7)),
                           reads=[hr, ("wqkv", k)], writes=[B[bQ[n2]]])
                for n2 in range(2):
                    act(lambda e, n2=n2: e.activation(out=sq_[:, n2 * 512:(n2 + 1) * 512], in_=banks[bQ[n2]][:], func=AF.Square),
                        reads=[B[bQ[n2]]], writes=["sq" + sfx])
                dve(lambda e: e.tensor_reduce(out=ss32_[:, 0:16], in_=sq_[:, 0:1024].rearrange("p (h d) -> p h d", h=16), axis=AX.X, op=ALU.add),
                    reads=["sq" + sfx], writes=["ss32" + sfx])
                rstd_from(ss32_[:, 0:16], "ss32" + sfx, 64, rs32_[:, 0:16], "rs32" + sfx)
                for n2 in range(2):
                    dve(lambda e, n2=n2: e.tensor_tensor(out=qkn[:, n2 * 512:(n2 + 1) * 512].rearrange("p (h d) -> p h d", h=8),
                                                         in0=banks[bQ[n2]][:].rearrange("p (h d) -> p h d", h=8),
                                                         in1=rs32_[:, 8 * n2:8 * n2 + 8].unsqueeze(2).to_broadcast([128, 8, 64]), op=ALU.mult),
                        reads=[B[bQ[n2]], "rs32" + sfx], writes=["qkn" + x_])
                pool(lambda e: e.tensor_tensor(out=qkb[:].rearrange("p (h d) -> p h d", h=16), in0=qkn[:].rearrange("p (h d) -> p h d", h=16),
                                               in1=gqk[:, u, :].unsqueeze(1).to_broadcast([128, 16, 64]), op=ALU.mult),
                     reads=["qkn" + x_, "gqk0", "gqk1"], writes=["qkb" + x_])
                for k in range(8):
                    pe(lambda e, k=k: e.matmul(banks[bV][:], lhsT=hTt[:, k, :], rhs=wqkv[:, k, 2048 + u * 512:2048 + (u + 1) * 512],
                                               start=(k == 0), stop=(k == 7)),
                       reads=[hr, ("wqkv", k)], writes=[B[bV]])
                act(lambda e: e.copy(out=vst[:], in_=banks[bV][:].rearrange("p (h d) -> p h d", h=4)), reads=[B[bV]], writes=["vst" + x_])
                P.dma("sp", lambda e: e.dma_start(out=nv_scr[4 * u:4 * u + 4, :, t, :].rearrange("h p d -> p h d"), in_=vst[:]),
                      reads=["vst" + x_], writes=[("nv_scr", n)])
                pT = bbf(bT)
                for j in range(8):
                    pe(lambda e, j=j: e.transpose(out=pT[:, j * 128:(j + 1) * 128], in_=qkb[:, j * 128:(j + 1) * 128], identity=ident[:]),
                       reads=["qkb" + x_, "ident"], writes=[B[bT]])
                act(lambda e: e.copy(out=qkst[:], in_=pT[:, 0:1024].rearrange("p (j t) -> p j t", j=8)), reads=[B[bT]], writes=["qkst" + x_])
                dst = nq_scr if u == 0 else nk_scr
                P.dma("sp", lambda e: e.dma_start(out=dst[:, :, t * 128:(t + 1) * 128].rearrange("h p t -> p h t"), in_=qkst[:]),
                      reads=["qkst" + x_], writes=[("nqk_scr", n)])

            P.pipeline(2 * NT, unit)
            barrier()

    def phase_G2():
        with contextlib.ExitStack() as S:
            cT = sb(S, "cT", [128, 8, WT], BF16)
            with contextlib.ExitStack() as S2:
                qT = sb(S2, "nqT", [128, WT], BF16)
                kT = sb(S2, "nkT", [128, WT], BF16)
                va = [sb(S2, "nva", [128, NT, 128], BF16) for _ in range(2)]
                bias32 = sb(S2, "nbias", [128, 13, 128], F32)
                bh = [sb(S2, "nbh", [128, 25, 128], BF16) for _ in range(2)]
                bl = [sb(S2, "nbl", [128, 25, 128], BF16) for _ in range(2)]
                E = [sb(S2, "nE", [128, 512], BF16) for _ in range(3)]
                bg = [SQ[1][:, 0:640].rearrange("p (a b) -> p a b", a=5), gn[:, 0:640].rearrange("p (a b) -> p a b", a=5)]
                St = [SQ[0][:, 0:512], SQ[0][:, 512:1024]]
                rec = [sb(S2, "rec", [128, 512], F32)] * 2
                dve(lambda e: e.memset(va[0][:, :, 64:128], 1.0), writes=[("va1", 0)])
                dve(lambda e: e.memset(va[1][:, :, 0:64], 1.0), writes=[("va1", 1)])
                scale = 64 ** -0.5
                inv_scale = 8.0

                def row_cls(r):
                    return {0: 1, 2: 2, 36: 3, 38: 4}.get(r, 0)

                def prep_bias(h):
                    p = h % 2
                    P.dma("sp", lambda e, h=h, p=p: e.dma_start(out=bg[p], in_=nab[h][:, 0:5, :]), writes=[("bg", p)])
                    for (v0, v1) in ((0, 13), (13, 25)):
                        nv = v1 - v0
                        P.dma("sp", lambda e, h=h, v0=v0, v1=v1, nv=nv: e.dma_start(out=bias32[:, 0:nv, :], in_=nab[h][:, v0:v1, :]), writes=["bias32"])
                        dve(lambda e, p=p, v0=v0, v1=v1, nv=nv: e.tensor_scalar(out=bh[p][:, v0:v1, :], in0=bias32[:, 0:nv, :], scalar1=inv_scale, scalar2=None, op0=ALU.mult),
                            reads=["bias32"], writes=[("bh", p)])
                        dve(lambda e, p=p, v0=v0, v1=v1, nv=nv: e.scalar_tensor_tensor(out=bl[p][:, v0:v1, :].rearrange("p a b -> p (a b)"),
                                                                                      in0=bias32[:, 0:nv, :].rearrange("p a b -> p (a b)"), scalar=inv_scale,
                                                                                      in1=bh[p][:, v0:v1, :].rearrange("p a b -> p (a b)"), op0=ALU.mult, op1=ALU.subtract),
                            reads=["bias32", ("bh", p)], writes=[("bl", p)])

                prep_bias(0)
                for hp in range(8):
                    P.dma("sp", lambda e, hp=hp: e.dma_start(out=qT[:], in_=nq_scr[hp]), writes=["nqT"])
                    P.dma("sp", lambda e, hp=hp: e.dma_start(out=kT[:], in_=nk_scr[hp]), writes=["nkT"])
                    P.dma("sp", lambda e, hp=hp: e.dma_start(out=va[0][:, :, 0:64], in_=nv_scr[hp][:, :, 0:64]), writes=[("va", 0)])
                    P.dma("sp", lambda e, hp=hp: e.dma_start(out=va[1][:, :, 64:128], in_=nv_scr[hp][:, :, 64:128]), writes=[("va", 1)])
                    for p in range(2):
                        h = 2 * hp + p
                        lo, hi = p * 64, p * 64 + 64
                        dlo, dhi = (1 - p) * 64, (1 - p) * 64 + 64
                        items = [(blk, j) for blk in range(NB) for j in range(5)]

                        def S_stage(i, p=p, lo=lo, hi=hi):
                            blk, j = items[i]
                            sbk = i % 3
                            groups = []
                            for rp in range(4):
                                c = row_cls(8 * blk + 2 * rp)
                                if groups and groups[-1][0] == c:
                                    groups[-1][2] += 1
                                else:
                                    groups.append([c, rp, 1])
                            first = True
                            use_dve = (len(groups) == 1 and i % 2 == 1)
                            for src in (() if use_dve else (bh, bl)):
                                for (c, rp0, n) in groups:
                                    pe(lambda e, src=src, c=c, rp0=rp0, n=n, j=j, sbk=sbk, first=first: e.matmul(
                                            banks[sbk][:, rp0 * 128:(rp0 + n) * 128], lhsT=ident[:],
                                            rhs=src[p][:, c * 5 + j, :].unsqueeze(1).to_broadcast([128, n, 128]),
                                            start=first, stop=False, skip_group_check=True),
                                       reads=[("bh", p), ("bl", p), "ident"], writes=[B[sbk]])
                                    first = False
                            for rp in range(4):
                                r = 8 * blk + 2 * rp
                                tb = min(max(r - 4, 0), 30)
                                kt0 = (tb + 2 * j) * 64
                                pe(lambda e, kt0=kt0, r=r, sbk=sbk, rp=rp: e.matmul(banks[sbk][:, rp * 128:(rp + 1) * 128], lhsT=kT[lo:hi, kt0:kt0 + 128],
                                                                                   rhs=qT[lo:hi, r * 64:r * 64 + 128], start=(use_dve and rp == 0), stop=(rp == 3), skip_group_check=True),
                                   reads=["nkT", "nqT"], writes=[B[sbk]])

                        def mid_stage(i, p=p):
                            sbk = i % 3
                            blk, j = items[i]
                            cls = set(row_cls(8 * blk + 2 * rp) for rp in range(4))
                            if len(cls) == 1 and i % 2 == 1:
                                st = St[(i // 2) % 2]
                                sr = ("St", (i // 2) % 2)
                                dve(lambda e, sbk=sbk, st=st, j=j: e.scalar_tensor_tensor(out=st.rearrange("p (a b) -> p a b", a=4),
                                                                                        in0=banks[sbk][:].rearrange("p (a b) -> p a b", a=4), scalar=scale,
                                                                                        in1=bg[p][:, j, :].unsqueeze(1).to_broadcast([128, 4, 128]), op0=ALU.mult, op1=ALU.add),
                                    reads=[B[sbk], ("bg", p)], writes=[sr])
                                act(lambda e, i=i, st=st: e.activation(out=E[i % 3][:], in_=st, func=AF.Exp), reads=[sr], writes=[("nE", i % 3)])
                            else:
                                act(lambda e, i=i, sbk=sbk: e.activation(out=E[i % 3][:], in_=banks[sbk][:], func=AF.Exp, scale=scale),
                                    reads=[B[sbk]], writes=[("nE", i % 3)])

                        def PV_stage(i, p=p, lo=lo, hi=hi, dlo=dlo, dhi=dhi, hp=hp):
                            blk, j = items[i]
                            ob = 3 + (blk % 2)
                            for rp in range(4):
                                r = 8 * blk + 2 * rp
                                tb = min(max(r - 4, 0), 30)
                                vt = (tb + 2 * j) // 2
                                pe(lambda e, vt=vt, i=i, ob=ob, rp=rp, j=j: e.matmul(banks[ob][:, rp * 128:(rp + 1) * 128], lhsT=va[p][:, vt, :],
                                                                                    rhs=E[i % 3][:, rp * 128:(rp + 1) * 128],
                                                                                    start=(j == 0 and rp == 0), stop=(j == 4), skip_group_check=True),
                                   reads=[("va", p), ("va1", p), ("nE", i % 3)], writes=[B[ob]])
                            if j == 4:
                                for step in range(3):
                                    pending.append((i + 1 + step, lambda blk=blk, ob=ob, step=step: epilogue(blk, ob, step)))

                        def epilogue(blk, ob, step, p=p, lo=lo, hi=hi, dlo=dlo, dhi=dhi, hp=hp):
                            rc = rec[blk % 2]
                            rr = ("rec", 0)
                            if step == 0:
                                dve(lambda e, ob=ob, rc=rc: e.tensor_copy(out=rc[lo:hi, :], in_=banks[ob][dlo:dhi, :]), reads=[B[ob]], writes=[rr])
                            elif step == 1:
                                act(lambda e, rc=rc: e.activation(out=rc[lo:hi, :], in_=rc[lo:hi, :], func=AF.Ln), reads=[rr], writes=[rr])
                                act(lambda e, rc=rc: e.activation(out=rc[lo:hi, :], in_=rc[lo:hi, :], func=AF.Exp, scale=-1.0), reads=[rr], writes=[rr])
                            else:
                                dve(lambda e, ob=ob, rc=rc, blk=blk: e.tensor_tensor(out=cT[lo:hi, hp, blk * 512:(blk + 1) * 512], in0=banks[ob][lo:hi, :],
                                                                                    in1=rc[lo:hi, :], op=ALU.mult),
                                    reads=[B[ob], rr], writes=[("cT", hp)])

                        n_it = len(items)
                        pending = []
                        S_stage(0)
                        S_stage(1)
                        for i in range(n_it):
                            mid_stage(i)
                            PV_stage(i)
                            if i + 2 < n_it:
                                S_stage(i + 2)
                            if i == 4 and h + 1 < 16:
                                prep_bias(h + 1)
                            pending.sort(key=lambda q: q[0])
                            while pending and pending[0][0] <= i:
                                pending.pop(0)[1]()
                        pending.sort(key=lambda q: q[0])
                        while pending:
                            pending.pop(0)[1]()
                barrier()
            with contextlib.ExitStack() as S2:
                wout = sb(S2, "wouto", [128, 8, D], BF16)
                load_w(wout, w_out_o[0], "wouto")
                for t in range(NT):
                    proj_resid(t, lambda k, t=t: cT[:, k, t * 128:(t + 1) * 128], [("cT", g) for g in range(8)], 8, wout, "wouto")
                barrier()

    import os
    PH = os.environ.get("PHASES", "A,B1,B2,C").split(",")
    if stage >= 1:
        if "A" in PH:
            phase_A()
        if "B1" in PH:
            phase_B1()
        if "B2" in PH:
            phase_B2()
        if "C" in PH:
            phase_C()
    if stage >= 2:
        phase_mem()
        phase_xattn(0)
    if stage >= 3:
        phase_mlp(0)
    if stage >= 4:
        phase_G1()
        phase_G2()
    if stage >= 5:
        phase_xattn(1)
        phase_mlp(1)
    fins = []
    for t in range(NT):
        fins.append(P.dma("sp", lambda e, t=t: e.dma_start(out=out[t * 128:(t + 1) * 128, :], in_=x[:, t, :]), reads=[("x", t)]))
    P.emit(nc, final_ops=fins)
    return nc, P


def _rope_table(pos):
    half = 16
    freqs = (np.float32(10000.0) ** (-np.arange(half, dtype=np.float32) / np.float32(half))).astype(np.float32)
    ang = (pos.astype(np.float32)[:, None] * freqs[None, :]).astype(np.float32)
    c = np.cos(ang).astype(np.float32)
    s = np.sin(ang).astype(np.float32)
    return np.concatenate([c, c, -s, s], axis=1).astype(np.float32)


def _invc_table(a_tok):
    tab = np.ones((4, 2568), np.float32)
    tg = a_tok + np.arange(2568) - 8
    valid = (tg >= 0) & (tg < SEQ)
    for g, w in enumerate((2, 4, 8, 16)):
        lo = np.clip(tg - w // 2, 0, SEQ - 1)
        hi = np.clip(tg + w - 1 - w // 2, 0, SEQ - 1)
        cnt = (hi - lo + 1).astype(np.float32)
        tab[g] = np.where(valid, np.float32(1.0) / cnt, np.float32(1.0))
    return tab


def _natten_bias(rpb):
    H = rpb.shape[0]
    outb = np.full((H, 25, 128, 128), NEG, np.float32)
    cols = np.arange(64)
    c0 = np.clip(cols - 8, 0, 48)
    classes = {0: 8, 1: 0, 2: 2, 3: 36, 4: 38}
    kc = np.arange(64)[:, None]
    qc = np.arange(64)[None, :]
    colvalid = (kc >= c0[None, :]) & (kc < c0[None, :] + 16)
    dc = np.clip(kc - qc + 15, 0, 30)
    for cls, r in classes.items():
        tb = min(max(r - 4, 0), 30)
        for j in range(5):
            for kr_i in range(2):
                kr = tb + 2 * j + kr_i
                for qr_i in range(2):
                    qr = r + qr_i
                    r0 = min(max(qr - 4, 0), 32)
                    if not (r0 <= kr <= r0 + 7):
                        continue
                    dr = kr - qr + 7
                    vals = rpb[:, dr][:, dc]
                    blk = np.where(colvalid[None], vals, np.float32(NEG))
                    outb[:, cls * 5 + j, kr_i * 64:(kr_i + 1) * 64, qr_i * 64:(qr_i + 1) * 64] = blk
    return np.ascontiguousarray(outb.transpose(0, 2, 1, 3))


def make_in_maps(inputs):
    f = lambda a: np.ascontiguousarray(np.asarray(a, dtype=np.float32))
    xfull = f(inputs["x"])
    memf = f(inputs["mem"])
    shared = {k: f(v) for k, v in inputs.items() if k not in ("x", "mem", "na_rpb")}
    nabt = _natten_bias(f(inputs["na_rpb"])[0])
    pos_b = np.arange(SEQ)
    cs_b = _rope_table(pos_b)
    maps, meta = [], []
    for c in range(8):
        b, j = c // 4, c % 4
        a = min(max(32 * j - 4, 0), 88)
        a_tok = a * 64
        xw = xfull[b, a_tok:a_tok + WT]
        xh = np.zeros((16, D), np.float32)
        if a_tok >= 8:
            xh[0:8] = xfull[b, a_tok - 8:a_tok]
        if a_tok + WT + 8 <= SEQ:
            xh[8:16] = xfull[b, a_tok + WT:a_tok + WT + 8]
        m = dict(shared)
        m.update(xw=np.ascontiguousarray(xw), xh=xh, xb=xfull[b], mem=memf[b],
                 cs_w=np.ascontiguousarray(cs_b[a_tok:a_tok + WT]), cs_b=cs_b, invc=_invc_table(a_tok), nab=nabt)
        maps.append(m)
        meta.append((b, a, 32 * j - a))
    return maps, meta


_CACHE = {}


def kernel(**inputs):
    if "nc" not in _CACHE:
        _CACHE["nc"] = build_program(99)[0]
    nc = _CACHE["nc"]
    maps, meta = make_in_maps(inputs)
    res = run_bass_kernel_spmd(nc, maps, core_ids=list(range(8)))
    outp = np.zeros((2, SEQ, D), np.float32)
    for c in range(8):
        b, a, off = meta[c]
        o = np.asarray(res.results[c]["out"]).reshape(WT, D)
        j = c % 4
        outp[b, j * 2048:(j + 1) * 2048] = o[off * 64:off * 64 + 2048]
    return outp
```

```python
import concourse.bass as bass
import concourse.mybir as mybir

ENGS = ("pe", "act", "dve", "pool", "sp")
SAME_ENG_SYNC = {"pe": False, "act": True, "dve": True, "pool": True, "sp": False}
NSLOT = 12


class Op:
    __slots__ = ("id", "eng", "fn", "deps", "dma", "flag", "sem", "val", "slot_guard", "nwaits")

    def __init__(self, id, eng, fn, dma):
        self.id = id
        self.eng = eng
        self.fn = fn
        self.dma = dma
        self.deps = []
        self.flag = False
        self.sem = None
        self.val = None
        self.slot_guard = None


class Prog:
    def __init__(self):
        self.ops = []
        self.by_eng = {e: [] for e in ENGS}
        self.last_w = {}
        self.readers = {}
        self.dma_count = {e: 0 for e in ENGS}

    _cap = None

    def capture(self):
        self._cap = []

    def end_capture(self):
        c = self._cap
        self._cap = None
        return c

    def mark(self):
        self._mark = len(self._cap)

    def replay(self, lst):
        for it in lst:
            self.op(*it)

    @staticmethod
    def zipmerge(a, b):
        out = []
        na, nb = len(a), len(b)
        ia = ib = 0
        while ia < na or ib < nb:
            if ib >= nb or (ia < na and ia * max(nb, 1) <= ib * max(na, 1)):
                out.append(a[ia]); ia += 1
            else:
                out.append(b[ib]); ib += 1
        return out

    def pipeline(self, n, tile_fn, split=0.5):
        prevB = []
        for t in range(n):
            self.capture()
            self._mark = None
            tile_fn(t)
            L = self.end_capture()
            h = self._mark if self._mark is not None else int(len(L) * split)
            self.replay(self.zipmerge(L[:h], prevB))
            prevB = L[h:]
        self.replay(prevB)

    def op(self, eng, fn, reads=(), writes=(), dma=False):
        if self._cap is not None:
            self._cap.append((eng, fn, reads, writes, dma))
            return None
        o = Op(len(self.ops), eng, fn, dma)
        reads = list(reads)
        writes = list(writes) + [r for r in reads if isinstance(r, str) and r.startswith("bank")]
        deps = set()
        for r in reads:
            w = self.last_w.get(r)
            if w is not None:
                deps.add(w)
        for w_ in writes:
            w = self.last_w.get(w_)
            if w is not None:
                deps.add(w)
            for rd in self.readers.get(w_, ()):
                deps.add(rd)
        deps.discard(o.id)
        o.deps = sorted(deps)
        for r in reads:
            self.readers.setdefault(r, []).append(o.id)
        for w_ in writes:
            self.last_w[w_] = o.id
            self.readers[w_] = []
        self.ops.append(o)
        self.by_eng[eng].append(o)
        return o

    def pe(self, fn, reads=(), writes=()):
        return self.op("pe", fn, reads, writes)

    def act(self, fn, reads=(), writes=()):
        return self.op("act", fn, reads, writes)

    def dve(self, fn, reads=(), writes=()):
        return self.op("dve", fn, reads, writes)

    def pool(self, fn, reads=(), writes=()):
        return self.op("pool", fn, reads, writes)

    def dma(self, eng, fn, reads=(), writes=()):
        return self.op(eng, fn, reads, writes, dma=True)

    def emit(self, nc, final_ops=()):
        ops = self.ops
        for o in ops:
            for d in o.deps:
                do = ops[d]
                if do.dma or do.eng != o.eng or SAME_ENG_SYNC[o.eng]:
                    do.flag = True
        for o in final_ops:
            o.flag = True
        import contextlib
        with contextlib.ExitStack() as es:
            esem = {e: es.enter_context(nc.semaphore("s_" + e)) for e in ENGS}
            dsem = {}
            for e in ENGS:
                if self.dma_count_total(e) > 0:
                    dsem[e] = [es.enter_context(nc.semaphore("d_%s_%d" % (e, i))) for i in range(NSLOT)]
            cnt = {e: 0 for e in ENGS}
            dcnt = {e: 0 for e in ENGS}
            semkey = {}
            for o in ops:
                if o.dma:
                    j = dcnt[o.eng]
                    dcnt[o.eng] += 1
                    s = dsem[o.eng][j % NSLOT]
                    o.sem = s
                    o.val = 16 * (j // NSLOT + 1)
                    o.slot_guard = (s, 16 * (j // NSLOT)) if j >= NSLOT else None
                    semkey[id(s)] = ("d", o.eng, j % NSLOT)
                elif o.flag:
                    cnt[o.eng] += 1
                    o.sem = esem[o.eng]
                    o.val = cnt[o.eng]
            clock = {e: {} for e in ENGS}
            opvc = [None] * len(ops)
            plan = {e: [] for e in ENGS}
            for o in ops:
                ck = clock[o.eng]
                waits = {}
                for d in o.deps:
                    do = ops[d]
                    if not (do.dma or do.eng != o.eng or SAME_ENG_SYNC[o.eng]):
                        continue
                    k = id(do.sem)
                    if ck.get(k, 0) >= do.val:
                        continue
                    if k not in waits or waits[k][1] < do.val:
                        waits[k] = (do.sem, do.val, d)
                if o.slot_guard is not None:
                    s, v = o.slot_guard
                    k = id(s)
                    if ck.get(k, 0) < v and (k not in waits or waits[k][1] < v):
                        waits[k] = (s, v, None)
                wl = list(waits.items())
                keep = []
                for k, (s, v, d) in wl:
                    implied = False
                    for k2, (s2, v2, d2) in wl:
                        if k2 == k or d2 is None:
                            continue
                        vc2 = opvc[d2]
                        if vc2 is not None and vc2.get(k, 0) >= v:
                            implied = True
                            break
                    if not implied:
                        keep.append((s, v))
                for k, (s, v, d) in wl:
                    if ck.get(k, 0) < v:
                        ck[k] = v
                    if d is not None and opvc[d] is not None:
                        for kk, vv in opvc[d].items():
                            if ck.get(kk, 0) < vv:
                                ck[kk] = vv
                if o.sem is not None:
                    vc = dict(ck)
                    vc[id(o.sem)] = o.val
                    opvc[o.id] = vc
                    if not o.dma:
                        pass
                plan[o.eng].append((o, keep))
            self.stats = {e: (len(plan[e]), sum(len(k) for _, k in plan[e])) for e in ENGS}

            def run(eng_handle, lst, final_waits):
                for o, keep in lst:
                    for (s, v) in keep[1:]:
                        eng_handle.wait_ge(s, v)
                    ins = o.fn(eng_handle)
                    if keep:
                        s, v = keep[0]
                        if isinstance(ins, tuple):
                            ins[0]._wait_ge(s, v)
                        else:
                            ins._wait_ge(s, v)
                    if o.sem is not None:
                        last = ins[1] if isinstance(ins, tuple) else ins
                        last.then_inc(o.sem, 16 if o.dma else 1)
                for (s, v) in final_waits:
                    eng_handle.wait_ge(s, v)

            fw = [(o.sem, o.val) for o in final_ops]
            with nc.Block() as block:
                @block.tensor
                def _(e):
                    run(e, plan["pe"], [])

                @block.scalar
                def _(e):
                    run(e, plan["act"], [])

                @block.vector
                def _(e):
                    run(e, plan["dve"], [])

                @block.gpsimd
                def _(e):
                    run(e, plan["pool"], [])

                @block.sync
                def _(e):
                    run(e, plan["sp"], fw)

    def dma_count_total(self, e):
        return sum(1 for o in self.by_eng[e] if o.dma)


import contextlib
import numpy as np
from concourse.bass_utils import run_bass_kernel_spmd

F32 = mybir.dt.float32
BF16 = mybir.dt.bfloat16
AF = mybir.ActivationFunctionType
ALU = mybir.AluOpType
AX = mybir.AxisListType

D = 1024
SEQ = 8192
WT = 2560
NT = 20
NB = 5
NTB = 64
EPS = 1e-6
NEG = -30000.0


def build_program(stage=99):
    nc = bass.Bass("TRN2", target_bir_lowering=False)
    P = Prog()
    G = contextlib.ExitStack()
    uid = [0]
    bar_from = [0]

    def di(name, shape, dt=F32):
        return nc.dram_tensor(name, list(shape), dt, kind="ExternalInput").ap()

    def sb(st, name, shape, dt):
        uid[0] += 1
        return st.enter_context(nc.sbuf_tensor("%s_%d" % (name, uid[0]), list(shape), dt))

    def barrier():
        lasts = [P.by_eng[e][-1].id for e in ENGS if P.by_eng[e]]
        dmas = [o.id for o in P.ops[bar_from[0]:] if o.dma]
        bar_from[0] = len(P.ops)
        deps = sorted(set(lasts + dmas))
        for e in ENGS:
            o = P.op(e, lambda eng: eng.nop())
            o.deps = list(deps)
        P.last_w = {}
        P.readers = {}

    xw = di("xw", [WT, D]); xh = di("xh", [16, D]); xb = di("xb", [SEQ, D]); mem = di("mem", [256, D])
    mix_norm_g = di("mix_norm_g", [2, D]); xattn_norm_g = di("xattn_norm_g", [2, D]); ff_norm_g = di("ff_norm_g", [2, D])
    w_mem_q = di("w_mem_q", [2, D, D]); mem_q_g = di("mem_q_g", [2, 256]); w_mem_o = di("w_mem_o", [2, D, D])
    w_ff1 = di("w_ff1", [2, D, 4096]); w_ff2 = di("w_ff2", [2, 4096, D])
    mem_tok_norm_g = di("mem_tok_norm_g", [D]); w_mem_kv = di("w_mem_kv", [D, 2048]); mem_k_g = di("mem_k_g", [256])
    w_in_e = di("w_in_e", [1, D, 928]); pool_w = di("pool_w", [1, 4, 128, 128]); pool_scale = di("pool_scale", [1, 512])
    q_lora_g = di("q_lora_g", [1, 256]); w_uq = di("w_uq", [1, 256, 768]); kv_lora_g = di("kv_lora_g", [1, 128])
    w_ukv = di("w_ukv", [1, 128, 1024]); mla_q_g = di("mla_q_g", [1, 96]); mla_k_g = di("mla_k_g", [1, 96])
    w_out_e = di("w_out_e", [1, D, D]); w_qkv_o = di("w_qkv_o", [1, D, 3072])
    na_q_g = di("na_q_g", [1, 64]); na_k_g = di("na_k_g", [1, 64]); w_out_o = di("w_out_o", [1, D, D])
    cs_w = di("cs_w", [WT, 64]); cs_b = di("cs_b", [SEQ, 64]); invc = di("invc", [4, 2568])
    nab = di("nab", [16, 128, 25, 128])
    out = nc.dram_tensor("out", [WT, D], F32, kind="ExternalOutput").ap()
    kT_scr = nc.dram_tensor("kT_scr", [8, 96, SEQ], BF16, kind="Internal").ap()
    v_scr = nc.dram_tensor("v_scr", [8, 128, NTB, 64], BF16, kind="Internal").ap()
    qT_scr = nc.dram_tensor("qT_scr", [8, 96, WT], BF16, kind="Internal").ap()
    nq_scr = nc.dram_tensor("nq_scr", [8, 128, WT], BF16, kind="Internal").ap()
    nk_scr = nc.dram_tensor("nk_scr", [8, 128, WT], BF16, kind="Internal").ap()
    nv_scr = nc.dram_tensor("nv_scr", [8, 128, NT, 128], BF16, kind="Internal").ap()

    x = sb(G, "x", [128, NT, D], F32)
    ident = sb(G, "ident", [128, 128], BF16)
    identf = sb(G, "identf", [128, 128], F32)
    ones_b = sb(G, "ones_b", [128, 128], BF16)
    eps_t = sb(G, "eps", [128, 1], F32)
    memkT = sb(G, "memkT", [128, 4, 2, 256], BF16)
    memv = sb(G, "memv", [128, 2, 4, 256], BF16)
    SQ = [sb(G, "sq", [128, 1024], F32), sb(G, "sq", [128, 1024], F32)]
    SSQ = [sb(G, "ssq", [128, 1], F32) for _ in range(2)]
    RSTD = [sb(G, "rstd", [128, 1], F32) for _ in range(2)]
    HB = [sb(G, "hb", [128, D], BF16) for _ in range(2)]
    SS32 = [sb(G, "ss32", [128, 32], F32) for _ in range(2)]
    RS32 = [sb(G, "rs32", [128, 32], F32) for _ in range(2)]
    sq, ssq, rstd, hb, ss32, rs32 = SQ[0], SSQ[0], RSTD[0], HB[0], SS32[0], RS32[0]
    gn = sb(G, "gn", [128, D], F32)
    psum_all = G.enter_context(nc.psum_tensor("psum_all", [128, 4096], F32))
    banks = [psum_all[:, i * 512:(i + 1) * 512] for i in range(8)]
    B = ["bank%d" % i for i in range(8)]

    def bbf(i):
        return banks[i][:].bitcast(BF16)

    act, dve, pe, pool = P.act, P.dve, P.pe, P.pool

    def rstd_from(ss_ap, ss_res, dim, out_ap, out_res):
        n = ss_ap.shape[0]
        act(lambda e: e.activation(out=out_ap, in_=ss_ap, func=AF.Ln, scale=1.0 / dim, bias=eps_t[0:n, :]),
            reads=[ss_res, "eps"], writes=[out_res])
        act(lambda e: e.activation(out=out_ap, in_=out_ap, func=AF.Exp, scale=-0.5), reads=[out_res], writes=[out_res])

    def norm_hT(src_ap, src_res, hT_dst, hT_res, n=128, par=0, bank=0):
        sq_, ssq_, rstd_, hb_ = SQ[par], SSQ[par], RSTD[par], HB[par]
        sfx = "" if par == 0 else "_p1"
        act(lambda e: e.activation(out=sq_[0:n, 0:D], in_=src_ap, func=AF.Square, accum_out=ssq_[0:n, :]),
            reads=[src_res], writes=["sq" + sfx, "ssq" + sfx])
        rstd_from(ssq_[0:n, :], "ssq" + sfx, D, rstd_[0:n, :], "rstd" + sfx)
        dve(lambda e: e.scalar_tensor_tensor(out=hb_[0:n, :], in0=src_ap, scalar=rstd_[0:n, 0:1], in1=gn[0:n, :],
                                             op0=ALU.mult, op1=ALU.mult),
            reads=[src_res, "rstd" + sfx, "gn"], writes=["hb" + sfx])
        pT = bbf(bank)
        for k in range(8):
            pe(lambda e, k=k: e.transpose(out=pT[:, k * n:(k + 1) * n], in_=hb_[0:n, k * 128:(k + 1) * 128],
                                          identity=ident[0:n, 0:n]),
               reads=["hb" + sfx, "ident"], writes=[B[bank]])
        act(lambda e: e.copy(out=hT_dst, in_=pT[:, 0:8 * n].rearrange("p (k t) -> p k t", k=8)),
            reads=[B[bank]], writes=[hT_res])

    def load_w(dst, w_ap, res, eng="pool"):
        K = w_ap.shape[0]
        for k in range((K + 127) // 128):
            r = min(128, K - k * 128)
            P.dma(eng, lambda e, k=k, r=r: e.dma_start(out=dst[0:r, k, :], in_=w_ap[k * 128:k * 128 + r, :]),
                  writes=[(res, k)])

    def wres(res, nk):
        return [(res, k) for k in range(nk)]

    def load_bc(dst, vec_ap, res, eng="sp"):
        P.dma(eng, lambda e: e.dma_start(out=dst, in_=vec_ap.partition_broadcast(128)), writes=[res])

    def resid_add(t, pbank_lo, pbank_hi):
        for half, bk in ((0, pbank_lo), (1, pbank_hi)):
            dve(lambda e, half=half, bk=bk: e.tensor_tensor(out=x[:, t, half * 512:(half + 1) * 512],
                                                            in0=banks[bk][:], in1=x[:, t, half * 512:(half + 1) * 512],
                                                            op=ALU.add),
                reads=[B[bk], ("x", t)], writes=[("x", t)])

    def proj_resid(t, lhs_fn, lhs_res, nk, w_sb, w_res, k0=0):
        for half in range(2):
            for k in range(nk):
                pe(lambda e, half=half, k=k: e.matmul(banks[6 + half][:], lhsT=lhs_fn(k), rhs=w_sb[:, k0 + k, half * 512:(half + 1) * 512],
                                                       start=(k == 0), stop=(k == nk - 1)),
                   reads=lhs_res + [(w_res, k0 + k)], writes=[B[6 + half]])
        resid_add(t, 6, 7)

    pool(lambda e: e.memset(identf[:], 0.0), writes=["identf"])
    pool(lambda e: e.affine_select(out=identf[:], in_=identf[:], pattern=[[-1, 128]], compare_op=ALU.not_equal,
                                   fill=1.0, base=0, channel_multiplier=1), reads=["identf"], writes=["identf"])
    dve(lambda e: e.tensor_copy(out=ident[:], in_=identf[:]), reads=["identf"], writes=["ident"])
    dve(lambda e: e.memset(ones_b[:], 1.0), writes=["ones_b"])
    dve(lambda e: e.memset(eps_t[:], EPS), writes=["eps"])
    for t in range(NT):
        P.dma("sp", lambda e, t=t: e.dma_start(out=x[:, t, :], in_=xw[t * 128:(t + 1) * 128, :]), writes=[("x", t)])
    barrier()

    def phase_mem():
        with contextlib.ExitStack() as S:
            wkv = sb(S, "wkv", [128, 8, 2048], BF16)
            gk = sb(S, "gk", [128, 256], F32)
            memt = sb(S, "memt", [128, D], F32)
            hTm = sb(S, "hTm", [128, 8, 128], BF16)
            kn = sb(S, "kn", [128, D], F32)
            kbm = sb(S, "kbm", [128, D], BF16)
            load_w(wkv, w_mem_kv, "wkv")
            load_bc(gn[:], mem_tok_norm_g, "gn")
            load_bc(gk[:], mem_k_g, "gk")
            for m in range(2):
                P.dma("sp", lambda e, m=m: e.dma_start(out=memt[:], in_=mem[m * 128:(m + 1) * 128, :]), writes=["memt"])
                norm_hT(memt[:], "memt", hTm[:], "hTm")
                for n4 in range(4):
                    for k in range(8):
                        pe(lambda e, n4=n4, k=k: e.matmul(banks[1 + n4][:], lhsT=hTm[:, k, :], rhs=wkv[:, k, n4 * 512:(n4 + 1) * 512],
                                                          start=(k == 0), stop=(k == 7)),
                           reads=["hTm", ("wkv", k)], writes=[B[1 + n4]])
                for j in range(2):
                    act(lambda e, j=j: e.activation(out=sq[:, j * 512:(j + 1) * 512], in_=banks[1 + j][:], func=AF.Square),
                        reads=[B[1 + j]], writes=["sq"])
                dve(lambda e: e.tensor_reduce(out=ss32[:, 0:4], in_=sq[:, 0:D].rearrange("p (h d) -> p h d", h=4), axis=AX.X, op=ALU.add),
                    reads=["sq"], writes=["ss32"])
                rstd_from(ss32[:, 0:4], "ss32", 256, rs32[:, 0:4], "rs32")
                for j in range(2):
                    dve(lambda e, j=j: e.tensor_tensor(out=kn[:, j * 512:(j + 1) * 512].rearrange("p (h d) -> p h d", h=2),
                                                       in0=banks[1 + j][:].rearrange("p (h d) -> p h d", h=2),
                                                       in1=rs32[:, 2 * j:2 * j + 2].unsqueeze(2).to_broadcast([128, 2, 256]), op=ALU.mult),
                        reads=[B[1 + j], "rs32"], writes=["kn"])
                dve(lambda e: e.tensor_tensor(out=kbm[:].rearrange("p (h d) -> p h d", h=4), in0=kn[:].rearrange("p (h d) -> p h d", h=4),
                                              in1=gk[:].unsqueeze(1).to_broadcast([128, 4, 256]), op=ALU.mult),
                    reads=["kn", "gk"], writes=["kbm"])
                pT = bbf(0)
                for j in range(8):
                    pe(lambda e, j=j: e.transpose(out=pT[:, j * 128:(j + 1) * 128], in_=kbm[:, j * 128:(j + 1) * 128], identity=ident[:]),
                       reads=["kbm", "ident"], writes=[B[0]])
                act(lambda e, m=m: e.copy(out=memkT[:, :, :, m * 128:(m + 1) * 128],
                                          in_=pT[:, 0:1024].rearrange("p (h j t) -> p h j t", h=4, j=2)),
                    reads=[B[0]], writes=["memkT"])
                for j in range(2):
                    act(lambda e, m=m, j=j: e.copy(out=memv[:, m, 2 * j:2 * j + 2, :], in_=banks[3 + j][:].rearrange("p (h d) -> p h d", h=2)),
                        reads=[B[3 + j]], writes=["memv"])
            barrier()

    def rotary(src, src_res, cst, cs_res, dst, dst_res, t1, t2, shape3, sfx=""):
        H = shape3
        cc = cst[:, 0:32].unsqueeze(1).to_broadcast([128, H, 32])
        ns = cst[:, 32:48].unsqueeze(1).to_broadcast([128, H, 16])
        ps_ = cst[:, 48:64].unsqueeze(1).to_broadcast([128, H, 16])
        dve(lambda e: e.tensor_tensor(out=t1, in0=src, in1=cc, op=ALU.mult), reads=[src_res, cs_res], writes=["rot_t1" + sfx])
        dve(lambda e: e.tensor_tensor(out=t2[:, :, 0:16], in0=src[:, :, 16:32], in1=ns, op=ALU.mult), reads=[src_res, cs_res], writes=["rot_t2a" + sfx])
        dve(lambda e: e.tensor_tensor(out=t2[:, :, 16:32], in0=src[:, :, 0:16], in1=ps_, op=ALU.mult), reads=[src_res, cs_res], writes=["rot_t2b" + sfx])
        dve(lambda e: e.tensor_tensor(out=dst, in0=t1, in1=t2, op=ALU.add), reads=["rot_t1" + sfx, "rot_t2a" + sfx, "rot_t2b" + sfx], writes=[dst_res])

    def phase_A():
        with contextlib.ExitStack() as S:
            w_kv = sb(S, "w_kv", [128, 8, 160], BF16)
            wukv = sb(S, "wukv", [128, 1, 1024], BF16)
            gkv = sb(S, "gkv", [128, 128], F32)
            gk = sb(S, "gk96", [128, 96], F32)
            csb = sb(S, "csb", [128, NTB, 64], F32)
            two = lambda name, shape, dt: [sb(S, name, shape, dt) for _ in range(2)]
            xt = two("xt", [128, D], F32)
            hTt_ = two("hTt", [128, 8, 128], BF16)
            ckn_ = two("ckn", [128, 128], BF16)
            ckT_ = two("ckT", [128, 128], BF16)
            kn_ = two("kn", [128, 8, 64], F32)
            kb2 = two("kb", [128, 8, 96], BF16)
            krg_ = two("krg", [128, 1, 32], F32)
            krr_ = two("krr", [128, 1, 32], F32)
            t1_ = two("t1", [128, 1, 32], F32)
            t2_ = two("t2", [128, 1, 32], F32)
            vb_ = two("vb", [128, 8, 64], BF16)
            kst_ = two("kst", [96, 8, 512], BF16)
            ssr_ = two("ssr", [128, 1], F32)
            load_w(w_kv, w_in_e[0][:, 768:928], "w_kv")
            load_w(wukv, w_ukv[0], "wukv")
            load_bc(gn[:], mix_norm_g[0], "gn")
            load_bc(gkv[:], kv_lora_g[0], "gkv")
            load_bc(gk[:], mla_k_g[0], "gk")
            P.dma("sp", lambda e: e.dma_start(out=csb[:], in_=cs_b.rearrange("(c p) f -> p c f", p=128)), writes=["csb"])
            def tileA(t):
                p = t % 2
                x_ = "_%d" % p
                bT, bC, bK = 4 * p, 4 * p + 1, (4 * p + 2, 4 * p + 3)
                xtt, hTt, ckn, ckT, kn, kb_, krg, krr, t1, t2, vb, ssr = (xt[p], hTt_[p], ckn_[p], ckT_[p], kn_[p], kb2[p], krg_[p],
                                                                          krr_[p], t1_[p], t2_[p], vb_[p], ssr_[p])
                sq_, ssq_, rstd_, ss32_, rs32_ = SQ[p], SSQ[p], RSTD[p], SS32[p], RS32[p]
                sfx = "" if p == 0 else "_p1"
                kst = kst_[(t // 4) % 2]
                kstr = "kst%d" % ((t // 4) % 2)
                P.dma("sp", lambda e, t=t, xtt=xtt: e.dma_start(out=xtt[:], in_=xb[t * 128:(t + 1) * 128, :]), writes=["xt" + x_])
                norm_hT(xtt[:], "xt" + x_, hTt[:], "hTt" + x_, par=p, bank=bT)
                pC = banks[bC]
                for k in range(8):
                    pe(lambda e, k=k, pC=pC, hTt=hTt: e.matmul(pC[:, 0:160], lhsT=hTt[:, k, :], rhs=w_kv[:, k, :], start=(k == 0), stop=(k == 7)),
                       reads=["hTt" + x_, ("w_kv", k)], writes=[B[bC]])
                act(lambda e, pC=pC, sq_=sq_, ssq_=ssq_: e.activation(out=sq_[:, 0:128], in_=pC[:, 0:128], func=AF.Square, accum_out=ssq_[:]),
                    reads=[B[bC]], writes=["sq" + sfx, "ssq" + sfx])
                rstd_from(ssq_[:], "ssq" + sfx, 128, rstd_[:], "rstd" + sfx)
                dve(lambda e, pC=pC, ckn=ckn, rstd_=rstd_: e.scalar_tensor_tensor(out=ckn[:], in0=pC[:, 0:128], scalar=rstd_[:, 0:1], in1=gkv[:], op0=ALU.mult, op1=ALU.mult),
                    reads=[B[bC], "rstd" + sfx, "gkv"], writes=["ckn" + x_])
                pT2 = bbf(bT)
                pe(lambda e, pT2=pT2, ckn=ckn: e.transpose(out=pT2[:, 0:128], in_=ckn[:], identity=ident[:]), reads=["ckn" + x_, "ident"], writes=[B[bT]])
                act(lambda e, pT2=pT2, ckT=ckT: e.copy(out=ckT[:], in_=pT2[:, 0:128]), reads=[B[bT]], writes=["ckT" + x_])
                for j in range(2):
                    pe(lambda e, j=j, ckT=ckT, bK=bK: e.matmul(banks[bK[j]][:], lhsT=ckT[:], rhs=wukv[:, 0, j * 512:(j + 1) * 512], start=True, stop=True),
                       reads=["ckT" + x_, ("wukv", 0)], writes=[B[bK[j]]])
                for j in range(2):
                    act(lambda e, j=j, bK=bK, sq_=sq_: e.activation(out=sq_[:, j * 256:(j + 1) * 256].rearrange("p (h d) -> p h d", h=4),
                                                              in_=banks[bK[j]][:].rearrange("p (h d) -> p h d", h=4)[:, :, 0:64], func=AF.Square),
                        reads=[B[bK[j]]], writes=["sq" + sfx])
                dve(lambda e, sq_=sq_, ss32_=ss32_: e.tensor_reduce(out=ss32_[:, 0:8], in_=sq_[:, 0:512].rearrange("p (h d) -> p h d", h=8), axis=AX.X, op=ALU.add),
                    reads=["sq" + sfx], writes=["ss32" + sfx])
                act(lambda e, pC=pC, sq_=sq_, ssr=ssr: e.activation(out=sq_[:, 512:544], in_=pC[:, 128:160], func=AF.Square, accum_out=ssr[:]),
                    reads=[B[bC]], writes=["sq2" + sfx, "ssr" + x_])
                dve(lambda e, ss32_=ss32_, ssr=ssr: e.tensor_scalar(out=ss32_[:, 0:8], in0=ss32_[:, 0:8], scalar1=ssr[:, 0:1], scalar2=None, op0=ALU.add),
                    reads=["ss32" + sfx, "ssr" + x_], writes=["ss32" + sfx])
                rstd_from(ss32_[:, 0:8], "ss32" + sfx, 96, rs32_[:, 0:8], "rs32" + sfx)
                for j in range(2):
                    dve(lambda e, j=j, bK=bK, kn=kn, rs32_=rs32_: e.tensor_tensor(out=kn[:, 4 * j:4 * j + 4, :], in0=banks[bK[j]][:].rearrange("p (h d) -> p h d", h=4)[:, :, 0:64],
                                                                             in1=rs32_[:, 4 * j:4 * j + 4].unsqueeze(2).to_broadcast([128, 4, 64]), op=ALU.mult),
                        reads=[B[bK[j]], "rs32" + sfx], writes=["kn" + x_])
                pool(lambda e, kb_=kb_, kn=kn: e.tensor_tensor(out=kb_[:, :, 0:64], in0=kn[:], in1=gk[:, 0:64].unsqueeze(1).to_broadcast([128, 8, 64]), op=ALU.mult),
                     reads=["kn" + x_, "gk"], writes=["kb_n" + x_])
                dve(lambda e, krg=krg, pC=pC: e.tensor_tensor(out=krg[:, 0, :], in0=pC[:, 128:160], in1=gk[:, 64:96], op=ALU.mult),
                    reads=[B[bC], "gk"], writes=["krg" + x_])
                rotary(krg[:], "krg" + x_, csb[:, t, :], "csb", krr[:], "krr" + x_, t1[:], t2[:], 1, sfx=x_)
                dve(lambda e, kb_=kb_, krr=krr, rs32_=rs32_: e.tensor_tensor(out=kb_[:, :, 64:96], in0=krr[:, 0, :].unsqueeze(1).to_broadcast([128, 8, 32]),
                                                                        in1=rs32_[:, 0:8].unsqueeze(2).to_broadcast([128, 8, 32]), op=ALU.mult),
                    reads=["krr" + x_, "rs32" + sfx], writes=["kb_r" + x_])
                for j in range(2):
                    act(lambda e, j=j, bK=bK, vb=vb: e.copy(out=vb[:, 4 * j:4 * j + 4, :], in_=banks[bK[j]][:].rearrange("p (h d) -> p h d", h=4)[:, :, 64:128]),
                        reads=[B[bK[j]]], writes=["vb" + x_])
                P.dma("sp", lambda e, t=t, vb=vb: e.dma_start(out=v_scr[:, :, t, :].rearrange("h p d -> p h d"), in_=vb[:]), reads=["vb" + x_], writes=[("v_scr", t)])
                pT3 = bbf(bT)
                for h in range(8):
                    pe(lambda e, h=h, pT3=pT3, kb_=kb_: e.transpose(out=pT3[0:96, h * 128:(h + 1) * 128], in_=kb_[:, h, :], identity=ident[:]),
                       reads=["kb_n" + x_, "kb_r" + x_, "ident"], writes=[B[bT]])
                tt = t % 4
                act(lambda e, tt=tt, pT3=pT3, kst=kst: e.copy(out=kst[:, :, tt * 128:(tt + 1) * 128], in_=pT3[0:96, 0:1024].rearrange("p (h t) -> p h t", h=8)),
                    reads=[B[bT]], writes=[kstr])
                if tt == 3:
                    b4 = t // 4
                    P.dma("sp", lambda e, b4=b4, kst=kst: e.dma_start(out=kT_scr[:, :, b4 * 512:(b4 + 1) * 512].rearrange("h d t -> d h t"), in_=kst[:]),
                          reads=[kstr], writes=[("kT_scr", b4)])
            P.pipeline(NTB, tileA)
            barrier()

    def phase_B1():
        with contextlib.ExitStack() as S:
            w_q = sb(S, "w_q", [128, 8, 256], BF16)
            wuq = sb(S, "wuq", [128, 2, 768], BF16)
            gql = sb(S, "gql", [128, 256], F32)
            gq = sb(S, "gq96", [128, 96], F32)
            csw = sb(S, "csw", [128, NT, 64], F32)
            two = lambda name, shape, dt: [sb(S, name, shape, dt) for _ in range(2)]
            hTt_ = two("hTt", [128, 8, 128], BF16)
            cqn_ = two("cqn", [128, 256], BF16)
            cqT_ = two("cqT", [128, 2, 128], BF16)
            qn_ = two("qn", [128, 8, 96], F32)
            qr_ = two("qr", [128, 8, 32], F32)
            qb2 = two("qb", [128, 8, 96], BF16)
            t1_ = two("t1", [128, 8, 32], F32)
            t2_ = two("t2", [128, 8, 32], F32)
            qst_ = two("qst", [96, 8, 512], BF16)
            load_w(w_q, w_in_e[0][:, 512:768], "w_q")
            load_w(wuq, w_uq[0], "wuq")
            load_bc(gn[:], mix_norm_g[0], "gn")
            load_bc(gql[:], q_lora_g[0], "gql")
            load_bc(gq[:], mla_q_g[0], "gq")
            P.dma("sp", lambda e: e.dma_start(out=csw[:], in_=cs_w.rearrange("(c p) f -> p c f", p=128)), writes=["csw"])

            def tileB(t):
                p = t % 2
                x_ = "_%d" % p
                sfx = "" if p == 0 else "_p1"
                bT, bC, bQ = 4 * p, 4 * p + 1, (4 * p + 2, 4 * p + 3)
                hTt, cqn, cqT, qn, qr, qb_, t1, t2 = hTt_[p], cqn_[p], cqT_[p], qn_[p], qr_[p], qb2[p], t1_[p], t2_[p]
                sq_, ssq_, rstd_, ss32_, rs32_ = SQ[p], SSQ[p], RSTD[p], SS32[p], RS32[p]
                qst = qst_[(t // 4) % 2]
                qstr = "qst%d" % ((t // 4) % 2)
                norm_hT(x[:, t, :], ("x", t), hTt[:], "hTt" + x_, par=p, bank=bT)
                pC = banks[bC]
                for k in range(8):
                    pe(lambda e, k=k: e.matmul(pC[:, 0:256], lhsT=hTt[:, k, :], rhs=w_q[:, k, :], start=(k == 0), stop=(k == 7)),
                       reads=["hTt" + x_, ("w_q", k)], writes=[B[bC]])
                act(lambda e: e.activation(out=sq_[:, 0:256], in_=pC[:, 0:256], func=AF.Square, accum_out=ssq_[:]),
                    reads=[B[bC]], writes=["sq" + sfx, "ssq" + sfx])
                rstd_from(ssq_[:], "ssq" + sfx, 256, rstd_[:], "rstd" + sfx)
                dve(lambda e: e.scalar_tensor_tensor(out=cqn[:], in0=pC[:, 0:256], scalar=rstd_[:, 0:1], in1=gql[:], op0=ALU.mult, op1=ALU.mult),
                    reads=[B[bC], "rstd" + sfx, "gql"], writes=["cqn" + x_])
                pT2 = bbf(bT)
                for j in range(2):
                    pe(lambda e, j=j: e.transpose(out=pT2[:, j * 128:(j + 1) * 128], in_=cqn[:, j * 128:(j + 1) * 128], identity=ident[:]),
                       reads=["cqn" + x_, "ident"], writes=[B[bT]])
                act(lambda e: e.copy(out=cqT[:], in_=pT2[:, 0:256].rearrange("p (j t) -> p j t", j=2)), reads=[B[bT]], writes=["cqT" + x_])
                for (bk, c0, c1) in ((bQ[0], 0, 480), (bQ[1], 480, 768)):
                    for j in range(2):
                        pe(lambda e, bk=bk, c0=c0, c1=c1, j=j: e.matmul(banks[bk][:, 0:c1 - c0], lhsT=cqT[:, j, :], rhs=wuq[:, j, c0:c1],
                                                                        start=(j == 0), stop=(j == 1)),
                           reads=["cqT" + x_, ("wuq", j)], writes=[B[bk]])
                segs = ((bQ[0], 0, 5), (bQ[1], 5, 8))
                for (bk, h0, h1) in segs:
                    nh = h1 - h0
                    act(lambda e, bk=bk, h0=h0, nh=nh: e.activation(out=sq_[:, h0 * 96:(h0 + nh) * 96], in_=banks[bk][:, 0:nh * 96], func=AF.Square),
                        reads=[B[bk]], writes=["sq" + sfx])
                dve(lambda e: e.tensor_reduce(out=ss32_[:, 0:8], in_=sq_[:, 0:768].rearrange("p (h d) -> p h d", h=8), axis=AX.X, op=ALU.add),
                    reads=["sq" + sfx], writes=["ss32" + sfx])
                rstd_from(ss32_[:, 0:8], "ss32" + sfx, 96, rs32_[:, 0:8], "rs32" + sfx)
                for (bk, h0, h1) in segs:
                    nh = h1 - h0
                    dve(lambda e, bk=bk, h0=h0, nh=nh: e.tensor_tensor(out=qn[:, h0:h0 + nh, :], in0=banks[bk][:, 0:nh * 96].rearrange("p (h d) -> p h d", h=nh),
                                                                       in1=rs32_[:, h0:h0 + nh].unsqueeze(2).to_broadcast([128, nh, 96]), op=ALU.mult),
                        reads=[B[bk], "rs32" + sfx], writes=["qn" + x_])
                pool(lambda e: e.tensor_tensor(out=qb_[:, :, 0:64], in0=qn[:, :, 0:64], in1=gq[:, 0:64].unsqueeze(1).to_broadcast([128, 8, 64]), op=ALU.mult),
                     reads=["qn" + x_, "gq"], writes=["qb_n" + x_])
                dve(lambda e: e.tensor_tensor(out=qr[:], in0=qn[:, :, 64:96], in1=gq[:, 64:96].unsqueeze(1).to_broadcast([128, 8, 32]), op=ALU.mult),
                    reads=["qn" + x_, "gq"], writes=["qr" + x_])
                rotary(qr[:], "qr" + x_, csw[:, t, :], "csw", qb_[:, :, 64:96], "qb_r" + x_, t1[:], t2[:], 8, sfx=x_)
                pT3 = bbf(bT)
                for h in range(8):
                    pe(lambda e, h=h: e.transpose(out=pT3[0:96, h * 128:(h + 1) * 128], in_=qb_[:, h, :], identity=ident[:]),
                       reads=["qb_n" + x_, "qb_r" + x_, "ident"], writes=[B[bT]])
                tt = t % 4
                act(lambda e, tt=tt: e.copy(out=qst[:, :, tt * 128:(tt + 1) * 128], in_=pT3[0:96, 0:1024].rearrange("p (h t) -> p h t", h=8)),
                    reads=[B[bT]], writes=[qstr])
                if tt == 3:
                    b4 = t // 4
                    P.dma("sp", lambda e, b4=b4: e.dma_start(out=qT_scr[:, :, b4 * 512:(b4 + 1) * 512].rearrange("h d t -> d h t"), in_=qst[:]),
                          reads=[qstr], writes=[("qT_scr", b4)])

            P.pipeline(NT, tileB)
            barrier()

    def phase_B2():
        with contextlib.ExitStack() as S:
            w_p = sb(S, "w_p", [128, 8, 512], BF16)
            pw = sb(S, "pw", [128, 4, 128], BF16)
            psc = sb(S, "psc", [128, 4], F32)
            wout = sb(S, "wout", [128, 4, D], BF16)
            inv_t = sb(S, "inv_t", [128, 4, 512], F32)
            U = sb(S, "U", [128, 4, 528], F32)
            UH = sb(S, "UH", [128, 4, 16], F32)
            a2 = sb(S, "a2", [128, 3, 528], F32)
            a4 = sb(S, "a4", [128, 2, 528], F32)
            a8 = sb(S, "a8", [128, 1, 528], F32)
            Sm = sb(S, "Sm", [128, 4, 512], F32)
            Dt = sb(S, "Dt", [128, 4, 512], BF16)
            hTb = sb(S, "hTb", [128, 8, 512], BF16)
            hTh = sb(S, "hTh", [128, 8, 16], BF16)
            xht = sb(S, "xht", [16, D], F32)
            yT = sb(S, "yT", [128, 4, WT], BF16)
            load_w(w_p, w_in_e[0][:, 0:512], "w_p")
            for g in range(4):
                P.dma("pool", lambda e, g=g: e.dma_start(out=pw[:, g, :], in_=pool_w[0, g]), writes=[("pw", g)])
            load_w(wout, w_out_e[0][0:512, :], "wout")
            load_bc(gn[:], mix_norm_g[0], "gn")
            P.dma("sp", lambda e: e.dma_start(out=psc[:], in_=pool_scale[0].rearrange("(g d) -> d g", g=4), allow_slow_non_contiguous=True), writes=["psc"])
            P.dma("sp", lambda e: e.dma_start(out=xht[:], in_=xh[:, :]), writes=["xht"])
            dve(lambda e: e.memset(U[:], 0.0), writes=["U"])
            norm_hT(xht[:], "xht", hTh[:], "hTh", n=16)
            for g in range(4):
                for k in range(8):
                    pe(lambda e, g=g, k=k: e.matmul(banks[1][:, g * 16:(g + 1) * 16], lhsT=w_p[:, k, g * 128:(g + 1) * 128], rhs=hTh[:, k, :],
                                                    start=(k == 0), stop=(k == 7)),
                       reads=["hTh", ("w_p", k)], writes=[B[1]])
            act(lambda e: e.copy(out=UH[:], in_=banks[1][:, 0:64].rearrange("p (g t) -> p g t", g=4)), reads=[B[1]], writes=["UH"])
            dve(lambda e: e.tensor_copy(out=U[:, :, 520:528], in_=UH[:, :, 0:8]), reads=["UH", "U"], writes=["U"])

            def pool_step(b, ncols, tok0, c0):
                P.dma("sp", lambda e: e.dma_start(out=inv_t[:, :, 0:(512 if b < NB else 8)],
                                                  in_=invc[:, b * 512:b * 512 + (512 if b < NB else 8)].partition_broadcast(128)),
                      writes=["inv_t"])
                pool(lambda e: e.tensor_tensor(out=a2[:, :, 0:527], in0=U[:, 1:4, 0:527], in1=U[:, 1:4, 1:528], op=ALU.add), reads=["U"], writes=["a2"])
                pool(lambda e: e.tensor_tensor(out=a4[:, :, 0:525], in0=a2[:, 1:3, 0:525], in1=a2[:, 1:3, 2:527], op=ALU.add), reads=["a2"], writes=["a4"])
                pool(lambda e: e.tensor_tensor(out=a8[:, :, 0:521], in0=a4[:, 1:2, 0:521], in1=a4[:, 1:2, 4:525], op=ALU.add), reads=["a4"], writes=["a8"])
                dve(lambda e: e.tensor_tensor(out=Sm[:, 0, :], in0=U[:, 0, 7:519], in1=U[:, 0, 8:520], op=ALU.add), reads=["U"], writes=["Sm0"])
                dve(lambda e: e.tensor_tensor(out=Sm[:, 1, :], in0=a2[:, 0, 6:518], in1=a2[:, 0, 8:520], op=ALU.add), reads=["a2"], writes=["Sm1"])
                dve(lambda e: e.tensor_tensor(out=Sm[:, 2, :], in0=a4[:, 0, 4:516], in1=a4[:, 0, 8:520], op=ALU.add), reads=["a4"], writes=["Sm2"])
                dve(lambda e: e.tensor_tensor(out=Sm[:, 3, :], in0=a8[:, 0, 0:512], in1=a8[:, 0, 8:520], op=ALU.add), reads=["a8"], writes=["Sm3"])
                dve(lambda e: e.tensor_tensor(out=Sm[:], in0=Sm[:], in1=inv_t[:], op=ALU.mult), reads=["Sm0", "Sm1", "Sm2", "Sm3", "inv_t"], writes=["Sm"])
                dve(lambda e: e.tensor_tensor(out=Dt[:], in0=Sm[:], in1=U[:, :, 8:520], op=ALU.subtract), reads=["Sm", "U"], writes=["Dt"])
                for g in range(4):
                    pe(lambda e, g=g: e.matmul(banks[2 + (g % 2)][:], lhsT=pw[:, g, :], rhs=Dt[:, g, :], start=True, stop=True),
                       reads=["Dt", ("pw", g)], writes=[B[2 + (g % 2)]])
                    act(lambda e, g=g: e.activation(out=yT[:, g, tok0:tok0 + ncols], in_=banks[2 + (g % 2)][:, c0:c0 + ncols], func=AF.Copy, scale=psc[:, g:g + 1]),
                        reads=[B[2 + (g % 2)], "psc"], writes=[("yT", g)])

            for b in range(NB):
                for tt in range(4):
                    t = 4 * b + tt
                    norm_hT(x[:, t, :], ("x", t), hTb[:, :, tt * 128:(tt + 1) * 128], "hTb")
                dve(lambda e: e.tensor_copy(out=U[:, :, 0:16], in_=U[:, :, 512:528]), reads=["U"], writes=["U"])
                for g in range(4):
                    for k in range(8):
                        pe(lambda e, g=g, k=k: e.matmul(banks[4 + (g % 2)][:], lhsT=w_p[:, k, g * 128:(g + 1) * 128], rhs=hTb[:, k, :],
                                                        start=(k == 0), stop=(k == 7)),
                           reads=["hTb", ("w_p", k)], writes=[B[4 + (g % 2)]])
                    act(lambda e, g=g: e.copy(out=U[:, g, 16:528], in_=banks[4 + (g % 2)][:]), reads=[B[4 + (g % 2)], "U"], writes=["U"])
                if b == 0:
                    pool_step(0, 504, 0, 8)
                else:
                    pool_step(b, 512, 512 * b - 8, 0)
            dve(lambda e: e.tensor_copy(out=U[:, :, 0:16], in_=U[:, :, 512:528]), reads=["U"], writes=["U"])
            dve(lambda e: e.tensor_copy(out=U[:, :, 16:24], in_=UH[:, :, 8:16]), reads=["UH", "U"], writes=["U"])
            pool_step(NB, 8, WT - 8, 0)
            for t in range(NT):
                proj_resid(t, lambda k, t=t: yT[:, k, t * 128:(t + 1) * 128], [("yT", g) for g in range(4)], 4, wout, "wout")
            barrier()

    def phase_C():
        with contextlib.ExitStack() as S:
            oT = sb(S, "oT", [128, 4, WT], BF16)
            with contextlib.ExitStack() as S2:
                kh = [sb(S2, "kh", [96, SEQ], BF16) for _ in range(2)]
                va = [sb(S2, "va", [128, NTB, 128], BF16) for _ in range(2)]
                qh = [sb(S2, "qh", [96, WT], BF16) for _ in range(2)]
                E2 = [sb(S2, "E", [128, 1024], BF16) for _ in range(2)]
                rec = sb(S2, "rec", [128, 512], F32)
                for p in range(2):
                    dve(lambda e, p=p: e.memset(va[p][:, :, (1 - p) * 64:(1 - p) * 64 + 64], 1.0), writes=[("va1", p)])
                scale = 96 ** -0.5
                NPAIR = NTB // 2
                for h in range(8):
                    p = h % 2
                    lo, hi = p * 64, p * 64 + 64
                    dlo, dhi = (1 - p) * 64, (1 - p) * 64 + 64
                    P.dma("sp", lambda e, h=h, p=p: e.dma_start(out=kh[p][:], in_=kT_scr[h]), writes=[("kh", p)])
                    P.dma("sp", lambda e, h=h, p=p: e.dma_start(out=va[p][:, :, p * 64:p * 64 + 64], in_=v_scr[h]), writes=[("va", p)])
                    P.dma("sp", lambda e, h=h, p=p: e.dma_start(out=qh[p][:], in_=qT_scr[h]), writes=[("qh", p)])
                    for qb in range(NB):
                        ob = 4 + (qb % 2)

                        def S_pair(cc, qb=qb, p=p):
                            for s_ in range(2):
                                c = 2 * cc + s_
                                bk = 2 * (cc % 2) + s_
                                pe(lambda e, c=c, bk=bk: e.matmul(banks[bk], lhsT=kh[p][:, c * 128:(c + 1) * 128], rhs=qh[p][:, qb * 512:(qb + 1) * 512],
                                                                  start=True, stop=True),
                                   reads=[("kh", p), ("qh", p)], writes=[B[bk]])
                        S_pair(0)
                        S_pair(1)
                        for cc in range(NPAIR):
                            pp = cc % 2
                            act(lambda e, pp=pp: e.activation(out=E2[pp][:], in_=psum_all[:, 2 * pp * 512:(2 * pp + 2) * 512], func=AF.Exp, scale=scale),
                                reads=[B[2 * pp], B[2 * pp + 1]], writes=[("E", pp)])
                            for s_ in range(2):
                                c = 2 * cc + s_
                                pe(lambda e, c=c, p=p, ob=ob, pp=pp, s_=s_: e.matmul(banks[ob], lhsT=va[p][:, c, :], rhs=E2[pp][:, s_ * 512:(s_ + 1) * 512],
                                                                                   start=(c == 0), stop=(c == NTB - 1)),
                                   reads=[("va", p), ("va1", p), ("E", pp)], writes=[B[ob]])
                            if cc + 2 < NPAIR:
                                S_pair(cc + 2)
                        dve(lambda e, ob=ob, lo=lo, hi=hi, dlo=dlo, dhi=dhi: e.tensor_copy(out=rec[lo:hi, :], in_=banks[ob][dlo:dhi, :]), reads=[B[ob]], writes=["rec"])
                        dve(lambda e, lo=lo, hi=hi: e.reciprocal(out=rec[lo:hi, :], in_=rec[lo:hi, :]), reads=["rec"], writes=["rec"])
                        dve(lambda e, ob=ob, h=h, qb=qb, lo=lo, hi=hi: e.tensor_tensor(out=oT[lo:hi, h // 2, qb * 512:(qb + 1) * 512], in0=banks[ob][lo:hi, :],
                                                                         in1=rec[lo:hi, :], op=ALU.mult),
                            reads=[B[ob], "rec"], writes=[("oT", h // 2)])
                barrier()
            with contextlib.ExitStack() as S2:
                wout = sb(S2, "wout2", [128, 4, D], BF16)
                load_w(wout, w_out_e[0][512:1024, :], "wout2")
                for t in range(NT):
                    proj_resid(t, lambda k, t=t: oT[:, k, t * 128:(t + 1) * 128], [("oT", g) for g in range(4)], 4, wout, "wout2")
                barrier()

    def phase_xattn(L):
        with contextlib.ExitStack() as S:
            wq = sb(S, "wq", [128, 8, D], BF16)
            wo = sb(S, "wo", [128, 8, D], BF16)
            gq = sb(S, "gq256", [128, 256], F32)
            hTt = sb(S, "hTt", [128, 8, 128], BF16)
            qn = sb(S, "qn", [128, D], F32)
            qb_ = sb(S, "qb", [128, D], BF16)
            qT2 = [sb(S, "qT", [128, 8, 512], BF16) for _ in range(2)]
            E = [sb(S, "E", [128, 512], BF16) for _ in range(2)]
            rec = sb(S, "rec", [128, 512], F32)
            oTb = sb(S, "oTb", [128, 8, 512], BF16)
            load_w(wq, w_mem_q[L], "wq")
            load_w(wo, w_mem_o[L], "wo")
            load_bc(gn[:], xattn_norm_g[L], "gn")
            load_bc(gq[:], mem_q_g[L], "gq")
            scale = 256 ** -0.5

            def block(b):
                qT = qT2[b % 2]
                qTr = "qT%d" % (b % 2)
                for tt in range(4):
                    t = 4 * b + tt
                    norm_hT(x[:, t, :], ("x", t), hTt[:], "hTt")
                    for half in range(2):
                        for k in range(8):
                            pe(lambda e, half=half, k=k: e.matmul(banks[1 + half][:], lhsT=hTt[:, k, :], rhs=wq[:, k, half * 512:(half + 1) * 512],
                                                                   start=(k == 0), stop=(k == 7)),
                               reads=["hTt", ("wq", k)], writes=[B[1 + half]])
                    for hh in range(4):
                        act(lambda e, hh=hh: e.activation(out=sq[:, hh * 256:(hh + 1) * 256], in_=banks[1 + hh // 2][:, (hh % 2) * 256:(hh % 2 + 1) * 256],
                                                          func=AF.Square, accum_out=ss32[:, hh:hh + 1]),
                            reads=[B[1 + hh // 2]], writes=["sq", "ss32"])
                    rstd_from(ss32[:, 0:4], "ss32", 256, rs32[:, 0:4], "rs32")
                    for hh in range(4):
                        dve(lambda e, hh=hh: e.scalar_tensor_tensor(out=qb_[:, hh * 256:(hh + 1) * 256], in0=banks[1 + hh // 2][:, (hh % 2) * 256:(hh % 2 + 1) * 256],
                                                                    scalar=rs32[:, hh:hh + 1], in1=gq[:], op0=ALU.mult, op1=ALU.mult),
                            reads=[B[1 + hh // 2], "rs32", "gq"], writes=["qb"])
                    pT = bbf(0)
                    for j in range(8):
                        pe(lambda e, j=j: e.transpose(out=pT[:, j * 128:(j + 1) * 128], in_=qb_[:, j * 128:(j + 1) * 128], identity=ident[:]),
                           reads=["qb", "ident"], writes=[B[0]])
                    act(lambda e, tt=tt, qT=qT: e.copy(out=qT[:, :, tt * 128:(tt + 1) * 128], in_=pT[:, 0:1024].rearrange("p (j t) -> p j t", j=8)),
                        reads=[B[0]], writes=[qTr])
                P.mark()
                for h in range(4):
                    for m in range(2):
                        for j in range(2):
                            pe(lambda e, h=h, m=m, j=j, qT=qT: e.matmul(banks[3 + m][:], lhsT=memkT[:, h, j, m * 128:(m + 1) * 128], rhs=qT[:, 2 * h + j, :],
                                                                         start=(j == 0), stop=(j == 1)),
                               reads=[qTr, "memkT"], writes=[B[3 + m]])
                        act(lambda e, m=m: e.activation(out=E[m][:], in_=banks[3 + m][:], func=AF.Exp, scale=scale), reads=[B[3 + m]], writes=[("E", m)])
                    for m in range(2):
                        pe(lambda e, m=m: e.matmul(banks[5][:], lhsT=ones_b[:], rhs=E[m][:], start=(m == 0), stop=(m == 1)),
                           reads=["ones_b", ("E", m)], writes=[B[5]])
                    act(lambda e: e.activation(out=rec[:], in_=banks[5][:], func=AF.Ln), reads=[B[5]], writes=["rec"])
                    act(lambda e: e.activation(out=rec[:], in_=rec[:], func=AF.Exp, scale=-1.0), reads=["rec"], writes=["rec"])
                    for dv in range(2):
                        for m in range(2):
                            pe(lambda e, h=h, m=m, dv=dv: e.matmul(banks[6 + dv][:], lhsT=memv[:, m, h, dv * 128:(dv + 1) * 128], rhs=E[m][:],
                                                                    start=(m == 0), stop=(m == 1)),
                               reads=["memv", ("E", m)], writes=[B[6 + dv]])
                        dve(lambda e, h=h, dv=dv: e.tensor_tensor(out=oTb[:, 2 * h + dv, :], in0=banks[6 + dv][:], in1=rec[:], op=ALU.mult),
                            reads=[B[6 + dv], "rec"], writes=["oTb"])
                for tt in range(4):
                    t = 4 * b + tt
                    proj_resid(t, lambda k, tt=tt: oTb[:, k, tt * 128:(tt + 1) * 128], ["oTb"], 8, wo, "wo")

            P.pipeline(NB, block)
            barrier()

    def phase_mlp(L, last=False):
        NPASS = 8
        with contextlib.ExitStack() as S:
            hT = sb(S, "hTall", [128, 8, WT], BF16)
            w1 = [sb(S, "w1", [128, 8, 512], BF16) for _ in range(2)]
            w2 = [sb(S, "w2", [128, 4, D], BF16) for _ in range(2)]
            aT = [sb(S, "aT", [128, 512], BF16) for _ in range(4)]
            s2 = [sb(S, "s2", [128, 512], F32) for _ in range(2)]
            load_bc(gn[:], ff_norm_g[L], "gn")

            def load_pass(ps_):
                b = ps_ % 2
                load_w(w1[b], w_ff1[L][:, ps_ * 512:(ps_ + 1) * 512], ("w1", b))
                load_w(w2[b], w_ff2[L][ps_ * 512:(ps_ + 1) * 512, :], ("w2", b))
            load_pass(0)

            def norm_block(b):
                P.capture()
                for tt in range(4):
                    t = 4 * b + tt
                    norm_hT(x[:, t, :], ("x", t), hT[:, :, t * 128:(t + 1) * 128], ("hT", t // 4), par=t % 2, bank=(0, 3)[t % 2])
                return P.end_capture()

            P.replay(norm_block(0))
            for ps_ in range(NPASS):
                pb = ps_ % 2
                if ps_ + 1 < NPASS:
                    load_pass(ps_ + 1)
                for b in range(NB):
                  if ps_ == 0:
                    P.capture()
                  if True:
                    for f in range(4):
                        zb = 1 + (f % 2)
                        for k in range(8):
                            pe(lambda e, f=f, k=k, zb=zb, pb=pb, b=b: e.matmul(banks[zb][:], lhsT=w1[pb][:, k, f * 128:(f + 1) * 128],
                                                                                 rhs=hT[:, k, b * 512:(b + 1) * 512], start=(k == 0), stop=(k == 7)),
                               reads=[("hT", b), (("w1", pb), k)], writes=[B[zb]])
                        act(lambda e, f=f, zb=zb: e.activation(out=s2[f % 2][:], in_=banks[zb][:], func=AF.Square), reads=[B[zb]], writes=[("s2", f % 2)])
                        dve(lambda e, f=f, zb=zb: e.scalar_tensor_tensor(out=aT[f][:], in0=banks[zb][:], scalar=0.0, in1=s2[f % 2][:],
                                                                          op0=ALU.is_gt, op1=ALU.mult),
                            reads=[B[zb], ("s2", f % 2)], writes=[("aT", f)])
                    for tt in range(4):
                        t = 4 * b + tt
                        proj_resid(t, lambda k, tt=tt: aT[k][:, tt * 128:(tt + 1) * 128], [("aT", f) for f in range(4)], 4, w2[pb], ("w2", pb))
                  if ps_ == 0:
                    Cb = P.end_capture()
                    Nb = norm_block(b + 1) if b + 1 < NB else []
                    P.replay(P.zipmerge(Cb, Nb))
            if last:
                for t in range(NT):
                    fins.append(P.dma("sp", lambda e, t=t: e.dma_start(out=out[t * 128:(t + 1) * 128, :], in_=x[:, t, :]), reads=[("x", t)]))
            barrier()

    def phase_G1():
        with contextlib.ExitStack() as S:
            wqkv = sb(S, "wqkv", [128, 8, 3072], BF16)
            gqk = sb(S, "gqk", [128, 2, 64], F32)
            two = lambda name, shape, dt: [sb(S, name, shape, dt) for _ in range(2)]
            hTt_ = two("hTt", [128, 8, 128], BF16)
            qkn_ = two("qkn", [128, 1024], F32)
            qkb_ = two("qkb", [128, 1024], BF16)
            qkst_ = two("qkst", [128, 8, 128], BF16)
            vst_ = two("vst", [128, 4, 128], BF16)
            load_w(wqkv, w_qkv_o[0], "wqkv")
            load_bc(gn[:], mix_norm_g[1], "gn")
            load_bc(gqk[:, 0, :], na_q_g[0], "gqk0")
            load_bc(gqk[:, 1, :], na_k_g[0], "gqk1")

            def unit(n):
                t, u = n // 2, n % 2
                x_ = "_%d" % u
                sfx = "" if u == 0 else "_p1"
                bT, bQ, bV = 4 * u, (4 * u + 1, 4 * u + 2), 4 * u + 3
                hTt = hTt_[t % 2]
                hr = "hTt_%d" % (t % 2)
                qkn, qkb, qkst, vst = qkn_[u], qkb_[u], qkst_[u], vst_[u]
                sq_, ss32_, rs32_ = SQ[u], SS32[u], RS32[u]
                if u == 0:
                    norm_hT(x[:, t, :], ("x", t), hTt[:], hr, par=0, bank=bT)
                for n2 in range(2):
                    for k in range(8):
                        pe(lambda e, n2=n2, k=k: e.matmul(banks[bQ[n2]][:], lhsT=hTt[:, k, :], rhs=wqkv[:, k, u * 1024 + n2 * 512:u * 1024 + (n2 + 1) * 512],
                                                          start=(k == 0), stop=(k == 7)),
                           reads=[hr, ("wqkv", k)], writes=[B[bQ[n2]]])
                for n2 in range(2):
                    act(lambda e, n2=n2: e.activation(out=sq_[:, n2 * 512:(n2 + 1) * 512], in_=banks[bQ[n2]][:], func=AF.Square),
                        reads=[B[bQ[n2]]], writes=["sq" + sfx])
                dve(lambda e: e.tensor_reduce(out=ss32_[:, 0:16], in_=sq_[:, 0:1024].rearrange("p (h d) -> p h d", h=16), axis=AX.X, op=ALU.add),
                    reads=["sq" + sfx], writes=["ss32" + sfx])
                rstd_from(ss32_[:, 0:16], "ss32" + sfx, 64, rs32_[:, 0:16], "rs32" + sfx)
                for n2 in range(2):
                    dve(lambda e, n2=n2: e.tensor_tensor(out=qkn[:, n2 * 512:(n2 + 1) * 512].rearrange("p (h d) -> p h d", h=8),
                                                         in0=banks[bQ[n2]][:].rearrange("p (h d) -> p h d", h=8),
                                                         in1=rs32_[:, 8 * n2:8 * n2 + 8].unsqueeze(2).to_broadcast([128, 8, 64]), op=ALU.mult),
                        reads=[B[bQ[n2]], "rs32" + sfx], writes=["qkn" + x_])
                pool(lambda e: e.tensor_tensor(out=qkb[:].rearrange("p (h d) -> p h d", h=16), in0=qkn[:].rearrange("p (h d) -> p h d", h=16),
                                               in1=gqk[:, u, :].unsqueeze(1).to_broadcast([128, 16, 64]), op=ALU.mult),
                     reads=["qkn" + x_, "gqk0", "gqk1"], writes=["qkb" + x_])
                for k in range(8):
                    pe(lambda e, k=k: e.matmul(banks[bV][:], lhsT=hTt[:, k, :], rhs=wqkv[:, k, 2048 + u * 512:2048 + (u + 1) * 512],
                                               start=(k == 0), stop=(k == 7)),
                       reads=[hr, ("wqkv", k)], writes=[B[bV]])
                act(lambda e: e.copy(out=vst[:], in_=banks[bV][:].rearrange("p (h d) -> p h d", h=4)), reads=[B[bV]], writes=["vst" + x_])
                P.dma("sp", lambda e: e.dma_start(out=nv_scr[4 * u:4 * u + 4, :, t, :].rearrange("h p d -> p h d"), in_=vst[:]),
                      reads=["vst" + x_], writes=[("nv_scr", n)])
                pT = bbf(bT)
                for j in range(8):
                    pe(lambda e, j=j: e.transpose(out=pT[:, j * 128:(j + 1) * 128], in_=qkb[:, j * 128:(j + 1) * 128], identity=ident[:]),
                       reads=["qkb" + x_, "ident"], writes=[B[bT]])
                act(lambda e: e.copy(out=qkst[:], in_=pT[:, 0:1024].rearrange("p (j t) -> p j t", j=8)), reads=[B[bT]], writes=["qkst" + x_])
                dst = nq_scr if u == 0 else nk_scr
                P.dma("sp", lambda e: e.dma_start(out=dst[:, :, t * 128:(t + 1) * 128].rearrange("h p t -> p h t"), in_=qkst[:]),
                      reads=["qkst" + x_], writes=[("nqk_scr", n)])

            P.pipeline(2 * NT, unit)
            barrier()

    def phase_G2():
        with contextlib.ExitStack() as S:
            cT = sb(S, "cT", [128, 8, WT], BF16)
            with contextlib.ExitStack() as S2:
                qT = sb(S2, "nqT", [128, WT], BF16)
                kT = sb(S2, "nkT", [128, WT], BF16)
                va = [sb(S2, "nva", [128, NT, 128], BF16) for _ in range(2)]
                bias32 = sb(S2, "nbias", [128, 13, 128], F32)
                bh = [sb(S2, "nbh", [128, 25, 128], BF16) for _ in range(2)]
                bl = [sb(S2, "nbl", [128, 25, 128], BF16) for _ in range(2)]
                E = [sb(S2, "nE", [128, 512], BF16) for _ in range(3)]
                bg = [SQ[1][:, 0:640].rearrange("p (a b) -> p a b", a=5), gn[:, 0:640].rearrange("p (a b) -> p a b", a=5)]
                St = [SQ[0][:, 0:512], SQ[0][:, 512:1024]]
                rec = [sb(S2, "rec", [128, 512], F32)] * 2
                dve(lambda e: e.memset(va[0][:, :, 64:128], 1.0), writes=[("va1", 0)])
                dve(lambda e: e.memset(va[1][:, :, 0:64], 1.0), writes=[("va1", 1)])
                scale = 64 ** -0.5
                inv_scale = 8.0

                def row_cls(r):
                    return {0: 1, 2: 2, 36: 3, 38: 4}.get(r, 0)

                def prep_bias(h):
                    p = h % 2
                    P.dma("sp", lambda e, h=h, p=p: e.dma_start(out=bg[p], in_=nab[h][:, 0:5, :]), writes=[("bg", p)])
                    for (v0, v1) in ((0, 13), (13, 25)):
                        nv = v1 - v0
                        P.dma("sp", lambda e, h=h, v0=v0, v1=v1, nv=nv: e.dma_start(out=bias32[:, 0:nv, :], in_=nab[h][:, v0:v1, :]), writes=["bias32"])
                        dve(lambda e, p=p, v0=v0, v1=v1, nv=nv: e.tensor_scalar(out=bh[p][:, v0:v1, :], in0=bias32[:, 0:nv, :], scalar1=inv_scale, scalar2=None, op0=ALU.mult),
                            reads=["bias32"], writes=[("bh", p)])
                        dve(lambda e, p=p, v0=v0, v1=v1, nv=nv: e.scalar_tensor_tensor(out=bl[p][:, v0:v1, :].rearrange("p a b -> p (a b)"),
                                                                                      in0=bias32[:, 0:nv, :].rearrange("p a b -> p (a b)"), scalar=inv_scale,
                                                                                      in1=bh[p][:, v0:v1, :].rearrange("p a b -> p (a b)"), op0=ALU.mult, op1=ALU.subtract),
                            reads=["bias32", ("bh", p)], writes=[("bl", p)])

                prep_bias(0)
                for hp in range(8):
                    P.dma("sp", lambda e, hp=hp: e.dma_start(out=qT[:], in_=nq_scr[hp]), writes=["nqT"])
                    P.dma("sp", lambda e, hp=hp: e.dma_start(out=kT[:], in_=nk_scr[hp]), writes=["nkT"])
                    P.dma("sp", lambda e, hp=hp: e.dma_start(out=va[0][:, :, 0:64], in_=nv_scr[hp][:, :, 0:64]), writes=[("va", 0)])
                    P.dma("sp", lambda e, hp=hp: e.dma_start(out=va[1][:, :, 64:128], in_=nv_scr[hp][:, :, 64:128]), writes=[("va", 1)])
                    for p in range(2):
                        h = 2 * hp + p
                        lo, hi = p * 64, p * 64 + 64
                        dlo, dhi = (1 - p) * 64, (1 - p) * 64 + 64
                        items = [(blk, j) for blk in range(NB) for j in range(5)]

                        def S_stage(i, p=p, lo=lo, hi=hi):
                            blk, j = items[i]
                            sbk = i % 3
                            groups = []
                            for rp in range(4):
                                c = row_cls(8 * blk + 2 * rp)
                                if groups and groups[-1][0] == c:
                                    groups[-1][2] += 1
                                else:
                                    groups.append([c, rp, 1])
                            first = True
                            use_dve = (len(groups) == 1 and i % 2 == 1)
                            for src in (() if use_dve else (bh, bl)):
                                for (c, rp0, n) in groups:
                                    pe(lambda e, src=src, c=c, rp0=rp0, n=n, j=j, sbk=sbk, first=first: e.matmul(
                                            banks[sbk][:, rp0 * 128:(rp0 + n) * 128], lhsT=ident[:],
                                            rhs=src[p][:, c * 5 + j, :].unsqueeze(1).to_broadcast([128, n, 128]),
                                            start=first, stop=False, skip_group_check=True),
                                       reads=[("bh", p), ("bl", p), "ident"], writes=[B[sbk]])
                                    first = False
                            for rp in range(4):
                                r = 8 * blk + 2 * rp
                                tb = min(max(r - 4, 0), 30)
                                kt0 = (tb + 2 * j) * 64
                                pe(lambda e, kt0=kt0, r=r, sbk=sbk, rp=rp: e.matmul(banks[sbk][:, rp * 128:(rp + 1) * 128], lhsT=kT[lo:hi, kt0:kt0 + 128],
                                                                                   rhs=qT[lo:hi, r * 64:r * 64 + 128], start=(use_dve and rp == 0), stop=(rp == 3), skip_group_check=True),
                                   reads=["nkT", "nqT"], writes=[B[sbk]])

                        def mid_stage(i, p=p):
                            sbk = i % 3
                            blk, j = items[i]
                            cls = set(row_cls(8 * blk + 2 * rp) for rp in range(4))
                            if len(cls) == 1 and i % 2 == 1:
                                st = St[(i // 2) % 2]
                                sr = ("St", (i // 2) % 2)
                                dve(lambda e, sbk=sbk, st=st, j=j: e.scalar_tensor_tensor(out=st.rearrange("p (a b) -> p a b", a=4),
                                                                                        in0=banks[sbk][:].rearrange("p (a b) -> p a b", a=4), scalar=scale,
                                                                                        in1=bg[p][:, j, :].unsqueeze(1).to_broadcast([128, 4, 128]), op0=ALU.mult, op1=ALU.add),
                                    reads=[B[sbk], ("bg", p)], writes=[sr])
                                act(lambda e, i=i, st=st: e.activation(out=E[i % 3][:], in_=st, func=AF.Exp), reads=[sr], writes=[("nE", i % 3)])
                            else:
                                act(lambda e, i=i, sbk=sbk: e.activation(out=E[i % 3][:], in_=banks[sbk][:], func=AF.Exp, scale=scale),
                                    reads=[B[sbk]], writes=[("nE", i % 3)])

                        def PV_stage(i, p=p, lo=lo, hi=hi, dlo=dlo, dhi=dhi, hp=hp):
                            blk, j = items[i]
                            ob = 3 + (blk % 2)
                            for rp in range(4):
                                r = 8 * blk + 2 * rp
                                tb = min(max(r - 4, 0), 30)
                                vt = (tb + 2 * j) // 2
                                pe(lambda e, vt=vt, i=i, ob=ob, rp=rp, j=j: e.matmul(banks[ob][:, rp * 128:(rp + 1) * 128], lhsT=va[p][:, vt, :],
                                                                                    rhs=E[i % 3][:, rp * 128:(rp + 1) * 128],
                                                                                    start=(j == 0 and rp == 0), stop=(j == 4), skip_group_check=True),
                                   reads=[("va", p), ("va1", p), ("nE", i % 3)], writes=[B[ob]])
                            if j == 4:
                                for step in range(3):
                                    pending.append((i + 1 + step, lambda blk=blk, ob=ob, step=step: epilogue(blk, ob, step)))

                        def epilogue(blk, ob, step, p=p, lo=lo, hi=hi, dlo=dlo, dhi=dhi, hp=hp):
                            rc = rec[blk % 2]
                            rr = ("rec", 0)
                            if step == 0:
                                dve(lambda e, ob=ob, rc=rc: e.tensor_copy(out=rc[lo:hi, :], in_=banks[ob][dlo:dhi, :]), reads=[B[ob]], writes=[rr])
                            elif step == 1:
                                act(lambda e, rc=rc: e.activation(out=rc[lo:hi, :], in_=rc[lo:hi, :], func=AF.Ln), reads=[rr], writes=[rr])
                                act(lambda e, rc=rc: e.activation(out=rc[lo:hi, :], in_=rc[lo:hi, :], func=AF.Exp, scale=-1.0), reads=[rr], writes=[rr])
                            else:
                                dve(lambda e, ob=ob, rc=rc, blk=blk: e.tensor_tensor(out=cT[lo:hi, hp, blk * 512:(blk + 1) * 512], in0=banks[ob][lo:hi, :],
                                                                                    in1=rc[lo:hi, :], op=ALU.mult),
                                    reads=[B[ob], rr], writes=[("cT", hp)])

                        n_it = len(items)
                        pending = []
                        S_stage(0)
                        S_stage(1)
                        for i in range(n_it):
                            mid_stage(i)
                            PV_stage(i)
                            if i + 2 < n_it:
                                S_stage(i + 2)
                            if i == 4 and h + 1 < 16:
                                prep_bias(h + 1)
                            pending.sort(key=lambda q: q[0])
                            while pending and pending[0][0] <= i:
                                pending.pop(0)[1]()
                        pending.sort(key=lambda q: q[0])
                        while pending:
                            pending.pop(0)[1]()
                barrier()
            with contextlib.ExitStack() as S2:
                wout = sb(S2, "wouto", [128, 8, D], BF16)
                load_w(wout, w_out_o[0], "wouto")
                for t in range(NT):
                    proj_resid(t, lambda k, t=t: cT[:, k, t * 128:(t + 1) * 128], [("cT", g) for g in range(8)], 8, wout, "wouto")
                barrier()

    fins = []
    import os
    PH = os.environ.get("PHASES", "A,B1,B2,C").split(",")
    if stage >= 1:
        if "A" in PH:
            phase_A()
        if "B1" in PH:
            phase_B1()
        if "B2" in PH:
            phase_B2()
        if "C" in PH:
            phase_C()
    if stage >= 2:
        phase_mem()
        phase_xattn(0)
    if stage >= 3:
        phase_mlp(0)
    if stage >= 4:
        phase_G1()
        phase_G2()
    if stage >= 5:
        phase_xattn(1)
        phase_mlp(1, last=True)
    if not fins:
        for t in range(NT):
            fins.append(P.dma("sp", lambda e, t=t: e.dma_start(out=out[t * 128:(t + 1) * 128, :], in_=x[:, t, :]), reads=[("x", t)]))
    P.emit(nc, final_ops=fins)
    return nc, P


def _rope_table(pos):
    half = 16
    freqs = (np.float32(10000.0) ** (-np.arange(half, dtype=np.float32) / np.float32(half))).astype(np.float32)
    ang = (pos.astype(np.float32)[:, None] * freqs[None, :]).astype(np.float32)
    c = np.cos(ang).astype(np.float32)
    s = np.sin(ang).astype(np.float32)
    return np.concatenate([c, c, -s, s], axis=1).astype(np.float32)


def _invc_table(a_tok):
    tab = np.ones((4, 2568), np.float32)
    tg = a_tok + np.arange(2568) - 8
    valid = (tg >= 0) & (tg < SEQ)
    for g, w in enumerate((2, 4, 8, 16)):
        lo = np.clip(tg - w // 2, 0, SEQ - 1)
        hi = np.clip(tg + w - 1 - w // 2, 0, SEQ - 1)
        cnt = (hi - lo + 1).astype(np.float32)
        tab[g] = np.where(valid, np.float32(1.0) / cnt, np.float32(1.0))
    return tab


def _natten_bias(rpb):
    H = rpb.shape[0]
    outb = np.full((H, 25, 128, 128), NEG, np.float32)
    cols = np.arange(64)
    c0 = np.clip(cols - 8, 0, 48)
    classes = {0: 8, 1: 0, 2: 2, 3: 36, 4: 38}
    kc = np.arange(64)[:, None]
    qc = np.arange(64)[None, :]
    colvalid = (kc >= c0[None, :]) & (kc < c0[None, :] + 16)
    dc = np.clip(kc - qc + 15, 0, 30)
    for cls, r in classes.items():
        tb = min(max(r - 4, 0), 30)
        for j in range(5):
            for kr_i in range(2):
                kr = tb + 2 * j + kr_i
                for qr_i in range(2):
                    qr = r + qr_i
                    r0 = min(max(qr - 4, 0), 32)
                    if not (r0 <= kr <= r0 + 7):
                        continue
                    dr = kr - qr + 7
                    vals = rpb[:, dr][:, dc]
                    blk = np.where(colvalid[None], vals, np.float32(NEG))
                    outb[:, cls * 5 + j, kr_i * 64:(kr_i + 1) * 64, qr_i * 64:(qr_i + 1) * 64] = blk
    return np.ascontiguousarray(outb.transpose(0, 2, 1, 3))


def make_in_maps(inputs):
    f = lambda a: np.ascontiguousarray(np.asarray(a, dtype=np.float32))
    xfull = f(inputs["x"])
    memf = f(inputs["mem"])
    shared = {k: f(v) for k, v in inputs.items() if k not in ("x", "mem", "na_rpb")}
    nabt = _natten_bias(f(inputs["na_rpb"])[0])
    pos_b = np.arange(SEQ)
    cs_b = _rope_table(pos_b)
    maps, meta = [], []
    for c in range(8):
        b, j = c // 4, c % 4
        a = min(max(32 * j - 4, 0), 88)
        a_tok = a * 64
        xw = xfull[b, a_tok:a_tok + WT]
        xh = np.zeros((16, D), np.float32)
        if a_tok >= 8:
            xh[0:8] = xfull[b, a_tok - 8:a_tok]
        if a_tok + WT + 8 <= SEQ:
            xh[8:16] = xfull[b, a_tok + WT:a_tok + WT + 8]
        m = dict(shared)
        m.update(xw=np.ascontiguousarray(xw), xh=xh, xb=xfull[b], mem=memf[b],
                 cs_w=np.ascontiguousarray(cs_b[a_tok:a_tok + WT]), cs_b=cs_b, invc=_invc_table(a_tok), nab=nabt)
        maps.append(m)
        meta.append((b, a, 32 * j - a))
    return maps, meta


_CACHE = {}


def kernel(**inputs):
    if "nc" not in _CACHE:
        _CACHE["nc"] = build_program(99)[0]
    nc = _CACHE["nc"]
    maps, meta = make_in_maps(inputs)
    res = run_bass_kernel_spmd(nc, maps, core_ids=list(range(8)))
    outp = np.zeros((2, SEQ, D), np.float32)
    for c in range(8):
        b, a, off = meta[c]
        o = np.asarray(res.results[c]["out"]).reshape(WT, D)
        j = c % 4
        outp[b, j * 2048:(j + 1) * 2048] = o[off * 64:off * 64 + 2048]
    return outp
```

```python
import concourse.bass as bass
import concourse.mybir as mybir

ENGS = ("pe", "act", "dve", "pool", "sp")
SAME_ENG_SYNC = {"pe": False, "act": True, "dve": True, "pool": True, "sp": False}
NSLOT = 12


class Op:
    __slots__ = ("id", "eng", "fn", "deps", "dma", "flag", "sem", "val", "slot_guard", "nwaits")

    def __init__(self, id, eng, fn, dma):
        self.id = id
        self.eng = eng
        self.fn = fn
        self.dma = dma
        self.deps = []
        self.flag = False
        self.sem = None
        self.val = None
        self.slot_guard = None


class Prog:
    def __init__(self):
        self.ops = []
        self.by_eng = {e: [] for e in ENGS}
        self.last_w = {}
        self.readers = {}
        self.dma_count = {e: 0 for e in ENGS}

    _cap = None

    def capture(self):
        self._cap = []

    def end_capture(self):
        c = self._cap
        self._cap = None
        return c

    def mark(self):
        self._mark = len(self._cap)

    def replay(self, lst):
        for it in lst:
            self.op(*it)

    @staticmethod
    def zipmerge(a, b):
        out = []
        na, nb = len(a), len(b)
        ia = ib = 0
        while ia < na or ib < nb:
            if ib >= nb or (ia < na and ia * max(nb, 1) <= ib * max(na, 1)):
                out.append(a[ia]); ia += 1
            else:
                out.append(b[ib]); ib += 1
        return out

    def pipeline(self, n, tile_fn, split=0.5):
        prevB = []
        for t in range(n):
            self.capture()
            self._mark = None
            tile_fn(t)
            L = self.end_capture()
            h = self._mark if self._mark is not None else int(len(L) * split)
            self.replay(self.zipmerge(L[:h], prevB))
            prevB = L[h:]
        self.replay(prevB)

    def op(self, eng, fn, reads=(), writes=(), dma=False):
        if self._cap is not None:
            self._cap.append((eng, fn, reads, writes, dma))
            return None
        o = Op(len(self.ops), eng, fn, dma)
        reads = list(reads)
        writes = list(writes) + [r for r in reads if isinstance(r, str) and r.startswith("bank")]
        deps = set()
        for r in reads:
            w = self.last_w.get(r)
            if w is not None:
                deps.add(w)
        for w_ in writes:
            w = self.last_w.get(w_)
            if w is not None:
                deps.add(w)
            for rd in self.readers.get(w_, ()):
                deps.add(rd)
        deps.discard(o.id)
        o.deps = sorted(deps)
        for r in reads:
            self.readers.setdefault(r, []).append(o.id)
        for w_ in writes:
            self.last_w[w_] = o.id
            self.readers[w_] = []
        self.ops.append(o)
        self.by_eng[eng].append(o)
        return o

    def pe(self, fn, reads=(), writes=()):
        return self.op("pe", fn, reads, writes)

    def act(self, fn, reads=(), writes=()):
        return self.op("act", fn, reads, writes)

    def dve(self, fn, reads=(), writes=()):
        return self.op("dve", fn, reads, writes)

    def pool(self, fn, reads=(), writes=()):
        return self.op("pool", fn, reads, writes)

    def dma(self, eng, fn, reads=(), writes=()):
        return self.op(eng, fn, reads, writes, dma=True)

    def emit(self, nc, final_ops=()):
        ops = self.ops
        for o in ops:
            for d in o.deps:
                do = ops[d]
                if do.dma or do.eng != o.eng or SAME_ENG_SYNC[o.eng]:
                    do.flag = True
        for o in final_ops:
            o.flag = True
        import contextlib
        with contextlib.ExitStack() as es:
            esem = {e: es.enter_context(nc.semaphore("s_" + e)) for e in ENGS}
            dsem = {}
            for e in ENGS:
                if self.dma_count_total(e) > 0:
                    dsem[e] = [es.enter_context(nc.semaphore("d_%s_%d" % (e, i))) for i in range(NSLOT)]
            cnt = {e: 0 for e in ENGS}
            dcnt = {e: 0 for e in ENGS}
            semkey = {}
            for o in ops:
                if o.dma:
                    j = dcnt[o.eng]
                    dcnt[o.eng] += 1
                    s = dsem[o.eng][j % NSLOT]
                    o.sem = s
                    o.val = 16 * (j // NSLOT + 1)
                    o.slot_guard = (s, 16 * (j // NSLOT)) if j >= NSLOT else None
                    semkey[id(s)] = ("d", o.eng, j % NSLOT)
                elif o.flag:
                    cnt[o.eng] += 1
                    o.sem = esem[o.eng]
                    o.val = cnt[o.eng]
            clock = {e: {} for e in ENGS}
            opvc = [None] * len(ops)
            plan = {e: [] for e in ENGS}
            for o in ops:
                ck = clock[o.eng]
                waits = {}
                for d in o.deps:
                    do = ops[d]
                    if not (do.dma or do.eng != o.eng or SAME_ENG_SYNC[o.eng]):
                        continue
                    k = id(do.sem)
                    if ck.get(k, 0) >= do.val:
                        continue
                    if k not in waits or waits[k][1] < do.val:
                        waits[k] = (do.sem, do.val, d)
                if o.slot_guard is not None:
                    s, v = o.slot_guard
                    k = id(s)
                    if ck.get(k, 0) < v and (k not in waits or waits[k][1] < v):
                        waits[k] = (s, v, None)
                wl = list(waits.items())
                keep = []
                for k, (s, v, d) in wl:
                    implied = False
                    for k2, (s2, v2, d2) in wl:
                        if k2 == k or d2 is None:
                            continue
                        vc2 = opvc[d2]
                        if vc2 is not None and vc2.get(k, 0) >= v:
                            implied = True
                            break
                    if not implied:
                        keep.append((s, v))
                for k, (s, v, d) in wl:
                    if ck.get(k, 0) < v:
                        ck[k] = v
                    if d is not None and opvc[d] is not None:
                        for kk, vv in opvc[d].items():
                            if ck.get(kk, 0) < vv:
                                ck[kk] = vv
                if o.sem is not None:
                    vc = dict(ck)
                    vc[id(o.sem)] = o.val
                    opvc[o.id] = vc
                    if not o.dma:
                        pass
                plan[o.eng].append((o, keep))
            self.stats = {e: (len(plan[e]), sum(len(k) for _, k in plan[e])) for e in ENGS}

            def run(eng_handle, lst, final_waits):
                for o, keep in lst:
                    for (s, v) in keep[1:]:
                        eng_handle.wait_ge(s, v)
                    ins = o.fn(eng_handle)
                    if keep:
                        s, v = keep[0]
                        if isinstance(ins, tuple):
                            ins[0]._wait_ge(s, v)
                        else:
                            ins._wait_ge(s, v)
                    if o.sem is not None:
                        last = ins[1] if isinstance(ins, tuple) else ins
                        last.then_inc(o.sem, 16 if o.dma else 1)
                for (s, v) in final_waits:
                    eng_handle.wait_ge(s, v)

            fw = [(o.sem, o.val) for o in final_ops]
            with nc.Block() as block:
                @block.tensor
                def _(e):
                    run(e, plan["pe"], [])

                @block.scalar
                def _(e):
                    run(e, plan["act"], [])

                @block.vector
                def _(e):
                    run(e, plan["dve"], [])

                @block.gpsimd
                def _(e):
                    run(e, plan["pool"], [])

                @block.sync
                def _(e):
                    run(e, plan["sp"], fw)

    def dma_count_total(self, e):
        return sum(1 for o in self.by_eng[e] if o.dma)


import contextlib
import numpy as np
from concourse.bass_utils import run_bass_kernel_spmd

F32 = mybir.dt.float32
BF16 = mybir.dt.bfloat16
AF = mybir.ActivationFunctionType
ALU = mybir.AluOpType
AX = mybir.AxisListType

D = 1024
SEQ = 8192
WT = 2560
NT = 20
NB = 5
NTB = 64
EPS = 1e-6
NEG = -30000.0


def build_program(stage=99):
    nc = bass.Bass("TRN2", target_bir_lowering=False)
    P = Prog()
    G = contextlib.ExitStack()
    uid = [0]
    bar_from = [0]

    def di(name, shape, dt=F32):
        return nc.dram_tensor(name, list(shape), dt, kind="ExternalInput").ap()

    def sb(st, name, shape, dt):
        uid[0] += 1
        return st.enter_context(nc.sbuf_tensor("%s_%d" % (name, uid[0]), list(shape), dt))

    def barrier():
        lasts = [P.by_eng[e][-1].id for e in ENGS if P.by_eng[e]]
        dmas = [o.id for o in P.ops[bar_from[0]:] if o.dma]
        bar_from[0] = len(P.ops)
        deps = sorted(set(lasts + dmas))
        for e in ENGS:
            o = P.op(e, lambda eng: eng.nop())
            o.deps = list(deps)
        P.last_w = {}
        P.readers = {}

    xw = di("xw", [WT, D]); xh = di("xh", [16, D]); xb = di("xb", [SEQ, D]); mem = di("mem", [256, D])
    mix_norm_g = di("mix_norm_g", [2, D]); xattn_norm_g = di("xattn_norm_g", [2, D]); ff_norm_g = di("ff_norm_g", [2, D])
    w_mem_q = di("w_mem_q", [2, D, D]); mem_q_g = di("mem_q_g", [2, 256]); w_mem_o = di("w_mem_o", [2, D, D])
    w_ff1 = di("w_ff1", [2, D, 4096]); w_ff2 = di("w_ff2", [2, 4096, D])
    mem_tok_norm_g = di("mem_tok_norm_g", [D]); w_mem_kv = di("w_mem_kv", [D, 2048]); mem_k_g = di("mem_k_g", [256])
    w_in_e = di("w_in_e", [1, D, 928]); pool_w = di("pool_w", [1, 4, 128, 128]); pool_scale = di("pool_scale", [1, 512])
    q_lora_g = di("q_lora_g", [1, 256]); w_uq = di("w_uq", [1, 256, 768]); kv_lora_g = di("kv_lora_g", [1, 128])
    w_ukv = di("w_ukv", [1, 128, 1024]); mla_q_g = di("mla_q_g", [1, 96]); mla_k_g = di("mla_k_g", [1, 96])
    w_out_e = di("w_out_e", [1, D, D]); w_qkv_o = di("w_qkv_o", [1, D, 3072])
    na_q_g = di("na_q_g", [1, 64]); na_k_g = di("na_k_g", [1, 64]); w_out_o = di("w_out_o", [1, D, D])
    cs_w = di("cs_w", [WT, 64]); cs_b = di("cs_b", [SEQ, 64]); invc = di("invc", [4, 2568])
    nab = di("nab", [16, 128, 25, 128])
    out = nc.dram_tensor("out", [WT, D], F32, kind="ExternalOutput").ap()
    kT_scr = nc.dram_tensor("kT_scr", [8, 96, SEQ], BF16, kind="Internal").ap()
    v_scr = nc.dram_tensor("v_scr", [8, 128, NTB, 64], BF16, kind="Internal").ap()
    qT_scr = nc.dram_tensor("qT_scr", [8, 96, WT], BF16, kind="Internal").ap()
    nq_scr = nc.dram_tensor("nq_scr", [8, 128, WT], BF16, kind="Internal").ap()
    nk_scr = nc.dram_tensor("nk_scr", [8, 128, WT], BF16, kind="Internal").ap()
    nv_scr = nc.dram_tensor("nv_scr", [8, 128, NT, 128], BF16, kind="Internal").ap()

    x = sb(G, "x", [128, NT, D], F32)
    ident = sb(G, "ident", [128, 128], BF16)
    identf = sb(G, "identf", [128, 128], F32)
    ones_b = sb(G, "ones_b", [128, 128], BF16)
    eps_t = sb(G, "eps", [128, 1], F32)
    memkT = sb(G, "memkT", [128, 4, 2, 256], BF16)
    memv = sb(G, "memv", [128, 2, 4, 256], BF16)
    SQ = [sb(G, "sq", [128, 1024], F32), sb(G, "sq", [128, 1024], F32)]
    SSQ = [sb(G, "ssq", [128, 1], F32) for _ in range(2)]
    RSTD = [sb(G, "rstd", [128, 1], F32) for _ in range(2)]
    HB = [sb(G, "hb", [128, D], BF16) for _ in range(2)]
    SS32 = [sb(G, "ss32", [128, 32], F32) for _ in range(2)]
    RS32 = [sb(G, "rs32", [128, 32], F32) for _ in range(2)]
    sq, ssq, rstd, hb, ss32, rs32 = SQ[0], SSQ[0], RSTD[0], HB[0], SS32[0], RS32[0]
    gn = sb(G, "gn", [128, D], F32)
    psum_all = G.enter_context(nc.psum_tensor("psum_all", [128, 4096], F32))
    banks = [psum_all[:, i * 512:(i + 1) * 512] for i in range(8)]
    B = ["bank%d" % i for i in range(8)]

    def bbf(i):
        return banks[i][:].bitcast(BF16)

    act, dve, pe, pool = P.act, P.dve, P.pe, P.pool

    def rstd_from(ss_ap, ss_res, dim, out_ap, out_res):
        n = ss_ap.shape[0]
        act(lambda e: e.activation(out=out_ap, in_=ss_ap, func=AF.Ln, scale=1.0 / dim, bias=eps_t[0:n, :]),
            reads=[ss_res, "eps"], writes=[out_res])
        act(lambda e: e.activation(out=out_ap, in_=out_ap, func=AF.Exp, scale=-0.5), reads=[out_res], writes=[out_res])

    def norm_hT(src_ap, src_res, hT_dst, hT_res, n=128, par=0, bank=0):
        sq_, ssq_, rstd_, hb_ = SQ[par], SSQ[par], RSTD[par], HB[par]
        sfx = "" if par == 0 else "_p1"
        act(lambda e: e.activation(out=sq_[0:n, 0:D], in_=src_ap, func=AF.Square, accum_out=ssq_[0:n, :]),
            reads=[src_res], writes=["sq" + sfx, "ssq" + sfx])
        rstd_from(ssq_[0:n, :], "ssq" + sfx, D, rstd_[0:n, :], "rstd" + sfx)
        dve(lambda e: e.scalar_tensor_tensor(out=hb_[0:n, :], in0=src_ap, scalar=rstd_[0:n, 0:1], in1=gn[0:n, :],
                                             op0=ALU.mult, op1=ALU.mult),
            reads=[src_res, "rstd" + sfx, "gn"], writes=["hb" + sfx])
        pT = bbf(bank)
        for k in range(8):
            pe(lambda e, k=k: e.transpose(out=pT[:, k * n:(k + 1) * n], in_=hb_[0:n, k * 128:(k + 1) * 128],
                                          identity=ident[0:n, 0:n]),
               reads=["hb" + sfx, "ident"], writes=[B[bank]])
        warm()
        act(lambda e: e.copy(out=hT_dst, in_=pT[:, 0:8 * n].rearrange("p (k t) -> p k t", k=8)),
            reads=[B[bank]], writes=[hT_res])

    import os
    NWARM = int(os.environ.get("NWARM", "0"))

    def warm(n=None):
        for _ in range(NWARM if n is None else n):
            pe(lambda e: e.ldweights(ident[:]), reads=["ident"])

    def load_w(dst, w_ap, res, eng="pool"):
        K = w_ap.shape[0]
        for k in range((K + 127) // 128):
            r = min(128, K - k * 128)
            P.dma(eng, lambda e, k=k, r=r: e.dma_start(out=dst[0:r, k, :], in_=w_ap[k * 128:k * 128 + r, :]),
                  writes=[(res, k)])

    def wres(res, nk):
        return [(res, k) for k in range(nk)]

    def load_bc(dst, vec_ap, res, eng="sp"):
        P.dma(eng, lambda e: e.dma_start(out=dst, in_=vec_ap.partition_broadcast(128)), writes=[res])

    def resid_add(t, pbank_lo, pbank_hi):
        for half, bk in ((0, pbank_lo), (1, pbank_hi)):
            dve(lambda e, half=half, bk=bk: e.tensor_tensor(out=x[:, t, half * 512:(half + 1) * 512],
                                                            in0=banks[bk][:], in1=x[:, t, half * 512:(half + 1) * 512],
                                                            op=ALU.add),
                reads=[B[bk], ("x", t)], writes=[("x", t)])

    def proj_resid(t, lhs_fn, lhs_res, nk, w_sb, w_res, k0=0):
        for half in range(2):
            for k in range(nk):
                pe(lambda e, half=half, k=k: e.matmul(banks[6 + half][:], lhsT=lhs_fn(k), rhs=w_sb[:, k0 + k, half * 512:(half + 1) * 512],
                                                       start=(k == 0), stop=(k == nk - 1)),
                   reads=lhs_res + [(w_res, k0 + k)], writes=[B[6 + half]])
        resid_add(t, 6, 7)

    pool(lambda e: e.memset(identf[:], 0.0), writes=["identf"])
    pool(lambda e: e.affine_select(out=identf[:], in_=identf[:], pattern=[[-1, 128]], compare_op=ALU.not_equal,
                                   fill=1.0, base=0, channel_multiplier=1), reads=["identf"], writes=["identf"])
    dve(lambda e: e.tensor_copy(out=ident[:], in_=identf[:]), reads=["identf"], writes=["ident"])
    dve(lambda e: e.memset(ones_b[:], 1.0), writes=["ones_b"])
    dve(lambda e: e.memset(eps_t[:], EPS), writes=["eps"])
    for t in range(NT):
        P.dma("sp", lambda e, t=t: e.dma_start(out=x[:, t, :], in_=xw[t * 128:(t + 1) * 128, :]), writes=[("x", t)])

    def phase_mem():
        with contextlib.ExitStack() as S:
            wkv = sb(S, "wkv", [128, 8, 2048], BF16)
            gk = sb(S, "gk", [128, 256], F32)
            memt = sb(S, "memt", [128, D], F32)
            hTm = sb(S, "hTm", [128, 8, 128], BF16)
            kn = sb(S, "kn", [128, D], F32)
            kbm = sb(S, "kbm", [128, D], BF16)
            load_w(wkv, w_mem_kv, "wkv")
            load_bc(gn[:], mem_tok_norm_g, "gn")
            load_bc(gk[:], mem_k_g, "gk")
            for m in range(2):
                P.dma("sp", lambda e, m=m: e.dma_start(out=memt[:], in_=mem[m * 128:(m + 1) * 128, :]), writes=["memt"])
                norm_hT(memt[:], "memt", hTm[:], "hTm")
                for n4 in range(4):
                    for k in range(8):
                        pe(lambda e, n4=n4, k=k: e.matmul(banks[1 + n4][:], lhsT=hTm[:, k, :], rhs=wkv[:, k, n4 * 512:(n4 + 1) * 512],
                                                          start=(k == 0), stop=(k == 7)),
                           reads=["hTm", ("wkv", k)], writes=[B[1 + n4]])
                for j in range(2):
                    act(lambda e, j=j: e.activation(out=sq[:, j * 512:(j + 1) * 512], in_=banks[1 + j][:], func=AF.Square),
                        reads=[B[1 + j]], writes=["sq"])
                dve(lambda e: e.tensor_reduce(out=ss32[:, 0:4], in_=sq[:, 0:D].rearrange("p (h d) -> p h d", h=4), axis=AX.X, op=ALU.add),
                    reads=["sq"], writes=["ss32"])
                rstd_from(ss32[:, 0:4], "ss32", 256, rs32[:, 0:4], "rs32")
                for j in range(2):
                    dve(lambda e, j=j: e.tensor_tensor(out=kn[:, j * 512:(j + 1) * 512].rearrange("p (h d) -> p h d", h=2),
                                                       in0=banks[1 + j][:].rearrange("p (h d) -> p h d", h=2),
                                                       in1=rs32[:, 2 * j:2 * j + 2].unsqueeze(2).to_broadcast([128, 2, 256]), op=ALU.mult),
                        reads=[B[1 + j], "rs32"], writes=["kn"])
                dve(lambda e: e.tensor_tensor(out=kbm[:].rearrange("p (h d) -> p h d", h=4), in0=kn[:].rearrange("p (h d) -> p h d", h=4),
                                              in1=gk[:].unsqueeze(1).to_broadcast([128, 4, 256]), op=ALU.mult),
                    reads=["kn", "gk"], writes=["kbm"])
                pT = bbf(0)
                for j in range(8):
                    pe(lambda e, j=j: e.transpose(out=pT[:, j * 128:(j + 1) * 128], in_=kbm[:, j * 128:(j + 1) * 128], identity=ident[:]),
                       reads=["kbm", "ident"], writes=[B[0]])
                act(lambda e, m=m: e.copy(out=memkT[:, :, :, m * 128:(m + 1) * 128],
                                          in_=pT[:, 0:1024].rearrange("p (h j t) -> p h j t", h=4, j=2)),
                    reads=[B[0]], writes=["memkT"])
                for j in range(2):
                    act(lambda e, m=m, j=j: e.copy(out=memv[:, m, 2 * j:2 * j + 2, :], in_=banks[3 + j][:].rearrange("p (h d) -> p h d", h=2)),
                        reads=[B[3 + j]], writes=["memv"])
            barrier()

    def rotary(src, src_res, cst, cs_res, dst, dst_res, t1, t2, shape3, sfx=""):
        H = shape3
        cc = cst[:, 0:32].unsqueeze(1).to_broadcast([128, H, 32])
        ns = cst[:, 32:48].unsqueeze(1).to_broadcast([128, H, 16])
        ps_ = cst[:, 48:64].unsqueeze(1).to_broadcast([128, H, 16])
        dve(lambda e: e.tensor_tensor(out=t1, in0=src, in1=cc, op=ALU.mult), reads=[src_res, cs_res], writes=["rot_t1" + sfx])
        dve(lambda e: e.tensor_tensor(out=t2[:, :, 0:16], in0=src[:, :, 16:32], in1=ns, op=ALU.mult), reads=[src_res, cs_res], writes=["rot_t2a" + sfx])
        dve(lambda e: e.tensor_tensor(out=t2[:, :, 16:32], in0=src[:, :, 0:16], in1=ps_, op=ALU.mult), reads=[src_res, cs_res], writes=["rot_t2b" + sfx])
        dve(lambda e: e.tensor_tensor(out=dst, in0=t1, in1=t2, op=ALU.add), reads=["rot_t1" + sfx, "rot_t2a" + sfx, "rot_t2b" + sfx], writes=[dst_res])

    def phase_A():
        with contextlib.ExitStack() as S:
            w_kv = sb(S, "w_kv", [128, 8, 160], BF16)
            wukv = sb(S, "wukv", [128, 1, 1024], BF16)
            gkv = sb(S, "gkv", [128, 128], F32)
            gk = sb(S, "gk96", [128, 96], F32)
            csb = sb(S, "csb", [128, NTB, 64], F32)
            two = lambda name, shape, dt: [sb(S, name, shape, dt) for _ in range(2)]
            xt = two("xt", [128, D], F32)
            hTt_ = two("hTt", [128, 8, 128], BF16)
            ckn_ = two("ckn", [128, 128], BF16)
            ckT_ = two("ckT", [128, 128], BF16)
            kn_ = two("kn", [128, 8, 64], F32)
            kb2 = two("kb", [128, 8, 96], BF16)
            krg_ = two("krg", [128, 1, 32], F32)
            krr_ = two("krr", [128, 1, 32], F32)
            t1_ = two("t1", [128, 1, 32], F32)
            t2_ = two("t2", [128, 1, 32], F32)
            vb_ = two("vb", [128, 8, 64], BF16)
            kst_ = two("kst", [96, 8, 512], BF16)
            ssr_ = two("ssr", [128, 1], F32)
            load_w(w_kv, w_in_e[0][:, 768:928], "w_kv")
            load_w(wukv, w_ukv[0], "wukv")
            load_bc(gn[:], mix_norm_g[0], "gn")
            load_bc(gkv[:], kv_lora_g[0], "gkv")
            load_bc(gk[:], mla_k_g[0], "gk")
            P.dma("sp", lambda e: e.dma_start(out=csb[:], in_=cs_b.rearrange("(c p) f -> p c f", p=128)), writes=["csb"])
            def tileA(t):
                p = t % 2
                x_ = "_%d" % p
                bT, bC, bK = 4 * p, 4 * p + 1, (4 * p + 2, 4 * p + 3)
                xtt, hTt, ckn, ckT, kn, kb_, krg, krr, t1, t2, vb, ssr = (xt[p], hTt_[p], ckn_[p], ckT_[p], kn_[p], kb2[p], krg_[p],
                                                                          krr_[p], t1_[p], t2_[p], vb_[p], ssr_[p])
                sq_, ssq_, rstd_, ss32_, rs32_ = SQ[p], SSQ[p], RSTD[p], SS32[p], RS32[p]
                sfx = "" if p == 0 else "_p1"
                kst = kst_[(t // 4) % 2]
                kstr = "kst%d" % ((t // 4) % 2)
                P.dma("sp", lambda e, t=t, xtt=xtt: e.dma_start(out=xtt[:], in_=xb[t * 128:(t + 1) * 128, :]), writes=["xt" + x_])
                norm_hT(xtt[:], "xt" + x_, hTt[:], "hTt" + x_, par=p, bank=bT)
                pC = banks[bC]
                for k in range(8):
                    pe(lambda e, k=k, pC=pC, hTt=hTt: e.matmul(pC[:, 0:160], lhsT=hTt[:, k, :], rhs=w_kv[:, k, :], start=(k == 0), stop=(k == 7)),
                       reads=["hTt" + x_, ("w_kv", k)], writes=[B[bC]])
                act(lambda e, pC=pC, sq_=sq_, ssq_=ssq_: e.activation(out=sq_[:, 0:128], in_=pC[:, 0:128], func=AF.Square, accum_out=ssq_[:]),
                    reads=[B[bC]], writes=["sq" + sfx, "ssq" + sfx])
                rstd_from(ssq_[:], "ssq" + sfx, 128, rstd_[:], "rstd" + sfx)
                dve(lambda e, pC=pC, ckn=ckn, rstd_=rstd_: e.scalar_tensor_tensor(out=ckn[:], in0=pC[:, 0:128], scalar=rstd_[:, 0:1], in1=gkv[:], op0=ALU.mult, op1=ALU.mult),
                    reads=[B[bC], "rstd" + sfx, "gkv"], writes=["ckn" + x_])
                pT2 = bbf(bT)
                pe(lambda e, pT2=pT2, ckn=ckn: e.transpose(out=pT2[:, 0:128], in_=ckn[:], identity=ident[:]), reads=["ckn" + x_, "ident"], writes=[B[bT]])
                act(lambda e, pT2=pT2, ckT=ckT: e.copy(out=ckT[:], in_=pT2[:, 0:128]), reads=[B[bT]], writes=["ckT" + x_])
                for j in range(2):
                    pe(lambda e, j=j, ckT=ckT, bK=bK: e.matmul(banks[bK[j]][:], lhsT=ckT[:], rhs=wukv[:, 0, j * 512:(j + 1) * 512], start=True, stop=True),
                       reads=["ckT" + x_, ("wukv", 0)], writes=[B[bK[j]]])
                for j in range(2):
                    act(lambda e, j=j, bK=bK, sq_=sq_: e.activation(out=sq_[:, j * 256:(j + 1) * 256].rearrange("p (h d) -> p h d", h=4),
                                                              in_=banks[bK[j]][:].rearrange("p (h d) -> p h d", h=4)[:, :, 0:64], func=AF.Square),
                        reads=[B[bK[j]]], writes=["sq" + sfx])
                dve(lambda e, sq_=sq_, ss32_=ss32_: e.tensor_reduce(out=ss32_[:, 0:8], in_=sq_[:, 0:512].rearrange("p (h d) -> p h d", h=8), axis=AX.X, op=ALU.add),
                    reads=["sq" + sfx], writes=["ss32" + sfx])
                act(lambda e, pC=pC, sq_=sq_, ssr=ssr: e.activation(out=sq_[:, 512:544], in_=pC[:, 128:160], func=AF.Square, accum_out=ssr[:]),
                    reads=[B[bC]], writes=["sq2" + sfx, "ssr" + x_])
                dve(lambda e, ss32_=ss32_, ssr=ssr: e.tensor_scalar(out=ss32_[:, 0:8], in0=ss32_[:, 0:8], scalar1=ssr[:, 0:1], scalar2=None, op0=ALU.add),
                    reads=["ss32" + sfx, "ssr" + x_], writes=["ss32" + sfx])
                rstd_from(ss32_[:, 0:8], "ss32" + sfx, 96, rs32_[:, 0:8], "rs32" + sfx)
                for j in range(2):
                    dve(lambda e, j=j, bK=bK, kn=kn, rs32_=rs32_: e.tensor_tensor(out=kn[:, 4 * j:4 * j + 4, :], in0=banks[bK[j]][:].rearrange("p (h d) -> p h d", h=4)[:, :, 0:64],
                                                                             in1=rs32_[:, 4 * j:4 * j + 4].unsqueeze(2).to_broadcast([128, 4, 64]), op=ALU.mult),
                        reads=[B[bK[j]], "rs32" + sfx], writes=["kn" + x_])
                pool(lambda e, kb_=kb_, kn=kn: e.tensor_tensor(out=kb_[:, :, 0:64], in0=kn[:], in1=gk[:, 0:64].unsqueeze(1).to_broadcast([128, 8, 64]), op=ALU.mult),
                     reads=["kn" + x_, "gk"], writes=["kb_n" + x_])
                dve(lambda e, krg=krg, pC=pC: e.tensor_tensor(out=krg[:, 0, :], in0=pC[:, 128:160], in1=gk[:, 64:96], op=ALU.mult),
                    reads=[B[bC], "gk"], writes=["krg" + x_])
                rotary(krg[:], "krg" + x_, csb[:, t, :], "csb", krr[:], "krr" + x_, t1[:], t2[:], 1, sfx=x_)
                dve(lambda e, kb_=kb_, krr=krr, rs32_=rs32_: e.tensor_tensor(out=kb_[:, :, 64:96], in0=krr[:, 0, :].unsqueeze(1).to_broadcast([128, 8, 32]),
                                                                        in1=rs32_[:, 0:8].unsqueeze(2).to_broadcast([128, 8, 32]), op=ALU.mult),
                    reads=["krr" + x_, "rs32" + sfx], writes=["kb_r" + x_])
                for j in range(2):
                    act(lambda e, j=j, bK=bK, vb=vb: e.copy(out=vb[:, 4 * j:4 * j + 4, :], in_=banks[bK[j]][:].rearrange("p (h d) -> p h d", h=4)[:, :, 64:128]),
                        reads=[B[bK[j]]], writes=["vb" + x_])
                P.dma("sp", lambda e, t=t, vb=vb: e.dma_start(out=v_scr[:, :, t, :].rearrange("h p d -> p h d"), in_=vb[:]), reads=["vb" + x_], writes=[("v_scr", t)])
                pT3 = bbf(bT)
                for h in range(8):
                    pe(lambda e, h=h, pT3=pT3, kb_=kb_: e.transpose(out=pT3[0:96, h * 128:(h + 1) * 128], in_=kb_[:, h, :], identity=ident[:]),
                       reads=["kb_n" + x_, "kb_r" + x_, "ident"], writes=[B[bT]])
                tt = t % 4
                act(lambda e, tt=tt, pT3=pT3, kst=kst: e.copy(out=kst[:, :, tt * 128:(tt + 1) * 128], in_=pT3[0:96, 0:1024].rearrange("p (h t) -> p h t", h=8)),
                    reads=[B[bT]], writes=[kstr])
                if tt == 3:
                    b4 = t // 4
                    P.dma("sp", lambda e, b4=b4, kst=kst: e.dma_start(out=kT_scr[:, :, b4 * 512:(b4 + 1) * 512].rearrange("h d t -> d h t"), in_=kst[:]),
                          reads=[kstr], writes=[("kT_scr", b4)])
            P.pipeline(NTB, tileA)
            barrier()

    def phase_B1():
        with contextlib.ExitStack() as S:
            w_q = sb(S, "w_q", [128, 8, 256], BF16)
            wuq = sb(S, "wuq", [128, 2, 768], BF16)
            gql = sb(S, "gql", [128, 256], F32)
            gq = sb(S, "gq96", [128, 96], F32)
            csw = sb(S, "csw", [128, NT, 64], F32)
            two = lambda name, shape, dt: [sb(S, name, shape, dt) for _ in range(2)]
            hTt_ = two("hTt", [128, 8, 128], BF16)
            cqn_ = two("cqn", [128, 256], BF16)
            cqT_ = two("cqT", [128, 2, 128], BF16)
            qn_ = two("qn", [128, 8, 96], F32)
            qr_ = two("qr", [128, 8, 32], F32)
            qb2 = two("qb", [128, 8, 96], BF16)
            t1_ = two("t1", [128, 8, 32], F32)
            t2_ = two("t2", [128, 8, 32], F32)
            qst_ = two("qst", [96, 8, 512], BF16)
            load_w(w_q, w_in_e[0][:, 512:768], "w_q")
            load_w(wuq, w_uq[0], "wuq")
            load_bc(gn[:], mix_norm_g[0], "gn")
            load_bc(gql[:], q_lora_g[0], "gql")
            load_bc(gq[:], mla_q_g[0], "gq")
            P.dma("sp", lambda e: e.dma_start(out=csw[:], in_=cs_w.rearrange("(c p) f -> p c f", p=128)), writes=["csw"])

            def tileB(t):
                p = t % 2
                x_ = "_%d" % p
                sfx = "" if p == 0 else "_p1"
                bT, bC, bQ = 4 * p, 4 * p + 1, (4 * p + 2, 4 * p + 3)
                hTt, cqn, cqT, qn, qr, qb_, t1, t2 = hTt_[p], cqn_[p], cqT_[p], qn_[p], qr_[p], qb2[p], t1_[p], t2_[p]
                sq_, ssq_, rstd_, ss32_, rs32_ = SQ[p], SSQ[p], RSTD[p], SS32[p], RS32[p]
                qst = qst_[(t // 4) % 2]
                qstr = "qst%d" % ((t // 4) % 2)
                norm_hT(x[:, t, :], ("x", t), hTt[:], "hTt" + x_, par=p, bank=bT)
                pC = banks[bC]
                for k in range(8):
                    pe(lambda e, k=k: e.matmul(pC[:, 0:256], lhsT=hTt[:, k, :], rhs=w_q[:, k, :], start=(k == 0), stop=(k == 7)),
                       reads=["hTt" + x_, ("w_q", k)], writes=[B[bC]])
                act(lambda e: e.activation(out=sq_[:, 0:256], in_=pC[:, 0:256], func=AF.Square, accum_out=ssq_[:]),
                    reads=[B[bC]], writes=["sq" + sfx, "ssq" + sfx])
                rstd_from(ssq_[:], "ssq" + sfx, 256, rstd_[:], "rstd" + sfx)
                dve(lambda e: e.scalar_tensor_tensor(out=cqn[:], in0=pC[:, 0:256], scalar=rstd_[:, 0:1], in1=gql[:], op0=ALU.mult, op1=ALU.mult),
                    reads=[B[bC], "rstd" + sfx, "gql"], writes=["cqn" + x_])
                pT2 = bbf(bT)
                for j in range(2):
                    pe(lambda e, j=j: e.transpose(out=pT2[:, j * 128:(j + 1) * 128], in_=cqn[:, j * 128:(j + 1) * 128], identity=ident[:]),
                       reads=["cqn" + x_, "ident"], writes=[B[bT]])
                act(lambda e: e.copy(out=cqT[:], in_=pT2[:, 0:256].rearrange("p (j t) -> p j t", j=2)), reads=[B[bT]], writes=["cqT" + x_])
                for (bk, c0, c1) in ((bQ[0], 0, 480), (bQ[1], 480, 768)):
                    for j in range(2):
                        pe(lambda e, bk=bk, c0=c0, c1=c1, j=j: e.matmul(banks[bk][:, 0:c1 - c0], lhsT=cqT[:, j, :], rhs=wuq[:, j, c0:c1],
                                                                        start=(j == 0), stop=(j == 1)),
                           reads=["cqT" + x_, ("wuq", j)], writes=[B[bk]])
                segs = ((bQ[0], 0, 5), (bQ[1], 5, 8))
                for (bk, h0, h1) in segs:
                    nh = h1 - h0
                    act(lambda e, bk=bk, h0=h0, nh=nh: e.activation(out=sq_[:, h0 * 96:(h0 + nh) * 96], in_=banks[bk][:, 0:nh * 96], func=AF.Square),
                        reads=[B[bk]], writes=["sq" + sfx])
                dve(lambda e: e.tensor_reduce(out=ss32_[:, 0:8], in_=sq_[:, 0:768].rearrange("p (h d) -> p h d", h=8), axis=AX.X, op=ALU.add),
                    reads=["sq" + sfx], writes=["ss32" + sfx])
                rstd_from(ss32_[:, 0:8], "ss32" + sfx, 96, rs32_[:, 0:8], "rs32" + sfx)
                for (bk, h0, h1) in segs:
                    nh = h1 - h0
                    dve(lambda e, bk=bk, h0=h0, nh=nh: e.tensor_tensor(out=qn[:, h0:h0 + nh, :], in0=banks[bk][:, 0:nh * 96].rearrange("p (h d) -> p h d", h=nh),
                                                                       in1=rs32_[:, h0:h0 + nh].unsqueeze(2).to_broadcast([128, nh, 96]), op=ALU.mult),
                        reads=[B[bk], "rs32" + sfx], writes=["qn" + x_])
                pool(lambda e: e.tensor_tensor(out=qb_[:, :, 0:64], in0=qn[:, :, 0:64], in1=gq[:, 0:64].unsqueeze(1).to_broadcast([128, 8, 64]), op=ALU.mult),
                     reads=["qn" + x_, "gq"], writes=["qb_n" + x_])
                dve(lambda e: e.tensor_tensor(out=qr[:], in0=qn[:, :, 64:96], in1=gq[:, 64:96].unsqueeze(1).to_broadcast([128, 8, 32]), op=ALU.mult),
                    reads=["qn" + x_, "gq"], writes=["qr" + x_])
                rotary(qr[:], "qr" + x_, csw[:, t, :], "csw", qb_[:, :, 64:96], "qb_r" + x_, t1[:], t2[:], 8, sfx=x_)
                pT3 = bbf(bT)
                for h in range(8):
                    pe(lambda e, h=h: e.transpose(out=pT3[0:96, h * 128:(h + 1) * 128], in_=qb_[:, h, :], identity=ident[:]),
                       reads=["qb_n" + x_, "qb_r" + x_, "ident"], writes=[B[bT]])
                tt = t % 4
                act(lambda e, tt=tt: e.copy(out=qst[:, :, tt * 128:(tt + 1) * 128], in_=pT3[0:96, 0:1024].rearrange("p (h t) -> p h t", h=8)),
                    reads=[B[bT]], writes=[qstr])
                if tt == 3:
                    b4 = t // 4
                    P.dma("sp", lambda e, b4=b4: e.dma_start(out=qT_scr[:, :, b4 * 512:(b4 + 1) * 512].rearrange("h d t -> d h t"), in_=qst[:]),
                          reads=[qstr], writes=[("qT_scr", b4)])

            P.pipeline(NT, tileB)
            barrier()

    def phase_B2():
        with contextlib.ExitStack() as S:
            w_p = sb(S, "w_p", [128, 8, 512], BF16)
            pw = sb(S, "pw", [128, 4, 128], BF16)
            psc = sb(S, "psc", [128, 4], F32)
            wout = sb(S, "wout", [128, 4, D], BF16)
            inv_t = sb(S, "inv_t", [128, 4, 512], F32)
            U = sb(S, "U", [128, 4, 528], F32)
            UH = sb(S, "UH", [128, 4, 16], F32)
            a2 = sb(S, "a2", [128, 3, 528], F32)
            a4 = sb(S, "a4", [128, 2, 528], F32)
            a8 = sb(S, "a8", [128, 1, 528], F32)
            Sm = sb(S, "Sm", [128, 4, 512], F32)
            Dt = sb(S, "Dt", [128, 4, 512], BF16)
            hTb = sb(S, "hTb", [128, 8, 512], BF16)
            hTh = sb(S, "hTh", [128, 8, 16], BF16)
            xht = sb(S, "xht", [16, D], F32)
            yT = sb(S, "yT", [128, 4, WT], BF16)
            load_w(w_p, w_in_e[0][:, 0:512], "w_p")
            for g in range(4):
                P.dma("pool", lambda e, g=g: e.dma_start(out=pw[:, g, :], in_=pool_w[0, g]), writes=[("pw", g)])
            load_w(wout, w_out_e[0][0:512, :], "wout")
            load_bc(gn[:], mix_norm_g[0], "gn")
            P.dma("sp", lambda e: e.dma_start(out=psc[:], in_=pool_scale[0].rearrange("(g d) -> d g", g=4), allow_slow_non_contiguous=True), writes=["psc"])
            P.dma("sp", lambda e: e.dma_start(out=xht[:], in_=xh[:, :]), writes=["xht"])
            dve(lambda e: e.memset(U[:], 0.0), writes=["U"])
            norm_hT(xht[:], "xht", hTh[:], "hTh", n=16)
            for g in range(4):
                for k in range(8):
                    pe(lambda e, g=g, k=k: e.matmul(banks[1][:, g * 16:(g + 1) * 16], lhsT=w_p[:, k, g * 128:(g + 1) * 128], rhs=hTh[:, k, :],
                                                    start=(k == 0), stop=(k == 7)),
                       reads=["hTh", ("w_p", k)], writes=[B[1]])
            act(lambda e: e.copy(out=UH[:], in_=banks[1][:, 0:64].rearrange("p (g t) -> p g t", g=4)), reads=[B[1]], writes=["UH"])
            dve(lambda e: e.tensor_copy(out=U[:, :, 520:528], in_=UH[:, :, 0:8]), reads=["UH", "U"], writes=["U"])

            def pool_step(b, ncols, tok0, c0):
                P.dma("sp", lambda e: e.dma_start(out=inv_t[:, :, 0:(512 if b < NB else 8)],
                                                  in_=invc[:, b * 512:b * 512 + (512 if b < NB else 8)].partition_broadcast(128)),
                      writes=["inv_t"])
                pool(lambda e: e.tensor_tensor(out=a2[:, :, 0:527], in0=U[:, 1:4, 0:527], in1=U[:, 1:4, 1:528], op=ALU.add), reads=["U"], writes=["a2"])
                pool(lambda e: e.tensor_tensor(out=a4[:, :, 0:525], in0=a2[:, 1:3, 0:525], in1=a2[:, 1:3, 2:527], op=ALU.add), reads=["a2"], writes=["a4"])
                pool(lambda e: e.tensor_tensor(out=a8[:, :, 0:521], in0=a4[:, 1:2, 0:521], in1=a4[:, 1:2, 4:525], op=ALU.add), reads=["a4"], writes=["a8"])
                dve(lambda e: e.tensor_tensor(out=Sm[:, 0, :], in0=U[:, 0, 7:519], in1=U[:, 0, 8:520], op=ALU.add), reads=["U"], writes=["Sm0"])
                dve(lambda e: e.tensor_tensor(out=Sm[:, 1, :], in0=a2[:, 0, 6:518], in1=a2[:, 0, 8:520], op=ALU.add), reads=["a2"], writes=["Sm1"])
                dve(lambda e: e.tensor_tensor(out=Sm[:, 2, :], in0=a4[:, 0, 4:516], in1=a4[:, 0, 8:520], op=ALU.add), reads=["a4"], writes=["Sm2"])
                dve(lambda e: e.tensor_tensor(out=Sm[:, 3, :], in0=a8[:, 0, 0:512], in1=a8[:, 0, 8:520], op=ALU.add), reads=["a8"], writes=["Sm3"])
                dve(lambda e: e.tensor_tensor(out=Sm[:], in0=Sm[:], in1=inv_t[:], op=ALU.mult), reads=["Sm0", "Sm1", "Sm2", "Sm3", "inv_t"], writes=["Sm"])
                dve(lambda e: e.tensor_tensor(out=Dt[:], in0=Sm[:], in1=U[:, :, 8:520], op=ALU.subtract), reads=["Sm", "U"], writes=["Dt"])
                for g in range(4):
                    pe(lambda e, g=g: e.matmul(banks[2 + (g % 2)][:], lhsT=pw[:, g, :], rhs=Dt[:, g, :], start=True, stop=True),
                       reads=["Dt", ("pw", g)], writes=[B[2 + (g % 2)]])
                    act(lambda e, g=g: e.activation(out=yT[:, g, tok0:tok0 + ncols], in_=banks[2 + (g % 2)][:, c0:c0 + ncols], func=AF.Copy, scale=psc[:, g:g + 1]),
                        reads=[B[2 + (g % 2)], "psc"], writes=[("yT", g)])

            for b in range(NB):
                def ntile(tt, b=b):
                    t = 4 * b + tt
                    norm_hT(x[:, t, :], ("x", t), hTb[:, :, tt * 128:(tt + 1) * 128], ("hTb", tt), par=tt % 2, bank=tt % 2)
                P.pipeline(4, ntile)
                dve(lambda e: e.tensor_copy(out=U[:, :, 0:16], in_=U[:, :, 512:528]), reads=["U"], writes=["U"])
                for g in range(4):
                    for k in range(8):
                        pe(lambda e, g=g, k=k: e.matmul(banks[4 + (g % 2)][:], lhsT=w_p[:, k, g * 128:(g + 1) * 128], rhs=hTb[:, k, :],
                                                        start=(k == 0), stop=(k == 7)),
                           reads=[("hTb", 0), ("hTb", 1), ("hTb", 2), ("hTb", 3), ("w_p", k)], writes=[B[4 + (g % 2)]])
                    act(lambda e, g=g: e.copy(out=U[:, g, 16:528], in_=banks[4 + (g % 2)][:]), reads=[B[4 + (g % 2)], "U"], writes=["U"])
                if b == 0:
                    pool_step(0, 504, 0, 8)
                else:
                    pool_step(b, 512, 512 * b - 8, 0)
            dve(lambda e: e.tensor_copy(out=U[:, :, 0:16], in_=U[:, :, 512:528]), reads=["U"], writes=["U"])
            dve(lambda e: e.tensor_copy(out=U[:, :, 16:24], in_=UH[:, :, 8:16]), reads=["UH", "U"], writes=["U"])
            pool_step(NB, 8, WT - 8, 0)
            for t in range(NT):
                proj_resid(t, lambda k, t=t: yT[:, k, t * 128:(t + 1) * 128], [("yT", g) for g in range(4)], 4, wout, "wout")
            barrier()

    def phase_C():
        with contextlib.ExitStack() as S:
            oT = sb(S, "oT", [128, 4, WT], BF16)
            with contextlib.ExitStack() as S2:
                kh = [sb(S2, "kh", [96, SEQ], BF16) for _ in range(2)]
                va = [sb(S2, "va", [128, NTB, 128], BF16) for _ in range(2)]
                qh = [sb(S2, "qh", [96, WT], BF16) for _ in range(2)]
                E2 = [sb(S2, "E", [128, 1024], BF16) for _ in range(2)]
                rec = sb(S2, "rec", [128, 512], F32)
                for p in range(2):
                    dve(lambda e, p=p: e.memset(va[p][:, :, (1 - p) * 64:(1 - p) * 64 + 64], 1.0), writes=[("va1", p)])
                scale = 96 ** -0.5
                NPAIR = NTB // 2
                for h in range(8):
                    p = h % 2
                    lo, hi = p * 64, p * 64 + 64
                    dlo, dhi = (1 - p) * 64, (1 - p) * 64 + 64
                    P.dma("sp", lambda e, h=h, p=p: e.dma_start(out=kh[p][:], in_=kT_scr[h]), writes=[("kh", p)])
                    P.dma("sp", lambda e, h=h, p=p: e.dma_start(out=va[p][:, :, p * 64:p * 64 + 64], in_=v_scr[h]), writes=[("va", p)])
                    P.dma("sp", lambda e, h=h, p=p: e.dma_start(out=qh[p][:], in_=qT_scr[h]), writes=[("qh", p)])
                    for qb in range(NB):
                        ob = 4 + (qb % 2)

                        def S_pair(cc, qb=qb, p=p):
                            for s_ in range(2):
                                c = 2 * cc + s_
                                bk = 2 * (cc % 2) + s_
                                pe(lambda e, c=c, bk=bk: e.matmul(banks[bk], lhsT=kh[p][:, c * 128:(c + 1) * 128], rhs=qh[p][:, qb * 512:(qb + 1) * 512],
                                                                  start=True, stop=True),
                                   reads=[("kh", p), ("qh", p)], writes=[B[bk]])
                        S_pair(0)
                        S_pair(1)
                        for cc in range(NPAIR):
                            pp = cc % 2
                            act(lambda e, pp=pp: e.activation(out=E2[pp][:], in_=psum_all[:, 2 * pp * 512:(2 * pp + 2) * 512], func=AF.Exp, scale=scale),
                                reads=[B[2 * pp], B[2 * pp + 1]], writes=[("E", pp)])
                            for s_ in range(2):
                                c = 2 * cc + s_
                                pe(lambda e, c=c, p=p, ob=ob, pp=pp, s_=s_: e.matmul(banks[ob], lhsT=va[p][:, c, :], rhs=E2[pp][:, s_ * 512:(s_ + 1) * 512],
                                                                                   start=(c == 0), stop=(c == NTB - 1)),
                                   reads=[("va", p), ("va1", p), ("E", pp)], writes=[B[ob]])
                            if cc + 2 < NPAIR:
                                S_pair(cc + 2)
                        dve(lambda e, ob=ob, lo=lo, hi=hi, dlo=dlo, dhi=dhi: e.tensor_copy(out=rec[lo:hi, :], in_=banks[ob][dlo:dhi, :]), reads=[B[ob]], writes=["rec"])
                        dve(lambda e, lo=lo, hi=hi: e.reciprocal(out=rec[lo:hi, :], in_=rec[lo:hi, :]), reads=["rec"], writes=["rec"])
                        dve(lambda e, ob=ob, h=h, qb=qb, lo=lo, hi=hi: e.tensor_tensor(out=oT[lo:hi, h // 2, qb * 512:(qb + 1) * 512], in0=banks[ob][lo:hi, :],
                                                                         in1=rec[lo:hi, :], op=ALU.mult),
                            reads=[B[ob], "rec"], writes=[("oT", h // 2)])
                barrier()
            with contextlib.ExitStack() as S2:
                wout = sb(S2, "wout2", [128, 4, D], BF16)
                load_w(wout, w_out_e[0][512:1024, :], "wout2")
                for t in range(NT):
                    proj_resid(t, lambda k, t=t: oT[:, k, t * 128:(t + 1) * 128], [("oT", g) for g in range(4)], 4, wout, "wout2")
                barrier()

    def phase_xattn(L):
        with contextlib.ExitStack() as S:
            wq = sb(S, "wq", [128, 8, D], BF16)
            wo = sb(S, "wo", [128, 8, D], BF16)
            gq = sb(S, "gq256", [128, 256], F32)
            hTt = sb(S, "hTt", [128, 8, 128], BF16)
            qn = sb(S, "qn", [128, D], F32)
            qb_ = sb(S, "qb", [128, D], BF16)
            qT2 = [sb(S, "qT", [128, 8, 512], BF16) for _ in range(2)]
            E = [sb(S, "E", [128, 512], BF16) for _ in range(2)]
            rec = sb(S, "rec", [128, 512], F32)
            oTb = sb(S, "oTb", [128, 8, 512], BF16)
            load_w(wq, w_mem_q[L], "wq")
            load_w(wo, w_mem_o[L], "wo")
            load_bc(gn[:], xattn_norm_g[L], "gn")
            load_bc(gq[:], mem_q_g[L], "gq")
            scale = 256 ** -0.5

            def block(b):
                qT = qT2[b % 2]
                qTr = "qT%d" % (b % 2)
                for tt in range(4):
                    t = 4 * b + tt
                    norm_hT(x[:, t, :], ("x", t), hTt[:], "hTt")
                    for half in range(2):
                        for k in range(8):
                            pe(lambda e, half=half, k=k: e.matmul(banks[1 + half][:], lhsT=hTt[:, k, :], rhs=wq[:, k, half * 512:(half + 1) * 512],
                                                                   start=(k == 0), stop=(k == 7)),
                               reads=["hTt", ("wq", k)], writes=[B[1 + half]])
                    for hh in range(4):
                        act(lambda e, hh=hh: e.activation(out=sq[:, hh * 256:(hh + 1) * 256], in_=banks[1 + hh // 2][:, (hh % 2) * 256:(hh % 2 + 1) * 256],
                                                          func=AF.Square, accum_out=ss32[:, hh:hh + 1]),
                            reads=[B[1 + hh // 2]], writes=["sq", "ss32"])
                    rstd_from(ss32[:, 0:4], "ss32", 256, rs32[:, 0:4], "rs32")
                    for hh in range(4):
                        dve(lambda e, hh=hh: e.scalar_tensor_tensor(out=qb_[:, hh * 256:(hh + 1) * 256], in0=banks[1 + hh // 2][:, (hh % 2) * 256:(hh % 2 + 1) * 256],
                                                                    scalar=rs32[:, hh:hh + 1], in1=gq[:], op0=ALU.mult, op1=ALU.mult),
                            reads=[B[1 + hh // 2], "rs32", "gq"], writes=["qb"])
                    pT = bbf(0)
                    for j in range(8):
                        pe(lambda e, j=j: e.transpose(out=pT[:, j * 128:(j + 1) * 128], in_=qb_[:, j * 128:(j + 1) * 128], identity=ident[:]),
                           reads=["qb", "ident"], writes=[B[0]])
                    act(lambda e, tt=tt, qT=qT: e.copy(out=qT[:, :, tt * 128:(tt + 1) * 128], in_=pT[:, 0:1024].rearrange("p (j t) -> p j t", j=8)),
                        reads=[B[0]], writes=[qTr])
                P.mark()
                for h in range(4):
                    for m in range(2):
                        for j in range(2):
                            pe(lambda e, h=h, m=m, j=j, qT=qT: e.matmul(banks[3 + m][:], lhsT=memkT[:, h, j, m * 128:(m + 1) * 128], rhs=qT[:, 2 * h + j, :],
                                                                         start=(j == 0), stop=(j == 1)),
                               reads=[qTr, "memkT"], writes=[B[3 + m]])
                        act(lambda e, m=m: e.activation(out=E[m][:], in_=banks[3 + m][:], func=AF.Exp, scale=scale), reads=[B[3 + m]], writes=[("E", m)])
                    for m in range(2):
                        pe(lambda e, m=m: e.matmul(banks[5][:], lhsT=ones_b[:], rhs=E[m][:], start=(m == 0), stop=(m == 1)),
                           reads=["ones_b", ("E", m)], writes=[B[5]])
                    act(lambda e: e.activation(out=rec[:], in_=banks[5][:], func=AF.Ln), reads=[B[5]], writes=["rec"])
                    act(lambda e: e.activation(out=rec[:], in_=rec[:], func=AF.Exp, scale=-1.0), reads=["rec"], writes=["rec"])
                    for dv in range(2):
                        for m in range(2):
                            pe(lambda e, h=h, m=m, dv=dv: e.matmul(banks[6 + dv][:], lhsT=memv[:, m, h, dv * 128:(dv + 1) * 128], rhs=E[m][:],
                                                                    start=(m == 0), stop=(m == 1)),
                               reads=["memv", ("E", m)], writes=[B[6 + dv]])
                        dve(lambda e, h=h, dv=dv: e.tensor_tensor(out=oTb[:, 2 * h + dv, :], in0=banks[6 + dv][:], in1=rec[:], op=ALU.mult),
                            reads=[B[6 + dv], "rec"], writes=["oTb"])
                for tt in range(4):
                    t = 4 * b + tt
                    proj_resid(t, lambda k, tt=tt: oTb[:, k, tt * 128:(tt + 1) * 128], ["oTb"], 8, wo, "wo")

            P.pipeline(NB, block)
            barrier()

    def phase_mlp(L, last=False):
        NPASS = 8
        with contextlib.ExitStack() as S:
            hT = sb(S, "hTall", [128, 8, WT], BF16)
            w1 = [sb(S, "w1", [128, 8, 512], BF16) for _ in range(2)]
            w2 = [sb(S, "w2", [128, 4, D], BF16) for _ in range(2)]
            aT = [sb(S, "aT", [128, 512], BF16) for _ in range(4)]
            s2 = [sb(S, "s2", [128, 512], F32) for _ in range(2)]
            load_bc(gn[:], ff_norm_g[L], "gn")

            def load_pass(ps_):
                b = ps_ % 2
                load_w(w1[b], w_ff1[L][:, ps_ * 512:(ps_ + 1) * 512], ("w1", b))
                load_w(w2[b], w_ff2[L][ps_ * 512:(ps_ + 1) * 512, :], ("w2", b))
            load_pass(0)

            def norm_block(b):
                P.capture()
                for tt in range(4):
                    t = 4 * b + tt
                    norm_hT(x[:, t, :], ("x", t), hT[:, :, t * 128:(t + 1) * 128], ("hT", t // 4), par=t % 2, bank=(0, 3)[t % 2])
                return P.end_capture()

            P.replay(norm_block(0))
            for ps_ in range(NPASS):
                pb = ps_ % 2
                if ps_ + 1 < NPASS:
                    load_pass(ps_ + 1)
                for b in range(NB):
                  if ps_ == 0:
                    P.capture()
                  if True:
                    for f in range(4):
                        zb = 1 + (f % 2)
                        for k in range(8):
                            pe(lambda e, f=f, k=k, zb=zb, pb=pb, b=b: e.matmul(banks[zb][:], lhsT=w1[pb][:, k, f * 128:(f + 1) * 128],
                                                                                 rhs=hT[:, k, b * 512:(b + 1) * 512], start=(k == 0), stop=(k == 7)),
                               reads=[("hT", b), (("w1", pb), k)], writes=[B[zb]])
                        act(lambda e, f=f, zb=zb: e.activation(out=s2[f % 2][:], in_=banks[zb][:], func=AF.Square), reads=[B[zb]], writes=[("s2", f % 2)])
                        dve(lambda e, f=f, zb=zb: e.scalar_tensor_tensor(out=aT[f][:], in0=banks[zb][:], scalar=0.0, in1=s2[f % 2][:],
                                                                          op0=ALU.is_gt, op1=ALU.mult),
                            reads=[B[zb], ("s2", f % 2)], writes=[("aT", f)])
                    for tt in range(4):
                        t = 4 * b + tt
                        proj_resid(t, lambda k, tt=tt: aT[k][:, tt * 128:(tt + 1) * 128], [("aT", f) for f in range(4)], 4, w2[pb], ("w2", pb))
                  if ps_ == 0:
                    Cb = P.end_capture()
                    Nb = norm_block(b + 1) if b + 1 < NB else []
                    P.replay(P.zipmerge(Cb, Nb))
            if last:
                for t in range(NT):
                    fins.append(P.dma("sp", lambda e, t=t: e.dma_start(out=out[t * 128:(t + 1) * 128, :], in_=x[:, t, :]), reads=[("x", t)]))
            barrier()

    def phase_G1():
        with contextlib.ExitStack() as S:
            wqkv = sb(S, "wqkv", [128, 8, 3072], BF16)
            gqk = sb(S, "gqk", [128, 2, 64], F32)
            two = lambda name, shape, dt: [sb(S, name, shape, dt) for _ in range(2)]
            hTt_ = two("hTt", [128, 8, 128], BF16)
            qkn_ = two("qkn", [128, 1024], F32)
            qkb_ = two("qkb", [128, 1024], BF16)
            qkst_ = two("qkst", [128, 8, 128], BF16)
            vst_ = two("vst", [128, 4, 128], BF16)
            load_w(wqkv, w_qkv_o[0], "wqkv")
            load_bc(gn[:], mix_norm_g[1], "gn")
            load_bc(gqk[:, 0, :], na_q_g[0], "gqk0")
            load_bc(gqk[:, 1, :], na_k_g[0], "gqk1")

            def unit(n):
                t, u = n // 2, n % 2
                x_ = "_%d" % u
                sfx = "" if u == 0 else "_p1"
                bT, bQ, bV = 4 * u, (4 * u + 1, 4 * u + 2), 4 * u + 3
                hTt = hTt_[t % 2]
                hr = "hTt_%d" % (t % 2)
                qkn, qkb, qkst, vst = qkn_[u], qkb_[u], qkst_[u], vst_[u]
                sq_, ss32_, rs32_ = SQ[u], SS32[u], RS32[u]
                if u == 0:
                    norm_hT(x[:, t, :], ("x", t), hTt[:], hr, par=0, bank=bT)
                for n2 in range(2):
                    for k in range(8):
                        pe(lambda e, n2=n2, k=k: e.matmul(banks[bQ[n2]][:], lhsT=hTt[:, k, :], rhs=wqkv[:, k, u * 1024 + n2 * 512:u * 1024 + (n2 + 1) * 512],
                                                          start=(k == 0), stop=(k == 7)),
                           reads=[hr, ("wqkv", k)], writes=[B[bQ[n2]]])
                for n2 in range(2):
                    act(lambda e, n2=n2: e.activation(out=sq_[:, n2 * 512:(n2 + 1) * 512], in_=banks[bQ[n2]][:], func=AF.Square),
                        reads=[B[bQ[n2]]], writes=["sq" + sfx])
                dve(lambda e: e.tensor_reduce(out=ss32_[:, 0:16], in_=sq_[:, 0:1024].rearrange("p (h d) -> p h d", h=16), axis=AX.X, op=ALU.add),
                    reads=["sq" + sfx], writes=["ss32" + sfx])
                rstd_from(ss32_[:, 0:16], "ss32" + sfx, 64, rs32_[:, 0:16], "rs32" + sfx)
                for n2 in range(2):
                    dve(lambda e, n2=n2: e.tensor_tensor(out=qkn[:, n2 * 512:(n2 + 1) * 512].rearrange("p (h d) -> p h d", h=8),
                                                         in0=banks[bQ[n2]][:].rearrange("p (h d) -> p h d", h=8),
                                                         in1=rs32_[:, 8 * n2:8 * n2 + 8].unsqueeze(2).to_broadcast([128, 8, 64]), op=ALU.mult),
                        reads=[B[bQ[n2]], "rs32" + sfx], writes=["qkn" + x_])
                pool(lambda e: e.tensor_tensor(out=qkb[:].rearrange("p (h d) -> p h d", h=16), in0=qkn[:].rearrange("p (h d) -> p h d", h=16),
                                               in1=gqk[:, u, :].unsqueeze(1).to_broadcast([128, 16, 64]), op=ALU.mult),
                     reads=["qkn" + x_, "gqk0", "gqk1"], writes=["qkb" + x_])
                for k in range(8):
                    pe(lambda e, k=k: e.matmul(banks[bV][:], lhsT=hTt[:, k, :], rhs=wqkv[:, k, 2048 + u * 512:2048 + (u + 1) * 512],
                                               start=(k == 0), stop=(k == 7)),
                       reads=[hr, ("wqkv", k)], writes=[B[bV]])
                act(lambda e: e.copy(out=vst[:], in_=banks[bV][:].rearrange("p (h d) -> p h d", h=4)), reads=[B[bV]], writes=["vst" + x_])
                P.dma("sp", lambda e: e.dma_start(out=nv_scr[4 * u:4 * u + 4, :, t, :].rearrange("h p d -> p h d"), in_=vst[:]),
                      reads=["vst" + x_], writes=[("nv_scr", n)])
                pT = bbf(bT)
                for j in range(8):
                    pe(lambda e, j=j: e.transpose(out=pT[:, j * 128:(j + 1) * 128], in_=qkb[:, j * 128:(j + 1) * 128], identity=ident[:]),
                       reads=["qkb" + x_, "ident"], writes=[B[bT]])
                act(lambda e: e.copy(out=qkst[:], in_=pT[:, 0:1024].rearrange("p (j t) -> p j t", j=8)), reads=[B[bT]], writes=["qkst" + x_])
                dst = nq_scr if u == 0 else nk_scr
                P.dma("sp", lambda e: e.dma_start(out=dst[:, :, t * 128:(t + 1) * 128].rearrange("h p t -> p h t"), in_=qkst[:]),
                      reads=["qkst" + x_], writes=[("nqk_scr", n)])

            P.pipeline(2 * NT, unit)
            barrier()

    def phase_G2():
        with contextlib.ExitStack() as S:
            cT = sb(S, "cT", [128, 8, WT], BF16)
            with contextlib.ExitStack() as S2:
                qT = sb(S2, "nqT", [128, WT], BF16)
                kT = sb(S2, "nkT", [128, WT], BF16)
                va = [sb(S2, "nva", [128, NT, 128], BF16) for _ in range(2)]
                bias32 = sb(S2, "nbias", [128, 13, 128], F32)
                bh = [sb(S2, "nbh", [128, 25, 128], BF16) for _ in range(2)]
                bl = [sb(S2, "nbl", [128, 25, 128], BF16) for _ in range(2)]
                E = [sb(S2, "nE", [128, 512], BF16) for _ in range(3)]
                bg = [SQ[1][:, 0:640].rearrange("p (a b) -> p a b", a=5), gn[:, 0:640].rearrange("p (a b) -> p a b", a=5)]
                St = [SQ[0][:, 0:512], SQ[0][:, 512:1024]]
                rec = [sb(S2, "rec", [128, 512], F32)] * 2
                dve(lambda e: e.memset(va[0][:, :, 64:128], 1.0), writes=[("va1", 0)])
                dve(lambda e: e.memset(va[1][:, :, 0:64], 1.0), writes=[("va1", 1)])
                scale = 64 ** -0.5
                inv_scale = 8.0

                def row_cls(r):
                    return {0: 1, 2: 2, 36: 3, 38: 4}.get(r, 0)

                def prep_bias(h):
                    p = h % 2
                    P.dma("sp", lambda e, h=h, p=p: e.dma_start(out=bg[p], in_=nab[h][:, 0:5, :]), writes=[("bg", p)])
                    for (v0, v1) in ((0, 13), (13, 25)):
                        nv = v1 - v0
                        P.dma("sp", lambda e, h=h, v0=v0, v1=v1, nv=nv: e.dma_start(out=bias32[:, 0:nv, :], in_=nab[h][:, v0:v1, :]), writes=["bias32"])
                        dve(lambda e, p=p, v0=v0, v1=v1, nv=nv: e.tensor_scalar(out=bh[p][:, v0:v1, :], in0=bias32[:, 0:nv, :], scalar1=inv_scale, scalar2=None, op0=ALU.mult),
                            reads=["bias32"], writes=[("bh", p)])
                        dve(lambda e, p=p, v0=v0, v1=v1, nv=nv: e.scalar_tensor_tensor(out=bl[p][:, v0:v1, :].rearrange("p a b -> p (a b)"),
                                                                                      in0=bias32[:, 0:nv, :].rearrange("p a b -> p (a b)"), scalar=inv_scale,
                                                                                      in1=bh[p][:, v0:v1, :].rearrange("p a b -> p (a b)"), op0=ALU.mult, op1=ALU.subtract),
                            reads=["bias32", ("bh", p)], writes=[("bl", p)])

                prep_bias(0)
                for hp in range(8):
                    P.dma("sp", lambda e, hp=hp: e.dma_start(out=qT[:], in_=nq_scr[hp]), writes=["nqT"])
                    P.dma("sp", lambda e, hp=hp: e.dma_start(out=kT[:], in_=nk_scr[hp]), writes=["nkT"])
                    P.dma("sp", lambda e, hp=hp: e.dma_start(out=va[0][:, :, 0:64], in_=nv_scr[hp][:, :, 0:64]), writes=[("va", 0)])
                    P.dma("sp", lambda e, hp=hp: e.dma_start(out=va[1][:, :, 64:128], in_=nv_scr[hp][:, :, 64:128]), writes=[("va", 1)])
                    for p in range(2):
                        h = 2 * hp + p
                        lo, hi = p * 64, p * 64 + 64
                        dlo, dhi = (1 - p) * 64, (1 - p) * 64 + 64
                        items = [(blk, j) for blk in range(NB) for j in range(5)]

                        def S_stage(i, p=p, lo=lo, hi=hi):
                            blk, j = items[i]
                            sbk = i % 3
                            groups = []
                            for rp in range(4):
                                c = row_cls(8 * blk + 2 * rp)
                                if groups and groups[-1][0] == c:
                                    groups[-1][2] += 1
                                else:
                                    groups.append([c, rp, 1])
                            first = True
                            use_dve = (len(groups) == 1 and i % 2 == 1)
                            for src in (() if use_dve else (bh, bl)):
                                for (c, rp0, n) in groups:
                                    pe(lambda e, src=src, c=c, rp0=rp0, n=n, j=j, sbk=sbk, first=first: e.matmul(
                                            banks[sbk][:, rp0 * 128:(rp0 + n) * 128], lhsT=ident[:],
                                            rhs=src[p][:, c * 5 + j, :].unsqueeze(1).to_broadcast([128, n, 128]),
                                            start=first, stop=False, skip_group_check=True),
                                       reads=[("bh", p), ("bl", p), "ident"], writes=[B[sbk]])
                                    first = False
                            for rp in range(4):
                                r = 8 * blk + 2 * rp
                                tb = min(max(r - 4, 0), 30)
                                kt0 = (tb + 2 * j) * 64
                                pe(lambda e, kt0=kt0, r=r, sbk=sbk, rp=rp: e.matmul(banks[sbk][:, rp * 128:(rp + 1) * 128], lhsT=kT[lo:hi, kt0:kt0 + 128],
                                                                                   rhs=qT[lo:hi, r * 64:r * 64 + 128], start=(use_dve and rp == 0), stop=(rp == 3), skip_group_check=True),
                                   reads=["nkT", "nqT"], writes=[B[sbk]])

                        def mid_stage(i, p=p):
                            sbk = i % 3
                            blk, j = items[i]
                            cls = set(row_cls(8 * blk + 2 * rp) for rp in range(4))
                            if len(cls) == 1 and i % 2 == 1:
                                st = St[(i // 2) % 2]
                                sr = ("St", (i // 2) % 2)
                                dve(lambda e, sbk=sbk, st=st, j=j: e.scalar_tensor_tensor(out=st.rearrange("p (a b) -> p a b", a=4),
                                                                                        in0=banks[sbk][:].rearrange("p (a b) -> p a b", a=4), scalar=scale,
                                                                                        in1=bg[p][:, j, :].unsqueeze(1).to_broadcast([128, 4, 128]), op0=ALU.mult, op1=ALU.add),
                                    reads=[B[sbk], ("bg", p)], writes=[sr])
                                act(lambda e, i=i, st=st: e.activation(out=E[i % 3][:], in_=st, func=AF.Exp), reads=[sr], writes=[("nE", i % 3)])
                            else:
                                act(lambda e, i=i, sbk=sbk: e.activation(out=E[i % 3][:], in_=banks[sbk][:], func=AF.Exp, scale=scale),
                                    reads=[B[sbk]], writes=[("nE", i % 3)])

                        def PV_stage(i, p=p, lo=lo, hi=hi, dlo=dlo, dhi=dhi, hp=hp):
                            blk, j = items[i]
                            ob = 3 + (blk % 2)
                            for rp in range(4):
                                r = 8 * blk + 2 * rp
                                tb = min(max(r - 4, 0), 30)
                                vt = (tb + 2 * j) // 2
                                pe(lambda e, vt=vt, i=i, ob=ob, rp=rp, j=j: e.matmul(banks[ob][:, rp * 128:(rp + 1) * 128], lhsT=va[p][:, vt, :],
                                                                                    rhs=E[i % 3][:, rp * 128:(rp + 1) * 128],
                                                                                    start=(j == 0 and rp == 0), stop=(j == 4), skip_group_check=True),
                                   reads=[("va", p), ("va1", p), ("nE", i % 3)], writes=[B[ob]])
                            if j == 4:
                                for step in range(3):
                                    pending.append((i + 1 + step, lambda blk=blk, ob=ob, step=step: epilogue(blk, ob, step)))

                        def epilogue(blk, ob, step, p=p, lo=lo, hi=hi, dlo=dlo, dhi=dhi, hp=hp):
                            rc = rec[blk % 2]
                            rr = ("rec", 0)
                            if step == 0:
                                dve(lambda e, ob=ob, rc=rc: e.tensor_copy(out=rc[lo:hi, :], in_=banks[ob][dlo:dhi, :]), reads=[B[ob]], writes=[rr])
                            elif step == 1:
                                act(lambda e, rc=rc: e.activation(out=rc[lo:hi, :], in_=rc[lo:hi, :], func=AF.Ln), reads=[rr], writes=[rr])
                                act(lambda e, rc=rc: e.activation(out=rc[lo:hi, :], in_=rc[lo:hi, :], func=AF.Exp, scale=-1.0), reads=[rr], writes=[rr])
                            else:
                                dve(lambda e, ob=ob, rc=rc, blk=blk: e.tensor_tensor(out=cT[lo:hi, hp, blk * 512:(blk + 1) * 512], in0=banks[ob][lo:hi, :],
                                                                                    in1=rc[lo:hi, :], op=ALU.mult),
                                    reads=[B[ob], rr], writes=[("cT", hp)])

                        n_it = len(items)
                        pending = []
                        S_stage(0)
                        S_stage(1)
                        for i in range(n_it):
                            mid_stage(i)
                            PV_stage(i)
                            if i + 2 < n_it:
                                S_stage(i + 2)
                            if i == 4 and h + 1 < 16:
                                prep_bias(h + 1)
                            pending.sort(key=lambda q: q[0])
                            while pending and pending[0][0] <= i:
                                pending.pop(0)[1]()
                        pending.sort(key=lambda q: q[0])
                        while pending:
                            pending.pop(0)[1]()
                barrier()
            with contextlib.ExitStack() as S2:
                wout = sb(S2, "wouto", [128, 8, D], BF16)
                load_w(wout, w_out_o[0], "wouto")
                for t in range(NT):
                    proj_resid(t, lambda k, t=t: cT[:, k, t * 128:(t + 1) * 128], [("cT", g) for g in range(8)], 8, wout, "wouto")
                barrier()

    fins = []
    import os
    PH = os.environ.get("PHASES", "A,B1,B2,C").split(",")
    if stage >= 1:
        if "A" in PH:
            phase_A()
        if "B1" in PH:
            phase_B1()
        if "B2" in PH:
            phase_B2()
        if "C" in PH:
            phase_C()
    if stage >= 2:
        phase_mem()
        phase_xattn(0)
    if stage >= 3:
        phase_mlp(0)
    if stage >= 4:
        phase_G1()
        phase_G2()
    if stage >= 5:
        phase_xattn(1)
        phase_mlp(1, last=True)
    if not fins:
        for t in range(NT):
            fins.append(P.dma("sp", lambda e, t=t: e.dma_start(out=out[t * 128:(t + 1) * 128, :], in_=x[:, t, :]), reads=[("x", t)]))
    P.emit(nc, final_ops=fins)
    return nc, P


def _rope_table(pos):
    half = 16
    freqs = (np.float32(10000.0) ** (-np.arange(half, dtype=np.float32) / np.float32(half))).astype(np.float32)
    ang = (pos.astype(np.float32)[:, None] * freqs[None, :]).astype(np.float32)
    c = np.cos(ang).astype(np.float32)
    s = np.sin(ang).astype(np.float32)
    return np.concatenate([c, c, -s, s], axis=1).astype(np.float32)


def _invc_table(a_tok):
    tab = np.ones((4, 2568), np.float32)
    tg = a_tok + np.arange(2568) - 8
    valid = (tg >= 0) & (tg < SEQ)
    for g, w in enumerate((2, 4, 8, 16)):
        lo = np.clip(tg - w // 2, 0, SEQ - 1)
        hi = np.clip(tg + w - 1 - w // 2, 0, SEQ - 1)
        cnt = (hi - lo + 1).astype(np.float32)
        tab[g] = np.where(valid, np.float32(1.0) / cnt, np.float32(1.0))
    return tab


def _natten_bias(rpb):
    H = rpb.shape[0]
    outb = np.full((H, 25, 128, 128), NEG, np.float32)
    cols = np.arange(64)
    c0 = np.clip(cols - 8, 0, 48)
    classes = {0: 8, 1: 0, 2: 2, 3: 36, 4: 38}
    kc = np.arange(64)[:, None]
    qc = np.arange(64)[None, :]
    colvalid = (kc >= c0[None, :]) & (kc < c0[None, :] + 16)
    dc = np.clip(kc - qc + 15, 0, 30)
    for cls, r in classes.items():
        tb = min(max(r - 4, 0), 30)
        for j in range(5):
            for kr_i in range(2):
                kr = tb + 2 * j + kr_i
                for qr_i in range(2):
                    qr = r + qr_i
                    r0 = min(max(qr - 4, 0), 32)
                    if not (r0 <= kr <= r0 + 7):
                        continue
                    dr = kr - qr + 7
                    vals = rpb[:, dr][:, dc]
                    blk = np.where(colvalid[None], vals, np.float32(NEG))
                    outb[:, cls * 5 + j, kr_i * 64:(kr_i + 1) * 64, qr_i * 64:(qr_i + 1) * 64] = blk
    return np.ascontiguousarray(outb.transpose(0, 2, 1, 3))


def make_in_maps(inputs):
    f = lambda a: np.ascontiguousarray(np.asarray(a, dtype=np.float32))
    xfull = f(inputs["x"])
    memf = f(inputs["mem"])
    shared = {k: f(v) for k, v in inputs.items() if k not in ("x", "mem", "na_rpb")}
    nabt = _natten_bias(f(inputs["na_rpb"])[0])
    pos_b = np.arange(SEQ)
    cs_b = _rope_table(pos_b)
    maps, meta = [], []
    for c in range(8):
        b, j = c // 4, c % 4
        a = min(max(32 * j - 4, 0), 88)
        a_tok = a * 64
        xw = xfull[b, a_tok:a_tok + WT]
        xh = np.zeros((16, D), np.float32)
        if a_tok >= 8:
            xh[0:8] = xfull[b, a_tok - 8:a_tok]
        if a_tok + WT + 8 <= SEQ:
            xh[8:16] = xfull[b, a_tok + WT:a_tok + WT + 8]
        m = dict(shared)
        m.update(xw=np.ascontiguousarray(xw), xh=xh, xb=xfull[b], mem=memf[b],
                 cs_w=np.ascontiguousarray(cs_b[a_tok:a_tok + WT]), cs_b=cs_b, invc=_invc_table(a_tok), nab=nabt)
        maps.append(m)
        meta.append((b, a, 32 * j - a))
    return maps, meta


_CACHE = {}


def kernel(**inputs):
    if "nc" not in _CACHE:
        _CACHE["nc"] = build_program(99)[0]
    nc = _CACHE["nc"]
    maps, meta = make_in_maps(inputs)
    res = run_bass_kernel_spmd(nc, maps, core_ids=list(range(8)))
    outp = np.zeros((2, SEQ, D), np.float32)
    for c in range(8):
        b, a, off = meta[c]
        o = np.asarray(res.results[c]["out"]).reshape(WT, D)
        j = c % 4
        outp[b, j * 2048:(j + 1) * 2048] = o[off * 64:off * 64 + 2048]
    return outp
```

```python
import concourse.bass as bass
import concourse.mybir as mybir

ENGS = ("pe", "act", "dve", "pool", "sp")
SAME_ENG_SYNC = {"pe": False, "act": True, "dve": True, "pool": True, "sp": False}
NSLOT = 12


class Op:
    __slots__ = ("id", "eng", "fn", "deps", "dma", "flag", "sem", "val", "slot_guard", "nwaits")

    def __init__(self, id, eng, fn, dma):
        self.id = id
        self.eng = eng
        self.fn = fn
        self.dma = dma
        self.deps = []
        self.flag = False
        self.sem = None
        self.val = None
        self.slot_guard = None


class Prog:
    def __init__(self):
        self.ops = []
        self.by_eng = {e: [] for e in ENGS}
        self.last_w = {}
        self.readers = {}
        self.dma_count = {e: 0 for e in ENGS}

    _cap = None

    def capture(self):
        self._cap = []

    def end_capture(self):
        c = self._cap
        self._cap = None
        return c

    def mark(self):
        self._mark = len(self._cap)

    def replay(self, lst):
        for it in lst:
            self.op(*it)

    @staticmethod
    def zipmerge(a, b):
        out = []
        na, nb = len(a), len(b)
        ia = ib = 0
        while ia < na or ib < nb:
            if ib >= nb or (ia < na and ia * max(nb, 1) <= ib * max(na, 1)):
                out.append(a[ia]); ia += 1
            else:
                out.append(b[ib]); ib += 1
        return out

    def pipeline_deep(self, n, tile_fn, depth):
        parts = {}
        for s in range(n + depth - 1):
            if s < n:
                self.capture()
                tile_fn(s)
                L = self.end_capture()
                m = len(L)
                cuts = [int(round(m * i / depth)) for i in range(depth + 1)]
                parts[s] = [L[cuts[i]:cuts[i + 1]] for i in range(depth)]
            merged = []
            for d in range(depth - 1, -1, -1):
                t = s - d
                if 0 <= t < n:
                    merged = self.zipmerge(merged, parts[t][d]) if merged else list(parts[t][d])
            self.replay(merged)
            if s - depth + 1 in parts and s - depth + 1 >= 0:
                del parts[s - depth + 1]

    def pipeline(self, n, tile_fn, split=0.5):
        prevB = []
        for t in range(n):
            self.capture()
            self._mark = None
            tile_fn(t)
            L = self.end_capture()
            h = self._mark if self._mark is not None else int(len(L) * split)
            self.replay(self.zipmerge(L[:h], prevB))
            prevB = L[h:]
        self.replay(prevB)

    def op(self, eng, fn, reads=(), writes=(), dma=False):
        if self._cap is not None:
            self._cap.append((eng, fn, reads, writes, dma))
            return None
        o = Op(len(self.ops), eng, fn, dma)
        reads = list(reads)
        writes = list(writes) + [r for r in reads if isinstance(r, str) and r.startswith("bank")]
        deps = set()
        for r in reads:
            w = self.last_w.get(r)
            if w is not None:
                deps.add(w)
        for w_ in writes:
            w = self.last_w.get(w_)
            if w is not None:
                deps.add(w)
            for rd in self.readers.get(w_, ()):
                deps.add(rd)
        deps.discard(o.id)
        o.deps = sorted(deps)
        for r in reads:
            self.readers.setdefault(r, []).append(o.id)
        for w_ in writes:
            self.last_w[w_] = o.id
            self.readers[w_] = []
        self.ops.append(o)
        self.by_eng[eng].append(o)
        return o

    def pe(self, fn, reads=(), writes=()):
        return self.op("pe", fn, reads, writes)

    def act(self, fn, reads=(), writes=()):
        return self.op("act", fn, reads, writes)

    def dve(self, fn, reads=(), writes=()):
        return self.op("dve", fn, reads, writes)

    def pool(self, fn, reads=(), writes=()):
        return self.op("pool", fn, reads, writes)

    def dma(self, eng, fn, reads=(), writes=()):
        return self.op(eng, fn, reads, writes, dma=True)

    def emit(self, nc, final_ops=()):
        ops = self.ops
        for o in ops:
            for d in o.deps:
                do = ops[d]
                if do.dma or do.eng != o.eng or SAME_ENG_SYNC[o.eng]:
                    do.flag = True
        for o in final_ops:
            o.flag = True
        import contextlib
        with contextlib.ExitStack() as es:
            esem = {e: es.enter_context(nc.semaphore("s_" + e)) for e in ENGS}
            dsem = {}
            for e in ENGS:
                if self.dma_count_total(e) > 0:
                    dsem[e] = [es.enter_context(nc.semaphore("d_%s_%d" % (e, i))) for i in range(NSLOT)]
            cnt = {e: 0 for e in ENGS}
            dcnt = {e: 0 for e in ENGS}
            semkey = {}
            for o in ops:
                if o.dma:
                    j = dcnt[o.eng]
                    dcnt[o.eng] += 1
                    s = dsem[o.eng][j % NSLOT]
                    o.sem = s
                    o.val = 16 * (j // NSLOT + 1)
                    o.slot_guard = (s, 16 * (j // NSLOT)) if j >= NSLOT else None
                    semkey[id(s)] = ("d", o.eng, j % NSLOT)
                elif o.flag:
                    cnt[o.eng] += 1
                    o.sem = esem[o.eng]
                    o.val = cnt[o.eng]
            clock = {e: {} for e in ENGS}
            opvc = [None] * len(ops)
            plan = {e: [] for e in ENGS}
            for o in ops:
                ck = clock[o.eng]
                waits = {}
                for d in o.deps:
                    do = ops[d]
                    if not (do.dma or do.eng != o.eng or SAME_ENG_SYNC[o.eng]):
                        continue
                    k = id(do.sem)
                    if ck.get(k, 0) >= do.val:
                        continue
                    if k not in waits or waits[k][1] < do.val:
                        waits[k] = (do.sem, do.val, d)
                if o.slot_guard is not None:
                    s, v = o.slot_guard
                    k = id(s)
                    if ck.get(k, 0) < v and (k not in waits or waits[k][1] < v):
                        waits[k] = (s, v, None)
                wl = list(waits.items())
                keep = []
                for k, (s, v, d) in wl:
                    implied = False
                    for k2, (s2, v2, d2) in wl:
                        if k2 == k or d2 is None:
                            continue
                        vc2 = opvc[d2]
                        if vc2 is not None and vc2.get(k, 0) >= v:
                            implied = True
                            break
                    if not implied:
                        keep.append((s, v))
                for k, (s, v, d) in wl:
                    if ck.get(k, 0) < v:
                        ck[k] = v
                    if d is not None and opvc[d] is not None:
                        for kk, vv in opvc[d].items():
                            if ck.get(kk, 0) < vv:
                                ck[kk] = vv
                if o.sem is not None:
                    vc = dict(ck)
                    vc[id(o.sem)] = o.val
                    opvc[o.id] = vc
                    if not o.dma:
                        pass
                plan[o.eng].append((o, keep))
            self.stats = {e: (len(plan[e]), sum(len(k) for _, k in plan[e])) for e in ENGS}

            def run(eng_handle, lst, final_waits):
                for o, keep in lst:
                    for (s, v) in keep[1:]:
                        eng_handle.wait_ge(s, v)
                    ins = o.fn(eng_handle)
                    if keep:
                        s, v = keep[0]
                        if isinstance(ins, tuple):
                            ins[0]._wait_ge(s, v)
                        else:
                            ins._wait_ge(s, v)
                    if o.sem is not None:
                        last = ins[1] if isinstance(ins, tuple) else ins
                        last.then_inc(o.sem, 16 if o.dma else 1)
                for (s, v) in final_waits:
                    eng_handle.wait_ge(s, v)

            fw = [(o.sem, o.val) for o in final_ops]
            with nc.Block() as block:
                @block.tensor
                def _(e):
                    run(e, plan["pe"], [])

                @block.scalar
                def _(e):
                    run(e, plan["act"], [])

                @block.vector
                def _(e):
                    run(e, plan["dve"], [])

                @block.gpsimd
                def _(e):
                    run(e, plan["pool"], [])

                @block.sync
                def _(e):
                    run(e, plan["sp"], fw)

    def dma_count_total(self, e):
        return sum(1 for o in self.by_eng[e] if o.dma)


import contextlib
import numpy as np
from concourse.bass_utils import run_bass_kernel_spmd

F32 = mybir.dt.float32
BF16 = mybir.dt.bfloat16
AF = mybir.ActivationFunctionType
ALU = mybir.AluOpType
AX = mybir.AxisListType

D = 1024
SEQ = 8192
WT = 2560
NT = 20
NB = 5
NTB = 64
EPS = 1e-6
NEG = -30000.0


def build_program(stage=99):
    nc = bass.Bass("TRN2", target_bir_lowering=False)
    P = Prog()
    G = contextlib.ExitStack()
    uid = [0]
    bar_from = [0]

    def di(name, shape, dt=F32):
        return nc.dram_tensor(name, list(shape), dt, kind="ExternalInput").ap()

    def sb(st, name, shape, dt):
        uid[0] += 1
        return st.enter_context(nc.sbuf_tensor("%s_%d" % (name, uid[0]), list(shape), dt))

    def barrier():
        lasts = [P.by_eng[e][-1].id for e in ENGS if P.by_eng[e]]
        dmas = [o.id for o in P.ops[bar_from[0]:] if o.dma]
        bar_from[0] = len(P.ops)
        deps = sorted(set(lasts + dmas))
        for e in ENGS:
            o = P.op(e, lambda eng: eng.nop())
            o.deps = list(deps)
        P.last_w = {}
        P.readers = {}

    xw = di("xw", [WT, D]); xh = di("xh", [16, D]); xb = di("xb", [SEQ, D]); mem = di("mem", [256, D])
    mix_norm_g = di("mix_norm_g", [2, D]); xattn_norm_g = di("xattn_norm_g", [2, D]); ff_norm_g = di("ff_norm_g", [2, D])
    w_mem_q = di("w_mem_q", [2, D, D]); mem_q_g = di("mem_q_g", [2, 256]); w_mem_o = di("w_mem_o", [2, D, D])
    w_ff1 = di("w_ff1", [2, D, 4096]); w_ff2 = di("w_ff2", [2, 4096, D])
    mem_tok_norm_g = di("mem_tok_norm_g", [D]); w_mem_kv = di("w_mem_kv", [D, 2048]); mem_k_g = di("mem_k_g", [256])
    w_in_e = di("w_in_e", [1, D, 928]); pool_w = di("pool_w", [1, 4, 128, 128]); pool_scale = di("pool_scale", [1, 512])
    q_lora_g = di("q_lora_g", [1, 256]); w_uq = di("w_uq", [1, 256, 768]); kv_lora_g = di("kv_lora_g", [1, 128])
    w_ukv = di("w_ukv", [1, 128, 1024]); mla_q_g = di("mla_q_g", [1, 96]); mla_k_g = di("mla_k_g", [1, 96])
    w_out_e = di("w_out_e", [1, D, D]); w_qkv_o = di("w_qkv_o", [1, D, 3072])
    na_q_g = di("na_q_g", [1, 64]); na_k_g = di("na_k_g", [1, 64]); w_out_o = di("w_out_o", [1, D, D])
    cs_w = di("cs_w", [WT, 64]); cs_b = di("cs_b", [SEQ, 64]); invc = di("invc", [4, 2568])
    nab = di("nab", [16, 128, 25, 128])
    out = nc.dram_tensor("out", [WT, D], F32, kind="ExternalOutput").ap()
    kT_scr = nc.dram_tensor("kT_scr", [8, 96, SEQ], BF16, kind="Internal").ap()
    v_scr = nc.dram_tensor("v_scr", [8, 128, NTB, 64], BF16, kind="Internal").ap()
    qT_scr = nc.dram_tensor("qT_scr", [8, 96, WT], BF16, kind="Internal").ap()
    nq_scr = nc.dram_tensor("nq_scr", [8, 128, WT], BF16, kind="Internal").ap()
    nk_scr = nc.dram_tensor("nk_scr", [8, 128, WT], BF16, kind="Internal").ap()
    nv_scr = nc.dram_tensor("nv_scr", [8, 128, NT, 128], BF16, kind="Internal").ap()

    x = sb(G, "x", [128, NT, D], F32)
    ident = sb(G, "ident", [128, 128], BF16)
    identf = sb(G, "identf", [128, 128], F32)
    ones_b = sb(G, "ones_b", [128, 128], BF16)
    eps_t = sb(G, "eps", [128, 1], F32)
    memkT = sb(G, "memkT", [128, 4, 2, 256], BF16)
    memv = sb(G, "memv", [128, 2, 4, 256], BF16)
    SQ = [sb(G, "sq", [128, 1024], F32), sb(G, "sq", [128, 1024], F32)]
    SSQ = [sb(G, "ssq", [128, 1], F32) for _ in range(2)]
    RSTD = [sb(G, "rstd", [128, 1], F32) for _ in range(2)]
    HB = [sb(G, "hb", [128, D], BF16) for _ in range(2)]
    SS32 = [sb(G, "ss32", [128, 32], F32) for _ in range(2)]
    RS32 = [sb(G, "rs32", [128, 32], F32) for _ in range(2)]
    sq, ssq, rstd, hb, ss32, rs32 = SQ[0], SSQ[0], RSTD[0], HB[0], SS32[0], RS32[0]
    gn = sb(G, "gn", [128, D], F32)
    psum_all = G.enter_context(nc.psum_tensor("psum_all", [128, 4096], F32))
    banks = [psum_all[:, i * 512:(i + 1) * 512] for i in range(8)]
    B = ["bank%d" % i for i in range(8)]

    def bbf(i):
        return banks[i][:].bitcast(BF16)

    act, dve, pe, pool = P.act, P.dve, P.pe, P.pool

    def rstd_from(ss_ap, ss_res, dim, out_ap, out_res):
        n = ss_ap.shape[0]
        act(lambda e: e.activation(out=out_ap, in_=ss_ap, func=AF.Ln, scale=1.0 / dim, bias=eps_t[0:n, :]),
            reads=[ss_res, "eps"], writes=[out_res])
        act(lambda e: e.activation(out=out_ap, in_=out_ap, func=AF.Exp, scale=-0.5), reads=[out_res], writes=[out_res])

    def norm_hT(src_ap, src_res, hT_dst, hT_res, n=128, par=0, bank=0, ws=None):
        if ws is None:
            sq_, ssq_, rstd_, hb_ = SQ[par], SSQ[par], RSTD[par], HB[par]
            sfx = "" if par == 0 else "_p1"
            hsfx = sfx
        else:
            sq_, ssq_, rstd_, hb_, sfx, hsfx = ws
        act(lambda e: e.activation(out=sq_[0:n, 0:D], in_=src_ap, func=AF.Square, accum_out=ssq_[0:n, :]),
            reads=[src_res], writes=["sq" + sfx, "ssq" + sfx])
        rstd_from(ssq_[0:n, :], "ssq" + sfx, D, rstd_[0:n, :], "rstd" + sfx)
        dve(lambda e: e.scalar_tensor_tensor(out=hb_[0:n, :], in0=src_ap, scalar=rstd_[0:n, 0:1], in1=gn[0:n, :],
                                             op0=ALU.mult, op1=ALU.mult),
            reads=[src_res, "rstd" + sfx, "gn"], writes=["hb" + hsfx])
        pT = bbf(bank)
        for k in range(8):
            pe(lambda e, k=k: e.transpose(out=pT[:, k * n:(k + 1) * n], in_=hb_[0:n, k * 128:(k + 1) * 128],
                                          identity=ident[0:n, 0:n]),
               reads=["hb" + hsfx, "ident"], writes=[B[bank]])
        act(lambda e: e.copy(out=hT_dst, in_=pT[:, 0:8 * n].rearrange("p (k t) -> p k t", k=8)),
            reads=[B[bank]], writes=[hT_res])

    def load_w(dst, w_ap, res, eng="pool"):
        K = w_ap.shape[0]
        for k in range((K + 127) // 128):
            r = min(128, K - k * 128)
            P.dma(eng, lambda e, k=k, r=r: e.dma_start(out=dst[0:r, k, :], in_=w_ap[k * 128:k * 128 + r, :]),
                  writes=[(res, k)])

    def wres(res, nk):
        return [(res, k) for k in range(nk)]

    def load_bc(dst, vec_ap, res, eng="sp"):
        P.dma(eng, lambda e: e.dma_start(out=dst, in_=vec_ap.partition_broadcast(128)), writes=[res])

    def resid_add(t, pbank_lo, pbank_hi):
        for half, bk in ((0, pbank_lo), (1, pbank_hi)):
            dve(lambda e, half=half, bk=bk: e.tensor_tensor(out=x[:, t, half * 512:(half + 1) * 512],
                                                            in0=banks[bk][:], in1=x[:, t, half * 512:(half + 1) * 512],
                                                            op=ALU.add),
                reads=[B[bk], ("x", t)], writes=[("x", t)])

    def proj_resid(t, lhs_fn, lhs_res, nk, w_sb, w_res, k0=0):
        for half in range(2):
            for k in range(nk):
                pe(lambda e, half=half, k=k: e.matmul(banks[6 + half][:], lhsT=lhs_fn(k), rhs=w_sb[:, k0 + k, half * 512:(half + 1) * 512],
                                                       start=(k == 0), stop=(k == nk - 1)),
                   reads=lhs_res + [(w_res, k0 + k)], writes=[B[6 + half]])
        resid_add(t, 6, 7)

    pool(lambda e: e.memset(identf[:], 0.0), writes=["identf"])
    pool(lambda e: e.affine_select(out=identf[:], in_=identf[:], pattern=[[-1, 128]], compare_op=ALU.not_equal,
                                   fill=1.0, base=0, channel_multiplier=1), reads=["identf"], writes=["identf"])
    dve(lambda e: e.tensor_copy(out=ident[:], in_=identf[:]), reads=["identf"], writes=["ident"])
    dve(lambda e: e.memset(ones_b[:], 1.0), writes=["ones_b"])
    dve(lambda e: e.memset(eps_t[:], EPS), writes=["eps"])
    for t in range(NT):
        P.dma("sp", lambda e, t=t: e.dma_start(out=x[:, t, :], in_=xw[t * 128:(t + 1) * 128, :]), writes=[("x", t)])
    barrier()

    def phase_mem():
        with contextlib.ExitStack() as S:
            wkv = sb(S, "wkv", [128, 8, 2048], BF16)
            gk = sb(S, "gk", [128, 256], F32)
            memt = sb(S, "memt", [128, D], F32)
            hTm = sb(S, "hTm", [128, 8, 128], BF16)
            kn = sb(S, "kn", [128, D], F32)
            kbm = sb(S, "kbm", [128, D], BF16)
            load_w(wkv, w_mem_kv, "wkv")
            load_bc(gn[:], mem_tok_norm_g, "gn")
            load_bc(gk[:], mem_k_g, "gk")
            for m in range(2):
                P.dma("sp", lambda e, m=m: e.dma_start(out=memt[:], in_=mem[m * 128:(m + 1) * 128, :]), writes=["memt"])
                norm_hT(memt[:], "memt", hTm[:], "hTm")
                for n4 in range(4):
                    for k in range(8):
                        pe(lambda e, n4=n4, k=k: e.matmul(banks[1 + n4][:], lhsT=hTm[:, k, :], rhs=wkv[:, k, n4 * 512:(n4 + 1) * 512],
                                                          start=(k == 0), stop=(k == 7)),
                           reads=["hTm", ("wkv", k)], writes=[B[1 + n4]])
                for j in range(2):
                    act(lambda e, j=j: e.activation(out=sq[:, j * 512:(j + 1) * 512], in_=banks[1 + j][:], func=AF.Square),
                        reads=[B[1 + j]], writes=["sq"])
                dve(lambda e: e.tensor_reduce(out=ss32[:, 0:4], in_=sq[:, 0:D].rearrange("p (h d) -> p h d", h=4), axis=AX.X, op=ALU.add),
                    reads=["sq"], writes=["ss32"])
                rstd_from(ss32[:, 0:4], "ss32", 256, rs32[:, 0:4], "rs32")
                for j in range(2):
                    dve(lambda e, j=j: e.tensor_tensor(out=kn[:, j * 512:(j + 1) * 512].rearrange("p (h d) -> p h d", h=2),
                                                       in0=banks[1 + j][:].rearrange("p (h d) -> p h d", h=2),
                                                       in1=rs32[:, 2 * j:2 * j + 2].unsqueeze(2).to_broadcast([128, 2, 256]), op=ALU.mult),
                        reads=[B[1 + j], "rs32"], writes=["kn"])
                dve(lambda e: e.tensor_tensor(out=kbm[:].rearrange("p (h d) -> p h d", h=4), in0=kn[:].rearrange("p (h d) -> p h d", h=4),
                                              in1=gk[:].unsqueeze(1).to_broadcast([128, 4, 256]), op=ALU.mult),
                    reads=["kn", "gk"], writes=["kbm"])
                pT = bbf(0)
                for j in range(8):
                    pe(lambda e, j=j: e.transpose(out=pT[:, j * 128:(j + 1) * 128], in_=kbm[:, j * 128:(j + 1) * 128], identity=ident[:]),
                       reads=["kbm", "ident"], writes=[B[0]])
                act(lambda e, m=m: e.copy(out=memkT[:, :, :, m * 128:(m + 1) * 128],
                                          in_=pT[:, 0:1024].rearrange("p (h j t) -> p h j t", h=4, j=2)),
                    reads=[B[0]], writes=["memkT"])
                for j in range(2):
                    act(lambda e, m=m, j=j: e.copy(out=memv[:, m, 2 * j:2 * j + 2, :], in_=banks[3 + j][:].rearrange("p (h d) -> p h d", h=2)),
                        reads=[B[3 + j]], writes=["memv"])
            barrier()

    def rotary(src, src_res, cst, cs_res, dst, dst_res, t1, t2, shape3, sfx=""):
        H = shape3
        cc = cst[:, 0:32].unsqueeze(1).to_broadcast([128, H, 32])
        ns = cst[:, 32:48].unsqueeze(1).to_broadcast([128, H, 16])
        ps_ = cst[:, 48:64].unsqueeze(1).to_broadcast([128, H, 16])
        dve(lambda e: e.tensor_tensor(out=t1, in0=src, in1=cc, op=ALU.mult), reads=[src_res, cs_res], writes=["rot_t1" + sfx])
        dve(lambda e: e.tensor_tensor(out=t2[:, :, 0:16], in0=src[:, :, 16:32], in1=ns, op=ALU.mult), reads=[src_res, cs_res], writes=["rot_t2a" + sfx])
        dve(lambda e: e.tensor_tensor(out=t2[:, :, 16:32], in0=src[:, :, 0:16], in1=ps_, op=ALU.mult), reads=[src_res, cs_res], writes=["rot_t2b" + sfx])
        dve(lambda e: e.tensor_tensor(out=dst, in0=t1, in1=t2, op=ALU.add), reads=["rot_t1" + sfx, "rot_t2a" + sfx, "rot_t2b" + sfx], writes=[dst_res])

    def phase_A():
        NP = 4
        with contextlib.ExitStack() as S:
            w_kv = sb(S, "w_kv", [128, 8, 160], BF16)
            wukv = sb(S, "wukv", [128, 1, 1024], BF16)
            gkv = sb(S, "gkv", [128, 128], F32)
            gk = sb(S, "gk96", [128, 96], F32)
            many = lambda name, shape, dt: [sb(S, name, shape, dt) for _ in range(NP)]
            xt = [sb(S, "xt", [128, D], F32) for _ in range(2)] * 2
            csb_ = many("csb", [128, 64], F32)
            hTt_ = many("hTt", [128, 8, 128], BF16)
            ckn_ = many("ckn", [128, 128], BF16)
            ckT_ = many("ckT", [128, 128], BF16)
            kn_ = many("kn", [128, 8, 64], F32)
            kb2 = many("kb", [128, 8, 96], BF16)
            krw_ = many("krw", [128, 32], F32)
            krg_ = many("krg", [128, 1, 32], F32)
            krr_ = many("krr", [128, 1, 32], F32)
            t1_ = many("t1", [128, 1, 32], F32)
            t2_ = many("t2", [128, 1, 32], F32)
            vb_ = many("vb", [128, 8, 64], BF16)
            ssr_ = many("ssr", [128, 1], F32)
            sqa_ = many("sqa", [128, D], F32)
            ssqa_ = many("ssqa", [128, 1], F32)
            rstda_ = many("rstda", [128, 1], F32)
            hba_ = [sb(S, "hba", [128, D], BF16) for _ in range(2)] * 2
            ss8_ = many("ss8", [128, 8], F32)
            rs8_ = many("rs8", [128, 8], F32)
            kst_ = [sb(S, "kst", [96, 8, 512], BF16) for _ in range(2)]
            load_w(w_kv, w_in_e[0][:, 768:928], "w_kv")
            load_w(wukv, w_ukv[0], "wukv")
            load_bc(gn[:], mix_norm_g[0], "gn")
            load_bc(gkv[:], kv_lora_g[0], "gkv")
            load_bc(gk[:], mla_k_g[0], "gk")

            def tileA(t):
                p = t % NP
                x_ = "_%d" % p
                bX, bY = 2 * p, 2 * p + 1
                bK = (bX, bY)
                xtt, hTt, ckn, ckT, kn, kb_, krw, krg, krr, t1, t2, vb, ssr = (xt[p], hTt_[p], ckn_[p], ckT_[p], kn_[p], kb2[p], krw_[p], krg_[p],
                                                                               krr_[p], t1_[p], t2_[p], vb_[p], ssr_[p])
                sq_, ssq_, rstd_, ss32_, rs32_ = sqa_[p], ssqa_[p], rstda_[p], ss8_[p], rs8_[p]
                sfx = "_a%d" % p
                kst = kst_[(t // 4) % 2]
                kstr = "kst%d" % ((t // 4) % 2)
                xr = "xt_%d" % (t % 2)
                csb = csb_[p]
                P.dma("sp", lambda e: e.dma_start(out=xtt[:], in_=xb[t * 128:(t + 1) * 128, :]), writes=[xr])
                P.dma("sp", lambda e: e.dma_start(out=csb[:], in_=cs_b[t * 128:(t + 1) * 128, :]), writes=["csb" + x_])
                norm_hT(xtt[:], xr, hTt[:], "hTt" + x_, bank=bX, ws=(sq_, ssq_, rstd_, hba_[p], sfx, "_h%d" % (t % 2)))
                pC = banks[bY]
                for k in range(8):
                    pe(lambda e, k=k: e.matmul(pC[:, 0:160], lhsT=hTt[:, k, :], rhs=w_kv[:, k, :], start=(k == 0), stop=(k == 7)),
                       reads=["hTt" + x_, ("w_kv", k)], writes=[B[bY]])
                act(lambda e: e.activation(out=sq_[:, 0:128], in_=pC[:, 0:128], func=AF.Square, accum_out=ssq_[:]),
                    reads=[B[bY]], writes=["sq" + sfx, "ssq" + sfx])
                act(lambda e: e.copy(out=krw[:], in_=pC[:, 128:160]), reads=[B[bY]], writes=["krw" + x_])
                rstd_from(ssq_[:], "ssq" + sfx, 128, rstd_[:], "rstd" + sfx)
                dve(lambda e: e.scalar_tensor_tensor(out=ckn[:], in0=pC[:, 0:128], scalar=rstd_[:, 0:1], in1=gkv[:], op0=ALU.mult, op1=ALU.mult),
                    reads=[B[bY], "rstd" + sfx, "gkv"], writes=["ckn" + x_])
                pT2 = bbf(bX)
                pe(lambda e: e.transpose(out=pT2[:, 0:128], in_=ckn[:], identity=ident[:]), reads=["ckn" + x_, "ident"], writes=[B[bX]])
                act(lambda e: e.copy(out=ckT[:], in_=pT2[:, 0:128]), reads=[B[bX]], writes=["ckT" + x_])
                act(lambda e: e.activation(out=sq_[:, 512:544], in_=krw[:], func=AF.Square, accum_out=ssr[:]),
                    reads=["krw" + x_], writes=["sq2" + sfx, "ssr" + x_])
                dve(lambda e: e.tensor_tensor(out=krg[:, 0, :], in0=krw[:], in1=gk[:, 64:96], op=ALU.mult),
                    reads=["krw" + x_, "gk"], writes=["krg" + x_])
                rotary(krg[:], "krg" + x_, csb[:], "csb" + x_, krr[:], "krr" + x_, t1[:], t2[:], 1, sfx=x_)
                for j in range(2):
                    pe(lambda e, j=j: e.matmul(banks[bK[j]][:], lhsT=ckT[:], rhs=wukv[:, 0, j * 512:(j + 1) * 512], start=True, stop=True),
                       reads=["ckT" + x_, ("wukv", 0)], writes=[B[bK[j]]])
                for j in range(2):
                    act(lambda e, j=j: e.activation(out=sq_[:, j * 256:(j + 1) * 256].rearrange("p (h d) -> p h d", h=4),
                                                    in_=banks[bK[j]][:].rearrange("p (h d) -> p h d", h=4)[:, :, 0:64], func=AF.Square),
                        reads=[B[bK[j]]], writes=["sq" + sfx])
                dve(lambda e: e.tensor_reduce(out=ss32_[:, 0:8], in_=sq_[:, 0:512].rearrange("p (h d) -> p h d", h=8), axis=AX.X, op=ALU.add),
                    reads=["sq" + sfx], writes=["ss32" + sfx])
                dve(lambda e: e.tensor_scalar(out=ss32_[:, 0:8], in0=ss32_[:, 0:8], scalar1=ssr[:, 0:1], scalar2=None, op0=ALU.add),
                    reads=["ss32" + sfx, "ssr" + x_], writes=["ss32" + sfx])
                rstd_from(ss32_[:, 0:8], "ss32" + sfx, 96, rs32_[:, 0:8], "rs32" + sfx)
                for j in range(2):
                    dve(lambda e, j=j: e.tensor_tensor(out=kn[:, 4 * j:4 * j + 4, :], in0=banks[bK[j]][:].rearrange("p (h d) -> p h d", h=4)[:, :, 0:64],
                                                       in1=rs32_[:, 4 * j:4 * j + 4].unsqueeze(2).to_broadcast([128, 4, 64]), op=ALU.mult),
                        reads=[B[bK[j]], "rs32" + sfx], writes=["kn" + x_])
                for j in range(2):
                    act(lambda e, j=j: e.copy(out=vb[:, 4 * j:4 * j + 4, :], in_=banks[bK[j]][:].rearrange("p (h d) -> p h d", h=4)[:, :, 64:128]),
                        reads=[B[bK[j]]], writes=["vb" + x_])
                pool(lambda e: e.tensor_tensor(out=kb_[:, :, 0:64], in0=kn[:], in1=gk[:, 0:64].unsqueeze(1).to_broadcast([128, 8, 64]), op=ALU.mult),
                     reads=["kn" + x_, "gk"], writes=["kb_n" + x_])
                dve(lambda e: e.tensor_tensor(out=kb_[:, :, 64:96], in0=krr[:, 0, :].unsqueeze(1).to_broadcast([128, 8, 32]),
                                              in1=rs32_[:, 0:8].unsqueeze(2).to_broadcast([128, 8, 32]), op=ALU.mult),
                    reads=["krr" + x_, "rs32" + sfx], writes=["kb_r" + x_])
                P.dma("pool", lambda e: e.dma_start(out=v_scr[:, :, t, :].rearrange("h p d -> p h d"), in_=vb[:]), reads=["vb" + x_], writes=[("v_scr", t)])
                pT3 = bbf(bX)
                for h in range(8):
                    pe(lambda e, h=h: e.transpose(out=pT3[0:96, h * 128:(h + 1) * 128], in_=kb_[:, h, :], identity=ident[:]),
                       reads=["kb_n" + x_, "kb_r" + x_, "ident"], writes=[B[bX]])
                tt = t % 4
                act(lambda e: e.copy(out=kst[:, :, tt * 128:(tt + 1) * 128], in_=pT3[0:96, 0:1024].rearrange("p (h t) -> p h t", h=8)),
                    reads=[B[bX]], writes=[kstr])
                if tt == 3:
                    b4 = t // 4
                    P.dma("pool", lambda e: e.dma_start(out=kT_scr[:, :, b4 * 512:(b4 + 1) * 512].rearrange("h d t -> d h t"), in_=kst[:]),
                          reads=[kstr], writes=[("kT_scr", b4)])

            P.pipeline_deep(NTB, tileA, NP)
            barrier()

    def phase_B1():
        with contextlib.ExitStack() as S:
            w_q = sb(S, "w_q", [128, 8, 256], BF16)
            wuq = sb(S, "wuq", [128, 2, 768], BF16)
            gql = sb(S, "gql", [128, 256], F32)
            gq = sb(S, "gq96", [128, 96], F32)
            csw = sb(S, "csw", [128, NT, 64], F32)
            two = lambda name, shape, dt: [sb(S, name, shape, dt) for _ in range(2)]
            hTt_ = two("hTt", [128, 8, 128], BF16)
            cqn_ = two("cqn", [128, 256], BF16)
            cqT_ = two("cqT", [128, 2, 128], BF16)
            qn_ = two("qn", [128, 8, 96], F32)
            qr_ = two("qr", [128, 8, 32], F32)
            qb2 = two("qb", [128, 8, 96], BF16)
            t1_ = two("t1", [128, 8, 32], F32)
            t2_ = two("t2", [128, 8, 32], F32)
            qst_ = two("qst", [96, 8, 512], BF16)
            load_w(w_q, w_in_e[0][:, 512:768], "w_q")
            load_w(wuq, w_uq[0], "wuq")
            load_bc(gn[:], mix_norm_g[0], "gn")
            load_bc(gql[:], q_lora_g[0], "gql")
            load_bc(gq[:], mla_q_g[0], "gq")
            P.dma("sp", lambda e: e.dma_start(out=csw[:], in_=cs_w.rearrange("(c p) f -> p c f", p=128)), writes=["csw"])

            def tileB(t):
                p = t % 2
                x_ = "_%d" % p
                sfx = "" if p == 0 else "_p1"
                bT, bC, bQ = 4 * p, 4 * p + 1, (4 * p + 2, 4 * p + 3)
                hTt, cqn, cqT, qn, qr, qb_, t1, t2 = hTt_[p], cqn_[p], cqT_[p], qn_[p], qr_[p], qb2[p], t1_[p], t2_[p]
                sq_, ssq_, rstd_, ss32_, rs32_ = SQ[p], SSQ[p], RSTD[p], SS32[p], RS32[p]
                qst = qst_[(t // 4) % 2]
                qstr = "qst%d" % ((t // 4) % 2)
                norm_hT(x[:, t, :], ("x", t), hTt[:], "hTt" + x_, par=p, bank=bT)
                pC = banks[bC]
                for k in range(8):
                    pe(lambda e, k=k: e.matmul(pC[:, 0:256], lhsT=hTt[:, k, :], rhs=w_q[:, k, :], start=(k == 0), stop=(k == 7)),
                       reads=["hTt" + x_, ("w_q", k)], writes=[B[bC]])
                act(lambda e: e.activation(out=sq_[:, 0:256], in_=pC[:, 0:256], func=AF.Square, accum_out=ssq_[:]),
                    reads=[B[bC]], writes=["sq" + sfx, "ssq" + sfx])
                rstd_from(ssq_[:], "ssq" + sfx, 256, rstd_[:], "rstd" + sfx)
                dve(lambda e: e.scalar_tensor_tensor(out=cqn[:], in0=pC[:, 0:256], scalar=rstd_[:, 0:1], in1=gql[:], op0=ALU.mult, op1=ALU.mult),
                    reads=[B[bC], "rstd" + sfx, "gql"], writes=["cqn" + x_])
                pT2 = bbf(bT)
                for j in range(2):
                    pe(lambda e, j=j: e.transpose(out=pT2[:, j * 128:(j + 1) * 128], in_=cqn[:, j * 128:(j + 1) * 128], identity=ident[:]),
                       reads=["cqn" + x_, "ident"], writes=[B[bT]])
                act(lambda e: e.copy(out=cqT[:], in_=pT2[:, 0:256].rearrange("p (j t) -> p j t", j=2)), reads=[B[bT]], writes=["cqT" + x_])
                for (bk, c0, c1) in ((bQ[0], 0, 480), (bQ[1], 480, 768)):
                    for j in range(2):
                        pe(lambda e, bk=bk, c0=c0, c1=c1, j=j: e.matmul(banks[bk][:, 0:c1 - c0], lhsT=cqT[:, j, :], rhs=wuq[:, j, c0:c1],
                                                                        start=(j == 0), stop=(j == 1)),
                           reads=["cqT" + x_, ("wuq", j)], writes=[B[bk]])
                segs = ((bQ[0], 0, 5), (bQ[1], 5, 8))
                for (bk, h0, h1) in segs:
                    nh = h1 - h0
                    act(lambda e, bk=bk, h0=h0, nh=nh: e.activation(out=sq_[:, h0 * 96:(h0 + nh) * 96], in_=banks[bk][:, 0:nh * 96], func=AF.Square),
                        reads=[B[bk]], writes=["sq" + sfx])
                dve(lambda e: e.tensor_reduce(out=ss32_[:, 0:8], in_=sq_[:, 0:768].rearrange("p (h d) -> p h d", h=8), axis=AX.X, op=ALU.add),
                    reads=["sq" + sfx], writes=["ss32" + sfx])
                rstd_from(ss32_[:, 0:8], "ss32" + sfx, 96, rs32_[:, 0:8], "rs32" + sfx)
                for (bk, h0, h1) in segs:
                    nh = h1 - h0
                    dve(lambda e, bk=bk, h0=h0, nh=nh: e.tensor_tensor(out=qn[:, h0:h0 + nh, :], in0=banks[bk][:, 0:nh * 96].rearrange("p (h d) -> p h d", h=nh),
                                                                       in1=rs32_[:, h0:h0 + nh].unsqueeze(2).to_broadcast([128, nh, 96]), op=ALU.mult),
                        reads=[B[bk], "rs32" + sfx], writes=["qn" + x_])
                pool(lambda e: e.tensor_tensor(out=qb_[:, :, 0:64], in0=qn[:, :, 0:64], in1=gq[:, 0:64].unsqueeze(1).to_broadcast([128, 8, 64]), op=ALU.mult),
                     reads=["qn" + x_, "gq"], writes=["qb_n" + x_])
                dve(lambda e: e.tensor_tensor(out=qr[:], in0=qn[:, :, 64:96], in1=gq[:, 64:96].unsqueeze(1).to_broadcast([128, 8, 32]), op=ALU.mult),
                    reads=["qn" + x_, "gq"], writes=["qr" + x_])
                rotary(qr[:], "qr" + x_, csw[:, t, :], "csw", qb_[:, :, 64:96], "qb_r" + x_, t1[:], t2[:], 8, sfx=x_)
                pT3 = bbf(bT)
                for h in range(8):
                    pe(lambda e, h=h: e.transpose(out=pT3[0:96, h * 128:(h + 1) * 128], in_=qb_[:, h, :], identity=ident[:]),
                       reads=["qb_n" + x_, "qb_r" + x_, "ident"], writes=[B[bT]])
                tt = t % 4
                act(lambda e, tt=tt: e.copy(out=qst[:, :, tt * 128:(tt + 1) * 128], in_=pT3[0:96, 0:1024].rearrange("p (h t) -> p h t", h=8)),
                    reads=[B[bT]], writes=[qstr])
                if tt == 3:
                    b4 = t // 4
                    P.dma("pool", lambda e, b4=b4: e.dma_start(out=qT_scr[:, :, b4 * 512:(b4 + 1) * 512].rearrange("h d t -> d h t"), in_=qst[:]),
                          reads=[qstr], writes=[("qT_scr", b4)])

            P.pipeline(NT, tileB)
            barrier()

    def phase_B2():
        with contextlib.ExitStack() as S:
            w_p = sb(S, "w_p", [128, 8, 512], BF16)
            pw = sb(S, "pw", [128, 4, 128], BF16)
            psc = sb(S, "psc", [128, 4], F32)
            wout = sb(S, "wout", [128, 4, D], BF16)
            inv_t = sb(S, "inv_t", [128, 4, 512], F32)
            U = sb(S, "U", [128, 4, 528], F32)
            UH = sb(S, "UH", [128, 4, 16], F32)
            a2 = sb(S, "a2", [128, 3, 528], F32)
            a4 = sb(S, "a4", [128, 2, 528], F32)
            a8 = sb(S, "a8", [128, 1, 528], F32)
            Sm = sb(S, "Sm", [128, 4, 512], F32)
            Dt = sb(S, "Dt", [128, 4, 512], BF16)
            hTb = sb(S, "hTb", [128, 8, 512], BF16)
            hTh = sb(S, "hTh", [128, 8, 16], BF16)
            xht = sb(S, "xht", [16, D], F32)
            yT = sb(S, "yT", [128, 4, WT], BF16)
            load_w(w_p, w_in_e[0][:, 0:512], "w_p")
            for g in range(4):
                P.dma("pool", lambda e, g=g: e.dma_start(out=pw[:, g, :], in_=pool_w[0, g]), writes=[("pw", g)])
            load_w(wout, w_out_e[0][0:512, :], "wout")
            load_bc(gn[:], mix_norm_g[0], "gn")
            P.dma("sp", lambda e: e.dma_start(out=psc[:], in_=pool_scale[0].rearrange("(g d) -> d g", g=4), allow_slow_non_contiguous=True), writes=["psc"])
            P.dma("sp", lambda e: e.dma_start(out=xht[:], in_=xh[:, :]), writes=["xht"])
            dve(lambda e: e.memset(U[:], 0.0), writes=["U"])
            norm_hT(xht[:], "xht", hTh[:], "hTh", n=16)
            for g in range(4):
                for k in range(8):
                    pe(lambda e, g=g, k=k: e.matmul(banks[1][:, g * 16:(g + 1) * 16], lhsT=w_p[:, k, g * 128:(g + 1) * 128], rhs=hTh[:, k, :],
                                                    start=(k == 0), stop=(k == 7)),
                       reads=["hTh", ("w_p", k)], writes=[B[1]])
            act(lambda e: e.copy(out=UH[:], in_=banks[1][:, 0:64].rearrange("p (g t) -> p g t", g=4)), reads=[B[1]], writes=["UH"])
            dve(lambda e: e.tensor_copy(out=U[:, :, 520:528], in_=UH[:, :, 0:8]), reads=["UH", "U"], writes=["U"])

            def pool_step(b, ncols, tok0, c0):
                P.dma("sp", lambda e: e.dma_start(out=inv_t[:, :, 0:(512 if b < NB else 8)],
                                                  in_=invc[:, b * 512:b * 512 + (512 if b < NB else 8)].partition_broadcast(128)),
                      writes=["inv_t"])
                pool(lambda e: e.tensor_tensor(out=a2[:, :, 0:527], in0=U[:, 1:4, 0:527], in1=U[:, 1:4, 1:528], op=ALU.add), reads=["U"], writes=["a2"])
                pool(lambda e: e.tensor_tensor(out=a4[:, :, 0:525], in0=a2[:, 1:3, 0:525], in1=a2[:, 1:3, 2:527], op=ALU.add), reads=["a2"], writes=["a4"])
                pool(lambda e: e.tensor_tensor(out=a8[:, :, 0:521], in0=a4[:, 1:2, 0:521], in1=a4[:, 1:2, 4:525], op=ALU.add), reads=["a4"], writes=["a8"])
                dve(lambda e: e.tensor_tensor(out=Sm[:, 0, :], in0=U[:, 0, 7:519], in1=U[:, 0, 8:520], op=ALU.add), reads=["U"], writes=["Sm0"])
                dve(lambda e: e.tensor_tensor(out=Sm[:, 1, :], in0=a2[:, 0, 6:518], in1=a2[:, 0, 8:520], op=ALU.add), reads=["a2"], writes=["Sm1"])
                dve(lambda e: e.tensor_tensor(out=Sm[:, 2, :], in0=a4[:, 0, 4:516], in1=a4[:, 0, 8:520], op=ALU.add), reads=["a4"], writes=["Sm2"])
                dve(lambda e: e.tensor_tensor(out=Sm[:, 3, :], in0=a8[:, 0, 0:512], in1=a8[:, 0, 8:520], op=ALU.add), reads=["a8"], writes=["Sm3"])
                dve(lambda e: e.tensor_tensor(out=Sm[:], in0=Sm[:], in1=inv_t[:], op=ALU.mult), reads=["Sm0", "Sm1", "Sm2", "Sm3", "inv_t"], writes=["Sm"])
                dve(lambda e: e.tensor_tensor(out=Dt[:], in0=Sm[:], in1=U[:, :, 8:520], op=ALU.subtract), reads=["Sm", "U"], writes=["Dt"])
                for g in range(4):
                    pe(lambda e, g=g: e.matmul(banks[2 + (g % 2)][:], lhsT=pw[:, g, :], rhs=Dt[:, g, :], start=True, stop=True),
                       reads=["Dt", ("pw", g)], writes=[B[2 + (g % 2)]])
                    act(lambda e, g=g: e.activation(out=yT[:, g, tok0:tok0 + ncols], in_=banks[2 + (g % 2)][:, c0:c0 + ncols], func=AF.Copy, scale=psc[:, g:g + 1]),
                        reads=[B[2 + (g % 2)], "psc"], writes=[("yT", g)])

            for b in range(NB):
                for tt in range(4):
                    t = 4 * b + tt
                    norm_hT(x[:, t, :], ("x", t), hTb[:, :, tt * 128:(tt + 1) * 128], "hTb")
                dve(lambda e: e.tensor_copy(out=U[:, :, 0:16], in_=U[:, :, 512:528]), reads=["U"], writes=["U"])
                for g in range(4):
                    for k in range(8):
                        pe(lambda e, g=g, k=k: e.matmul(banks[4 + (g % 2)][:], lhsT=w_p[:, k, g * 128:(g + 1) * 128], rhs=hTb[:, k, :],
                                                        start=(k == 0), stop=(k == 7)),
                           reads=["hTb", ("w_p", k)], writes=[B[4 + (g % 2)]])
                    act(lambda e, g=g: e.copy(out=U[:, g, 16:528], in_=banks[4 + (g % 2)][:]), reads=[B[4 + (g % 2)], "U"], writes=["U"])
                if b == 0:
                    pool_step(0, 504, 0, 8)
                else:
                    pool_step(b, 512, 512 * b - 8, 0)
            dve(lambda e: e.tensor_copy(out=U[:, :, 0:16], in_=U[:, :, 512:528]), reads=["U"], writes=["U"])
            dve(lambda e: e.tensor_copy(out=U[:, :, 16:24], in_=UH[:, :, 8:16]), reads=["UH", "U"], writes=["U"])
            pool_step(NB, 8, WT - 8, 0)
            for t in range(NT):
                proj_resid(t, lambda k, t=t: yT[:, k, t * 128:(t + 1) * 128], [("yT", g) for g in range(4)], 4, wout, "wout")
            barrier()

    def phase_C():
        with contextlib.ExitStack() as S:
            oT = sb(S, "oT", [128, 4, WT], BF16)
            with contextlib.ExitStack() as S2:
                kh = [sb(S2, "kh", [96, SEQ], BF16) for _ in range(2)]
                va = [sb(S2, "va", [128, NTB, 128], BF16) for _ in range(2)]
                qh = [sb(S2, "qh", [96, WT], BF16) for _ in range(2)]
                E2 = [sb(S2, "E", [128, 1024], BF16) for _ in range(2)]
                rec = sb(S2, "rec", [128, 512], F32)
                for p in range(2):
                    dve(lambda e, p=p: e.memset(va[p][:, :, (1 - p) * 64:(1 - p) * 64 + 64], 1.0), writes=[("va1", p)])
                scale = 96 ** -0.5
                NPAIR = NTB // 2
                for h in range(8):
                    p = h % 2
                    lo, hi = p * 64, p * 64 + 64
                    dlo, dhi = (1 - p) * 64, (1 - p) * 64 + 64
                    P.dma("sp", lambda e, h=h, p=p: e.dma_start(out=kh[p][:], in_=kT_scr[h]), writes=[("kh", p)])
                    P.dma("sp", lambda e, h=h, p=p: e.dma_start(out=va[p][:, :, p * 64:p * 64 + 64], in_=v_scr[h]), writes=[("va", p)])
                    P.dma("sp", lambda e, h=h, p=p: e.dma_start(out=qh[p][:], in_=qT_scr[h]), writes=[("qh", p)])
                    for qb in range(NB):
                        ob = 4 + (qb % 2)

                        def S_pair(cc, qb=qb, p=p):
                            for s_ in range(2):
                                c = 2 * cc + s_
                                bk = 2 * (cc % 2) + s_
                                pe(lambda e, c=c, bk=bk: e.matmul(banks[bk], lhsT=kh[p][:, c * 128:(c + 1) * 128], rhs=qh[p][:, qb * 512:(qb + 1) * 512],
                                                                  start=True, stop=True),
                                   reads=[("kh", p), ("qh", p)], writes=[B[bk]])
                        S_pair(0)
                        S_pair(1)
                        for cc in range(NPAIR):
                            pp = cc % 2
                            act(lambda e, pp=pp: e.activation(out=E2[pp][:], in_=psum_all[:, 2 * pp * 512:(2 * pp + 2) * 512], func=AF.Exp, scale=scale),
                                reads=[B[2 * pp], B[2 * pp + 1]], writes=[("E", pp)])
                            for s_ in range(2):
                                c = 2 * cc + s_
                                pe(lambda e, c=c, p=p, ob=ob, pp=pp, s_=s_: e.matmul(banks[ob], lhsT=va[p][:, c, :], rhs=E2[pp][:, s_ * 512:(s_ + 1) * 512],
                                                                                   start=(c == 0), stop=(c == NTB - 1)),
                                   reads=[("va", p), ("va1", p), ("E", pp)], writes=[B[ob]])
                            if cc + 2 < NPAIR:
                                S_pair(cc + 2)
                        dve(lambda e, ob=ob, lo=lo, hi=hi, dlo=dlo, dhi=dhi: e.tensor_copy(out=rec[lo:hi, :], in_=banks[ob][dlo:dhi, :]), reads=[B[ob]], writes=["rec"])
                        dve(lambda e, lo=lo, hi=hi: e.reciprocal(out=rec[lo:hi, :], in_=rec[lo:hi, :]), reads=["rec"], writes=["rec"])
                        dve(lambda e, ob=ob, h=h, qb=qb, lo=lo, hi=hi: e.tensor_tensor(out=oT[lo:hi, h // 2, qb * 512:(qb + 1) * 512], in0=banks[ob][lo:hi, :],
                                                                         in1=rec[lo:hi, :], op=ALU.mult),
                            reads=[B[ob], "rec"], writes=[("oT", h // 2)])
                barrier()
            with contextlib.ExitStack() as S2:
                wout = sb(S2, "wout2", [128, 4, D], BF16)
                load_w(wout, w_out_e[0][512:1024, :], "wout2")
                for t in range(NT):
                    proj_resid(t, lambda k, t=t: oT[:, k, t * 128:(t + 1) * 128], [("oT", g) for g in range(4)], 4, wout, "wout2")
                barrier()

    def phase_xattn(L):
        with contextlib.ExitStack() as S:
            wq = sb(S, "wq", [128, 8, D], BF16)
            wo = sb(S, "wo", [128, 8, D], BF16)
            gq = sb(S, "gq256", [128, 256], F32)
            hTt = sb(S, "hTt", [128, 8, 128], BF16)
            qn = sb(S, "qn", [128, D], F32)
            qb_ = sb(S, "qb", [128, D], BF16)
            qT2 = [sb(S, "qT", [128, 8, 512], BF16) for _ in range(2)]
            E = [sb(S, "E", [128, 512], BF16) for _ in range(2)]
            rec = sb(S, "rec", [128, 512], F32)
            oTb = sb(S, "oTb", [128, 8, 512], BF16)
            load_w(wq, w_mem_q[L], "wq")
            load_w(wo, w_mem_o[L], "wo")
            load_bc(gn[:], xattn_norm_g[L], "gn")
            load_bc(gq[:], mem_q_g[L], "gq")
            scale = 256 ** -0.5

            def block(b):
                qT = qT2[b % 2]
                qTr = "qT%d" % (b % 2)
                for tt in range(4):
                    t = 4 * b + tt
                    norm_hT(x[:, t, :], ("x", t), hTt[:], "hTt")
                    for half in range(2):
                        for k in range(8):
                            pe(lambda e, half=half, k=k: e.matmul(banks[1 + half][:], lhsT=hTt[:, k, :], rhs=wq[:, k, half * 512:(half + 1) * 512],
                                                                   start=(k == 0), stop=(k == 7)),
                               reads=["hTt", ("wq", k)], writes=[B[1 + half]])
                    for hh in range(4):
                        act(lambda e, hh=hh: e.activation(out=sq[:, hh * 256:(hh + 1) * 256], in_=banks[1 + hh // 2][:, (hh % 2) * 256:(hh % 2 + 1) * 256],
                                                          func=AF.Square, accum_out=ss32[:, hh:hh + 1]),
                            reads=[B[1 + hh // 2]], writes=["sq", "ss32"])
                    rstd_from(ss32[:, 0:4], "ss32", 256, rs32[:, 0:4], "rs32")
                    for hh in range(4):
                        dve(lambda e, hh=hh: e.scalar_tensor_tensor(out=qb_[:, hh * 256:(hh + 1) * 256], in0=banks[1 + hh // 2][:, (hh % 2) * 256:(hh % 2 + 1) * 256],
                                                                    scalar=rs32[:, hh:hh + 1], in1=gq[:], op0=ALU.mult, op1=ALU.mult),
                            reads=[B[1 + hh // 2], "rs32", "gq"], writes=["qb"])
                    pT = bbf(0)
                    for j in range(8):
                        pe(lambda e, j=j: e.transpose(out=pT[:, j * 128:(j + 1) * 128], in_=qb_[:, j * 128:(j + 1) * 128], identity=ident[:]),
                           reads=["qb", "ident"], writes=[B[0]])
                    act(lambda e, tt=tt, qT=qT: e.copy(out=qT[:, :, tt * 128:(tt + 1) * 128], in_=pT[:, 0:1024].rearrange("p (j t) -> p j t", j=8)),
                        reads=[B[0]], writes=[qTr])
                P.mark()
                for h in range(4):
                    for m in range(2):
                        for j in range(2):
                            pe(lambda e, h=h, m=m, j=j, qT=qT: e.matmul(banks[3 + m][:], lhsT=memkT[:, h, j, m * 128:(m + 1) * 128], rhs=qT[:, 2 * h + j, :],
                                                                         start=(j == 0), stop=(j == 1)),
                               reads=[qTr, "memkT"], writes=[B[3 + m]])
                        act(lambda e, m=m: e.activation(out=E[m][:], in_=banks[3 + m][:], func=AF.Exp, scale=scale), reads=[B[3 + m]], writes=[("E", m)])
                    for m in range(2):
                        pe(lambda e, m=m: e.matmul(banks[5][:], lhsT=ones_b[:], rhs=E[m][:], start=(m == 0), stop=(m == 1)),
                           reads=["ones_b", ("E", m)], writes=[B[5]])
                    act(lambda e: e.activation(out=rec[:], in_=banks[5][:], func=AF.Ln), reads=[B[5]], writes=["rec"])
                    act(lambda e: e.activation(out=rec[:], in_=rec[:], func=AF.Exp, scale=-1.0), reads=["rec"], writes=["rec"])
                    for dv in range(2):
                        for m in range(2):
                            pe(lambda e, h=h, m=m, dv=dv: e.matmul(banks[6 + dv][:], lhsT=memv[:, m, h, dv * 128:(dv + 1) * 128], rhs=E[m][:],
                                                                    start=(m == 0), stop=(m == 1)),
                               reads=["memv", ("E", m)], writes=[B[6 + dv]])
                        dve(lambda e, h=h, dv=dv: e.tensor_tensor(out=oTb[:, 2 * h + dv, :], in0=banks[6 + dv][:], in1=rec[:], op=ALU.mult),
                            reads=[B[6 + dv], "rec"], writes=["oTb"])
                for tt in range(4):
                    t = 4 * b + tt
                    proj_resid(t, lambda k, tt=tt: oTb[:, k, tt * 128:(tt + 1) * 128], ["oTb"], 8, wo, "wo")

            P.pipeline(NB, block)
            barrier()

    def phase_mlp(L, last=False):
        NPASS = 8
        with contextlib.ExitStack() as S:
            hT = sb(S, "hTall", [128, 8, WT], BF16)
            w1 = [sb(S, "w1", [128, 8, 512], BF16) for _ in range(2)]
            w2 = [sb(S, "w2", [128, 4, D], BF16) for _ in range(2)]
            aT = [sb(S, "aT", [128, 512], BF16) for _ in range(4)]
            s2 = [sb(S, "s2", [128, 512], F32) for _ in range(2)]
            load_bc(gn[:], ff_norm_g[L], "gn")

            def load_pass(ps_):
                b = ps_ % 2
                load_w(w1[b], w_ff1[L][:, ps_ * 512:(ps_ + 1) * 512], ("w1", b))
                load_w(w2[b], w_ff2[L][ps_ * 512:(ps_ + 1) * 512, :], ("w2", b))
            load_pass(0)

            def norm_block(b):
                P.capture()
                for tt in range(4):
                    t = 4 * b + tt
                    norm_hT(x[:, t, :], ("x", t), hT[:, :, t * 128:(t + 1) * 128], ("hT", t // 4), par=t % 2, bank=(0, 3)[t % 2])
                return P.end_capture()

            P.replay(norm_block(0))
            for ps_ in range(NPASS):
                pb = ps_ % 2
                if ps_ + 1 < NPASS:
                    load_pass(ps_ + 1)
                for b in range(NB):
                  if ps_ == 0:
                    P.capture()
                  if True:
                    for f in range(4):
                        zb = 1 + (f % 2)
                        for k in range(8):
                            pe(lambda e, f=f, k=k, zb=zb, pb=pb, b=b: e.matmul(banks[zb][:], lhsT=w1[pb][:, k, f * 128:(f + 1) * 128],
                                                                                 rhs=hT[:, k, b * 512:(b + 1) * 512], start=(k == 0), stop=(k == 7)),
                               reads=[("hT", b), (("w1", pb), k)], writes=[B[zb]])
                        act(lambda e, f=f, zb=zb: e.activation(out=s2[f % 2][:], in_=banks[zb][:], func=AF.Square), reads=[B[zb]], writes=[("s2", f % 2)])
                        dve(lambda e, f=f, zb=zb: e.scalar_tensor_tensor(out=aT[f][:], in0=banks[zb][:], scalar=0.0, in1=s2[f % 2][:],
                                                                          op0=ALU.is_gt, op1=ALU.mult),
                            reads=[B[zb], ("s2", f % 2)], writes=[("aT", f)])
                    for tt in range(4):
                        t = 4 * b + tt
                        proj_resid(t, lambda k, tt=tt: aT[k][:, tt * 128:(tt + 1) * 128], [("aT", f) for f in range(4)], 4, w2[pb], ("w2", pb))
                  if ps_ == 0:
                    Cb = P.end_capture()
                    Nb = norm_block(b + 1) if b + 1 < NB else []
                    P.replay(P.zipmerge(Cb, Nb))
            if last:
                for t in range(NT):
                    fins.append(P.dma("sp", lambda e, t=t: e.dma_start(out=out[t * 128:(t + 1) * 128, :], in_=x[:, t, :]), reads=[("x", t)]))
            barrier()

    def phase_G1():
        with contextlib.ExitStack() as S:
            wqkv = sb(S, "wqkv", [128, 8, 3072], BF16)
            gqk = sb(S, "gqk", [128, 2, 64], F32)
            two = lambda name, shape, dt: [sb(S, name, shape, dt) for _ in range(2)]
            hTt_ = two("hTt", [128, 8, 128], BF16)
            qkn_ = two("qkn", [128, 1024], F32)
            qkb_ = two("qkb", [128, 1024], BF16)
            qkst_ = two("qkst", [128, 8, 128], BF16)
            vst_ = two("vst", [128, 4, 128], BF16)
            load_w(wqkv, w_qkv_o[0], "wqkv")
            load_bc(gn[:], mix_norm_g[1], "gn")
            load_bc(gqk[:, 0, :], na_q_g[0], "gqk0")
            load_bc(gqk[:, 1, :], na_k_g[0], "gqk1")

            def unit(n):
                t, u = n // 2, n % 2
                x_ = "_%d" % u
                sfx = "" if u == 0 else "_p1"
                bT, bQ, bV = 4 * u, (4 * u + 1, 4 * u + 2), 4 * u + 3
                hTt = hTt_[t % 2]
                hr = "hTt_%d" % (t % 2)
                qkn, qkb, qkst, vst = qkn_[u], qkb_[u], qkst_[u], vst_[u]
                sq_, ss32_, rs32_ = SQ[u], SS32[u], RS32[u]
                if u == 0:
                    norm_hT(x[:, t, :], ("x", t), hTt[:], hr, par=0, bank=bT)
                for n2 in range(2):
                    for k in range(8):
                        pe(lambda e, n2=n2, k=k: e.matmul(banks[bQ[n2]][:], lhsT=hTt[:, k, :], rhs=wqkv[:, k, u * 1024 + n2 * 512:u * 1024 + (n2 + 1) * 512],
                                                          start=(k == 0), stop=(k == 7)),
                           reads=[hr, ("wqkv", k)], writes=[B[bQ[n2]]])
                for n2 in range(2):
                    act(lambda e, n2=n2: e.activation(out=sq_[:, n2 * 512:(n2 + 1) * 512], in_=banks[bQ[n2]][:], func=AF.Square),
                        reads=[B[bQ[n2]]], writes=["sq" + sfx])
                dve(lambda e: e.tensor_reduce(out=ss32_[:, 0:16], in_=sq_[:, 0:1024].rearrange("p (h d) -> p h d", h=16), axis=AX.X, op=ALU.add),
                    reads=["sq" + sfx], writes=["ss32" + sfx])
                rstd_from(ss32_[:, 0:16], "ss32" + sfx, 64, rs32_[:, 0:16], "rs32" + sfx)
                for n2 in range(2):
                    dve(lambda e, n2=n2: e.tensor_tensor(out=qkn[:, n2 * 512:(n2 + 1) * 512].rearrange("p (h d) -> p h d", h=8),
                                                         in0=banks[bQ[n2]][:].rearrange("p (h d) -> p h d", h=8),
                                                         in1=rs32_[:, 8 * n2:8 * n2 + 8].unsqueeze(2).to_broadcast([128, 8, 64]), op=ALU.mult),
                        reads=[B[bQ[n2]], "rs32" + sfx], writes=["qkn" + x_])
                pool(lambda e: e.tensor_tensor(out=qkb[:].rearrange("p (h d) -> p h d", h=16), in0=qkn[:].rearrange("p (h d) -> p h d", h=16),
                                               in1=gqk[:, u, :].unsqueeze(1).to_broadcast([128, 16, 64]), op=ALU.mult),
                     reads=["qkn" + x_, "gqk0", "gqk1"], writes=["qkb" + x_])
                for k in range(8):
                    pe(lambda e, k=k: e.matmul(banks[bV][:], lhsT=hTt[:, k, :], rhs=wqkv[:, k, 2048 + u * 512:2048 + (u + 1) * 512],
                                               start=(k == 0), stop=(k == 7)),
                       reads=[hr, ("wqkv", k)], writes=[B[bV]])
                act(lambda e: e.copy(out=vst[:], in_=banks[bV][:].rearrange("p (h d) -> p h d", h=4)), reads=[B[bV]], writes=["vst" + x_])
                P.dma("pool", lambda e: e.dma_start(out=nv_scr[4 * u:4 * u + 4, :, t, :].rearrange("h p d -> p h d"), in_=vst[:]),
                      reads=["vst" + x_], writes=[("nv_scr", n)])
                pT = bbf(bT)
                for j in range(8):
                    pe(lambda e, j=j: e.transpose(out=pT[:, j * 128:(j + 1) * 128], in_=qkb[:, j * 128:(j + 1) * 128], identity=ident[:]),
                       reads=["qkb" + x_, "ident"], writes=[B[bT]])
                act(lambda e: e.copy(out=qkst[:], in_=pT[:, 0:1024].rearrange("p (j t) -> p j t", j=8)), reads=[B[bT]], writes=["qkst" + x_])
                dst = nq_scr if u == 0 else nk_scr
                P.dma("pool", lambda e: e.dma_start(out=dst[:, :, t * 128:(t + 1) * 128].rearrange("h p t -> p h t"), in_=qkst[:]),
                      reads=["qkst" + x_], writes=[("nqk_scr", n)])

            P.pipeline(2 * NT, unit)
            barrier()

    def phase_G2():
        with contextlib.ExitStack() as S:
            cT = sb(S, "cT", [128, 8, WT], BF16)
            with contextlib.ExitStack() as S2:
                qT = sb(S2, "nqT", [128, WT], BF16)
                kT = sb(S2, "nkT", [128, WT], BF16)
                va = [sb(S2, "nva", [128, NT, 128], BF16) for _ in range(2)]
                bias32 = sb(S2, "nbias", [128, 13, 128], F32)
                bh = [sb(S2, "nbh", [128, 25, 128], BF16) for _ in range(2)]
                bl = [sb(S2, "nbl", [128, 25, 128], BF16) for _ in range(2)]
                E = [sb(S2, "nE", [128, 512], BF16) for _ in range(3)]
                bg = [SQ[1][:, 0:640].rearrange("p (a b) -> p a b", a=5), gn[:, 0:640].rearrange("p (a b) -> p a b", a=5)]
                St = [SQ[0][:, 0:512], SQ[0][:, 512:1024]]
                rec = [sb(S2, "rec", [128, 512], F32)] * 2
                dve(lambda e: e.memset(va[0][:, :, 64:128], 1.0), writes=[("va1", 0)])
                dve(lambda e: e.memset(va[1][:, :, 0:64], 1.0), writes=[("va1", 1)])
                scale = 64 ** -0.5
                inv_scale = 8.0

                def row_cls(r):
                    return {0: 1, 2: 2, 36: 3, 38: 4}.get(r, 0)

                def prep_bias(h):
                    p = h % 2
                    P.dma("sp", lambda e, h=h, p=p: e.dma_start(out=bg[p], in_=nab[h][:, 0:5, :]), writes=[("bg", p)])
                    for (v0, v1) in ((0, 13), (13, 25)):
                        nv = v1 - v0
                        P.dma("sp", lambda e, h=h, v0=v0, v1=v1, nv=nv: e.dma_start(out=bias32[:, 0:nv, :], in_=nab[h][:, v0:v1, :]), writes=["bias32"])
                        dve(lambda e, p=p, v0=v0, v1=v1, nv=nv: e.tensor_scalar(out=bh[p][:, v0:v1, :], in0=bias32[:, 0:nv, :], scalar1=inv_scale, scalar2=None, op0=ALU.mult),
                            reads=["bias32"], writes=[("bh", p)])
                        dve(lambda e, p=p, v0=v0, v1=v1, nv=nv: e.scalar_tensor_tensor(out=bl[p][:, v0:v1, :].rearrange("p a b -> p (a b)"),
                                                                                      in0=bias32[:, 0:nv, :].rearrange("p a b -> p (a b)"), scalar=inv_scale,
                                                                                      in1=bh[p][:, v0:v1, :].rearrange("p a b -> p (a b)"), op0=ALU.mult, op1=ALU.subtract),
                            reads=["bias32", ("bh", p)], writes=[("bl", p)])

                prep_bias(0)
                for hp in range(8):
                    P.dma("sp", lambda e, hp=hp: e.dma_start(out=qT[:], in_=nq_scr[hp]), writes=["nqT"])
                    P.dma("sp", lambda e, hp=hp: e.dma_start(out=kT[:], in_=nk_scr[hp]), writes=["nkT"])
                    P.dma("sp", lambda e, hp=hp: e.dma_start(out=va[0][:, :, 0:64], in_=nv_scr[hp][:, :, 0:64]), writes=[("va", 0)])
                    P.dma("sp", lambda e, hp=hp: e.dma_start(out=va[1][:, :, 64:128], in_=nv_scr[hp][:, :, 64:128]), writes=[("va", 1)])
                    for p in range(2):
                        h = 2 * hp + p
                        lo, hi = p * 64, p * 64 + 64
                        dlo, dhi = (1 - p) * 64, (1 - p) * 64 + 64
                        items = [(blk, j) for blk in range(NB) for j in range(5)]

                        def S_stage(i, p=p, lo=lo, hi=hi):
                            blk, j = items[i]
                            sbk = i % 3
                            groups = []
                            for rp in range(4):
                                c = row_cls(8 * blk + 2 * rp)
                                if groups and groups[-1][0] == c:
                                    groups[-1][2] += 1
                                else:
                                    groups.append([c, rp, 1])
                            first = True
                            use_dve = (len(groups) == 1 and i % 2 == 1)
                            for src in (() if use_dve else (bh, bl)):
                                for (c, rp0, n) in groups:
                                    pe(lambda e, src=src, c=c, rp0=rp0, n=n, j=j, sbk=sbk, first=first: e.matmul(
                                            banks[sbk][:, rp0 * 128:(rp0 + n) * 128], lhsT=ident[:],
                                            rhs=src[p][:, c * 5 + j, :].unsqueeze(1).to_broadcast([128, n, 128]),
                                            start=first, stop=False, skip_group_check=True),
                                       reads=[("bh", p), ("bl", p), "ident"], writes=[B[sbk]])
                                    first = False
                            for rp in range(4):
                                r = 8 * blk + 2 * rp
                                tb = min(max(r - 4, 0), 30)
                                kt0 = (tb + 2 * j) * 64
                                pe(lambda e, kt0=kt0, r=r, sbk=sbk, rp=rp: e.matmul(banks[sbk][:, rp * 128:(rp + 1) * 128], lhsT=kT[lo:hi, kt0:kt0 + 128],
                                                                                   rhs=qT[lo:hi, r * 64:r * 64 + 128], start=(use_dve and rp == 0), stop=(rp == 3), skip_group_check=True),
                                   reads=["nkT", "nqT"], writes=[B[sbk]])

                        def mid_stage(i, p=p):
                            sbk = i % 3
                            blk, j = items[i]
                            cls = set(row_cls(8 * blk + 2 * rp) for rp in range(4))
                            if len(cls) == 1 and i % 2 == 1:
                                st = St[(i // 2) % 2]
                                sr = ("St", (i // 2) % 2)
                                dve(lambda e, sbk=sbk, st=st, j=j: e.scalar_tensor_tensor(out=st.rearrange("p (a b) -> p a b", a=4),
                                                                                        in0=banks[sbk][:].rearrange("p (a b) -> p a b", a=4), scalar=scale,
                                                                                        in1=bg[p][:, j, :].unsqueeze(1).to_broadcast([128, 4, 128]), op0=ALU.mult, op1=ALU.add),
                                    reads=[B[sbk], ("bg", p)], writes=[sr])
                                act(lambda e, i=i, st=st: e.activation(out=E[i % 3][:], in_=st, func=AF.Exp), reads=[sr], writes=[("nE", i % 3)])
                            else:
                                act(lambda e, i=i, sbk=sbk: e.activation(out=E[i % 3][:], in_=banks[sbk][:], func=AF.Exp, scale=scale),
                                    reads=[B[sbk]], writes=[("nE", i % 3)])

                        def PV_stage(i, p=p, lo=lo, hi=hi, dlo=dlo, dhi=dhi, hp=hp):
                            blk, j = items[i]
                            ob = 3 + (blk % 2)
                            for rp in range(4):
                                r = 8 * blk + 2 * rp
                                tb = min(max(r - 4, 0), 30)
                                vt = (tb + 2 * j) // 2
                                pe(lambda e, vt=vt, i=i, ob=ob, rp=rp, j=j: e.matmul(banks[ob][:, rp * 128:(rp + 1) * 128], lhsT=va[p][:, vt, :],
                                                                                    rhs=E[i % 3][:, rp * 128:(rp + 1) * 128],
                                                                                    start=(j == 0 and rp == 0), stop=(j == 4), skip_group_check=True),
                                   reads=[("va", p), ("va1", p), ("nE", i % 3)], writes=[B[ob]])
                            if j == 4:
                                for step in range(3):
                                    pending.append((i + 1 + step, lambda blk=blk, ob=ob, step=step: epilogue(blk, ob, step)))

                        def epilogue(blk, ob, step, p=p, lo=lo, hi=hi, dlo=dlo, dhi=dhi, hp=hp):
                            rc = rec[blk % 2]
                            rr = ("rec", 0)
                            if step == 0:
                                dve(lambda e, ob=ob, rc=rc: e.tensor_copy(out=rc[lo:hi, :], in_=banks[ob][dlo:dhi, :]), reads=[B[ob]], writes=[rr])
                            elif step == 1:
                                act(lambda e, rc=rc: e.activation(out=rc[lo:hi, :], in_=rc[lo:hi, :], func=AF.Ln), reads=[rr], writes=[rr])
                                act(lambda e, rc=rc: e.activation(out=rc[lo:hi, :], in_=rc[lo:hi, :], func=AF.Exp, scale=-1.0), reads=[rr], writes=[rr])
                            else:
                                dve(lambda e, ob=ob, rc=rc, blk=blk: e.tensor_tensor(out=cT[lo:hi, hp, blk * 512:(blk + 1) * 512], in0=banks[ob][lo:hi, :],
                                                                                    in1=rc[lo:hi, :], op=ALU.mult),
                                    reads=[B[ob], rr], writes=[("cT", hp)])

                        n_it = len(items)
                        pending = []
                        S_stage(0)
                        S_stage(1)
                        for i in range(n_it):
                            mid_stage(i)
                            PV_stage(i)
                            if i + 2 < n_it:
                                S_stage(i + 2)
                            if i == 4 and h + 1 < 16:
                                prep_bias(h + 1)
                            pending.sort(key=lambda q: q[0])
                            while pending and pending[0][0] <= i:
                                pending.pop(0)[1]()
                        pending.sort(key=lambda q: q[0])
                        while pending:
                            pending.pop(0)[1]()
                barrier()
            with contextlib.ExitStack() as S2:
                wout = sb(S2, "wouto", [128, 8, D], BF16)
                load_w(wout, w_out_o[0], "wouto")
                for t in range(NT):
                    proj_resid(t, lambda k, t=t: cT[:, k, t * 128:(t + 1) * 128], [("cT", g) for g in range(8)], 8, wout, "wouto")
                barrier()

    fins = []
    import os
    PH = os.environ.get("PHASES", "A,B1,B2,C").split(",")
    if stage >= 1:
        if "A" in PH:
            phase_A()
        if "B1" in PH:
            phase_B1()
        if "B2" in PH:
            phase_B2()
        if "C" in PH:
            phase_C()
    if stage >= 2:
        phase_mem()
        phase_xattn(0)
    if stage >= 3:
        phase_mlp(0)
    if stage >= 4:
        phase_G1()
        phase_G2()
    if stage >= 5:
        phase_xattn(1)
        phase_mlp(1, last=True)
    if not fins:
        for t in range(NT):
            fins.append(P.dma("sp", lambda e, t=t: e.dma_start(out=out[t * 128:(t + 1) * 128, :], in_=x[:, t, :]), reads=[("x", t)]))
    P.emit(nc, final_ops=fins)
    return nc, P


def _rope_table(pos):
    half = 16
    freqs = (np.float32(10000.0) ** (-np.arange(half, dtype=np.float32) / np.float32(half))).astype(np.float32)
    ang = (pos.astype(np.float32)[:, None] * freqs[None, :]).astype(np.float32)
    c = np.cos(ang).astype(np.float32)
    s = np.sin(ang).astype(np.float32)
    return np.concatenate([c, c, -s, s], axis=1).astype(np.float32)


def _invc_table(a_tok):
    tab = np.ones((4, 2568), np.float32)
    tg = a_tok + np.arange(2568) - 8
    valid = (tg >= 0) & (tg < SEQ)
    for g, w in enumerate((2, 4, 8, 16)):
        lo = np.clip(tg - w // 2, 0, SEQ - 1)
        hi = np.clip(tg + w - 1 - w // 2, 0, SEQ - 1)
        cnt = (hi - lo + 1).astype(np.float32)
        tab[g] = np.where(valid, np.float32(1.0) / cnt, np.float32(1.0))
    return tab


def _natten_bias(rpb):
    H = rpb.shape[0]
    outb = np.full((H, 25, 128, 128), NEG, np.float32)
    cols = np.arange(64)
    c0 = np.clip(cols - 8, 0, 48)
    classes = {0: 8, 1: 0, 2: 2, 3: 36, 4: 38}
    kc = np.arange(64)[:, None]
    qc = np.arange(64)[None, :]
    colvalid = (kc >= c0[None, :]) & (kc < c0[None, :] + 16)
    dc = np.clip(kc - qc + 15, 0, 30)
    for cls, r in classes.items():
        tb = min(max(r - 4, 0), 30)
        for j in range(5):
            for kr_i in range(2):
                kr = tb + 2 * j + kr_i
                for qr_i in range(2):
                    qr = r + qr_i
                    r0 = min(max(qr - 4, 0), 32)
                    if not (r0 <= kr <= r0 + 7):
                        continue
                    dr = kr - qr + 7
                    vals = rpb[:, dr][:, dc]
                    blk = np.where(colvalid[None], vals, np.float32(NEG))
                    outb[:, cls * 5 + j, kr_i * 64:(kr_i + 1) * 64, qr_i * 64:(qr_i + 1) * 64] = blk
    return np.ascontiguousarray(outb.transpose(0, 2, 1, 3))


def make_in_maps(inputs):
    f = lambda a: np.ascontiguousarray(np.asarray(a, dtype=np.float32))
    xfull = f(inputs["x"])
    memf = f(inputs["mem"])
    shared = {k: f(v) for k, v in inputs.items() if k not in ("x", "mem", "na_rpb")}
    nabt = _natten_bias(f(inputs["na_rpb"])[0])
    pos_b = np.arange(SEQ)
    cs_b = _rope_table(pos_b)
    maps, meta = [], []
    for c in range(8):
        b, j = c // 4, c % 4
        a = min(max(32 * j - 4, 0), 88)
        a_tok = a * 64
        xw = xfull[b, a_tok:a_tok + WT]
        xh = np.zeros((16, D), np.float32)
        if a_tok >= 8:
            xh[0:8] = xfull[b, a_tok - 8:a_tok]
        if a_tok + WT + 8 <= SEQ:
            xh[8:16] = xfull[b, a_tok + WT:a_tok + WT + 8]
        m = dict(shared)
        m.update(xw=np.ascontiguousarray(xw), xh=xh, xb=xfull[b], mem=memf[b],
                 cs_w=np.ascontiguousarray(cs_b[a_tok:a_tok + WT]), cs_b=cs_b, invc=_invc_table(a_tok), nab=nabt)
        maps.append(m)
        meta.append((b, a, 32 * j - a))
    return maps, meta


_CACHE = {}


def kernel(**inputs):
    if "nc" not in _CACHE:
        _CACHE["nc"] = build_program(99)[0]
    nc = _CACHE["nc"]
    maps, meta = make_in_maps(inputs)
    res = run_bass_kernel_spmd(nc, maps, core_ids=list(range(8)))
    outp = np.zeros((2, SEQ, D), np.float32)
    for c in range(8):
        b, a, off = meta[c]
        o = np.asarray(res.results[c]["out"]).reshape(WT, D)
        j = c % 4
        outp[b, j * 2048:(j + 1) * 2048] = o[off * 64:off * 64 + 2048]
    return outp
```

```python
import concourse.bass as bass
import concourse.mybir as mybir

ENGS = ("pe", "act", "dve", "pool", "sp")
SAME_ENG_SYNC = {"pe": False, "act": True, "dve": True, "pool": True, "sp": False}
NSLOT = 12


class Op:
    __slots__ = ("id", "eng", "fn", "deps", "dma", "flag", "sem", "val", "slot_guard", "nwaits")

    def __init__(self, id, eng, fn, dma):
        self.id = id
        self.eng = eng
        self.fn = fn
        self.dma = dma
        self.deps = []
        self.flag = False
        self.sem = None
        self.val = None
        self.slot_guard = None


class Prog:
    def __init__(self):
        self.ops = []
        self.by_eng = {e: [] for e in ENGS}
        self.last_w = {}
        self.readers = {}
        self.dma_count = {e: 0 for e in ENGS}

    _cap = None

    def capture(self):
        self._cap = []

    def end_capture(self):
        c = self._cap
        self._cap = None
        return c

    def mark(self):
        self._mark = len(self._cap)

    def replay(self, lst):
        for it in lst:
            self.op(*it)

    @staticmethod
    def zipmerge(a, b):
        out = []
        na, nb = len(a), len(b)
        ia = ib = 0
        while ia < na or ib < nb:
            if ib >= nb or (ia < na and ia * max(nb, 1) <= ib * max(na, 1)):
                out.append(a[ia]); ia += 1
            else:
                out.append(b[ib]); ib += 1
        return out

    def pipeline_deep(self, n, tile_fn, depth):
        parts = {}
        for s in range(n + depth - 1):
            if s < n:
                self.capture()
                tile_fn(s)
                L = self.end_capture()
                m = len(L)
                cuts = [int(round(m * i / depth)) for i in range(depth + 1)]
                parts[s] = [L[cuts[i]:cuts[i + 1]] for i in range(depth)]
            merged = []
            for d in range(depth - 1, -1, -1):
                t = s - d
                if 0 <= t < n:
                    merged = self.zipmerge(merged, parts[t][d]) if merged else list(parts[t][d])
            self.replay(merged)
            if s - depth + 1 in parts and s - depth + 1 >= 0:
                del parts[s - depth + 1]

    def pipeline(self, n, tile_fn, split=0.5):
        prevB = []
        for t in range(n):
            self.capture()
            self._mark = None
            tile_fn(t)
            L = self.end_capture()
            h = self._mark if self._mark is not None else int(len(L) * split)
            self.replay(self.zipmerge(L[:h], prevB))
            prevB = L[h:]
        self.replay(prevB)

    def op(self, eng, fn, reads=(), writes=(), dma=False):
        if self._cap is not None:
            self._cap.append((eng, fn, reads, writes, dma))
            return None
        o = Op(len(self.ops), eng, fn, dma)
        reads = list(reads)
        writes = list(writes) + [r for r in reads if isinstance(r, str) and r.startswith("bank")]
        deps = set()
        for r in reads:
            w = self.last_w.get(r)
            if w is not None:
                deps.add(w)
        for w_ in writes:
            w = self.last_w.get(w_)
            if w is not None:
                deps.add(w)
            for rd in self.readers.get(w_, ()):
                deps.add(rd)
        deps.discard(o.id)
        o.deps = sorted(deps)
        for r in reads:
            self.readers.setdefault(r, []).append(o.id)
        for w_ in writes:
            self.last_w[w_] = o.id
            self.readers[w_] = []
        self.ops.append(o)
        self.by_eng[eng].append(o)
        return o

    def pe(self, fn, reads=(), writes=()):
        return self.op("pe", fn, reads, writes)

    def act(self, fn, reads=(), writes=()):
        return self.op("act", fn, reads, writes)

    def dve(self, fn, reads=(), writes=()):
        return self.op("dve", fn, reads, writes)

    def pool(self, fn, reads=(), writes=()):
        return self.op("pool", fn, reads, writes)

    def dma(self, eng, fn, reads=(), writes=()):
        return self.op(eng, fn, reads, writes, dma=True)

    def emit(self, nc, final_ops=()):
        ops = self.ops
        for o in ops:
            for d in o.deps:
                do = ops[d]
                if do.dma or do.eng != o.eng or SAME_ENG_SYNC[o.eng]:
                    do.flag = True
        for o in final_ops:
            o.flag = True
        import contextlib
        with contextlib.ExitStack() as es:
            esem = {e: es.enter_context(nc.semaphore("s_" + e)) for e in ENGS}
            dsem = {}
            for e in ENGS:
                if self.dma_count_total(e) > 0:
                    dsem[e] = [es.enter_context(nc.semaphore("d_%s_%d" % (e, i))) for i in range(NSLOT)]
            cnt = {e: 0 for e in ENGS}
            dcnt = {e: 0 for e in ENGS}
            semkey = {}
            for o in ops:
                if o.dma:
                    j = dcnt[o.eng]
                    dcnt[o.eng] += 1
                    s = dsem[o.eng][j % NSLOT]
                    o.sem = s
                    o.val = 16 * (j // NSLOT + 1)
                    o.slot_guard = (s, 16 * (j // NSLOT)) if j >= NSLOT else None
                    semkey[id(s)] = ("d", o.eng, j % NSLOT)
                elif o.flag:
                    cnt[o.eng] += 1
                    o.sem = esem[o.eng]
                    o.val = cnt[o.eng]
            clock = {e: {} for e in ENGS}
            opvc = [None] * len(ops)
            plan = {e: [] for e in ENGS}
            for o in ops:
                ck = clock[o.eng]
                waits = {}
                for d in o.deps:
                    do = ops[d]
                    if not (do.dma or do.eng != o.eng or SAME_ENG_SYNC[o.eng]):
                        continue
                    k = id(do.sem)
                    if ck.get(k, 0) >= do.val:
                        continue
                    if k not in waits or waits[k][1] < do.val:
                        waits[k] = (do.sem, do.val, d)
                if o.slot_guard is not None:
                    s, v = o.slot_guard
                    k = id(s)
                    if ck.get(k, 0) < v and (k not in waits or waits[k][1] < v):
                        waits[k] = (s, v, None)
                wl = list(waits.items())
                keep = []
                for k, (s, v, d) in wl:
                    implied = False
                    for k2, (s2, v2, d2) in wl:
                        if k2 == k or d2 is None:
                            continue
                        vc2 = opvc[d2]
                        if vc2 is not None and vc2.get(k, 0) >= v:
                            implied = True
                            break
                    if not implied:
                        keep.append((s, v))
                for k, (s, v, d) in wl:
                    if ck.get(k, 0) < v:
                        ck[k] = v
                    if d is not None and opvc[d] is not None:
                        for kk, vv in opvc[d].items():
                            if ck.get(kk, 0) < vv:
                                ck[kk] = vv
                if o.sem is not None:
                    vc = dict(ck)
                    vc[id(o.sem)] = o.val
                    opvc[o.id] = vc
                    if not o.dma:
                        pass
                plan[o.eng].append((o, keep))
            self.stats = {e: (len(plan[e]), sum(len(k) for _, k in plan[e])) for e in ENGS}

            def run(eng_handle, lst, final_waits):
                for o, keep in lst:
                    for (s, v) in keep[1:]:
                        eng_handle.wait_ge(s, v)
                    ins = o.fn(eng_handle)
                    if keep:
                        s, v = keep[0]
                        if isinstance(ins, tuple):
                            ins[0]._wait_ge(s, v)
                        else:
                            ins._wait_ge(s, v)
                    if o.sem is not None:
                        last = ins[1] if isinstance(ins, tuple) else ins
                        last.then_inc(o.sem, 16 if o.dma else 1)
                for (s, v) in final_waits:
                    eng_handle.wait_ge(s, v)

            fw = [(o.sem, o.val) for o in final_ops]
            with nc.Block() as block:
                @block.tensor
                def _(e):
                    run(e, plan["pe"], [])

                @block.scalar
                def _(e):
                    run(e, plan["act"], [])

                @block.vector
                def _(e):
                    run(e, plan["dve"], [])

                @block.gpsimd
                def _(e):
                    run(e, plan["pool"], [])

                @block.sync
                def _(e):
                    run(e, plan["sp"], fw)

    def dma_count_total(self, e):
        return sum(1 for o in self.by_eng[e] if o.dma)


import contextlib
import numpy as np
from concourse.bass_utils import run_bass_kernel_spmd

F32 = mybir.dt.float32
BF16 = mybir.dt.bfloat16
AF = mybir.ActivationFunctionType
ALU = mybir.AluOpType
AX = mybir.AxisListType

D = 1024
SEQ = 8192
WT = 2560
NT = 20
NB = 5
NTB = 64
EPS = 1e-6
NEG = -30000.0


def build_program(stage=99):
    nc = bass.Bass("TRN2", target_bir_lowering=False)
    P = Prog()
    G = contextlib.ExitStack()
    uid = [0]
    bar_from = [0]

    def di(name, shape, dt=F32):
        return nc.dram_tensor(name, list(shape), dt, kind="ExternalInput").ap()

    def sb(st, name, shape, dt):
        uid[0] += 1
        return st.enter_context(nc.sbuf_tensor("%s_%d" % (name, uid[0]), list(shape), dt))

    def barrier():
        lasts = [P.by_eng[e][-1].id for e in ENGS if P.by_eng[e]]
        dmas = [o.id for o in P.ops[bar_from[0]:] if o.dma]
        bar_from[0] = len(P.ops)
        deps = sorted(set(lasts + dmas))
        for e in ENGS:
            o = P.op(e, lambda eng: eng.nop())
            o.deps = list(deps)
        P.last_w = {}
        P.readers = {}

    xw = di("xw", [WT, D]); xh = di("xh", [16, D]); xb = di("xb", [SEQ, D]); mem = di("mem", [256, D])
    mix_norm_g = di("mix_norm_g", [2, D]); xattn_norm_g = di("xattn_norm_g", [2, D]); ff_norm_g = di("ff_norm_g", [2, D])
    w_mem_q = di("w_mem_q", [2, D, D]); mem_q_g = di("mem_q_g", [2, 256]); w_mem_o = di("w_mem_o", [2, D, D])
    w_ff1 = di("w_ff1", [2, D, 4096]); w_ff2 = di("w_ff2", [2, 4096, D])
    mem_tok_norm_g = di("mem_tok_norm_g", [D]); w_mem_kv = di("w_mem_kv", [D, 2048]); mem_k_g = di("mem_k_g", [256])
    w_in_e = di("w_in_e", [1, D, 928]); pool_w = di("pool_w", [1, 4, 128, 128]); pool_scale = di("pool_scale", [1, 512])
    q_lora_g = di("q_lora_g", [1, 256]); w_uq = di("w_uq", [1, 256, 768]); kv_lora_g = di("kv_lora_g", [1, 128])
    w_ukv = di("w_ukv", [1, 128, 1024]); mla_q_g = di("mla_q_g", [1, 96]); mla_k_g = di("mla_k_g", [1, 96])
    w_out_e = di("w_out_e", [1, D, D]); w_qkv_o = di("w_qkv_o", [1, D, 3072])
    na_q_g = di("na_q_g", [1, 64]); na_k_g = di("na_k_g", [1, 64]); w_out_o = di("w_out_o", [1, D, D])
    cs_w = di("cs_w", [WT, 64]); cs_b = di("cs_b", [SEQ, 64]); invc = di("invc", [4, 2568])
    nab = di("nab", [16, 128, 25, 128])
    out = nc.dram_tensor("out", [WT, D], F32, kind="ExternalOutput").ap()
    kT_scr = nc.dram_tensor("kT_scr", [8, 96, SEQ], BF16, kind="Internal").ap()
    v_scr = nc.dram_tensor("v_scr", [8, 128, NTB, 64], BF16, kind="Internal").ap()
    qT_scr = nc.dram_tensor("qT_scr", [8, 96, WT], BF16, kind="Internal").ap()
    nq_scr = nc.dram_tensor("nq_scr", [8, 128, WT], BF16, kind="Internal").ap()
    nk_scr = nc.dram_tensor("nk_scr", [8, 128, WT], BF16, kind="Internal").ap()
    nv_scr = nc.dram_tensor("nv_scr", [8, 128, NT, 128], BF16, kind="Internal").ap()

    x = sb(G, "x", [128, NT, D], F32)
    ident = sb(G, "ident", [128, 128], BF16)
    identf = sb(G, "identf", [128, 128], F32)
    ones_b = sb(G, "ones_b", [128, 128], BF16)
    eps_t = sb(G, "eps", [128, 1], F32)
    memkT = sb(G, "memkT", [128, 4, 2, 256], BF16)
    memv = sb(G, "memv", [128, 2, 4, 256], BF16)
    SQ = [sb(G, "sq", [128, 1024], F32), sb(G, "sq", [128, 1024], F32)]
    SSQ = [sb(G, "ssq", [128, 1], F32) for _ in range(2)]
    RSTD = [sb(G, "rstd", [128, 1], F32) for _ in range(2)]
    HB = [sb(G, "hb", [128, D], BF16) for _ in range(2)]
    SS32 = [sb(G, "ss32", [128, 32], F32) for _ in range(2)]
    RS32 = [sb(G, "rs32", [128, 32], F32) for _ in range(2)]
    sq, ssq, rstd, hb, ss32, rs32 = SQ[0], SSQ[0], RSTD[0], HB[0], SS32[0], RS32[0]
    gn = sb(G, "gn", [128, D], F32)
    psum_all = G.enter_context(nc.psum_tensor("psum_all", [128, 4096], F32))
    banks = [psum_all[:, i * 512:(i + 1) * 512] for i in range(8)]
    B = ["bank%d" % i for i in range(8)]

    def bbf(i):
        return banks[i][:].bitcast(BF16)

    act, dve, pe, pool = P.act, P.dve, P.pe, P.pool

    def rstd_from(ss_ap, ss_res, dim, out_ap, out_res):
        n = ss_ap.shape[0]
        act(lambda e: e.activation(out=out_ap, in_=ss_ap, func=AF.Ln, scale=1.0 / dim, bias=eps_t[0:n, :]),
            reads=[ss_res, "eps"], writes=[out_res])
        act(lambda e: e.activation(out=out_ap, in_=out_ap, func=AF.Exp, scale=-0.5), reads=[out_res], writes=[out_res])

    def norm_hT(src_ap, src_res, hT_dst, hT_res, n=128, par=0, bank=0, ws=None):
        if ws is None:
            sq_, ssq_, rstd_, hb_ = SQ[par], SSQ[par], RSTD[par], HB[par]
            sfx = "" if par == 0 else "_p1"
            hsfx = sfx
        else:
            sq_, ssq_, rstd_, hb_, sfx, hsfx = ws
        act(lambda e: e.activation(out=sq_[0:n, 0:D], in_=src_ap, func=AF.Square, accum_out=ssq_[0:n, :]),
            reads=[src_res], writes=["sq" + sfx, "ssq" + sfx])
        rstd_from(ssq_[0:n, :], "ssq" + sfx, D, rstd_[0:n, :], "rstd" + sfx)
        dve(lambda e: e.scalar_tensor_tensor(out=hb_[0:n, :], in0=src_ap, scalar=rstd_[0:n, 0:1], in1=gn[0:n, :],
                                             op0=ALU.mult, op1=ALU.mult),
            reads=[src_res, "rstd" + sfx, "gn"], writes=["hb" + hsfx])
        pT = bbf(bank)
        for k in range(8):
            pe(lambda e, k=k: e.transpose(out=pT[:, k * n:(k + 1) * n], in_=hb_[0:n, k * 128:(k + 1) * 128],
                                          identity=ident[0:n, 0:n]),
               reads=["hb" + hsfx, "ident"], writes=[B[bank]])
        act(lambda e: e.copy(out=hT_dst, in_=pT[:, 0:8 * n].rearrange("p (k t) -> p k t", k=8)),
            reads=[B[bank]], writes=[hT_res])

    def load_w(dst, w_ap, res, eng="pool"):
        K = w_ap.shape[0]
        for k in range((K + 127) // 128):
            r = min(128, K - k * 128)
            P.dma(eng, lambda e, k=k, r=r: e.dma_start(out=dst[0:r, k, :], in_=w_ap[k * 128:k * 128 + r, :]),
                  writes=[(res, k)])

    def wres(res, nk):
        return [(res, k) for k in range(nk)]

    def load_bc(dst, vec_ap, res, eng="sp"):
        P.dma(eng, lambda e: e.dma_start(out=dst, in_=vec_ap.partition_broadcast(128)), writes=[res])

    def resid_add(t, pbank_lo, pbank_hi):
        for half, bk in ((0, pbank_lo), (1, pbank_hi)):
            dve(lambda e, half=half, bk=bk: e.tensor_tensor(out=x[:, t, half * 512:(half + 1) * 512],
                                                            in0=banks[bk][:], in1=x[:, t, half * 512:(half + 1) * 512],
                                                            op=ALU.add),
                reads=[B[bk], ("x", t)], writes=[("x", t)])

    def proj_resid(t, lhs_fn, lhs_res, nk, w_sb, w_res, k0=0):
        for half in range(2):
            for k in range(nk):
                pe(lambda e, half=half, k=k: e.matmul(banks[6 + half][:], lhsT=lhs_fn(k), rhs=w_sb[:, k0 + k, half * 512:(half + 1) * 512],
                                                       start=(k == 0), stop=(k == nk - 1)),
                   reads=lhs_res + [(w_res, k0 + k)], writes=[B[6 + half]])
        resid_add(t, 6, 7)

    pool(lambda e: e.memset(identf[:], 0.0), writes=["identf"])
    pool(lambda e: e.affine_select(out=identf[:], in_=identf[:], pattern=[[-1, 128]], compare_op=ALU.not_equal,
                                   fill=1.0, base=0, channel_multiplier=1), reads=["identf"], writes=["identf"])
    dve(lambda e: e.tensor_copy(out=ident[:], in_=identf[:]), reads=["identf"], writes=["ident"])
    dve(lambda e: e.memset(ones_b[:], 1.0), writes=["ones_b"])
    dve(lambda e: e.memset(eps_t[:], EPS), writes=["eps"])
    for t in range(NT):
        P.dma("sp", lambda e, t=t: e.dma_start(out=x[:, t, :], in_=xw[t * 128:(t + 1) * 128, :]), writes=[("x", t)])
    barrier()

    def phase_mem():
        with contextlib.ExitStack() as S:
            wkv = sb(S, "wkv", [128, 8, 2048], BF16)
            gk = sb(S, "gk", [128, 256], F32)
            memt = sb(S, "memt", [128, D], F32)
            hTm = sb(S, "hTm", [128, 8, 128], BF16)
            kn = sb(S, "kn", [128, D], F32)
            kbm = sb(S, "kbm", [128, D], BF16)
            load_w(wkv, w_mem_kv, "wkv")
            load_bc(gn[:], mem_tok_norm_g, "gn")
            load_bc(gk[:], mem_k_g, "gk")
            for m in range(2):
                P.dma("sp", lambda e, m=m: e.dma_start(out=memt[:], in_=mem[m * 128:(m + 1) * 128, :]), writes=["memt"])
                norm_hT(memt[:], "memt", hTm[:], "hTm")
                for n4 in range(4):
                    for k in range(8):
                        pe(lambda e, n4=n4, k=k: e.matmul(banks[1 + n4][:], lhsT=hTm[:, k, :], rhs=wkv[:, k, n4 * 512:(n4 + 1) * 512],
                                                          start=(k == 0), stop=(k == 7)),
                           reads=["hTm", ("wkv", k)], writes=[B[1 + n4]])
                for j in range(2):
                    act(lambda e, j=j: e.activation(out=sq[:, j * 512:(j + 1) * 512], in_=banks[1 + j][:], func=AF.Square),
                        reads=[B[1 + j]], writes=["sq"])
                dve(lambda e: e.tensor_reduce(out=ss32[:, 0:4], in_=sq[:, 0:D].rearrange("p (h d) -> p h d", h=4), axis=AX.X, op=ALU.add),
                    reads=["sq"], writes=["ss32"])
                rstd_from(ss32[:, 0:4], "ss32", 256, rs32[:, 0:4], "rs32")
                for j in range(2):
                    dve(lambda e, j=j: e.tensor_tensor(out=kn[:, j * 512:(j + 1) * 512].rearrange("p (h d) -> p h d", h=2),
                                                       in0=banks[1 + j][:].rearrange("p (h d) -> p h d", h=2),
                                                       in1=rs32[:, 2 * j:2 * j + 2].unsqueeze(2).to_broadcast([128, 2, 256]), op=ALU.mult),
                        reads=[B[1 + j], "rs32"], writes=["kn"])
                dve(lambda e: e.tensor_tensor(out=kbm[:].rearrange("p (h d) -> p h d", h=4), in0=kn[:].rearrange("p (h d) -> p h d", h=4),
                                              in1=gk[:].unsqueeze(1).to_broadcast([128, 4, 256]), op=ALU.mult),
                    reads=["kn", "gk"], writes=["kbm"])
                pT = bbf(0)
                for j in range(8):
                    pe(lambda e, j=j: e.transpose(out=pT[:, j * 128:(j + 1) * 128], in_=kbm[:, j * 128:(j + 1) * 128], identity=ident[:]),
                       reads=["kbm", "ident"], writes=[B[0]])
                act(lambda e, m=m: e.copy(out=memkT[:, :, :, m * 128:(m + 1) * 128],
                                          in_=pT[:, 0:1024].rearrange("p (h j t) -> p h j t", h=4, j=2)),
                    reads=[B[0]], writes=["memkT"])
                for j in range(2):
                    act(lambda e, m=m, j=j: e.copy(out=memv[:, m, 2 * j:2 * j + 2, :], in_=banks[3 + j][:].rearrange("p (h d) -> p h d", h=2)),
                        reads=[B[3 + j]], writes=["memv"])
            barrier()

    def rotary(src, src_res, cst, cs_res, dst, dst_res, t1, t2, shape3, sfx=""):
        H = shape3
        cc = cst[:, 0:32].unsqueeze(1).to_broadcast([128, H, 32])
        ns = cst[:, 32:48].unsqueeze(1).to_broadcast([128, H, 16])
        ps_ = cst[:, 48:64].unsqueeze(1).to_broadcast([128, H, 16])
        dve(lambda e: e.tensor_tensor(out=t1, in0=src, in1=cc, op=ALU.mult), reads=[src_res, cs_res], writes=["rot_t1" + sfx])
        dve(lambda e: e.tensor_tensor(out=t2[:, :, 0:16], in0=src[:, :, 16:32], in1=ns, op=ALU.mult), reads=[src_res, cs_res], writes=["rot_t2a" + sfx])
        dve(lambda e: e.tensor_tensor(out=t2[:, :, 16:32], in0=src[:, :, 0:16], in1=ps_, op=ALU.mult), reads=[src_res, cs_res], writes=["rot_t2b" + sfx])
        dve(lambda e: e.tensor_tensor(out=dst, in0=t1, in1=t2, op=ALU.add), reads=["rot_t1" + sfx, "rot_t2a" + sfx, "rot_t2b" + sfx], writes=[dst_res])

    def phase_A():
        NP = 4
        with contextlib.ExitStack() as S:
            w_kv = sb(S, "w_kv", [128, 8, 160], BF16)
            wukv = sb(S, "wukv", [128, 1, 1024], BF16)
            gkv = sb(S, "gkv", [128, 128], F32)
            gk = sb(S, "gk96", [128, 96], F32)
            many = lambda name, shape, dt: [sb(S, name, shape, dt) for _ in range(NP)]
            xt = [sb(S, "xt", [128, D], F32) for _ in range(2)] * 2
            csb_ = many("csb", [128, 64], F32)
            hTt_ = many("hTt", [128, 8, 128], BF16)
            ckn_ = many("ckn", [128, 128], BF16)
            ckT_ = many("ckT", [128, 128], BF16)
            kn_ = many("kn", [128, 8, 64], F32)
            kb2 = many("kb", [128, 8, 96], BF16)
            krw_ = many("krw", [128, 32], F32)
            krg_ = many("krg", [128, 1, 32], F32)
            krr_ = many("krr", [128, 1, 32], F32)
            t1_ = many("t1", [128, 1, 32], F32)
            t2_ = many("t2", [128, 1, 32], F32)
            vb_ = many("vb", [128, 8, 64], BF16)
            ssr_ = many("ssr", [128, 1], F32)
            sqa_ = many("sqa", [128, D], F32)
            ssqa_ = many("ssqa", [128, 1], F32)
            rstda_ = many("rstda", [128, 1], F32)
            hba_ = [sb(S, "hba", [128, D], BF16) for _ in range(2)] * 2
            ss8_ = many("ss8", [128, 8], F32)
            rs8_ = many("rs8", [128, 8], F32)
            kst_ = [sb(S, "kst", [96, 8, 512], BF16) for _ in range(2)]
            load_w(w_kv, w_in_e[0][:, 768:928], "w_kv")
            load_w(wukv, w_ukv[0], "wukv")
            load_bc(gn[:], mix_norm_g[0], "gn")
            load_bc(gkv[:], kv_lora_g[0], "gkv")
            load_bc(gk[:], mla_k_g[0], "gk")

            def tileA(t):
                p = t % NP
                x_ = "_%d" % p
                bX, bY = 2 * p, 2 * p + 1
                bK = (bX, bY)
                xtt, hTt, ckn, ckT, kn, kb_, krw, krg, krr, t1, t2, vb, ssr = (xt[p], hTt_[p], ckn_[p], ckT_[p], kn_[p], kb2[p], krw_[p], krg_[p],
                                                                               krr_[p], t1_[p], t2_[p], vb_[p], ssr_[p])
                sq_, ssq_, rstd_, ss32_, rs32_ = sqa_[p], ssqa_[p], rstda_[p], ss8_[p], rs8_[p]
                sfx = "_a%d" % p
                kst = kst_[(t // 4) % 2]
                kstr = "kst%d" % ((t // 4) % 2)
                xr = "xt_%d" % (t % 2)
                csb = csb_[p]
                P.dma("sp", lambda e: e.dma_start(out=xtt[:], in_=xb[t * 128:(t + 1) * 128, :]), writes=[xr])
                P.dma("sp", lambda e: e.dma_start(out=csb[:], in_=cs_b[t * 128:(t + 1) * 128, :]), writes=["csb" + x_])
                norm_hT(xtt[:], xr, hTt[:], "hTt" + x_, bank=bX, ws=(sq_, ssq_, rstd_, hba_[p], sfx, "_h%d" % (t % 2)))
                pC = banks[bY]
                for k in range(8):
                    pe(lambda e, k=k: e.matmul(pC[:, 0:160], lhsT=hTt[:, k, :], rhs=w_kv[:, k, :], start=(k == 0), stop=(k == 7)),
                       reads=["hTt" + x_, ("w_kv", k)], writes=[B[bY]])
                act(lambda e: e.activation(out=sq_[:, 0:128], in_=pC[:, 0:128], func=AF.Square, accum_out=ssq_[:]),
                    reads=[B[bY]], writes=["sq" + sfx, "ssq" + sfx])
                act(lambda e: e.copy(out=krw[:], in_=pC[:, 128:160]), reads=[B[bY]], writes=["krw" + x_])
                rstd_from(ssq_[:], "ssq" + sfx, 128, rstd_[:], "rstd" + sfx)
                dve(lambda e: e.scalar_tensor_tensor(out=ckn[:], in0=pC[:, 0:128], scalar=rstd_[:, 0:1], in1=gkv[:], op0=ALU.mult, op1=ALU.mult),
                    reads=[B[bY], "rstd" + sfx, "gkv"], writes=["ckn" + x_])
                pT2 = bbf(bX)
                pe(lambda e: e.transpose(out=pT2[:, 0:128], in_=ckn[:], identity=ident[:]), reads=["ckn" + x_, "ident"], writes=[B[bX]])
                act(lambda e: e.copy(out=ckT[:], in_=pT2[:, 0:128]), reads=[B[bX]], writes=["ckT" + x_])
                act(lambda e: e.activation(out=sq_[:, 512:544], in_=krw[:], func=AF.Square, accum_out=ssr[:]),
                    reads=["krw" + x_], writes=["sq2" + sfx, "ssr" + x_])
                dve(lambda e: e.tensor_tensor(out=krg[:, 0, :], in0=krw[:], in1=gk[:, 64:96], op=ALU.mult),
                    reads=["krw" + x_, "gk"], writes=["krg" + x_])
                rotary(krg[:], "krg" + x_, csb[:], "csb" + x_, krr[:], "krr" + x_, t1[:], t2[:], 1, sfx=x_)
                for j in range(2):
                    pe(lambda e, j=j: e.matmul(banks[bK[j]][:], lhsT=ckT[:], rhs=wukv[:, 0, j * 512:(j + 1) * 512], start=True, stop=True),
                       reads=["ckT" + x_, ("wukv", 0)], writes=[B[bK[j]]])
                for j in range(2):
                    act(lambda e, j=j: e.activation(out=sq_[:, j * 256:(j + 1) * 256].rearrange("p (h d) -> p h d", h=4),
                                                    in_=banks[bK[j]][:].rearrange("p (h d) -> p h d", h=4)[:, :, 0:64], func=AF.Square),
                        reads=[B[bK[j]]], writes=["sq" + sfx])
                dve(lambda e: e.tensor_reduce(out=ss32_[:, 0:8], in_=sq_[:, 0:512].rearrange("p (h d) -> p h d", h=8), axis=AX.X, op=ALU.add),
                    reads=["sq" + sfx], writes=["ss32" + sfx])
                dve(lambda e: e.tensor_scalar(out=ss32_[:, 0:8], in0=ss32_[:, 0:8], scalar1=ssr[:, 0:1], scalar2=None, op0=ALU.add),
                    reads=["ss32" + sfx, "ssr" + x_], writes=["ss32" + sfx])
                rstd_from(ss32_[:, 0:8], "ss32" + sfx, 96, rs32_[:, 0:8], "rs32" + sfx)
                for j in range(2):
                    dve(lambda e, j=j: e.tensor_tensor(out=kn[:, 4 * j:4 * j + 4, :], in0=banks[bK[j]][:].rearrange("p (h d) -> p h d", h=4)[:, :, 0:64],
                                                       in1=rs32_[:, 4 * j:4 * j + 4].unsqueeze(2).to_broadcast([128, 4, 64]), op=ALU.mult),
                        reads=[B[bK[j]], "rs32" + sfx], writes=["kn" + x_])
                for j in range(2):
                    act(lambda e, j=j: e.copy(out=vb[:, 4 * j:4 * j + 4, :], in_=banks[bK[j]][:].rearrange("p (h d) -> p h d", h=4)[:, :, 64:128]),
                        reads=[B[bK[j]]], writes=["vb" + x_])
                pool(lambda e: e.tensor_tensor(out=kb_[:, :, 0:64], in0=kn[:], in1=gk[:, 0:64].unsqueeze(1).to_broadcast([128, 8, 64]), op=ALU.mult),
                     reads=["kn" + x_, "gk"], writes=["kb_n" + x_])
                dve(lambda e: e.tensor_tensor(out=kb_[:, :, 64:96], in0=krr[:, 0, :].unsqueeze(1).to_broadcast([128, 8, 32]),
                                              in1=rs32_[:, 0:8].unsqueeze(2).to_broadcast([128, 8, 32]), op=ALU.mult),
                    reads=["krr" + x_, "rs32" + sfx], writes=["kb_r" + x_])
                P.dma("pool", lambda e: e.dma_start(out=v_scr[:, :, t, :].rearrange("h p d -> p h d"), in_=vb[:]), reads=["vb" + x_], writes=[("v_scr", t)])
                pT3 = bbf(bX)
                for h in range(8):
                    pe(lambda e, h=h: e.transpose(out=pT3[0:96, h * 128:(h + 1) * 128], in_=kb_[:, h, :], identity=ident[:]),
                       reads=["kb_n" + x_, "kb_r" + x_, "ident"], writes=[B[bX]])
                tt = t % 4
                act(lambda e: e.copy(out=kst[:, :, tt * 128:(tt + 1) * 128], in_=pT3[0:96, 0:1024].rearrange("p (h t) -> p h t", h=8)),
                    reads=[B[bX]], writes=[kstr])
                if tt == 3:
                    b4 = t // 4
                    P.dma("pool", lambda e: e.dma_start(out=kT_scr[:, :, b4 * 512:(b4 + 1) * 512].rearrange("h d t -> d h t"), in_=kst[:]),
                          reads=[kstr], writes=[("kT_scr", b4)])

            P.pipeline_deep(NTB, tileA, NP)
            barrier()

    def phase_B1():
        NP = 4
        with contextlib.ExitStack() as S:
            w_q = sb(S, "w_q", [128, 8, 256], BF16)
            wuq = sb(S, "wuq", [128, 2, 768], BF16)
            gql = sb(S, "gql", [128, 256], F32)
            gq = sb(S, "gq96", [128, 96], F32)
            csw = sb(S, "csw", [128, NT, 64], F32)
            many = lambda name, shape, dt: [sb(S, name, shape, dt) for _ in range(NP)]
            hTt_ = many("hTt", [128, 8, 128], BF16)
            cqn_ = many("cqn", [128, 256], BF16)
            cqT_ = many("cqT", [128, 2, 128], BF16)
            qn_ = many("qn", [128, 8, 96], F32)
            qr_ = many("qr", [128, 8, 32], F32)
            qb2 = many("qb", [128, 8, 96], BF16)
            t1_ = many("t1", [128, 8, 32], F32)
            t2_ = many("t2", [128, 8, 32], F32)
            sqa_ = many("sqa", [128, D], F32)
            ssqa_ = many("ssqa", [128, 1], F32)
            rstda_ = many("rstda", [128, 1], F32)
            hba_ = [sb(S, "hba", [128, D], BF16) for _ in range(2)] * 2
            ss8_ = many("ss8", [128, 8], F32)
            rs8_ = many("rs8", [128, 8], F32)
            qst_ = [sb(S, "qst", [96, 8, 512], BF16) for _ in range(2)]
            load_w(w_q, w_in_e[0][:, 512:768], "w_q")
            load_w(wuq, w_uq[0], "wuq")
            load_bc(gn[:], mix_norm_g[0], "gn")
            load_bc(gql[:], q_lora_g[0], "gql")
            load_bc(gq[:], mla_q_g[0], "gq")
            P.dma("sp", lambda e: e.dma_start(out=csw[:], in_=cs_w.rearrange("(c p) f -> p c f", p=128)), writes=["csw"])

            def tileB(t):
                p = t % NP
                x_ = "_%d" % p
                sfx = "_b%d" % p
                bX, bY = 2 * p, 2 * p + 1
                bQ = (bX, bY)
                hTt, cqn, cqT, qn, qr, qb_, t1, t2 = hTt_[p], cqn_[p], cqT_[p], qn_[p], qr_[p], qb2[p], t1_[p], t2_[p]
                sq_, ssq_, rstd_, ss32_, rs32_ = sqa_[p], ssqa_[p], rstda_[p], ss8_[p], rs8_[p]
                qst = qst_[(t // 4) % 2]
                qstr = "qst%d" % ((t // 4) % 2)
                norm_hT(x[:, t, :], ("x", t), hTt[:], "hTt" + x_, bank=bX, ws=(sq_, ssq_, rstd_, hba_[p], sfx, "_h%d" % (t % 2)))
                pC = banks[bY]
                for k in range(8):
                    pe(lambda e, k=k: e.matmul(pC[:, 0:256], lhsT=hTt[:, k, :], rhs=w_q[:, k, :], start=(k == 0), stop=(k == 7)),
                       reads=["hTt" + x_, ("w_q", k)], writes=[B[bY]])
                act(lambda e: e.activation(out=sq_[:, 0:256], in_=pC[:, 0:256], func=AF.Square, accum_out=ssq_[:]),
                    reads=[B[bY]], writes=["sq" + sfx, "ssq" + sfx])
                rstd_from(ssq_[:], "ssq" + sfx, 256, rstd_[:], "rstd" + sfx)
                dve(lambda e: e.scalar_tensor_tensor(out=cqn[:], in0=pC[:, 0:256], scalar=rstd_[:, 0:1], in1=gql[:], op0=ALU.mult, op1=ALU.mult),
                    reads=[B[bY], "rstd" + sfx, "gql"], writes=["cqn" + x_])
                pT2 = bbf(bX)
                for j in range(2):
                    pe(lambda e, j=j: e.transpose(out=pT2[:, j * 128:(j + 1) * 128], in_=cqn[:, j * 128:(j + 1) * 128], identity=ident[:]),
                       reads=["cqn" + x_, "ident"], writes=[B[bX]])
                act(lambda e: e.copy(out=cqT[:], in_=pT2[:, 0:256].rearrange("p (j t) -> p j t", j=2)), reads=[B[bX]], writes=["cqT" + x_])
                for (bk, c0, c1) in ((bQ[0], 0, 480), (bQ[1], 480, 768)):
                    for j in range(2):
                        pe(lambda e, bk=bk, c0=c0, c1=c1, j=j: e.matmul(banks[bk][:, 0:c1 - c0], lhsT=cqT[:, j, :], rhs=wuq[:, j, c0:c1],
                                                                        start=(j == 0), stop=(j == 1)),
                           reads=["cqT" + x_, ("wuq", j)], writes=[B[bk]])
                segs = ((bQ[0], 0, 5), (bQ[1], 5, 8))
                for (bk, h0, h1) in segs:
                    nh = h1 - h0
                    act(lambda e, bk=bk, h0=h0, nh=nh: e.activation(out=sq_[:, h0 * 96:(h0 + nh) * 96], in_=banks[bk][:, 0:nh * 96], func=AF.Square),
                        reads=[B[bk]], writes=["sq" + sfx])
                dve(lambda e: e.tensor_reduce(out=ss32_[:, 0:8], in_=sq_[:, 0:768].rearrange("p (h d) -> p h d", h=8), axis=AX.X, op=ALU.add),
                    reads=["sq" + sfx], writes=["ss32" + sfx])
                rstd_from(ss32_[:, 0:8], "ss32" + sfx, 96, rs32_[:, 0:8], "rs32" + sfx)
                for (bk, h0, h1) in segs:
                    nh = h1 - h0
                    dve(lambda e, bk=bk, h0=h0, nh=nh: e.tensor_tensor(out=qn[:, h0:h0 + nh, :], in0=banks[bk][:, 0:nh * 96].rearrange("p (h d) -> p h d", h=nh),
                                                                       in1=rs32_[:, h0:h0 + nh].unsqueeze(2).to_broadcast([128, nh, 96]), op=ALU.mult),
                        reads=[B[bk], "rs32" + sfx], writes=["qn" + x_])
                pool(lambda e: e.tensor_tensor(out=qb_[:, :, 0:64], in0=qn[:, :, 0:64], in1=gq[:, 0:64].unsqueeze(1).to_broadcast([128, 8, 64]), op=ALU.mult),
                     reads=["qn" + x_, "gq"], writes=["qb_n" + x_])
                dve(lambda e: e.tensor_tensor(out=qr[:], in0=qn[:, :, 64:96], in1=gq[:, 64:96].unsqueeze(1).to_broadcast([128, 8, 32]), op=ALU.mult),
                    reads=["qn" + x_, "gq"], writes=["qr" + x_])
                rotary(qr[:], "qr" + x_, csw[:, t, :], "csw", qb_[:, :, 64:96], "qb_r" + x_, t1[:], t2[:], 8, sfx=x_)
                pT3 = bbf(bX)
                for h in range(8):
                    pe(lambda e, h=h: e.transpose(out=pT3[0:96, h * 128:(h + 1) * 128], in_=qb_[:, h, :], identity=ident[:]),
                       reads=["qb_n" + x_, "qb_r" + x_, "ident"], writes=[B[bX]])
                tt = t % 4
                act(lambda e: e.copy(out=qst[:, :, tt * 128:(tt + 1) * 128], in_=pT3[0:96, 0:1024].rearrange("p (h t) -> p h t", h=8)),
                    reads=[B[bX]], writes=[qstr])
                if tt == 3:
                    b4 = t // 4
                    P.dma("pool", lambda e: e.dma_start(out=qT_scr[:, :, b4 * 512:(b4 + 1) * 512].rearrange("h d t -> d h t"), in_=qst[:]),
                          reads=[qstr], writes=[("qT_scr", b4)])

            P.pipeline_deep(NT, tileB, NP)
            barrier()

    def phase_B2():
        with contextlib.ExitStack() as S:
            w_p = sb(S, "w_p", [128, 8, 512], BF16)
            pw = sb(S, "pw", [128, 4, 128], BF16)
            psc = sb(S, "psc", [128, 4], F32)
            wout = sb(S, "wout", [128, 4, D], BF16)
            inv_t = sb(S, "inv_t", [128, 4, 512], F32)
            U = sb(S, "U", [128, 4, 528], F32)
            UH = sb(S, "UH", [128, 4, 16], F32)
            a2 = sb(S, "a2", [128, 3, 528], F32)
            a4 = sb(S, "a4", [128, 2, 528], F32)
            a8 = sb(S, "a8", [128, 1, 528], F32)
            Sm = sb(S, "Sm", [128, 4, 512], F32)
            Dt = sb(S, "Dt", [128, 4, 512], BF16)
            hTb = sb(S, "hTb", [128, 8, 512], BF16)
            hTh = sb(S, "hTh", [128, 8, 16], BF16)
            xht = sb(S, "xht", [16, D], F32)
            yT = sb(S, "yT", [128, 4, WT], BF16)
            load_w(w_p, w_in_e[0][:, 0:512], "w_p")
            for g in range(4):
                P.dma("pool", lambda e, g=g: e.dma_start(out=pw[:, g, :], in_=pool_w[0, g]), writes=[("pw", g)])
            load_w(wout, w_out_e[0][0:512, :], "wout")
            load_bc(gn[:], mix_norm_g[0], "gn")
            P.dma("sp", lambda e: e.dma_start(out=psc[:], in_=pool_scale[0].rearrange("(g d) -> d g", g=4), allow_slow_non_contiguous=True), writes=["psc"])
            P.dma("sp", lambda e: e.dma_start(out=xht[:], in_=xh[:, :]), writes=["xht"])
            dve(lambda e: e.memset(U[:], 0.0), writes=["U"])
            norm_hT(xht[:], "xht", hTh[:], "hTh", n=16)
            for g in range(4):
                for k in range(8):
                    pe(lambda e, g=g, k=k: e.matmul(banks[1][:, g * 16:(g + 1) * 16], lhsT=w_p[:, k, g * 128:(g + 1) * 128], rhs=hTh[:, k, :],
                                                    start=(k == 0), stop=(k == 7)),
                       reads=["hTh", ("w_p", k)], writes=[B[1]])
            act(lambda e: e.copy(out=UH[:], in_=banks[1][:, 0:64].rearrange("p (g t) -> p g t", g=4)), reads=[B[1]], writes=["UH"])
            dve(lambda e: e.tensor_copy(out=U[:, :, 520:528], in_=UH[:, :, 0:8]), reads=["UH", "U"], writes=["U"])

            def pool_step(b, ncols, tok0, c0):
                P.dma("sp", lambda e: e.dma_start(out=inv_t[:, :, 0:(512 if b < NB else 8)],
                                                  in_=invc[:, b * 512:b * 512 + (512 if b < NB else 8)].partition_broadcast(128)),
                      writes=["inv_t"])
                pool(lambda e: e.tensor_tensor(out=a2[:, :, 0:527], in0=U[:, 1:4, 0:527], in1=U[:, 1:4, 1:528], op=ALU.add), reads=["U"], writes=["a2"])
                pool(lambda e: e.tensor_tensor(out=a4[:, :, 0:525], in0=a2[:, 1:3, 0:525], in1=a2[:, 1:3, 2:527], op=ALU.add), reads=["a2"], writes=["a4"])
                pool(lambda e: e.tensor_tensor(out=a8[:, :, 0:521], in0=a4[:, 1:2, 0:521], in1=a4[:, 1:2, 4:525], op=ALU.add), reads=["a4"], writes=["a8"])
                dve(lambda e: e.tensor_tensor(out=Sm[:, 0, :], in0=U[:, 0, 7:519], in1=U[:, 0, 8:520], op=ALU.add), reads=["U"], writes=["Sm0"])
                dve(lambda e: e.tensor_tensor(out=Sm[:, 1, :], in0=a2[:, 0, 6:518], in1=a2[:, 0, 8:520], op=ALU.add), reads=["a2"], writes=["Sm1"])
                dve(lambda e: e.tensor_tensor(out=Sm[:, 2, :], in0=a4[:, 0, 4:516], in1=a4[:, 0, 8:520], op=ALU.add), reads=["a4"], writes=["Sm2"])
                dve(lambda e: e.tensor_tensor(out=Sm[:, 3, :], in0=a8[:, 0, 0:512], in1=a8[:, 0, 8:520], op=ALU.add), reads=["a8"], writes=["Sm3"])
                dve(lambda e: e.tensor_tensor(out=Sm[:], in0=Sm[:], in1=inv_t[:], op=ALU.mult), reads=["Sm0", "Sm1", "Sm2", "Sm3", "inv_t"], writes=["Sm"])
                dve(lambda e: e.tensor_tensor(out=Dt[:], in0=Sm[:], in1=U[:, :, 8:520], op=ALU.subtract), reads=["Sm", "U"], writes=["Dt"])
                for g in range(4):
                    pe(lambda e, g=g: e.matmul(banks[2 + (g % 2)][:], lhsT=pw[:, g, :], rhs=Dt[:, g, :], start=True, stop=True),
                       reads=["Dt", ("pw", g)], writes=[B[2 + (g % 2)]])
                    act(lambda e, g=g: e.activation(out=yT[:, g, tok0:tok0 + ncols], in_=banks[2 + (g % 2)][:, c0:c0 + ncols], func=AF.Copy, scale=psc[:, g:g + 1]),
                        reads=[B[2 + (g % 2)], "psc"], writes=[("yT", g)])

            for b in range(NB):
                for tt in range(4):
                    t = 4 * b + tt
                    norm_hT(x[:, t, :], ("x", t), hTb[:, :, tt * 128:(tt + 1) * 128], "hTb")
                dve(lambda e: e.tensor_copy(out=U[:, :, 0:16], in_=U[:, :, 512:528]), reads=["U"], writes=["U"])
                for g in range(4):
                    for k in range(8):
                        pe(lambda e, g=g, k=k: e.matmul(banks[4 + (g % 2)][:], lhsT=w_p[:, k, g * 128:(g + 1) * 128], rhs=hTb[:, k, :],
                                                        start=(k == 0), stop=(k == 7)),
                           reads=["hTb", ("w_p", k)], writes=[B[4 + (g % 2)]])
                    act(lambda e, g=g: e.copy(out=U[:, g, 16:528], in_=banks[4 + (g % 2)][:]), reads=[B[4 + (g % 2)], "U"], writes=["U"])
                if b == 0:
                    pool_step(0, 504, 0, 8)
                else:
                    pool_step(b, 512, 512 * b - 8, 0)
            dve(lambda e: e.tensor_copy(out=U[:, :, 0:16], in_=U[:, :, 512:528]), reads=["U"], writes=["U"])
            dve(lambda e: e.tensor_copy(out=U[:, :, 16:24], in_=UH[:, :, 8:16]), reads=["UH", "U"], writes=["U"])
            pool_step(NB, 8, WT - 8, 0)
            for t in range(NT):
                proj_resid(t, lambda k, t=t: yT[:, k, t * 128:(t + 1) * 128], [("yT", g) for g in range(4)], 4, wout, "wout")
            barrier()

    def phase_C():
        with contextlib.ExitStack() as S:
            oT = sb(S, "oT", [128, 4, WT], BF16)
            with contextlib.ExitStack() as S2:
                kh = [sb(S2, "kh", [96, SEQ], BF16) for _ in range(2)]
                va = [sb(S2, "va", [128, NTB, 128], BF16) for _ in range(2)]
                qh = [sb(S2, "qh", [96, WT], BF16) for _ in range(2)]
                E2 = [sb(S2, "E", [128, 1024], BF16) for _ in range(2)]
                rec = sb(S2, "rec", [128, 512], F32)
                for p in range(2):
                    dve(lambda e, p=p: e.memset(va[p][:, :, (1 - p) * 64:(1 - p) * 64 + 64], 1.0), writes=[("va1", p)])
                scale = 96 ** -0.5
                NPAIR = NTB // 2
                for h in range(8):
                    p = h % 2
                    lo, hi = p * 64, p * 64 + 64
                    dlo, dhi = (1 - p) * 64, (1 - p) * 64 + 64
                    P.dma("sp", lambda e, h=h, p=p: e.dma_start(out=kh[p][:], in_=kT_scr[h]), writes=[("kh", p)])
                    P.dma("sp", lambda e, h=h, p=p: e.dma_start(out=va[p][:, :, p * 64:p * 64 + 64], in_=v_scr[h]), writes=[("va", p)])
                    P.dma("sp", lambda e, h=h, p=p: e.dma_start(out=qh[p][:], in_=qT_scr[h]), writes=[("qh", p)])
                    for qb in range(NB):
                        ob = 4 + (qb % 2)

                        def S_pair(cc, qb=qb, p=p):
                            for s_ in range(2):
                                c = 2 * cc + s_
                                bk = 2 * (cc % 2) + s_
                                pe(lambda e, c=c, bk=bk: e.matmul(banks[bk], lhsT=kh[p][:, c * 128:(c + 1) * 128], rhs=qh[p][:, qb * 512:(qb + 1) * 512],
                                                                  start=True, stop=True),
                                   reads=[("kh", p), ("qh", p)], writes=[B[bk]])
                        S_pair(0)
                        S_pair(1)
                        for cc in range(NPAIR):
                            pp = cc % 2
                            act(lambda e, pp=pp: e.activation(out=E2[pp][:], in_=psum_all[:, 2 * pp * 512:(2 * pp + 2) * 512], func=AF.Exp, scale=scale),
                                reads=[B[2 * pp], B[2 * pp + 1]], writes=[("E", pp)])
                            for s_ in range(2):
                                c = 2 * cc + s_
                                pe(lambda e, c=c, p=p, ob=ob, pp=pp, s_=s_: e.matmul(banks[ob], lhsT=va[p][:, c, :], rhs=E2[pp][:, s_ * 512:(s_ + 1) * 512],
                                                                                   start=(c == 0), stop=(c == NTB - 1)),
                                   reads=[("va", p), ("va1", p), ("E", pp)], writes=[B[ob]])
                            if cc + 2 < NPAIR:
                                S_pair(cc + 2)
                        dve(lambda e, ob=ob, lo=lo, hi=hi, dlo=dlo, dhi=dhi: e.tensor_copy(out=rec[lo:hi, :], in_=banks[ob][dlo:dhi, :]), reads=[B[ob]], writes=["rec"])
                        dve(lambda e, lo=lo, hi=hi: e.reciprocal(out=rec[lo:hi, :], in_=rec[lo:hi, :]), reads=["rec"], writes=["rec"])
                        dve(lambda e, ob=ob, h=h, qb=qb, lo=lo, hi=hi: e.tensor_tensor(out=oT[lo:hi, h // 2, qb * 512:(qb + 1) * 512], in0=banks[ob][lo:hi, :],
                                                                         in1=rec[lo:hi, :], op=ALU.mult),
                            reads=[B[ob], "rec"], writes=[("oT", h // 2)])
                barrier()
            with contextlib.ExitStack() as S2:
                wout = sb(S2, "wout2", [128, 4, D], BF16)
                load_w(wout, w_out_e[0][512:1024, :], "wout2")
                for t in range(NT):
                    proj_resid(t, lambda k, t=t: oT[:, k, t * 128:(t + 1) * 128], [("oT", g) for g in range(4)], 4, wout, "wout2")
                barrier()

    def phase_xattn(L):
        with contextlib.ExitStack() as S:
            wq = sb(S, "wq", [128, 8, D], BF16)
            wo = sb(S, "wo", [128, 8, D], BF16)
            gq = sb(S, "gq256", [128, 256], F32)
            hTt = sb(S, "hTt", [128, 8, 128], BF16)
            qn = sb(S, "qn", [128, D], F32)
            qb_ = sb(S, "qb", [128, D], BF16)
            qT2 = [sb(S, "qT", [128, 8, 512], BF16) for _ in range(2)]
            E = [sb(S, "E", [128, 512], BF16) for _ in range(2)]
            rec = sb(S, "rec", [128, 512], F32)
            oTb = sb(S, "oTb", [128, 8, 512], BF16)
            load_w(wq, w_mem_q[L], "wq")
            load_w(wo, w_mem_o[L], "wo")
            load_bc(gn[:], xattn_norm_g[L], "gn")
            load_bc(gq[:], mem_q_g[L], "gq")
            scale = 256 ** -0.5

            def block(b):
                qT = qT2[b % 2]
                qTr = "qT%d" % (b % 2)
                for tt in range(4):
                    t = 4 * b + tt
                    norm_hT(x[:, t, :], ("x", t), hTt[:], "hTt")
                    for half in range(2):
                        for k in range(8):
                            pe(lambda e, half=half, k=k: e.matmul(banks[1 + half][:], lhsT=hTt[:, k, :], rhs=wq[:, k, half * 512:(half + 1) * 512],
                                                                   start=(k == 0), stop=(k == 7)),
                               reads=["hTt", ("wq", k)], writes=[B[1 + half]])
                    for hh in range(4):
                        act(lambda e, hh=hh: e.activation(out=sq[:, hh * 256:(hh + 1) * 256], in_=banks[1 + hh // 2][:, (hh % 2) * 256:(hh % 2 + 1) * 256],
                                                          func=AF.Square, accum_out=ss32[:, hh:hh + 1]),
                            reads=[B[1 + hh // 2]], writes=["sq", "ss32"])
                    rstd_from(ss32[:, 0:4], "ss32", 256, rs32[:, 0:4], "rs32")
                    for hh in range(4):
                        dve(lambda e, hh=hh: e.scalar_tensor_tensor(out=qb_[:, hh * 256:(hh + 1) * 256], in0=banks[1 + hh // 2][:, (hh % 2) * 256:(hh % 2 + 1) * 256],
                                                                    scalar=rs32[:, hh:hh + 1], in1=gq[:], op0=ALU.mult, op1=ALU.mult),
                            reads=[B[1 + hh // 2], "rs32", "gq"], writes=["qb"])
                    pT = bbf(0)
                    for j in range(8):
                        pe(lambda e, j=j: e.transpose(out=pT[:, j * 128:(j + 1) * 128], in_=qb_[:, j * 128:(j + 1) * 128], identity=ident[:]),
                           reads=["qb", "ident"], writes=[B[0]])
                    act(lambda e, tt=tt, qT=qT: e.copy(out=qT[:, :, tt * 128:(tt + 1) * 128], in_=pT[:, 0:1024].rearrange("p (j t) -> p j t", j=8)),
                        reads=[B[0]], writes=[qTr])
                P.mark()
                for h in range(4):
                    for m in range(2):
                        for j in range(2):
                            pe(lambda e, h=h, m=m, j=j, qT=qT: e.matmul(banks[3 + m][:], lhsT=memkT[:, h, j, m * 128:(m + 1) * 128], rhs=qT[:, 2 * h + j, :],
                                                                         start=(j == 0), stop=(j == 1)),
                               reads=[qTr, "memkT"], writes=[B[3 + m]])
                        act(lambda e, m=m: e.activation(out=E[m][:], in_=banks[3 + m][:], func=AF.Exp, scale=scale), reads=[B[3 + m]], writes=[("E", m)])
                    for m in range(2):
                        pe(lambda e, m=m: e.matmul(banks[5][:], lhsT=ones_b[:], rhs=E[m][:], start=(m == 0), stop=(m == 1)),
                           reads=["ones_b", ("E", m)], writes=[B[5]])
                    act(lambda e: e.activation(out=rec[:], in_=banks[5][:], func=AF.Ln), reads=[B[5]], writes=["rec"])
                    act(lambda e: e.activation(out=rec[:], in_=rec[:], func=AF.Exp, scale=-1.0), reads=["rec"], writes=["rec"])
                    for dv in range(2):
                        for m in range(2):
                            pe(lambda e, h=h, m=m, dv=dv: e.matmul(banks[6 + dv][:], lhsT=memv[:, m, h, dv * 128:(dv + 1) * 128], rhs=E[m][:],
                                                                    start=(m == 0), stop=(m == 1)),
                               reads=["memv", ("E", m)], writes=[B[6 + dv]])
                        dve(lambda e, h=h, dv=dv: e.tensor_tensor(out=oTb[:, 2 * h + dv, :], in0=banks[6 + dv][:], in1=rec[:], op=ALU.mult),
                            reads=[B[6 + dv], "rec"], writes=["oTb"])
                for tt in range(4):
                    t = 4 * b + tt
                    proj_resid(t, lambda k, tt=tt: oTb[:, k, tt * 128:(tt + 1) * 128], ["oTb"], 8, wo, "wo")

            P.pipeline(NB, block)
            barrier()

    def phase_mlp(L, last=False):
        NPASS = 8
        with contextlib.ExitStack() as S:
            hT = sb(S, "hTall", [128, 8, WT], BF16)
            w1 = [sb(S, "w1", [128, 8, 512], BF16) for _ in range(2)]
            w2 = [sb(S, "w2", [128, 4, D], BF16) for _ in range(2)]
            aT = [sb(S, "aT", [128, 512], BF16) for _ in range(4)]
            s2 = [sb(S, "s2", [128, 512], F32) for _ in range(2)]
            load_bc(gn[:], ff_norm_g[L], "gn")

            def load_pass(ps_):
                b = ps_ % 2
                load_w(w1[b], w_ff1[L][:, ps_ * 512:(ps_ + 1) * 512], ("w1", b))
                load_w(w2[b], w_ff2[L][ps_ * 512:(ps_ + 1) * 512, :], ("w2", b))
            load_pass(0)

            def norm_block(b):
                P.capture()
                for tt in range(4):
                    t = 4 * b + tt
                    norm_hT(x[:, t, :], ("x", t), hT[:, :, t * 128:(t + 1) * 128], ("hT", t // 4), par=t % 2, bank=(0, 3)[t % 2])
                return P.end_capture()

            P.replay(norm_block(0))
            for ps_ in range(NPASS):
                pb = ps_ % 2
                if ps_ + 1 < NPASS:
                    load_pass(ps_ + 1)
                for b in range(NB):
                  if ps_ == 0:
                    P.capture()
                  if True:
                    for f in range(4):
                        zb = 1 + (f % 2)
                        for k in range(8):
                            pe(lambda e, f=f, k=k, zb=zb, pb=pb, b=b: e.matmul(banks[zb][:], lhsT=w1[pb][:, k, f * 128:(f + 1) * 128],
                                                                                 rhs=hT[:, k, b * 512:(b + 1) * 512], start=(k == 0), stop=(k == 7)),
                               reads=[("hT", b), (("w1", pb), k)], writes=[B[zb]])
                        act(lambda e, f=f, zb=zb: e.activation(out=s2[f % 2][:], in_=banks[zb][:], func=AF.Square), reads=[B[zb]], writes=[("s2", f % 2)])
                        dve(lambda e, f=f, zb=zb: e.scalar_tensor_tensor(out=aT[f][:], in0=banks[zb][:], scalar=0.0, in1=s2[f % 2][:],
                                                                          op0=ALU.is_gt, op1=ALU.mult),
                            reads=[B[zb], ("s2", f % 2)], writes=[("aT", f)])
                    for tt in range(4):
                        t = 4 * b + tt
                        proj_resid(t, lambda k, tt=tt: aT[k][:, tt * 128:(tt + 1) * 128], [("aT", f) for f in range(4)], 4, w2[pb], ("w2", pb))
                  if ps_ == 0:
                    Cb = P.end_capture()
                    Nb = norm_block(b + 1) if b + 1 < NB else []
                    P.replay(P.zipmerge(Cb, Nb))
            if last:
                for t in range(NT):
                    fins.append(P.dma("sp", lambda e, t=t: e.dma_start(out=out[t * 128:(t + 1) * 128, :], in_=x[:, t, :]), reads=[("x", t)]))
            barrier()

    def phase_G1():
        with contextlib.ExitStack() as S:
            wqkv = sb(S, "wqkv", [128, 8, 3072], BF16)
            gqk = sb(S, "gqk", [128, 2, 64], F32)
            two = lambda name, shape, dt: [sb(S, name, shape, dt) for _ in range(2)]
            hTt_ = two("hTt", [128, 8, 128], BF16)
            qkn_ = two("qkn", [128, 1024], F32)
            qkb_ = two("qkb", [128, 1024], BF16)
            qkst_ = two("qkst", [128, 8, 128], BF16)
            vst_ = two("vst", [128, 4, 128], BF16)
            load_w(wqkv, w_qkv_o[0], "wqkv")
            load_bc(gn[:], mix_norm_g[1], "gn")
            load_bc(gqk[:, 0, :], na_q_g[0], "gqk0")
            load_bc(gqk[:, 1, :], na_k_g[0], "gqk1")

            def unit(n):
                t, u = n // 2, n % 2
                x_ = "_%d" % u
                sfx = "" if u == 0 else "_p1"
                bT, bQ, bV = 4 * u, (4 * u + 1, 4 * u + 2), 4 * u + 3
                hTt = hTt_[t % 2]
                hr = "hTt_%d" % (t % 2)
                qkn, qkb, qkst, vst = qkn_[u], qkb_[u], qkst_[u], vst_[u]
                sq_, ss32_, rs32_ = SQ[u], SS32[u], RS32[u]
                if u == 0:
                    norm_hT(x[:, t, :], ("x", t), hTt[:], hr, par=0, bank=bT)
                for n2 in range(2):
                    for k in range(8):
                        pe(lambda e, n2=n2, k=k: e.matmul(banks[bQ[n2]][:], lhsT=hTt[:, k, :], rhs=wqkv[:, k, u * 1024 + n2 * 512:u * 1024 + (n2 + 1) * 512],
                                                          start=(k == 0), stop=(k == 7)),
                           reads=[hr, ("wqkv", k)], writes=[B[bQ[n2]]])
                for n2 in range(2):
                    act(lambda e, n2=n2: e.activation(out=sq_[:, n2 * 512:(n2 + 1) * 512], in_=banks[bQ[n2]][:], func=AF.Square),
                        reads=[B[bQ[n2]]], writes=["sq" + sfx])
                dve(lambda e: e.tensor_reduce(out=ss32_[:, 0:16], in_=sq_[:, 0:1024].rearrange("p (h d) -> p h d", h=16), axis=AX.X, op=ALU.add),
                    reads=["sq" + sfx], writes=["ss32" + sfx])
                rstd_from(ss32_[:, 0:16], "ss32" + sfx, 64, rs32_[:, 0:16], "rs32" + sfx)
                for n2 in range(2):
                    dve(lambda e, n2=n2: e.tensor_tensor(out=qkn[:, n2 * 512:(n2 + 1) * 512].rearrange("p (h d) -> p h d", h=8),
                                                         in0=banks[bQ[n2]][:].rearrange("p (h d) -> p h d", h=8),
                                                         in1=rs32_[:, 8 * n2:8 * n2 + 8].unsqueeze(2).to_broadcast([128, 8, 64]), op=ALU.mult),
                        reads=[B[bQ[n2]], "rs32" + sfx], writes=["qkn" + x_])
                pool(lambda e: e.tensor_tensor(out=qkb[:].rearrange("p (h d) -> p h d", h=16), in0=qkn[:].rearrange("p (h d) -> p h d", h=16),
                                               in1=gqk[:, u, :].unsqueeze(1).to_broadcast([128, 16, 64]), op=ALU.mult),
                     reads=["qkn" + x_, "gqk0", "gqk1"], writes=["qkb" + x_])
                for k in range(8):
                    pe(lambda e, k=k: e.matmul(banks[bV][:], lhsT=hTt[:, k, :], rhs=wqkv[:, k, 2048 + u * 512:2048 + (u + 1) * 512],
                                               start=(k == 0), stop=(k == 7)),
                       reads=[hr, ("wqkv", k)], writes=[B[bV]])
                act(lambda e: e.copy(out=vst[:], in_=banks[bV][:].rearrange("p (h d) -> p h d", h=4)), reads=[B[bV]], writes=["vst" + x_])
                P.dma("pool", lambda e: e.dma_start(out=nv_scr[4 * u:4 * u + 4, :, t, :].rearrange("h p d -> p h d"), in_=vst[:]),
                      reads=["vst" + x_], writes=[("nv_scr", n)])
                pT = bbf(bT)
                for j in range(8):
                    pe(lambda e, j=j: e.transpose(out=pT[:, j * 128:(j + 1) * 128], in_=qkb[:, j * 128:(j + 1) * 128], identity=ident[:]),
                       reads=["qkb" + x_, "ident"], writes=[B[bT]])
                act(lambda e: e.copy(out=qkst[:], in_=pT[:, 0:1024].rearrange("p (j t) -> p j t", j=8)), reads=[B[bT]], writes=["qkst" + x_])
                dst = nq_scr if u == 0 else nk_scr
                P.dma("pool", lambda e: e.dma_start(out=dst[:, :, t * 128:(t + 1) * 128].rearrange("h p t -> p h t"), in_=qkst[:]),
                      reads=["qkst" + x_], writes=[("nqk_scr", n)])

            P.pipeline(2 * NT, unit)
            barrier()

    def phase_G2():
        with contextlib.ExitStack() as S:
            cT = sb(S, "cT", [128, 8, WT], BF16)
            with contextlib.ExitStack() as S2:
                qT = sb(S2, "nqT", [128, WT], BF16)
                kT = sb(S2, "nkT", [128, WT], BF16)
                va = [sb(S2, "nva", [128, NT, 128], BF16) for _ in range(2)]
                bias32 = sb(S2, "nbias", [128, 13, 128], F32)
                bh = [sb(S2, "nbh", [128, 25, 128], BF16) for _ in range(2)]
                bl = [sb(S2, "nbl", [128, 25, 128], BF16) for _ in range(2)]
                E = [sb(S2, "nE", [128, 512], BF16) for _ in range(3)]
                bg = [SQ[1][:, 0:640].rearrange("p (a b) -> p a b", a=5), gn[:, 0:640].rearrange("p (a b) -> p a b", a=5)]
                St = [SQ[0][:, 0:512], SQ[0][:, 512:1024]]
                rec = [sb(S2, "rec", [128, 512], F32)] * 2
                dve(lambda e: e.memset(va[0][:, :, 64:128], 1.0), writes=[("va1", 0)])
                dve(lambda e: e.memset(va[1][:, :, 0:64], 1.0), writes=[("va1", 1)])
                scale = 64 ** -0.5
                inv_scale = 8.0

                def row_cls(r):
                    return {0: 1, 2: 2, 36: 3, 38: 4}.get(r, 0)

                def prep_bias(h):
                    p = h % 2
                    P.dma("sp", lambda e, h=h, p=p: e.dma_start(out=bg[p], in_=nab[h][:, 0:5, :]), writes=[("bg", p)])
                    for (v0, v1) in ((0, 13), (13, 25)):
                        nv = v1 - v0
                        P.dma("sp", lambda e, h=h, v0=v0, v1=v1, nv=nv: e.dma_start(out=bias32[:, 0:nv, :], in_=nab[h][:, v0:v1, :]), writes=["bias32"])
                        dve(lambda e, p=p, v0=v0, v1=v1, nv=nv: e.tensor_scalar(out=bh[p][:, v0:v1, :], in0=bias32[:, 0:nv, :], scalar1=inv_scale, scalar2=None, op0=ALU.mult),
                            reads=["bias32"], writes=[("bh", p)])
                        dve(lambda e, p=p, v0=v0, v1=v1, nv=nv: e.scalar_tensor_tensor(out=bl[p][:, v0:v1, :].rearrange("p a b -> p (a b)"),
                                                                                      in0=bias32[:, 0:nv, :].rearrange("p a b -> p (a b)"), scalar=inv_scale,
                                                                                      in1=bh[p][:, v0:v1, :].rearrange("p a b -> p (a b)"), op0=ALU.mult, op1=ALU.subtract),
                            reads=["bias32", ("bh", p)], writes=[("bl", p)])

                prep_bias(0)
                for hp in range(8):
                    P.dma("sp", lambda e, hp=hp: e.dma_start(out=qT[:], in_=nq_scr[hp]), writes=["nqT"])
                    P.dma("sp", lambda e, hp=hp: e.dma_start(out=kT[:], in_=nk_scr[hp]), writes=["nkT"])
                    P.dma("sp", lambda e, hp=hp: e.dma_start(out=va[0][:, :, 0:64], in_=nv_scr[hp][:, :, 0:64]), writes=[("va", 0)])
                    P.dma("sp", lambda e, hp=hp: e.dma_start(out=va[1][:, :, 64:128], in_=nv_scr[hp][:, :, 64:128]), writes=[("va", 1)])
                    for p in range(2):
                        h = 2 * hp + p
                        lo, hi = p * 64, p * 64 + 64
                        dlo, dhi = (1 - p) * 64, (1 - p) * 64 + 64
                        items = [(blk, j) for blk in range(NB) for j in range(5)]

                        def S_stage(i, p=p, lo=lo, hi=hi):
                            blk, j = items[i]
                            sbk = i % 3
                            groups = []
                            for rp in range(4):
                                c = row_cls(8 * blk + 2 * rp)
                                if groups and groups[-1][0] == c:
                                    groups[-1][2] += 1
                                else:
                                    groups.append([c, rp, 1])
                            first = True
                            use_dve = (len(groups) == 1 and i % 2 == 1)
                            for src in (() if use_dve else (bh, bl)):
                                for (c, rp0, n) in groups:
                                    pe(lambda e, src=src, c=c, rp0=rp0, n=n, j=j, sbk=sbk, first=first: e.matmul(
                                            banks[sbk][:, rp0 * 128:(rp0 + n) * 128], lhsT=ident[:],
                                            rhs=src[p][:, c * 5 + j, :].unsqueeze(1).to_broadcast([128, n, 128]),
                                            start=first, stop=False, skip_group_check=True),
                                       reads=[("bh", p), ("bl", p), "ident"], writes=[B[sbk]])
                                    first = False
                            for rp in range(4):
                                r = 8 * blk + 2 * rp
                                tb = min(max(r - 4, 0), 30)
                                kt0 = (tb + 2 * j) * 64
                                pe(lambda e, kt0=kt0, r=r, sbk=sbk, rp=rp: e.matmul(banks[sbk][:, rp * 128:(rp + 1) * 128], lhsT=kT[lo:hi, kt0:kt0 + 128],
                                                                                   rhs=qT[lo:hi, r * 64:r * 64 + 128], start=(use_dve and rp == 0), stop=(rp == 3), skip_group_check=True),
                                   reads=["nkT", "nqT"], writes=[B[sbk]])

                        def mid_stage(i, p=p):
                            sbk = i % 3
                            blk, j = items[i]
                            cls = set(row_cls(8 * blk + 2 * rp) for rp in range(4))
                            if len(cls) == 1 and i % 2 == 1:
                                st = St[(i // 2) % 2]
                                sr = ("St", (i // 2) % 2)
                                dve(lambda e, sbk=sbk, st=st, j=j: e.scalar_tensor_tensor(out=st.rearrange("p (a b) -> p a b", a=4),
                                                                                        in0=banks[sbk][:].rearrange("p (a b) -> p a b", a=4), scalar=scale,
                                                                                        in1=bg[p][:, j, :].unsqueeze(1).to_broadcast([128, 4, 128]), op0=ALU.mult, op1=ALU.add),
                                    reads=[B[sbk], ("bg", p)], writes=[sr])
                                act(lambda e, i=i, st=st: e.activation(out=E[i % 3][:], in_=st, func=AF.Exp), reads=[sr], writes=[("nE", i % 3)])
                            else:
                                act(lambda e, i=i, sbk=sbk: e.activation(out=E[i % 3][:], in_=banks[sbk][:], func=AF.Exp, scale=scale),
                                    reads=[B[sbk]], writes=[("nE", i % 3)])

                        def PV_stage(i, p=p, lo=lo, hi=hi, dlo=dlo, dhi=dhi, hp=hp):
                            blk, j = items[i]
                            ob = 3 + (blk % 2)
                            for rp in range(4):
                                r = 8 * blk + 2 * rp
                                tb = min(max(r - 4, 0), 30)
                                vt = (tb + 2 * j) // 2
                                pe(lambda e, vt=vt, i=i, ob=ob, rp=rp, j=j: e.matmul(banks[ob][:, rp * 128:(rp + 1) * 128], lhsT=va[p][:, vt, :],
                                                                                    rhs=E[i % 3][:, rp * 128:(rp + 1) * 128],
                                                                                    start=(j == 0 and rp == 0), stop=(j == 4), skip_group_check=True),
                                   reads=[("va", p), ("va1", p), ("nE", i % 3)], writes=[B[ob]])
                            if j == 4:
                                for step in range(3):
                                    pending.append((i + 1 + step, lambda blk=blk, ob=ob, step=step: epilogue(blk, ob, step)))

                        def epilogue(blk, ob, step, p=p, lo=lo, hi=hi, dlo=dlo, dhi=dhi, hp=hp):
                            rc = rec[blk % 2]
                            rr = ("rec", 0)
                            if step == 0:
                                dve(lambda e, ob=ob, rc=rc: e.tensor_copy(out=rc[lo:hi, :], in_=banks[ob][dlo:dhi, :]), reads=[B[ob]], writes=[rr])
                            elif step == 1:
                                act(lambda e, rc=rc: e.activation(out=rc[lo:hi, :], in_=rc[lo:hi, :], func=AF.Ln), reads=[rr], writes=[rr])
                                act(lambda e, rc=rc: e.activation(out=rc[lo:hi, :], in_=rc[lo:hi, :], func=AF.Exp, scale=-1.0), reads=[rr], writes=[rr])
                            else:
                                dve(lambda e, ob=ob, rc=rc, blk=blk: e.tensor_tensor(out=cT[lo:hi, hp, blk * 512:(blk + 1) * 512], in0=banks[ob][lo:hi, :],
                                                                                    in1=rc[lo:hi, :], op=ALU.mult),
                                    reads=[B[ob], rr], writes=[("cT", hp)])

                        n_it = len(items)
                        pending = []
                        S_stage(0)
                        S_stage(1)
                        for i in range(n_it):
                            mid_stage(i)
                            PV_stage(i)
                            if i + 2 < n_it:
                                S_stage(i + 2)
                            if i == 4 and h + 1 < 16:
                                prep_bias(h + 1)
                            pending.sort(key=lambda q: q[0])
                            while pending and pending[0][0] <= i:
                                pending.pop(0)[1]()
                        pending.sort(key=lambda q: q[0])
                        while pending:
                            pending.pop(0)[1]()
                barrier()
            with contextlib.ExitStack() as S2:
                wout = sb(S2, "wouto", [128, 8, D], BF16)
                load_w(wout, w_out_o[0], "wouto")
                for t in range(NT):
                    proj_resid(t, lambda k, t=t: cT[:, k, t * 128:(t + 1) * 128], [("cT", g) for g in range(8)], 8, wout, "wouto")
                barrier()

    fins = []
    import os
    PH = os.environ.get("PHASES", "A,B1,B2,C").split(",")
    if stage >= 1:
        if "A" in PH:
            phase_A()
        if "B1" in PH:
            phase_B1()
        if "B2" in PH:
            phase_B2()
        if "C" in PH:
            phase_C()
    if stage >= 2:
        phase_mem()
        phase_xattn(0)
    if stage >= 3:
        phase_mlp(0)
    if stage >= 4:
        phase_G1()
        phase_G2()
    if stage >= 5:
        phase_xattn(1)
        phase_mlp(1, last=True)
    if not fins:
        for t in range(NT):
            fins.append(P.dma("sp", lambda e, t=t: e.dma_start(out=out[t * 128:(t + 1) * 128, :], in_=x[:, t, :]), reads=[("x", t)]))
    P.emit(nc, final_ops=fins)
    return nc, P


def _rope_table(pos):
    half = 16
    freqs = (np.float32(10000.0) ** (-np.arange(half, dtype=np.float32) / np.float32(half))).astype(np.float32)
    ang = (pos.astype(np.float32)[:, None] * freqs[None, :]).astype(np.float32)
    c = np.cos(ang).astype(np.float32)
    s = np.sin(ang).astype(np.float32)
    return np.concatenate([c, c, -s, s], axis=1).astype(np.float32)


def _invc_table(a_tok):
    tab = np.ones((4, 2568), np.float32)
    tg = a_tok + np.arange(2568) - 8
    valid = (tg >= 0) & (tg < SEQ)
    for g, w in enumerate((2, 4, 8, 16)):
        lo = np.clip(tg - w // 2, 0, SEQ - 1)
        hi = np.clip(tg + w - 1 - w // 2, 0, SEQ - 1)
        cnt = (hi - lo + 1).astype(np.float32)
        tab[g] = np.where(valid, np.float32(1.0) / cnt, np.float32(1.0))
    return tab


def _natten_bias(rpb):
    H = rpb.shape[0]
    outb = np.full((H, 25, 128, 128), NEG, np.float32)
    cols = np.arange(64)
    c0 = np.clip(cols - 8, 0, 48)
    classes = {0: 8, 1: 0, 2: 2, 3: 36, 4: 38}
    kc = np.arange(64)[:, None]
    qc = np.arange(64)[None, :]
    colvalid = (kc >= c0[None, :]) & (kc < c0[None, :] + 16)
    dc = np.clip(kc - qc + 15, 0, 30)
    for cls, r in classes.items():
        tb = min(max(r - 4, 0), 30)
        for j in range(5):
            for kr_i in range(2):
                kr = tb + 2 * j + kr_i
                for qr_i in range(2):
                    qr = r + qr_i
                    r0 = min(max(qr - 4, 0), 32)
                    if not (r0 <= kr <= r0 + 7):
                        continue
                    dr = kr - qr + 7
                    vals = rpb[:, dr][:, dc]
                    blk = np.where(colvalid[None], vals, np.float32(NEG))
                    outb[:, cls * 5 + j, kr_i * 64:(kr_i + 1) * 64, qr_i * 64:(qr_i + 1) * 64] = blk
    return np.ascontiguousarray(outb.transpose(0, 2, 1, 3))


def make_in_maps(inputs):
    f = lambda a: np.ascontiguousarray(np.asarray(a, dtype=np.float32))
    xfull = f(inputs["x"])
    memf = f(inputs["mem"])
    shared = {k: f(v) for k, v in inputs.items() if k not in ("x", "mem", "na_rpb")}
    nabt = _natten_bias(f(inputs["na_rpb"])[0])
    pos_b = np.arange(SEQ)
    cs_b = _rope_table(pos_b)
    maps, meta = [], []
    for c in range(8):
        b, j = c // 4, c % 4
        a = min(max(32 * j - 4, 0), 88)
        a_tok = a * 64
        xw = xfull[b, a_tok:a_tok + WT]
        xh = np.zeros((16, D), np.float32)
        if a_tok >= 8:
            xh[0:8] = xfull[b, a_tok - 8:a_tok]
        if a_tok + WT + 8 <= SEQ:
            xh[8:16] = xfull[b, a_tok + WT:a_tok + WT + 8]
        m = dict(shared)
        m.update(xw=np.ascontiguousarray(xw), xh=xh, xb=xfull[b], mem=memf[b],
                 cs_w=np.ascontiguousarray(cs_b[a_tok:a_tok + WT]), cs_b=cs_b, invc=_invc_table(a_tok), nab=nabt)
        maps.append(m)
        meta.append((b, a, 32 * j - a))
    return maps, meta


_CACHE = {}


def kernel(**inputs):
    if "nc" not in _CACHE:
        _CACHE["nc"] = build_program(99)[0]
    nc = _CACHE["nc"]
    maps, meta = make_in_maps(inputs)
    res = run_bass_kernel_spmd(nc, maps, core_ids=list(range(8)))
    outp = np.zeros((2, SEQ, D), np.float32)
    for c in range(8):
        b, a, off = meta[c]
        o = np.asarray(res.results[c]["out"]).reshape(WT, D)
        j = c % 4
        outp[b, j * 2048:(j + 1) * 2048] = o[off * 64:off * 64 + 2048]
    return outp
```

```python
import concourse.bass as bass
import concourse.mybir as mybir

ENGS = ("pe", "act", "dve", "pool", "sp")
SAME_ENG_SYNC = {"pe": False, "act": True, "dve": True, "pool": True, "sp": False}
NSLOT = 12


class Op:
    __slots__ = ("id", "eng", "fn", "deps", "dma", "flag", "sem", "val", "slot_guard", "nwaits")

    def __init__(self, id, eng, fn, dma):
        self.id = id
        self.eng = eng
        self.fn = fn
        self.dma = dma
        self.deps = []
        self.flag = False
        self.sem = None
        self.val = None
        self.slot_guard = None


class Prog:
    def __init__(self):
        self.ops = []
        self.by_eng = {e: [] for e in ENGS}
        self.last_w = {}
        self.readers = {}
        self.dma_count = {e: 0 for e in ENGS}

    _cap = None

    def capture(self):
        self._cap = []

    def end_capture(self):
        c = self._cap
        self._cap = None
        return c

    def mark(self):
        self._mark = len(self._cap)

    def replay(self, lst):
        for it in lst:
            self.op(*it)

    @staticmethod
    def zipmerge(a, b):
        out = []
        na, nb = len(a), len(b)
        ia = ib = 0
        while ia < na or ib < nb:
            if ib >= nb or (ia < na and ia * max(nb, 1) <= ib * max(na, 1)):
                out.append(a[ia]); ia += 1
            else:
                out.append(b[ib]); ib += 1
        return out

    def pipeline_deep(self, n, tile_fn, depth):
        parts = {}
        for s in range(n + depth - 1):
            if s < n:
                self.capture()
                tile_fn(s)
                L = self.end_capture()
                m = len(L)
                cuts = [int(round(m * i / depth)) for i in range(depth + 1)]
                parts[s] = [L[cuts[i]:cuts[i + 1]] for i in range(depth)]
            merged = []
            for d in range(depth - 1, -1, -1):
                t = s - d
                if 0 <= t < n:
                    merged = self.zipmerge(merged, parts[t][d]) if merged else list(parts[t][d])
            self.replay(merged)
            if s - depth + 1 in parts and s - depth + 1 >= 0:
                del parts[s - depth + 1]

    def pipeline(self, n, tile_fn, split=0.5):
        prevB = []
        for t in range(n):
            self.capture()
            self._mark = None
            tile_fn(t)
            L = self.end_capture()
            h = self._mark if self._mark is not None else int(len(L) * split)
            self.replay(self.zipmerge(L[:h], prevB))
            prevB = L[h:]
        self.replay(prevB)

    def op(self, eng, fn, reads=(), writes=(), dma=False):
        if self._cap is not None:
            self._cap.append((eng, fn, reads, writes, dma))
            return None
        o = Op(len(self.ops), eng, fn, dma)
        reads = list(reads)
        writes = list(writes) + [r for r in reads if isinstance(r, str) and r.startswith("bank")]
        deps = set()
        for r in reads:
            w = self.last_w.get(r)
            if w is not None:
                deps.add(w)
        for w_ in writes:
            w = self.last_w.get(w_)
            if w is not None:
                deps.add(w)
            for rd in self.readers.get(w_, ()):
                deps.add(rd)
        deps.discard(o.id)
        o.deps = sorted(deps)
        for r in reads:
            self.readers.setdefault(r, []).append(o.id)
        for w_ in writes:
            self.last_w[w_] = o.id
            self.readers[w_] = []
        self.ops.append(o)
        self.by_eng[eng].append(o)
        return o

    def pe(self, fn, reads=(), writes=()):
        return self.op("pe", fn, reads, writes)

    def act(self, fn, reads=(), writes=()):
        return self.op("act", fn, reads, writes)

    def dve(self, fn, reads=(), writes=()):
        return self.op("dve", fn, reads, writes)

    def pool(self, fn, reads=(), writes=()):
        return self.op("pool", fn, reads, writes)

    def dma(self, eng, fn, reads=(), writes=()):
        return self.op(eng, fn, reads, writes, dma=True)

    def emit(self, nc, final_ops=()):
        ops = self.ops
        for o in ops:
            for d in o.deps:
                do = ops[d]
                if do.dma or do.eng != o.eng or SAME_ENG_SYNC[o.eng]:
                    do.flag = True
        for o in final_ops:
            o.flag = True
        import contextlib
        with contextlib.ExitStack() as es:
            esem = {e: es.enter_context(nc.semaphore("s_" + e)) for e in ENGS}
            dsem = {}
            for e in ENGS:
                if self.dma_count_total(e) > 0:
                    dsem[e] = [es.enter_context(nc.semaphore("d_%s_%d" % (e, i))) for i in range(NSLOT)]
            cnt = {e: 0 for e in ENGS}
            dcnt = {e: 0 for e in ENGS}
            semkey = {}
            for o in ops:
                if o.dma:
                    j = dcnt[o.eng]
                    dcnt[o.eng] += 1
                    s = dsem[o.eng][j % NSLOT]
                    o.sem = s
                    o.val = 16 * (j // NSLOT + 1)
                    o.slot_guard = (s, 16 * (j // NSLOT)) if j >= NSLOT else None
                    semkey[id(s)] = ("d", o.eng, j % NSLOT)
                elif o.flag:
                    cnt[o.eng] += 1
                    o.sem = esem[o.eng]
                    o.val = cnt[o.eng]
            clock = {e: {} for e in ENGS}
            opvc = [None] * len(ops)
            plan = {e: [] for e in ENGS}
            for o in ops:
                ck = clock[o.eng]
                waits = {}
                for d in o.deps:
                    do = ops[d]
                    if not (do.dma or do.eng != o.eng or SAME_ENG_SYNC[o.eng]):
                        continue
                    k = id(do.sem)
                    if ck.get(k, 0) >= do.val:
                        continue
                    if k not in waits or waits[k][1] < do.val:
                        waits[k] = (do.sem, do.val, d)
                if o.slot_guard is not None:
                    s, v = o.slot_guard
                    k = id(s)
                    if ck.get(k, 0) < v and (k not in waits or waits[k][1] < v):
                        waits[k] = (s, v, None)
                wl = list(waits.items())
                keep = []
                for k, (s, v, d) in wl:
                    implied = False
                    for k2, (s2, v2, d2) in wl:
                        if k2 == k or d2 is None:
                            continue
                        vc2 = opvc[d2]
                        if vc2 is not None and vc2.get(k, 0) >= v:
                            implied = True
                            break
                    if not implied:
                        keep.append((s, v))
                for k, (s, v, d) in wl:
                    if ck.get(k, 0) < v:
                        ck[k] = v
                    if d is not None and opvc[d] is not None:
                        for kk, vv in opvc[d].items():
                            if ck.get(kk, 0) < vv:
                                ck[kk] = vv
                if o.sem is not None:
                    vc = dict(ck)
                    vc[id(o.sem)] = o.val
                    opvc[o.id] = vc
                    if not o.dma:
                        pass
                plan[o.eng].append((o, keep))
            self.stats = {e: (len(plan[e]), sum(len(k) for _, k in plan[e])) for e in ENGS}

            def run(eng_handle, lst, final_waits):
                for o, keep in lst:
                    for (s, v) in keep[1:]:
                        eng_handle.wait_ge(s, v)
                    ins = o.fn(eng_handle)
                    if keep:
                        s, v = keep[0]
                        if isinstance(ins, tuple):
                            ins[0]._wait_ge(s, v)
                        else:
                            ins._wait_ge(s, v)
                    if o.sem is not None:
                        last = ins[1] if isinstance(ins, tuple) else ins
                        last.then_inc(o.sem, 16 if o.dma else 1)
                for (s, v) in final_waits:
                    eng_handle.wait_ge(s, v)

            fw = [(o.sem, o.val) for o in final_ops]
            with nc.Block() as block:
                @block.tensor
                def _(e):
                    run(e, plan["pe"], [])

                @block.scalar
                def _(e):
                    run(e, plan["act"], [])

                @block.vector
                def _(e):
                    run(e, plan["dve"], [])

                @block.gpsimd
                def _(e):
                    run(e, plan["pool"], [])

                @block.sync
                def _(e):
                    run(e, plan["sp"], fw)

    def dma_count_total(self, e):
        return sum(1 for o in self.by_eng[e] if o.dma)


import contextlib
import numpy as np
from concourse.bass_utils import run_bass_kernel_spmd

F32 = mybir.dt.float32
BF16 = mybir.dt.bfloat16
AF = mybir.ActivationFunctionType
ALU = mybir.AluOpType
AX = mybir.AxisListType

D = 1024
SEQ = 8192
WT = 2560
NT = 20
NB = 5
NTB = 64
EPS = 1e-6
NEG = -30000.0


def build_program(stage=99):
    nc = bass.Bass("TRN2", target_bir_lowering=False)
    P = Prog()
    G = contextlib.ExitStack()
    uid = [0]
    bar_from = [0]

    def di(name, shape, dt=F32):
        return nc.dram_tensor(name, list(shape), dt, kind="ExternalInput").ap()

    def sb(st, name, shape, dt):
        uid[0] += 1
        return st.enter_context(nc.sbuf_tensor("%s_%d" % (name, uid[0]), list(shape), dt))

    def barrier():
        lasts = [P.by_eng[e][-1].id for e in ENGS if P.by_eng[e]]
        dmas = [o.id for o in P.ops[bar_from[0]:] if o.dma]
        bar_from[0] = len(P.ops)
        deps = sorted(set(lasts + dmas))
        for e in ENGS:
            o = P.op(e, lambda eng: eng.nop())
            o.deps = list(deps)
        P.last_w = {}
        P.readers = {}

    xw = di("xw", [WT, D]); xh = di("xh", [16, D]); xb = di("xb", [SEQ, D]); mem = di("mem", [256, D])
    mix_norm_g = di("mix_norm_g", [2, D]); xattn_norm_g = di("xattn_norm_g", [2, D]); ff_norm_g = di("ff_norm_g", [2, D])
    w_mem_q = di("w_mem_q", [2, D, D]); mem_q_g = di("mem_q_g", [2, 256]); w_mem_o = di("w_mem_o", [2, D, D])
    w_ff1 = di("w_ff1", [2, D, 4096]); w_ff2 = di("w_ff2", [2, 4096, D])
    mem_tok_norm_g = di("mem_tok_norm_g", [D]); w_mem_kv = di("w_mem_kv", [D, 2048]); mem_k_g = di("mem_k_g", [256])
    w_in_e = di("w_in_e", [1, D, 928]); pool_w = di("pool_w", [1, 4, 128, 128]); pool_scale = di("pool_scale", [1, 512])
    q_lora_g = di("q_lora_g", [1, 256]); w_uq = di("w_uq", [1, 256, 768]); kv_lora_g = di("kv_lora_g", [1, 128])
    w_ukv = di("w_ukv", [1, 128, 1024]); mla_q_g = di("mla_q_g", [1, 96]); mla_k_g = di("mla_k_g", [1, 96])
    w_out_e = di("w_out_e", [1, D, D]); w_qkv_o = di("w_qkv_o", [1, D, 3072])
    na_q_g = di("na_q_g", [1, 64]); na_k_g = di("na_k_g", [1, 64]); w_out_o = di("w_out_o", [1, D, D])
    cs_w = di("cs_w", [WT, 64]); cs_b = di("cs_b", [SEQ, 64]); invc = di("invc", [4, 2568])
    nab = di("nab", [16, 128, 25, 128])
    out = nc.dram_tensor("out", [WT, D], F32, kind="ExternalOutput").ap()
    kT_scr = nc.dram_tensor("kT_scr", [8, 96, SEQ], BF16, kind="Internal").ap()
    v_scr = nc.dram_tensor("v_scr", [8, 128, NTB, 64], BF16, kind="Internal").ap()
    qT_scr = nc.dram_tensor("qT_scr", [8, 96, WT], BF16, kind="Internal").ap()
    nq_scr = nc.dram_tensor("nq_scr", [8, 128, WT], BF16, kind="Internal").ap()
    nk_scr = nc.dram_tensor("nk_scr", [8, 128, WT], BF16, kind="Internal").ap()
    nv_scr = nc.dram_tensor("nv_scr", [8, 128, NT, 128], BF16, kind="Internal").ap()

    x = sb(G, "x", [128, NT, D], F32)
    ident = sb(G, "ident", [128, 128], BF16)
    identf = sb(G, "identf", [128, 128], F32)
    ones_b = sb(G, "ones_b", [128, 128], BF16)
    eps_t = sb(G, "eps", [128, 1], F32)
    memkT = sb(G, "memkT", [128, 4, 2, 256], BF16)
    memv = sb(G, "memv", [128, 2, 4, 256], BF16)
    SQ = [sb(G, "sq", [128, 1024], F32), sb(G, "sq", [128, 1024], F32)]
    SSQ = [sb(G, "ssq", [128, 1], F32) for _ in range(2)]
    RSTD = [sb(G, "rstd", [128, 1], F32) for _ in range(2)]
    HB = [sb(G, "hb", [128, D], BF16) for _ in range(2)]
    SS32 = [sb(G, "ss32", [128, 32], F32) for _ in range(2)]
    RS32 = [sb(G, "rs32", [128, 32], F32) for _ in range(2)]
    sq, ssq, rstd, hb, ss32, rs32 = SQ[0], SSQ[0], RSTD[0], HB[0], SS32[0], RS32[0]
    gn = sb(G, "gn", [128, D], F32)
    psum_all = G.enter_context(nc.psum_tensor("psum_all", [128, 4096], F32))
    banks = [psum_all[:, i * 512:(i + 1) * 512] for i in range(8)]
    B = ["bank%d" % i for i in range(8)]

    def bbf(i):
        return banks[i][:].bitcast(BF16)

    act, dve, pe, pool = P.act, P.dve, P.pe, P.pool

    def rstd_from(ss_ap, ss_res, dim, out_ap, out_res):
        n = ss_ap.shape[0]
        act(lambda e: e.activation(out=out_ap, in_=ss_ap, func=AF.Ln, scale=1.0 / dim, bias=eps_t[0:n, :]),
            reads=[ss_res, "eps"], writes=[out_res])
        act(lambda e: e.activation(out=out_ap, in_=out_ap, func=AF.Exp, scale=-0.5), reads=[out_res], writes=[out_res])

    def norm_hT(src_ap, src_res, hT_dst, hT_res, n=128, par=0, bank=0, ws=None):
        if ws is None:
            sq_, ssq_, rstd_, hb_ = SQ[par], SSQ[par], RSTD[par], HB[par]
            sfx = "" if par == 0 else "_p1"
            hsfx = sfx
        else:
            sq_, ssq_, rstd_, hb_, sfx, hsfx = ws
        act(lambda e: e.activation(out=sq_[0:n, 0:D], in_=src_ap, func=AF.Square, accum_out=ssq_[0:n, :]),
            reads=[src_res], writes=["sq" + sfx, "ssq" + sfx])
        rstd_from(ssq_[0:n, :], "ssq" + sfx, D, rstd_[0:n, :], "rstd" + sfx)
        dve(lambda e: e.scalar_tensor_tensor(out=hb_[0:n, :], in0=src_ap, scalar=rstd_[0:n, 0:1], in1=gn[0:n, :],
                                             op0=ALU.mult, op1=ALU.mult),
            reads=[src_res, "rstd" + sfx, "gn"], writes=["hb" + hsfx])
        pT = bbf(bank)
        for k in range(8):
            pe(lambda e, k=k: e.transpose(out=pT[:, k * n:(k + 1) * n], in_=hb_[0:n, k * 128:(k + 1) * 128],
                                          identity=ident[0:n, 0:n]),
               reads=["hb" + hsfx, "ident"], writes=[B[bank]])
        act(lambda e: e.copy(out=hT_dst, in_=pT[:, 0:8 * n].rearrange("p (k t) -> p k t", k=8)),
            reads=[B[bank]], writes=[hT_res])

    def load_w(dst, w_ap, res, eng="pool"):
        K = w_ap.shape[0]
        for k in range((K + 127) // 128):
            r = min(128, K - k * 128)
            P.dma(eng, lambda e, k=k, r=r: e.dma_start(out=dst[0:r, k, :], in_=w_ap[k * 128:k * 128 + r, :]),
                  writes=[(res, k)])

    def wres(res, nk):
        return [(res, k) for k in range(nk)]

    def load_bc(dst, vec_ap, res, eng="sp"):
        P.dma(eng, lambda e: e.dma_start(out=dst, in_=vec_ap.partition_broadcast(128)), writes=[res])

    def resid_add(t, pbank_lo, pbank_hi):
        for half, bk in ((0, pbank_lo), (1, pbank_hi)):
            dve(lambda e, half=half, bk=bk: e.tensor_tensor(out=x[:, t, half * 512:(half + 1) * 512],
                                                            in0=banks[bk][:], in1=x[:, t, half * 512:(half + 1) * 512],
                                                            op=ALU.add),
                reads=[B[bk], ("x", t)], writes=[("x", t)])

    def proj_resid(t, lhs_fn, lhs_res, nk, w_sb, w_res, k0=0):
        for half in range(2):
            for k in range(nk):
                pe(lambda e, half=half, k=k: e.matmul(banks[6 + half][:], lhsT=lhs_fn(k), rhs=w_sb[:, k0 + k, half * 512:(half + 1) * 512],
                                                       start=(k == 0), stop=(k == nk - 1)),
                   reads=lhs_res + [(w_res, k0 + k)], writes=[B[6 + half]])
        resid_add(t, 6, 7)

    pool(lambda e: e.memset(identf[:], 0.0), writes=["identf"])
    pool(lambda e: e.affine_select(out=identf[:], in_=identf[:], pattern=[[-1, 128]], compare_op=ALU.not_equal,
                                   fill=1.0, base=0, channel_multiplier=1), reads=["identf"], writes=["identf"])
    dve(lambda e: e.tensor_copy(out=ident[:], in_=identf[:]), reads=["identf"], writes=["ident"])
    dve(lambda e: e.memset(ones_b[:], 1.0), writes=["ones_b"])
    dve(lambda e: e.memset(eps_t[:], EPS), writes=["eps"])
    for t in range(NT):
        P.dma("sp", lambda e, t=t: e.dma_start(out=x[:, t, :], in_=xw[t * 128:(t + 1) * 128, :]), writes=[("x", t)])

    def phase_mem():
        with contextlib.ExitStack() as S:
            wkv = sb(S, "wkv", [128, 8, 2048], BF16)
            gk = sb(S, "gk", [128, 256], F32)
            memt = sb(S, "memt", [128, D], F32)
            hTm = sb(S, "hTm", [128, 8, 128], BF16)
            kn = sb(S, "kn", [128, D], F32)
            kbm = sb(S, "kbm", [128, D], BF16)
            load_w(wkv, w_mem_kv, "wkv")
            load_bc(gn[:], mem_tok_norm_g, "gn")
            load_bc(gk[:], mem_k_g, "gk")
            for m in range(2):
                P.dma("sp", lambda e, m=m: e.dma_start(out=memt[:], in_=mem[m * 128:(m + 1) * 128, :]), writes=["memt"])
                norm_hT(memt[:], "memt", hTm[:], "hTm")
                for n4 in range(4):
                    for k in range(8):
                        pe(lambda e, n4=n4, k=k: e.matmul(banks[1 + n4][:], lhsT=hTm[:, k, :], rhs=wkv[:, k, n4 * 512:(n4 + 1) * 512],
                                                          start=(k == 0), stop=(k == 7)),
                           reads=["hTm", ("wkv", k)], writes=[B[1 + n4]])
                for j in range(2):
                    act(lambda e, j=j: e.activation(out=sq[:, j * 512:(j + 1) * 512], in_=banks[1 + j][:], func=AF.Square),
                        reads=[B[1 + j]], writes=["sq"])
                dve(lambda e: e.tensor_reduce(out=ss32[:, 0:4], in_=sq[:, 0:D].rearrange("p (h d) -> p h d", h=4), axis=AX.X, op=ALU.add),
                    reads=["sq"], writes=["ss32"])
                rstd_from(ss32[:, 0:4], "ss32", 256, rs32[:, 0:4], "rs32")
                for j in range(2):
                    dve(lambda e, j=j: e.tensor_tensor(out=kn[:, j * 512:(j + 1) * 512].rearrange("p (h d) -> p h d", h=2),
                                                       in0=banks[1 + j][:].rearrange("p (h d) -> p h d", h=2),
                                                       in1=rs32[:, 2 * j:2 * j + 2].unsqueeze(2).to_broadcast([128, 2, 256]), op=ALU.mult),
                        reads=[B[1 + j], "rs32"], writes=["kn"])
                dve(lambda e: e.tensor_tensor(out=kbm[:].rearrange("p (h d) -> p h d", h=4), in0=kn[:].rearrange("p (h d) -> p h d", h=4),
                                              in1=gk[:].unsqueeze(1).to_broadcast([128, 4, 256]), op=ALU.mult),
                    reads=["kn", "gk"], writes=["kbm"])
                pT = bbf(0)
                for j in range(8):
                    pe(lambda e, j=j: e.transpose(out=pT[:, j * 128:(j + 1) * 128], in_=kbm[:, j * 128:(j + 1) * 128], identity=ident[:]),
                       reads=["kbm", "ident"], writes=[B[0]])
                act(lambda e, m=m: e.copy(out=memkT[:, :, :, m * 128:(m + 1) * 128],
                                          in_=pT[:, 0:1024].rearrange("p (h j t) -> p h j t", h=4, j=2)),
                    reads=[B[0]], writes=["memkT"])
                for j in range(2):
                    act(lambda e, m=m, j=j: e.copy(out=memv[:, m, 2 * j:2 * j + 2, :], in_=banks[3 + j][:].rearrange("p (h d) -> p h d", h=2)),
                        reads=[B[3 + j]], writes=["memv"])
            barrier()

    def rotary(src, src_res, cst, cs_res, dst, dst_res, t1, t2, shape3, sfx=""):
        H = shape3
        cc = cst[:, 0:32].unsqueeze(1).to_broadcast([128, H, 32])
        ns = cst[:, 32:48].unsqueeze(1).to_broadcast([128, H, 16])
        ps_ = cst[:, 48:64].unsqueeze(1).to_broadcast([128, H, 16])
        dve(lambda e: e.tensor_tensor(out=t1, in0=src, in1=cc, op=ALU.mult), reads=[src_res, cs_res], writes=["rot_t1" + sfx])
        dve(lambda e: e.tensor_tensor(out=t2[:, :, 0:16], in0=src[:, :, 16:32], in1=ns, op=ALU.mult), reads=[src_res, cs_res], writes=["rot_t2a" + sfx])
        dve(lambda e: e.tensor_tensor(out=t2[:, :, 16:32], in0=src[:, :, 0:16], in1=ps_, op=ALU.mult), reads=[src_res, cs_res], writes=["rot_t2b" + sfx])
        dve(lambda e: e.tensor_tensor(out=dst, in0=t1, in1=t2, op=ALU.add), reads=["rot_t1" + sfx, "rot_t2a" + sfx, "rot_t2b" + sfx], writes=[dst_res])

    def phase_A():
        NP = 4
        with contextlib.ExitStack() as S:
            w_kv = sb(S, "w_kv", [128, 8, 160], BF16)
            wukv = sb(S, "wukv", [128, 1, 1024], BF16)
            gkv = sb(S, "gkv", [128, 128], F32)
            gk = sb(S, "gk96", [128, 96], F32)
            many = lambda name, shape, dt: [sb(S, name, shape, dt) for _ in range(NP)]
            xt = [sb(S, "xt", [128, D], F32) for _ in range(2)] * 2
            csb_ = many("csb", [128, 64], F32)
            hTt_ = many("hTt", [128, 8, 128], BF16)
            ckn_ = many("ckn", [128, 128], BF16)
            ckT_ = many("ckT", [128, 128], BF16)
            kn_ = many("kn", [128, 8, 64], F32)
            kb2 = many("kb", [128, 8, 96], BF16)
            krw_ = many("krw", [128, 32], F32)
            krg_ = many("krg", [128, 1, 32], F32)
            krr_ = many("krr", [128, 1, 32], F32)
            t1_ = many("t1", [128, 1, 32], F32)
            t2_ = many("t2", [128, 1, 32], F32)
            vb_ = many("vb", [128, 8, 64], BF16)
            ssr_ = many("ssr", [128, 1], F32)
            sqa_ = many("sqa", [128, D], F32)
            ssqa_ = many("ssqa", [128, 1], F32)
            rstda_ = many("rstda", [128, 1], F32)
            hba_ = [sb(S, "hba", [128, D], BF16) for _ in range(2)] * 2
            ss8_ = many("ss8", [128, 8], F32)
            rs8_ = many("rs8", [128, 8], F32)
            kst_ = [sb(S, "kst", [96, 8, 512], BF16) for _ in range(2)]
            load_w(w_kv, w_in_e[0][:, 768:928], "w_kv")
            load_w(wukv, w_ukv[0], "wukv")
            load_bc(gn[:], mix_norm_g[0], "gn")
            load_bc(gkv[:], kv_lora_g[0], "gkv")
            load_bc(gk[:], mla_k_g[0], "gk")

            def tileA(t):
                p = t % NP
                x_ = "_%d" % p
                bX, bY = 2 * p, 2 * p + 1
                bK = (bX, bY)
                xtt, hTt, ckn, ckT, kn, kb_, krw, krg, krr, t1, t2, vb, ssr = (xt[p], hTt_[p], ckn_[p], ckT_[p], kn_[p], kb2[p], krw_[p], krg_[p],
                                                                               krr_[p], t1_[p], t2_[p], vb_[p], ssr_[p])
                sq_, ssq_, rstd_, ss32_, rs32_ = sqa_[p], ssqa_[p], rstda_[p], ss8_[p], rs8_[p]
                sfx = "_a%d" % p
                kst = kst_[(t // 4) % 2]
                kstr = "kst%d" % ((t // 4) % 2)
                xr = "xt_%d" % (t % 2)
                csb = csb_[p]
                P.dma("sp", lambda e: e.dma_start(out=xtt[:], in_=xb[t * 128:(t + 1) * 128, :]), writes=[xr])
                P.dma("sp", lambda e: e.dma_start(out=csb[:], in_=cs_b[t * 128:(t + 1) * 128, :]), writes=["csb" + x_])
                norm_hT(xtt[:], xr, hTt[:], "hTt" + x_, bank=bX, ws=(sq_, ssq_, rstd_, hba_[p], sfx, "_h%d" % (t % 2)))
                pC = banks[bY]
                for k in range(8):
                    pe(lambda e, k=k: e.matmul(pC[:, 0:160], lhsT=hTt[:, k, :], rhs=w_kv[:, k, :], start=(k == 0), stop=(k == 7)),
                       reads=["hTt" + x_, ("w_kv", k)], writes=[B[bY]])
                act(lambda e: e.activation(out=sq_[:, 0:128], in_=pC[:, 0:128], func=AF.Square, accum_out=ssq_[:]),
                    reads=[B[bY]], writes=["sq" + sfx, "ssq" + sfx])
                act(lambda e: e.copy(out=krw[:], in_=pC[:, 128:160]), reads=[B[bY]], writes=["krw" + x_])
                rstd_from(ssq_[:], "ssq" + sfx, 128, rstd_[:], "rstd" + sfx)
                dve(lambda e: e.scalar_tensor_tensor(out=ckn[:], in0=pC[:, 0:128], scalar=rstd_[:, 0:1], in1=gkv[:], op0=ALU.mult, op1=ALU.mult),
                    reads=[B[bY], "rstd" + sfx, "gkv"], writes=["ckn" + x_])
                pT2 = bbf(bX)
                pe(lambda e: e.transpose(out=pT2[:, 0:128], in_=ckn[:], identity=ident[:]), reads=["ckn" + x_, "ident"], writes=[B[bX]])
                act(lambda e: e.copy(out=ckT[:], in_=pT2[:, 0:128]), reads=[B[bX]], writes=["ckT" + x_])
                act(lambda e: e.activation(out=sq_[:, 512:544], in_=krw[:], func=AF.Square, accum_out=ssr[:]),
                    reads=["krw" + x_], writes=["sq2" + sfx, "ssr" + x_])
                dve(lambda e: e.tensor_tensor(out=krg[:, 0, :], in0=krw[:], in1=gk[:, 64:96], op=ALU.mult),
                    reads=["krw" + x_, "gk"], writes=["krg" + x_])
                rotary(krg[:], "krg" + x_, csb[:], "csb" + x_, krr[:], "krr" + x_, t1[:], t2[:], 1, sfx=x_)
                for j in range(2):
                    pe(lambda e, j=j: e.matmul(banks[bK[j]][:], lhsT=ckT[:], rhs=wukv[:, 0, j * 512:(j + 1) * 512], start=True, stop=True),
                       reads=["ckT" + x_, ("wukv", 0)], writes=[B[bK[j]]])
                for j in range(2):
                    act(lambda e, j=j: e.activation(out=sq_[:, j * 256:(j + 1) * 256].rearrange("p (h d) -> p h d", h=4),
                                                    in_=banks[bK[j]][:].rearrange("p (h d) -> p h d", h=4)[:, :, 0:64], func=AF.Square),
                        reads=[B[bK[j]]], writes=["sq" + sfx])
                dve(lambda e: e.tensor_reduce(out=ss32_[:, 0:8], in_=sq_[:, 0:512].rearrange("p (h d) -> p h d", h=8), axis=AX.X, op=ALU.add),
                    reads=["sq" + sfx], writes=["ss32" + sfx])
                dve(lambda e: e.tensor_scalar(out=ss32_[:, 0:8], in0=ss32_[:, 0:8], scalar1=ssr[:, 0:1], scalar2=None, op0=ALU.add),
                    reads=["ss32" + sfx, "ssr" + x_], writes=["ss32" + sfx])
                rstd_from(ss32_[:, 0:8], "ss32" + sfx, 96, rs32_[:, 0:8], "rs32" + sfx)
                for j in range(2):
                    dve(lambda e, j=j: e.tensor_tensor(out=kn[:, 4 * j:4 * j + 4, :], in0=banks[bK[j]][:].rearrange("p (h d) -> p h d", h=4)[:, :, 0:64],
                                                       in1=rs32_[:, 4 * j:4 * j + 4].unsqueeze(2).to_broadcast([128, 4, 64]), op=ALU.mult),
                        reads=[B[bK[j]], "rs32" + sfx], writes=["kn" + x_])
                for j in range(2):
                    act(lambda e, j=j: e.copy(out=vb[:, 4 * j:4 * j + 4, :], in_=banks[bK[j]][:].rearrange("p (h d) -> p h d", h=4)[:, :, 64:128]),
                        reads=[B[bK[j]]], writes=["vb" + x_])
                pool(lambda e: e.tensor_tensor(out=kb_[:, :, 0:64], in0=kn[:], in1=gk[:, 0:64].unsqueeze(1).to_broadcast([128, 8, 64]), op=ALU.mult),
                     reads=["kn" + x_, "gk"], writes=["kb_n" + x_])
                dve(lambda e: e.tensor_tensor(out=kb_[:, :, 64:96], in0=krr[:, 0, :].unsqueeze(1).to_broadcast([128, 8, 32]),
                                              in1=rs32_[:, 0:8].unsqueeze(2).to_broadcast([128, 8, 32]), op=ALU.mult),
                    reads=["krr" + x_, "rs32" + sfx], writes=["kb_r" + x_])
                P.dma("pool", lambda e: e.dma_start(out=v_scr[:, :, t, :].rearrange("h p d -> p h d"), in_=vb[:]), reads=["vb" + x_], writes=[("v_scr", t)])
                pT3 = bbf(bX)
                for h in range(8):
                    pe(lambda e, h=h: e.transpose(out=pT3[0:96, h * 128:(h + 1) * 128], in_=kb_[:, h, :], identity=ident[:]),
                       reads=["kb_n" + x_, "kb_r" + x_, "ident"], writes=[B[bX]])
                tt = t % 4
                act(lambda e: e.copy(out=kst[:, :, tt * 128:(tt + 1) * 128], in_=pT3[0:96, 0:1024].rearrange("p (h t) -> p h t", h=8)),
                    reads=[B[bX]], writes=[kstr])
                if tt == 3:
                    b4 = t // 4
                    P.dma("pool", lambda e: e.dma_start(out=kT_scr[:, :, b4 * 512:(b4 + 1) * 512].rearrange("h d t -> d h t"), in_=kst[:]),
                          reads=[kstr], writes=[("kT_scr", b4)])

            P.pipeline_deep(NTB, tileA, NP)
            barrier()

    def phase_B1():
        NP = 4
        with contextlib.ExitStack() as S:
            w_q = sb(S, "w_q", [128, 8, 256], BF16)
            wuq = sb(S, "wuq", [128, 2, 768], BF16)
            gql = sb(S, "gql", [128, 256], F32)
            gq = sb(S, "gq96", [128, 96], F32)
            csw = sb(S, "csw", [128, NT, 64], F32)
            many = lambda name, shape, dt: [sb(S, name, shape, dt) for _ in range(NP)]
            hTt_ = many("hTt", [128, 8, 128], BF16)
            cqn_ = many("cqn", [128, 256], BF16)
            cqT_ = many("cqT", [128, 2, 128], BF16)
            qn_ = many("qn", [128, 8, 96], F32)
            qr_ = many("qr", [128, 8, 32], F32)
            qb2 = many("qb", [128, 8, 96], BF16)
            t1_ = many("t1", [128, 8, 32], F32)
            t2_ = many("t2", [128, 8, 32], F32)
            sqa_ = many("sqa", [128, D], F32)
            ssqa_ = many("ssqa", [128, 1], F32)
            rstda_ = many("rstda", [128, 1], F32)
            hba_ = [sb(S, "hba", [128, D], BF16) for _ in range(2)] * 2
            ss8_ = many("ss8", [128, 8], F32)
            rs8_ = many("rs8", [128, 8], F32)
            qst_ = [sb(S, "qst", [96, 8, 512], BF16) for _ in range(2)]
            load_w(w_q, w_in_e[0][:, 512:768], "w_q")
            load_w(wuq, w_uq[0], "wuq")
            load_bc(gn[:], mix_norm_g[0], "gn")
            load_bc(gql[:], q_lora_g[0], "gql")
            load_bc(gq[:], mla_q_g[0], "gq")
            P.dma("sp", lambda e: e.dma_start(out=csw[:], in_=cs_w.rearrange("(c p) f -> p c f", p=128)), writes=["csw"])

            def tileB(t):
                p = t % NP
                x_ = "_%d" % p
                sfx = "_b%d" % p
                bX, bY = 2 * p, 2 * p + 1
                bQ = (bX, bY)
                hTt, cqn, cqT, qn, qr, qb_, t1, t2 = hTt_[p], cqn_[p], cqT_[p], qn_[p], qr_[p], qb2[p], t1_[p], t2_[p]
                sq_, ssq_, rstd_, ss32_, rs32_ = sqa_[p], ssqa_[p], rstda_[p], ss8_[p], rs8_[p]
                qst = qst_[(t // 4) % 2]
                qstr = "qst%d" % ((t // 4) % 2)
                norm_hT(x[:, t, :], ("x", t), hTt[:], "hTt" + x_, bank=bX, ws=(sq_, ssq_, rstd_, hba_[p], sfx, "_h%d" % (t % 2)))
                pC = banks[bY]
                for k in range(8):
                    pe(lambda e, k=k: e.matmul(pC[:, 0:256], lhsT=hTt[:, k, :], rhs=w_q[:, k, :], start=(k == 0), stop=(k == 7)),
                       reads=["hTt" + x_, ("w_q", k)], writes=[B[bY]])
                act(lambda e: e.activation(out=sq_[:, 0:256], in_=pC[:, 0:256], func=AF.Square, accum_out=ssq_[:]),
                    reads=[B[bY]], writes=["sq" + sfx, "ssq" + sfx])
                rstd_from(ssq_[:], "ssq" + sfx, 256, rstd_[:], "rstd" + sfx)
                dve(lambda e: e.scalar_tensor_tensor(out=cqn[:], in0=pC[:, 0:256], scalar=rstd_[:, 0:1], in1=gql[:], op0=ALU.mult, op1=ALU.mult),
                    reads=[B[bY], "rstd" + sfx, "gql"], writes=["cqn" + x_])
                pT2 = bbf(bX)
                for j in range(2):
                    pe(lambda e, j=j: e.transpose(out=pT2[:, j * 128:(j + 1) * 128], in_=cqn[:, j * 128:(j + 1) * 128], identity=ident[:]),
                       reads=["cqn" + x_, "ident"], writes=[B[bX]])
                act(lambda e: e.copy(out=cqT[:], in_=pT2[:, 0:256].rearrange("p (j t) -> p j t", j=2)), reads=[B[bX]], writes=["cqT" + x_])
                for (bk, c0, c1) in ((bQ[0], 0, 480), (bQ[1], 480, 768)):
                    for j in range(2):
                        pe(lambda e, bk=bk, c0=c0, c1=c1, j=j: e.matmul(banks[bk][:, 0:c1 - c0], lhsT=cqT[:, j, :], rhs=wuq[:, j, c0:c1],
                                                                        start=(j == 0), stop=(j == 1)),
                           reads=["cqT" + x_, ("wuq", j)], writes=[B[bk]])
                segs = ((bQ[0], 0, 5), (bQ[1], 5, 8))
                for (bk, h0, h1) in segs:
                    nh = h1 - h0
                    act(lambda e, bk=bk, h0=h0, nh=nh: e.activation(out=sq_[:, h0 * 96:(h0 + nh) * 96], in_=banks[bk][:, 0:nh * 96], func=AF.Square),
                        reads=[B[bk]], writes=["sq" + sfx])
                dve(lambda e: e.tensor_reduce(out=ss32_[:, 0:8], in_=sq_[:, 0:768].rearrange("p (h d) -> p h d", h=8), axis=AX.X, op=ALU.add),
                    reads=["sq" + sfx], writes=["ss32" + sfx])
                rstd_from(ss32_[:, 0:8], "ss32" + sfx, 96, rs32_[:, 0:8], "rs32" + sfx)
                for (bk, h0, h1) in segs:
                    nh = h1 - h0
                    dve(lambda e, bk=bk, h0=h0, nh=nh: e.tensor_tensor(out=qn[:, h0:h0 + nh, :], in0=banks[bk][:, 0:nh * 96].rearrange("p (h d) -> p h d", h=nh),
                                                                       in1=rs32_[:, h0:h0 + nh].unsqueeze(2).to_broadcast([128, nh, 96]), op=ALU.mult),
                        reads=[B[bk], "rs32" + sfx], writes=["qn" + x_])
                pool(lambda e: e.tensor_tensor(out=qb_[:, :, 0:64], in0=qn[:, :, 0:64], in1=gq[:, 0:64].unsqueeze(1).to_broadcast([128, 8, 64]), op=ALU.mult),
                     reads=["qn" + x_, "gq"], writes=["qb_n" + x_])
                dve(lambda e: e.tensor_tensor(out=qr[:], in0=qn[:, :, 64:96], in1=gq[:, 64:96].unsqueeze(1).to_broadcast([128, 8, 32]), op=ALU.mult),
                    reads=["qn" + x_, "gq"], writes=["qr" + x_])
                rotary(qr[:], "qr" + x_, csw[:, t, :], "csw", qb_[:, :, 64:96], "qb_r" + x_, t1[:], t2[:], 8, sfx=x_)
                pT3 = bbf(bX)
                for h in range(8):
                    pe(lambda e, h=h: e.transpose(out=pT3[0:96, h * 128:(h + 1) * 128], in_=qb_[:, h, :], identity=ident[:]),
                       reads=["qb_n" + x_, "qb_r" + x_, "ident"], writes=[B[bX]])
                tt = t % 4
                act(lambda e: e.copy(out=qst[:, :, tt * 128:(tt + 1) * 128], in_=pT3[0:96, 0:1024].rearrange("p (h t) -> p h t", h=8)),
                    reads=[B[bX]], writes=[qstr])
                if tt == 3:
                    b4 = t // 4
                    P.dma("pool", lambda e: e.dma_start(out=qT_scr[:, :, b4 * 512:(b4 + 1) * 512].rearrange("h d t -> d h t"), in_=qst[:]),
                          reads=[qstr], writes=[("qT_scr", b4)])

            P.pipeline_deep(NT, tileB, NP)
            barrier()

    def phase_B2():
        with contextlib.ExitStack() as S:
            w_p = sb(S, "w_p", [128, 8, 512], BF16)
            pw = sb(S, "pw", [128, 4, 128], BF16)
            psc = sb(S, "psc", [128, 4], F32)
            wout = sb(S, "wout", [128, 4, D], BF16)
            inv_t = sb(S, "inv_t", [128, 4, 512], F32)
            U = sb(S, "U", [128, 4, 528], F32)
            UH = sb(S, "UH", [128, 4, 16], F32)
            a2 = sb(S, "a2", [128, 3, 528], F32)
            a4 = sb(S, "a4", [128, 2, 528], F32)
            a8 = sb(S, "a8", [128, 1, 528], F32)
            Sm = sb(S, "Sm", [128, 4, 512], F32)
            Dt = sb(S, "Dt", [128, 4, 512], BF16)
            hTb = sb(S, "hTb", [128, 8, 512], BF16)
            hTh = sb(S, "hTh", [128, 8, 16], BF16)
            xht = sb(S, "xht", [16, D], F32)
            yT = sb(S, "yT", [128, 4, WT], BF16)
            load_w(w_p, w_in_e[0][:, 0:512], "w_p")
            for g in range(4):
                P.dma("pool", lambda e, g=g: e.dma_start(out=pw[:, g, :], in_=pool_w[0, g]), writes=[("pw", g)])
            load_w(wout, w_out_e[0][0:512, :], "wout")
            load_bc(gn[:], mix_norm_g[0], "gn")
            P.dma("sp", lambda e: e.dma_start(out=psc[:], in_=pool_scale[0].rearrange("(g d) -> d g", g=4), allow_slow_non_contiguous=True), writes=["psc"])
            P.dma("sp", lambda e: e.dma_start(out=xht[:], in_=xh[:, :]), writes=["xht"])
            dve(lambda e: e.memset(U[:], 0.0), writes=["U"])
            norm_hT(xht[:], "xht", hTh[:], "hTh", n=16)
            for g in range(4):
                for k in range(8):
                    pe(lambda e, g=g, k=k: e.matmul(banks[1][:, g * 16:(g + 1) * 16], lhsT=w_p[:, k, g * 128:(g + 1) * 128], rhs=hTh[:, k, :],
                                                    start=(k == 0), stop=(k == 7)),
                       reads=["hTh", ("w_p", k)], writes=[B[1]])
            act(lambda e: e.copy(out=UH[:], in_=banks[1][:, 0:64].rearrange("p (g t) -> p g t", g=4)), reads=[B[1]], writes=["UH"])
            dve(lambda e: e.tensor_copy(out=U[:, :, 520:528], in_=UH[:, :, 0:8]), reads=["UH", "U"], writes=["U"])

            def pool_step(b, ncols, tok0, c0):
                P.dma("sp", lambda e: e.dma_start(out=inv_t[:, :, 0:(512 if b < NB else 8)],
                                                  in_=invc[:, b * 512:b * 512 + (512 if b < NB else 8)].partition_broadcast(128)),
                      writes=["inv_t"])
                pool(lambda e: e.tensor_tensor(out=a2[:, :, 0:527], in0=U[:, 1:4, 0:527], in1=U[:, 1:4, 1:528], op=ALU.add), reads=["U"], writes=["a2"])
                pool(lambda e: e.tensor_tensor(out=a4[:, :, 0:525], in0=a2[:, 1:3, 0:525], in1=a2[:, 1:3, 2:527], op=ALU.add), reads=["a2"], writes=["a4"])
                pool(lambda e: e.tensor_tensor(out=a8[:, :, 0:521], in0=a4[:, 1:2, 0:521], in1=a4[:, 1:2, 4:525], op=ALU.add), reads=["a4"], writes=["a8"])
                dve(lambda e: e.tensor_tensor(out=Sm[:, 0, :], in0=U[:, 0, 7:519], in1=U[:, 0, 8:520], op=ALU.add), reads=["U"], writes=["Sm0"])
                dve(lambda e: e.tensor_tensor(out=Sm[:, 1, :], in0=a2[:, 0, 6:518], in1=a2[:, 0, 8:520], op=ALU.add), reads=["a2"], writes=["Sm1"])
                dve(lambda e: e.tensor_tensor(out=Sm[:, 2, :], in0=a4[:, 0, 4:516], in1=a4[:, 0, 8:520], op=ALU.add), reads=["a4"], writes=["Sm2"])
                dve(lambda e: e.tensor_tensor(out=Sm[:, 3, :], in0=a8[:, 0, 0:512], in1=a8[:, 0, 8:520], op=ALU.add), reads=["a8"], writes=["Sm3"])
                dve(lambda e: e.tensor_tensor(out=Sm[:], in0=Sm[:], in1=inv_t[:], op=ALU.mult), reads=["Sm0", "Sm1", "Sm2", "Sm3", "inv_t"], writes=["Sm"])
                dve(lambda e: e.tensor_tensor(out=Dt[:], in0=Sm[:], in1=U[:, :, 8:520], op=ALU.subtract), reads=["Sm", "U"], writes=["Dt"])
                for g in range(4):
                    pe(lambda e, g=g: e.matmul(banks[2 + (g % 2)][:], lhsT=pw[:, g, :], rhs=Dt[:, g, :], start=True, stop=True),
                       reads=["Dt", ("pw", g)], writes=[B[2 + (g % 2)]])
                    act(lambda e, g=g: e.activation(out=yT[:, g, tok0:tok0 + ncols], in_=banks[2 + (g % 2)][:, c0:c0 + ncols], func=AF.Copy, scale=psc[:, g:g + 1]),
                        reads=[B[2 + (g % 2)], "psc"], writes=[("yT", g)])

            for b in range(NB):
                def ntile(tt, b=b):
                    t = 4 * b + tt
                    norm_hT(x[:, t, :], ("x", t), hTb[:, :, tt * 128:(tt + 1) * 128], ("hTb", tt), par=tt % 2, bank=tt % 2)
                P.pipeline(4, ntile)
                dve(lambda e: e.tensor_copy(out=U[:, :, 0:16], in_=U[:, :, 512:528]), reads=["U"], writes=["U"])
                for g in range(4):
                    for k in range(8):
                        pe(lambda e, g=g, k=k: e.matmul(banks[4 + (g % 2)][:], lhsT=w_p[:, k, g * 128:(g + 1) * 128], rhs=hTb[:, k, :],
                                                        start=(k == 0), stop=(k == 7)),
                           reads=[("hTb", 0), ("hTb", 1), ("hTb", 2), ("hTb", 3), ("w_p", k)], writes=[B[4 + (g % 2)]])
                    act(lambda e, g=g: e.copy(out=U[:, g, 16:528], in_=banks[4 + (g % 2)][:]), reads=[B[4 + (g % 2)], "U"], writes=["U"])
                if b == 0:
                    pool_step(0, 504, 0, 8)
                else:
                    pool_step(b, 512, 512 * b - 8, 0)
            dve(lambda e: e.tensor_copy(out=U[:, :, 0:16], in_=U[:, :, 512:528]), reads=["U"], writes=["U"])
            dve(lambda e: e.tensor_copy(out=U[:, :, 16:24], in_=UH[:, :, 8:16]), reads=["UH", "U"], writes=["U"])
            pool_step(NB, 8, WT - 8, 0)
            for t in range(NT):
                proj_resid(t, lambda k, t=t: yT[:, k, t * 128:(t + 1) * 128], [("yT", g) for g in range(4)], 4, wout, "wout")
            barrier()

    def phase_C():
        with contextlib.ExitStack() as S:
            oT = sb(S, "oT", [128, 4, WT], BF16)
            with contextlib.ExitStack() as S2:
                kh = [sb(S2, "kh", [96, SEQ], BF16) for _ in range(2)]
                va = [sb(S2, "va", [128, NTB, 128], BF16) for _ in range(2)]
                qh = [sb(S2, "qh", [96, WT], BF16) for _ in range(2)]
                E2 = [sb(S2, "E", [128, 1024], BF16) for _ in range(2)]
                rec = sb(S2, "rec", [128, 512], F32)
                for p in range(2):
                    dve(lambda e, p=p: e.memset(va[p][:, :, (1 - p) * 64:(1 - p) * 64 + 64], 1.0), writes=[("va1", p)])
                scale = 96 ** -0.5
                NPAIR = NTB // 2
                for h in range(8):
                    p = h % 2
                    lo, hi = p * 64, p * 64 + 64
                    dlo, dhi = (1 - p) * 64, (1 - p) * 64 + 64
                    P.dma("sp", lambda e, h=h, p=p: e.dma_start(out=kh[p][:], in_=kT_scr[h]), writes=[("kh", p)])
                    P.dma("sp", lambda e, h=h, p=p: e.dma_start(out=va[p][:, :, p * 64:p * 64 + 64], in_=v_scr[h]), writes=[("va", p)])
                    P.dma("sp", lambda e, h=h, p=p: e.dma_start(out=qh[p][:], in_=qT_scr[h]), writes=[("qh", p)])
                    for qb in range(NB):
                        ob = 4 + (qb % 2)

                        def S_pair(cc, qb=qb, p=p):
                            for s_ in range(2):
                                c = 2 * cc + s_
                                bk = 2 * (cc % 2) + s_
                                pe(lambda e, c=c, bk=bk: e.matmul(banks[bk], lhsT=kh[p][:, c * 128:(c + 1) * 128], rhs=qh[p][:, qb * 512:(qb + 1) * 512],
                                                                  start=True, stop=True),
                                   reads=[("kh", p), ("qh", p)], writes=[B[bk]])
                        S_pair(0)
                        S_pair(1)
                        for cc in range(NPAIR):
                            pp = cc % 2
                            act(lambda e, pp=pp: e.activation(out=E2[pp][:], in_=psum_all[:, 2 * pp * 512:(2 * pp + 2) * 512], func=AF.Exp, scale=scale),
                                reads=[B[2 * pp], B[2 * pp + 1]], writes=[("E", pp)])
                            for s_ in range(2):
                                c = 2 * cc + s_
                                pe(lambda e, c=c, p=p, ob=ob, pp=pp, s_=s_: e.matmul(banks[ob], lhsT=va[p][:, c, :], rhs=E2[pp][:, s_ * 512:(s_ + 1) * 512],
                                                                                   start=(c == 0), stop=(c == NTB - 1)),
                                   reads=[("va", p), ("va1", p), ("E", pp)], writes=[B[ob]])
                            if cc + 2 < NPAIR:
                                S_pair(cc + 2)
                        dve(lambda e, ob=ob, lo=lo, hi=hi, dlo=dlo, dhi=dhi: e.tensor_copy(out=rec[lo:hi, :], in_=banks[ob][dlo:dhi, :]), reads=[B[ob]], writes=["rec"])
                        dve(lambda e, lo=lo, hi=hi: e.reciprocal(out=rec[lo:hi, :], in_=rec[lo:hi, :]), reads=["rec"], writes=["rec"])
                        dve(lambda e, ob=ob, h=h, qb=qb, lo=lo, hi=hi: e.tensor_tensor(out=oT[lo:hi, h // 2, qb * 512:(qb + 1) * 512], in0=banks[ob][lo:hi, :],
                                                                         in1=rec[lo:hi, :], op=ALU.mult),
                            reads=[B[ob], "rec"], writes=[("oT", h // 2)])
                barrier()
            with contextlib.ExitStack() as S2:
                wout = sb(S2, "wout2", [128, 4, D], BF16)
                load_w(wout, w_out_e[0][512:1024, :], "wout2")
                for t in range(NT):
                    proj_resid(t, lambda k, t=t: oT[:, k, t * 128:(t + 1) * 128], [("oT", g) for g in range(4)], 4, wout, "wout2")
                barrier()

    def phase_xattn(L):
        with contextlib.ExitStack() as S:
            wq = sb(S, "wq", [128, 8, D], BF16)
            wo = sb(S, "wo", [128, 8, D], BF16)
            gq = sb(S, "gq256", [128, 256], F32)
            hTt = sb(S, "hTt", [128, 8, 128], BF16)
            qn = sb(S, "qn", [128, D], F32)
            qb_ = sb(S, "qb", [128, D], BF16)
            qT2 = [sb(S, "qT", [128, 8, 512], BF16) for _ in range(2)]
            E = [sb(S, "E", [128, 512], BF16) for _ in range(2)]
            rec = sb(S, "rec", [128, 512], F32)
            oTb = sb(S, "oTb", [128, 8, 512], BF16)
            load_w(wq, w_mem_q[L], "wq")
            load_w(wo, w_mem_o[L], "wo")
            load_bc(gn[:], xattn_norm_g[L], "gn")
            load_bc(gq[:], mem_q_g[L], "gq")
            scale = 256 ** -0.5

            def block(b):
                qT = qT2[b % 2]
                qTr = "qT%d" % (b % 2)
                for tt in range(4):
                    t = 4 * b + tt
                    norm_hT(x[:, t, :], ("x", t), hTt[:], "hTt")
                    for half in range(2):
                        for k in range(8):
                            pe(lambda e, half=half, k=k: e.matmul(banks[1 + half][:], lhsT=hTt[:, k, :], rhs=wq[:, k, half * 512:(half + 1) * 512],
                                                                   start=(k == 0), stop=(k == 7)),
                               reads=["hTt", ("wq", k)], writes=[B[1 + half]])
                    for hh in range(4):
                        act(lambda e, hh=hh: e.activation(out=sq[:, hh * 256:(hh + 1) * 256], in_=banks[1 + hh // 2][:, (hh % 2) * 256:(hh % 2 + 1) * 256],
                                                          func=AF.Square, accum_out=ss32[:, hh:hh + 1]),
                            reads=[B[1 + hh // 2]], writes=["sq", "ss32"])
                    rstd_from(ss32[:, 0:4], "ss32", 256, rs32[:, 0:4], "rs32")
                    for hh in range(4):
                        dve(lambda e, hh=hh: e.scalar_tensor_tensor(out=qb_[:, hh * 256:(hh + 1) * 256], in0=banks[1 + hh // 2][:, (hh % 2) * 256:(hh % 2 + 1) * 256],
                                                                    scalar=rs32[:, hh:hh + 1], in1=gq[:], op0=ALU.mult, op1=ALU.mult),
                            reads=[B[1 + hh // 2], "rs32", "gq"], writes=["qb"])
                    pT = bbf(0)
                    for j in range(8):
                        pe(lambda e, j=j: e.transpose(out=pT[:, j * 128:(j + 1) * 128], in_=qb_[:, j * 128:(j + 1) * 128], identity=ident[:]),
                           reads=["qb", "ident"], writes=[B[0]])
                    act(lambda e, tt=tt, qT=qT: e.copy(out=qT[:, :, tt * 128:(tt + 1) * 128], in_=pT[:, 0:1024].rearrange("p (j t) -> p j t", j=8)),
                        reads=[B[0]], writes=[qTr])
                P.mark()
                for h in range(4):
                    for m in range(2):
                        for j in range(2):
                            pe(lambda e, h=h, m=m, j=j, qT=qT: e.matmul(banks[3 + m][:], lhsT=memkT[:, h, j, m * 128:(m + 1) * 128], rhs=qT[:, 2 * h + j, :],
                                                                         start=(j == 0), stop=(j == 1)),
                               reads=[qTr, "memkT"], writes=[B[3 + m]])
                        act(lambda e, m=m: e.activation(out=E[m][:], in_=banks[3 + m][:], func=AF.Exp, scale=scale), reads=[B[3 + m]], writes=[("E", m)])
                    for m in range(2):
                        pe(lambda e, m=m: e.matmul(banks[5][:], lhsT=ones_b[:], rhs=E[m][:], start=(m == 0), stop=(m == 1)),
                           reads=["ones_b", ("E", m)], writes=[B[5]])
                    act(lambda e: e.activation(out=rec[:], in_=banks[5][:], func=AF.Ln), reads=[B[5]], writes=["rec"])
                    act(lambda e: e.activation(out=rec[:], in_=rec[:], func=AF.Exp, scale=-1.0), reads=["rec"], writes=["rec"])
                    for dv in range(2):
                        for m in range(2):
                            pe(lambda e, h=h, m=m, dv=dv: e.matmul(banks[6 + dv][:], lhsT=memv[:, m, h, dv * 128:(dv + 1) * 128], rhs=E[m][:],
                                                                    start=(m == 0), stop=(m == 1)),
                               reads=["memv", ("E", m)], writes=[B[6 + dv]])
                        dve(lambda e, h=h, dv=dv: e.tensor_tensor(out=oTb[:, 2 * h + dv, :], in0=banks[6 + dv][:], in1=rec[:], op=ALU.mult),
                            reads=[B[6 + dv], "rec"], writes=["oTb"])
                for tt in range(4):
                    t = 4 * b + tt
                    proj_resid(t, lambda k, tt=tt: oTb[:, k, tt * 128:(tt + 1) * 128], ["oTb"], 8, wo, "wo")

            P.pipeline(NB, block)
            barrier()

    def phase_mlp(L, last=False):
        NPASS = 8
        with contextlib.ExitStack() as S:
            hT = sb(S, "hTall", [128, 8, WT], BF16)
            w1 = [sb(S, "w1", [128, 8, 512], BF16) for _ in range(2)]
            w2 = [sb(S, "w2", [128, 4, D], BF16) for _ in range(2)]
            aT = [sb(S, "aT", [128, 512], BF16) for _ in range(4)]
            s2 = [sb(S, "s2", [128, 512], F32) for _ in range(2)]
            load_bc(gn[:], ff_norm_g[L], "gn")

            def load_pass(ps_):
                b = ps_ % 2
                load_w(w1[b], w_ff1[L][:, ps_ * 512:(ps_ + 1) * 512], ("w1", b))
                load_w(w2[b], w_ff2[L][ps_ * 512:(ps_ + 1) * 512, :], ("w2", b))
            load_pass(0)

            def norm_block(b):
                P.capture()
                for tt in range(4):
                    t = 4 * b + tt
                    norm_hT(x[:, t, :], ("x", t), hT[:, :, t * 128:(t + 1) * 128], ("hT", t // 4), par=t % 2, bank=(0, 3)[t % 2])
                return P.end_capture()

            P.replay(norm_block(0))
            for ps_ in range(NPASS):
                pb = ps_ % 2
                if ps_ + 1 < NPASS:
                    load_pass(ps_ + 1)
                for b in range(NB):
                  if ps_ == 0:
                    P.capture()
                  if True:
                    for f in range(4):
                        zb = 1 + (f % 2)
                        for k in range(8):
                            pe(lambda e, f=f, k=k, zb=zb, pb=pb, b=b: e.matmul(banks[zb][:], lhsT=w1[pb][:, k, f * 128:(f + 1) * 128],
                                                                                 rhs=hT[:, k, b * 512:(b + 1) * 512], start=(k == 0), stop=(k == 7)),
                               reads=[("hT", b), (("w1", pb), k)], writes=[B[zb]])
                        act(lambda e, f=f, zb=zb: e.activation(out=s2[f % 2][:], in_=banks[zb][:], func=AF.Square), reads=[B[zb]], writes=[("s2", f % 2)])
                        dve(lambda e, f=f, zb=zb: e.scalar_tensor_tensor(out=aT[f][:], in0=banks[zb][:], scalar=0.0, in1=s2[f % 2][:],
                                                                          op0=ALU.is_gt, op1=ALU.mult),
                            reads=[B[zb], ("s2", f % 2)], writes=[("aT", f)])
                    for tt in range(4):
                        t = 4 * b + tt
                        proj_resid(t, lambda k, tt=tt: aT[k][:, tt * 128:(tt + 1) * 128], [("aT", f) for f in range(4)], 4, w2[pb], ("w2", pb))
                  if ps_ == 0:
                    Cb = P.end_capture()
                    Nb = norm_block(b + 1) if b + 1 < NB else []
                    P.replay(P.zipmerge(Cb, Nb))
            if last:
                for t in range(NT):
                    fins.append(P.dma("sp", lambda e, t=t: e.dma_start(out=out[t * 128:(t + 1) * 128, :], in_=x[:, t, :]), reads=[("x", t)]))
            barrier()

    def phase_G1():
        with contextlib.ExitStack() as S:
            wqkv = sb(S, "wqkv", [128, 8, 3072], BF16)
            gqk = sb(S, "gqk", [128, 2, 64], F32)
            two = lambda name, shape, dt: [sb(S, name, shape, dt) for _ in range(2)]
            hTt_ = two("hTt", [128, 8, 128], BF16)
            qkn_ = two("qkn", [128, 1024], F32)
            qkb_ = two("qkb", [128, 1024], BF16)
            qkst_ = two("qkst", [128, 8, 128], BF16)
            vst_ = two("vst", [128, 4, 128], BF16)
            load_w(wqkv, w_qkv_o[0], "wqkv")
            load_bc(gn[:], mix_norm_g[1], "gn")
            load_bc(gqk[:, 0, :], na_q_g[0], "gqk0")
            load_bc(gqk[:, 1, :], na_k_g[0], "gqk1")

            def unit(n):
                t, u = n // 2, n % 2
                x_ = "_%d" % u
                sfx = "" if u == 0 else "_p1"
                bT, bQ, bV = 4 * u, (4 * u + 1, 4 * u + 2), 4 * u + 3
                hTt = hTt_[t % 2]
                hr = "hTt_%d" % (t % 2)
                qkn, qkb, qkst, vst = qkn_[u], qkb_[u], qkst_[u], vst_[u]
                sq_, ss32_, rs32_ = SQ[u], SS32[u], RS32[u]
                if u == 0:
                    norm_hT(x[:, t, :], ("x", t), hTt[:], hr, par=0, bank=bT)
                for n2 in range(2):
                    for k in range(8):
                        pe(lambda e, n2=n2, k=k: e.matmul(banks[bQ[n2]][:], lhsT=hTt[:, k, :], rhs=wqkv[:, k, u * 1024 + n2 * 512:u * 1024 + (n2 + 1) * 512],
                                                          start=(k == 0), stop=(k == 7)),
                           reads=[hr, ("wqkv", k)], writes=[B[bQ[n2]]])
                for n2 in range(2):
                    act(lambda e, n2=n2: e.activation(out=sq_[:, n2 * 512:(n2 + 1) * 512], in_=banks[bQ[n2]][:], func=AF.Square),
                        reads=[B[bQ[n2]]], writes=["sq" + sfx])
                dve(lambda e: e.tensor_reduce(out=ss32_[:, 0:16], in_=sq_[:, 0:1024].rearrange("p (h d) -> p h d", h=16), axis=AX.X, op=ALU.add),
                    reads=["sq" + sfx], writes=["ss32" + sfx])
                rstd_from(ss32_[:, 0:16], "ss32" + sfx, 64, rs32_[:, 0:16], "rs32" + sfx)
                for n2 in range(2):
                    dve(lambda e, n2=n2: e.tensor_tensor(out=qkn[:, n2 * 512:(n2 + 1) * 512].rearrange("p (h d) -> p h d", h=8),
                                                         in0=banks[bQ[n2]][:].rearrange("p (h d) -> p h d", h=8),
                                                         in1=rs32_[:, 8 * n2:8 * n2 + 8].unsqueeze(2).to_broadcast([128, 8, 64]), op=ALU.mult),
                        reads=[B[bQ[n2]], "rs32" + sfx], writes=["qkn" + x_])
                pool(lambda e: e.tensor_tensor(out=qkb[:].rearrange("p (h d) -> p h d", h=16), in0=qkn[:].rearrange("p (h d) -> p h d", h=16),
                                               in1=gqk[:, u, :].unsqueeze(1).to_broadcast([128, 16, 64]), op=ALU.mult),
                     reads=["qkn" + x_, "gqk0", "gqk1"], writes=["qkb" + x_])
                for k in range(8):
                    pe(lambda e, k=k: e.matmul(banks[bV][:], lhsT=hTt[:, k, :], rhs=wqkv[:, k, 2048 + u * 512:2048 + (u + 1) * 512],
                                               start=(k == 0), stop=(k == 7)),
                       reads=[hr, ("wqkv", k)], writes=[B[bV]])
                act(lambda e: e.copy(out=vst[:], in_=banks[bV][:].rearrange("p (h d) -> p h d", h=4)), reads=[B[bV]], writes=["vst" + x_])
                P.dma("pool", lambda e: e.dma_start(out=nv_scr[4 * u:4 * u + 4, :, t, :].rearrange("h p d -> p h d"), in_=vst[:]),
                      reads=["vst" + x_], writes=[("nv_scr", n)])
                pT = bbf(bT)
                for j in range(8):
                    pe(lambda e, j=j: e.transpose(out=pT[:, j * 128:(j + 1) * 128], in_=qkb[:, j * 128:(j + 1) * 128], identity=ident[:]),
                       reads=["qkb" + x_, "ident"], writes=[B[bT]])
                act(lambda e: e.copy(out=qkst[:], in_=pT[:, 0:1024].rearrange("p (j t) -> p j t", j=8)), reads=[B[bT]], writes=["qkst" + x_])
                dst = nq_scr if u == 0 else nk_scr
                P.dma("pool", lambda e: e.dma_start(out=dst[:, :, t * 128:(t + 1) * 128].rearrange("h p t -> p h t"), in_=qkst[:]),
                      reads=["qkst" + x_], writes=[("nqk_scr", n)])

            P.pipeline(2 * NT, unit)
            barrier()

    def phase_G2():
        with contextlib.ExitStack() as S:
            cT = sb(S, "cT", [128, 8, WT], BF16)
            with contextlib.ExitStack() as S2:
                qT = sb(S2, "nqT", [128, WT], BF16)
                kT = sb(S2, "nkT", [128, WT], BF16)
                va = [sb(S2, "nva", [128, NT, 128], BF16) for _ in range(2)]
                bias32 = sb(S2, "nbias", [128, 13, 128], F32)
                bh = [sb(S2, "nbh", [128, 25, 128], BF16) for _ in range(2)]
                bl = [sb(S2, "nbl", [128, 25, 128], BF16) for _ in range(2)]
                E = [sb(S2, "nE", [128, 512], BF16) for _ in range(3)]
                bg = [SQ[1][:, 0:640].rearrange("p (a b) -> p a b", a=5), gn[:, 0:640].rearrange("p (a b) -> p a b", a=5)]
                St = [SQ[0][:, 0:512], SQ[0][:, 512:1024]]
                rec = [sb(S2, "rec", [128, 512], F32)] * 2
                dve(lambda e: e.memset(va[0][:, :, 64:128], 1.0), writes=[("va1", 0)])
                dve(lambda e: e.memset(va[1][:, :, 0:64], 1.0), writes=[("va1", 1)])
                scale = 64 ** -0.5
                inv_scale = 8.0

                def row_cls(r):
                    return {0: 1, 2: 2, 36: 3, 38: 4}.get(r, 0)

                def prep_bias(h):
                    p = h % 2
                    P.dma("sp", lambda e, h=h, p=p: e.dma_start(out=bg[p], in_=nab[h][:, 0:5, :]), writes=[("bg", p)])
                    for (v0, v1) in ((0, 13), (13, 25)):
                        nv = v1 - v0
                        P.dma("sp", lambda e, h=h, v0=v0, v1=v1, nv=nv: e.dma_start(out=bias32[:, 0:nv, :], in_=nab[h][:, v0:v1, :]), writes=["bias32"])
                        dve(lambda e, p=p, v0=v0, v1=v1, nv=nv: e.tensor_scalar(out=bh[p][:, v0:v1, :], in0=bias32[:, 0:nv, :], scalar1=inv_scale, scalar2=None, op0=ALU.mult),
                            reads=["bias32"], writes=[("bh", p)])
                        dve(lambda e, p=p, v0=v0, v1=v1, nv=nv: e.scalar_tensor_tensor(out=bl[p][:, v0:v1, :].rearrange("p a b -> p (a b)"),
                                                                                      in0=bias32[:, 0:nv, :].rearrange("p a b -> p (a b)"), scalar=inv_scale,
                                                                                      in1=bh[p][:, v0:v1, :].rearrange("p a b -> p (a b)"), op0=ALU.mult, op1=ALU.subtract),
                            reads=["bias32", ("bh", p)], writes=[("bl", p)])

                prep_bias(0)
                for hp in range(8):
                    P.dma("sp", lambda e, hp=hp: e.dma_start(out=qT[:], in_=nq_scr[hp]), writes=["nqT"])
                    P.dma("sp", lambda e, hp=hp: e.dma_start(out=kT[:], in_=nk_scr[hp]), writes=["nkT"])
                    P.dma("sp", lambda e, hp=hp: e.dma_start(out=va[0][:, :, 0:64], in_=nv_scr[hp][:, :, 0:64]), writes=[("va", 0)])
                    P.dma("sp", lambda e, hp=hp: e.dma_start(out=va[1][:, :, 64:128], in_=nv_scr[hp][:, :, 64:128]), writes=[("va", 1)])
                    for p in range(2):
                        h = 2 * hp + p
                        lo, hi = p * 64, p * 64 + 64
                        dlo, dhi = (1 - p) * 64, (1 - p) * 64 + 64
                        items = [(blk, j) for blk in range(NB) for j in range(5)]

                        def S_stage(i, p=p, lo=lo, hi=hi):
                            blk, j = items[i]
                            sbk = i % 3
                            groups = []
                            for rp in range(4):
                                c = row_cls(8 * blk + 2 * rp)
                                if groups and groups[-1][0] == c:
                                    groups[-1][2] += 1
                                else:
                                    groups.append([c, rp, 1])
                            first = True
                            use_dve = (len(groups) == 1 and i % 2 == 1)
                            for src in (() if use_dve else (bh, bl)):
                                for (c, rp0, n) in groups:
                                    pe(lambda e, src=src, c=c, rp0=rp0, n=n, j=j, sbk=sbk, first=first: e.matmul(
                                            banks[sbk][:, rp0 * 128:(rp0 + n) * 128], lhsT=ident[:],
                                            rhs=src[p][:, c * 5 + j, :].unsqueeze(1).to_broadcast([128, n, 128]),
                                            start=first, stop=False, skip_group_check=True),
                                       reads=[("bh", p), ("bl", p), "ident"], writes=[B[sbk]])
                                    first = False
                            for rp in range(4):
                                r = 8 * blk + 2 * rp
                                tb = min(max(r - 4, 0), 30)
                                kt0 = (tb + 2 * j) * 64
                                pe(lambda e, kt0=kt0, r=r, sbk=sbk, rp=rp: e.matmul(banks[sbk][:, rp * 128:(rp + 1) * 128], lhsT=kT[lo:hi, kt0:kt0 + 128],
                                                                                   rhs=qT[lo:hi, r * 64:r * 64 + 128], start=(use_dve and rp == 0), stop=(rp == 3), skip_group_check=True),
                                   reads=["nkT", "nqT"], writes=[B[sbk]])

                        def mid_stage(i, p=p):
                            sbk = i % 3
                            blk, j = items[i]
                            cls = set(row_cls(8 * blk + 2 * rp) for rp in range(4))
                            if len(cls) == 1 and i % 2 == 1:
                                st = St[(i // 2) % 2]
                                sr = ("St", (i // 2) % 2)
                                dve(lambda e, sbk=sbk, st=st, j=j: e.scalar_tensor_tensor(out=st.rearrange("p (a b) -> p a b", a=4),
                                                                                        in0=banks[sbk][:].rearrange("p (a b) -> p a b", a=4), scalar=scale,
                                                                                        in1=bg[p][:, j, :].unsqueeze(1).to_broadcast([128, 4, 128]), op0=ALU.mult, op1=ALU.add),
                                    reads=[B[sbk], ("bg", p)], writes=[sr])
                                act(lambda e, i=i, st=st: e.activation(out=E[i % 3][:], in_=st, func=AF.Exp), reads=[sr], writes=[("nE", i % 3)])
                            else:
                                act(lambda e, i=i, sbk=sbk: e.activation(out=E[i % 3][:], in_=banks[sbk][:], func=AF.Exp, scale=scale),
                                    reads=[B[sbk]], writes=[("nE", i % 3)])

                        def PV_stage(i, p=p, lo=lo, hi=hi, dlo=dlo, dhi=dhi, hp=hp):
                            blk, j = items[i]
                            ob = 3 + (blk % 2)
                            for rp in range(4):
                                r = 8 * blk + 2 * rp
                                tb = min(max(r - 4, 0), 30)
                                vt = (tb + 2 * j) // 2
                                pe(lambda e, vt=vt, i=i, ob=ob, rp=rp, j=j: e.matmul(banks[ob][:, rp * 128:(rp + 1) * 128], lhsT=va[p][:, vt, :],
                                                                                    rhs=E[i % 3][:, rp * 128:(rp + 1) * 128],
                                                                                    start=(j == 0 and rp == 0), stop=(j == 4), skip_group_check=True),
                                   reads=[("va", p), ("va1", p), ("nE", i % 3)], writes=[B[ob]])
                            if j == 4:
                                for step in range(3):
                                    pending.append((i + 1 + step, lambda blk=blk, ob=ob, step=step: epilogue(blk, ob, step)))

                        def epilogue(blk, ob, step, p=p, lo=lo, hi=hi, dlo=dlo, dhi=dhi, hp=hp):
                            rc = rec[blk % 2]
                            rr = ("rec", 0)
                            if step == 0:
                                dve(lambda e, ob=ob, rc=rc: e.tensor_copy(out=rc[lo:hi, :], in_=banks[ob][dlo:dhi, :]), reads=[B[ob]], writes=[rr])
                            elif step == 1:
                                act(lambda e, rc=rc: e.activation(out=rc[lo:hi, :], in_=rc[lo:hi, :], func=AF.Ln), reads=[rr], writes=[rr])
                                act(lambda e, rc=rc: e.activation(out=rc[lo:hi, :], in_=rc[lo:hi, :], func=AF.Exp, scale=-1.0), reads=[rr], writes=[rr])
                            else:
                                dve(lambda e, ob=ob, rc=rc, blk=blk: e.tensor_tensor(out=cT[lo:hi, hp, blk * 512:(blk + 1) * 512], in0=banks[ob][lo:hi, :],
                                                                                    in1=rc[lo:hi, :], op=ALU.mult),
                                    reads=[B[ob], rr], writes=[("cT", hp)])

                        n_it = len(items)
                        pending = []
                        S_stage(0)
                        S_stage(1)
                        for i in range(n_it):
                            mid_stage(i)
                            PV_stage(i)
                            if i + 2 < n_it:
                                S_stage(i + 2)
                            if i == 4 and h + 1 < 16:
                                prep_bias(h + 1)
                            pending.sort(key=lambda q: q[0])
                            while pending and pending[0][0] <= i:
                                pending.pop(0)[1]()
                        pending.sort(key=lambda q: q[0])
                        while pending:
                            pending.pop(0)[1]()
                barrier()
            with contextlib.ExitStack() as S2:
                wout = sb(S2, "wouto", [128, 8, D], BF16)
                load_w(wout, w_out_o[0], "wouto")
                for t in range(NT):
                    proj_resid(t, lambda k, t=t: cT[:, k, t * 128:(t + 1) * 128], [("cT", g) for g in range(8)], 8, wout, "wouto")
                barrier()

    fins = []
    import os
    PH = os.environ.get("PHASES", "A,B1,B2,C").split(",")
    if stage >= 1:
        if "A" in PH:
            phase_A()
        if "B1" in PH:
            phase_B1()
        if "B2" in PH:
            phase_B2()
        if "C" in PH:
            phase_C()
    if stage >= 2:
        phase_mem()
        phase_xattn(0)
    if stage >= 3:
        phase_mlp(0)
    if stage >= 4:
        phase_G1()
        phase_G2()
    if stage >= 5:
        phase_xattn(1)
        phase_mlp(1, last=True)
    if not fins:
        for t in range(NT):
            fins.append(P.dma("sp", lambda e, t=t: e.dma_start(out=out[t * 128:(t + 1) * 128, :], in_=x[:, t, :]), reads=[("x", t)]))
    P.emit(nc, final_ops=fins)
    return nc, P


def _rope_table(pos):
    half = 16
    freqs = (np.float32(10000.0) ** (-np.arange(half, dtype=np.float32) / np.float32(half))).astype(np.float32)
    ang = (pos.astype(np.float32)[:, None] * freqs[None, :]).astype(np.float32)
    c = np.cos(ang).astype(np.float32)
    s = np.sin(ang).astype(np.float32)
    return np.concatenate([c, c, -s, s], axis=1).astype(np.float32)


def _invc_table(a_tok):
    tab = np.ones((4, 2568), np.float32)
    tg = a_tok + np.arange(2568) - 8
    valid = (tg >= 0) & (tg < SEQ)
    for g, w in enumerate((2, 4, 8, 16)):
        lo = np.clip(tg - w // 2, 0, SEQ - 1)
        hi = np.clip(tg + w - 1 - w // 2, 0, SEQ - 1)
        cnt = (hi - lo + 1).astype(np.float32)
        tab[g] = np.where(valid, np.float32(1.0) / cnt, np.float32(1.0))
    return tab


def _natten_bias(rpb):
    H = rpb.shape[0]
    outb = np.full((H, 25, 128, 128), NEG, np.float32)
    cols = np.arange(64)
    c0 = np.clip(cols - 8, 0, 48)
    classes = {0: 8, 1: 0, 2: 2, 3: 36, 4: 38}
    kc = np.arange(64)[:, None]
    qc = np.arange(64)[None, :]
    colvalid = (kc >= c0[None, :]) & (kc < c0[None, :] + 16)
    dc = np.clip(kc - qc + 15, 0, 30)
    for cls, r in classes.items():
        tb = min(max(r - 4, 0), 30)
        for j in range(5):
            for kr_i in range(2):
                kr = tb + 2 * j + kr_i
                for qr_i in range(2):
                    qr = r + qr_i
                    r0 = min(max(qr - 4, 0), 32)
                    if not (r0 <= kr <= r0 + 7):
                        continue
                    dr = kr - qr + 7
                    vals = rpb[:, dr][:, dc]
                    blk = np.where(colvalid[None], vals, np.float32(NEG))
                    outb[:, cls * 5 + j, kr_i * 64:(kr_i + 1) * 64, qr_i * 64:(qr_i + 1) * 64] = blk
    return np.ascontiguousarray(outb.transpose(0, 2, 1, 3))


def make_in_maps(inputs):
    f = lambda a: np.ascontiguousarray(np.asarray(a, dtype=np.float32))
    xfull = f(inputs["x"])
    memf = f(inputs["mem"])
    shared = {k: f(v) for k, v in inputs.items() if k not in ("x", "mem", "na_rpb")}
    nabt = _natten_bias(f(inputs["na_rpb"])[0])
    pos_b = np.arange(SEQ)
    cs_b = _rope_table(pos_b)
    maps, meta = [], []
    for c in range(8):
        b, j = c // 4, c % 4
        a = min(max(32 * j - 4, 0), 88)
        a_tok = a * 64
        xw = xfull[b, a_tok:a_tok + WT]
        xh = np.zeros((16, D), np.float32)
        if a_tok >= 8:
            xh[0:8] = xfull[b, a_tok - 8:a_tok]
        if a_tok + WT + 8 <= SEQ:
            xh[8:16] = xfull[b, a_tok + WT:a_tok + WT + 8]
        m = dict(shared)
        m.update(xw=np.ascontiguousarray(xw), xh=xh, xb=xfull[b], mem=memf[b],
                 cs_w=np.ascontiguousarray(cs_b[a_tok:a_tok + WT]), cs_b=cs_b, invc=_invc_table(a_tok), nab=nabt)
        maps.append(m)
        meta.append((b, a, 32 * j - a))
    return maps, meta


_CACHE = {}


def kernel(**inputs):
    if "nc" not in _CACHE:
        _CACHE["nc"] = build_program(99)[0]
    nc = _CACHE["nc"]
    maps, meta = make_in_maps(inputs)
    res = run_bass_kernel_spmd(nc, maps, core_ids=list(range(8)))
    outp = np.zeros((2, SEQ, D), np.float32)
    for c in range(8):
        b, a, off = meta[c]
        o = np.asarray(res.results[c]["out"]).reshape(WT, D)
        j = c % 4
        outp[b, j * 2048:(j + 1) * 2048] = o[off * 64:off * 64 + 2048]
    return outp
```
